# Optimizing a Trainium2 kernel written in Bass

```python
import jax, jax.numpy as jnp
from jax import lax
import numpy as np

D_MODEL = 2048
BATCH = 2
SEQ = 4096
DEPTH = 2

CHUNK = 64
NORM_EPS = 1e-6

A_HEADS = 12
A_HEAD_DIM = 64
A_WIDTH = A_HEADS * A_HEAD_DIM
DECAY_LORA = 96
ICLR_LORA = 96
VRES_LORA = 64
GATE_LORA = 256
A_GN_EPS = 64e-5
A_COLS = 3 * A_WIDTH + DECAY_LORA + ICLR_LORA + GATE_LORA
A_SPLITS = (A_WIDTH, 2 * A_WIDTH, 3 * A_WIDTH, 3 * A_WIDTH + DECAY_LORA, 3 * A_WIDTH + DECAY_LORA + ICLR_LORA)

B_WIDTH = 512
B_BLOCKS = 8
B_BLOCK_DIM = B_WIDTH // B_BLOCKS
CONV_WIDTH = 4
LRU_C = 8.0
B_COLS = 2 * B_WIDTH

C_HEADS = 6
C_QK_DIM = 64
C_V_DIM = 128
C_QK_WIDTH = C_HEADS * C_QK_DIM
C_WIDTH = C_HEADS * C_V_DIM
C_GN_EPS = 1e-5
ROPE_THETA = 10000.0
C_COLS = 2 * C_QK_WIDTH + 2 * C_WIDTH
C_SPLITS = (C_QK_WIDTH, 2 * C_QK_WIDTH, 2 * C_QK_WIDTH + C_WIDTH)

MIX_WIDTH = A_WIDTH + B_WIDTH + C_WIDTH
IN_COLS = A_COLS + B_COLS + C_COLS
D_FF = ((8 * D_MODEL // 3 + 255) // 256) * 256

kernel_name = "hybrid_rwkv7_rglru_retention_block"


def rms_norm(x, g):
    xf = x.astype(jnp.float32)
    y = xf * lax.rsqrt(jnp.mean(xf * xf, axis=-1, keepdims=True) + NORM_EPS)
    return (y * g.astype(jnp.float32)).astype(x.dtype)


def head_group_norm(x, n_heads, eps):
    b, s, w = x.shape
    xf = x.astype(jnp.float32).reshape(b, s, n_heads, w // n_heads)
    mu = jnp.mean(xf, axis=-1, keepdims=True)
    var = jnp.mean(jnp.square(xf - mu), axis=-1, keepdims=True)
    return ((xf - mu) * lax.rsqrt(var + eps)).reshape(b, s, w)


def token_shift(z):
    return jnp.pad(z[:, :-1], ((0, 0), (1, 0), (0, 0)))


def rwkv7_scan(r, w, k, v, kk, b):
    def step(state, inp):
        r_t, w_t, k_t, v_t, kk_t, b_t = inp
        sa = -jnp.einsum('bhij,bhj->bhi', state, kk_t)
        state = (state * w_t[:, :, None, :] + sa[..., None] * b_t[:, :, None, :]
                 + v_t[..., None] * k_t[:, :, None, :])
        return state, jnp.einsum('bhij,bhj->bhi', state, r_t)
    bsz, _, h, n = r.shape
    xs = tuple(jnp.moveaxis(t, 1, 0) for t in (r, w, k, v, kk, b))
    _, y = lax.scan(step, jnp.zeros((bsz, h, n, n), jnp.float32), xs)
    return jnp.moveaxis(y, 0, 1)


def rwkv7_mixer(r, k, v, wd, ad, gd, w0, w2, a0, a2, g2, k_k, k_a, r_k, ln_g, ln_b, v_first, vres):
    f32 = jnp.float32
    bsz, s, _ = r.shape
    hd = lambda t: t.reshape(bsz, s, A_HEADS, A_HEAD_DIM)
    r, k, v, wd, ad, gd = (t.astype(f32) for t in (r, k, v, wd, ad, gd))
    log_w = -jax.nn.softplus(-(w0 + jnp.tanh(wd) @ w2)) - 0.5
    decay = jnp.exp(-jnp.exp(log_w))
    a = jax.nn.sigmoid(a0 + ad @ a2)
    g = jax.nn.sigmoid(gd) @ g2
    if vres is not None:
        vd, v0, v2 = vres
        v = v + (v_first - v) * jax.nn.sigmoid(v0 + vd.astype(f32) @ v2)
    kk = hd(k * k_k)
    kk = kk / jnp.maximum(jnp.linalg.norm(kk, axis=-1, keepdims=True), 1e-12)
    k = k * (1.0 + (a - 1.0) * k_a)
    y = rwkv7_scan(hd(r), hd(decay), hd(k), hd(v), kk, kk * hd(a))
    y = head_group_norm(y.reshape(bsz, s, A_WIDTH), A_HEADS, A_GN_EPS) * ln_g + ln_b
    bonus = jnp.sum(hd(r) * hd(k) * r_k, axis=-1, keepdims=True) * hd(v)
    return (y + bonus.reshape(bsz, s, A_WIDTH)) * g, v


def rglru_mixer(gate_in, x_in, conv_w, conv_b, wa, ba, wx, bx, lam):
    f32 = jnp.float32
    bsz, s, _ = x_in.shape
    xc = lax.conv_general_dilated(x_in, conv_w[:, None, :], window_strides=(1,),
                                  padding=[(CONV_WIDTH - 1, 0)],
                                  dimension_numbers=('NWC', 'WIO', 'NWC'),
                                  feature_group_count=B_WIDTH) + conv_b
    xb = xc.reshape(bsz, s, B_BLOCKS, B_BLOCK_DIM)
    r = jax.nn.sigmoid(jnp.einsum('bsgi,gij->bsgj', xb, wa).reshape(bsz, s, B_WIDTH) + ba).astype(f32)
    i = jax.nn.sigmoid(jnp.einsum('bsgi,gij->bsgj', xb, wx).reshape(bsz, s, B_WIDTH) + bx).astype(f32)
    log_a = -LRU_C * r * jax.nn.softplus(-lam.astype(f32))
    a = jnp.exp(log_a)
    u = jnp.sqrt(-jnp.expm1(2.0 * log_a)) * (i * xc.astype(f32))

    def combine(e1, e2):
        a1, b1 = e1
        a2, b2 = e2
        return a1 * a2, a2 * b1 + b2

    _, h = lax.associative_scan(combine, (a, u), axis=1)
    return jax.nn.gelu(gate_in.astype(f32)) * h


def rope_tables(positions):
    inv_freq = ROPE_THETA ** (-jnp.arange(0, C_QK_DIM, 2, dtype=jnp.float32) / C_QK_DIM)
    ang = positions.astype(jnp.float32)[..., None] * inv_freq
    return jnp.cos(ang)[:, :, None, :], jnp.sin(ang)[:, :, None, :]


def apply_rope(t, cos, sin):
    t1, t2 = jnp.split(t, 2, axis=-1)
    return jnp.concatenate([t1 * cos - t2 * sin, t2 * cos + t1 * sin], axis=-1)


def retention_mixer(q, k, v, g, cos, sin, log_gamma, gn_g):
    f32 = jnp.float32
    bsz, s, _ = q.shape
    nc = s // CHUNK
    q = apply_rope(q.astype(f32).reshape(bsz, s, C_HEADS, C_QK_DIM), cos, sin)
    k = apply_rope(k.astype(f32).reshape(bsz, s, C_HEADS, C_QK_DIM), cos, sin) * (C_QK_DIM ** -0.5)
    v = v.astype(f32).reshape(bsz, s, C_HEADS, C_V_DIM)
    qc = q.reshape(bsz, nc, CHUNK, C_HEADS, C_QK_DIM)
    kc = k.reshape(bsz, nc, CHUNK, C_HEADS, C_QK_DIM)
    vc = v.reshape(bsz, nc, CHUNK, C_HEADS, C_V_DIM)
    idx = jnp.arange(CHUNK, dtype=f32)
    dist = jnp.abs(idx[:, None] - idx[None, :])
    intra_decay = jnp.exp(log_gamma[:, None, None] * dist)
    scores = jnp.einsum('bnchd,bnmhd->bnhcm', qc, kc) * intra_decay
    intra = jnp.einsum('bnhcm,bnmhe->bnche', scores, vc)
    k_decay = jnp.exp(log_gamma[None, :] * (CHUNK - 1 - idx)[:, None])
    kv = jnp.einsum('bnmhd,bnmhe->bnhde', kc * k_decay[:, :, None], vc)
    chunk_decay = jnp.exp(log_gamma * CHUNK)[None, :, None, None]

    def step(state, kv_n):
        return state * chunk_decay + kv_n, state

    _, states = lax.scan(step, jnp.zeros((bsz, C_HEADS, C_QK_DIM, C_V_DIM), f32),
                         jnp.moveaxis(kv, 1, 0))
    states = jnp.moveaxis(states, 0, 1)
    q_decay = jnp.exp(log_gamma[None, :] * (idx + 1.0)[:, None])
    cross = jnp.einsum('bnchd,bnhde->bnche', qc * q_decay[:, :, None], states)
    o = (intra + cross).reshape(bsz, s, C_WIDTH)
    o = head_group_norm(o, C_HEADS, C_GN_EPS) * gn_g
    return jax.nn.silu(g.astype(f32)) * o


def setup_inputs(seed: int = 0) -> dict:
    key = jax.random.key(seed)
    ks = iter(jax.random.split(key, 48))
    nrm = lambda shape, scale: jax.random.normal(next(ks), shape, jnp.float32) * scale
    L = DEPTH
    L1 = DEPTH - 1
    x = nrm((BATCH, SEQ, D_MODEL), 1.0)
    start = jax.random.randint(next(ks), (BATCH, 1), 0, 4096, dtype=jnp.int32)
    positions = start + jnp.arange(SEQ, dtype=jnp.int32)[None, :]
    ratio = jnp.linspace(0.0, 1.0, A_WIDTH, dtype=jnp.float32)
    decay_speed = -7.0 + 5.0 * ratio ** 0.85
    u = jax.random.uniform(next(ks), (L, B_WIDTH), jnp.float32, minval=0.9, maxval=0.999)
    base = u ** (1.0 / LRU_C)
    return {
        "x": x,
        "positions": positions,
        "norm1_g": 1.0 + nrm((L, D_MODEL), 0.02),
        "w_in": nrm((L, D_MODEL, IN_COLS), D_MODEL ** -0.5),
        "tshift_mu": jax.random.uniform(next(ks), (L, A_COLS), jnp.float32),
        "rwkv_w0": decay_speed + 0.5 + nrm((L, A_WIDTH), 0.1),
        "rwkv_w2": nrm((L, DECAY_LORA, A_WIDTH), 0.1),
        "rwkv_a0": nrm((L, A_WIDTH), 0.1),
        "rwkv_a2": nrm((L, ICLR_LORA, A_WIDTH), ICLR_LORA ** -0.5),
        "rwkv_g2": nrm((L, GATE_LORA, A_WIDTH), GATE_LORA ** -0.5),
        "rwkv_k_k": 0.85 + nrm((L, A_WIDTH), 0.02),
        "rwkv_k_a": 1.0 + nrm((L, A_WIDTH), 0.02),
        "rwkv_r_k": nrm((L, A_HEADS, A_HEAD_DIM), 0.1),
        "rwkv_ln_g": 1.0 + nrm((L, A_WIDTH), 0.02),
        "rwkv_ln_b": nrm((L, A_WIDTH), 0.02),
        "w_in_vres": nrm((L1, D_MODEL, VRES_LORA), D_MODEL ** -0.5),
        "tshift_mu_vres": jax.random.uniform(next(ks), (L1, VRES_LORA), jnp.float32),
        "rwkv_v0": 1.0 + nrm((L1, A_WIDTH), 0.1),
        "rwkv_v2": nrm((L1, VRES_LORA, A_WIDTH), VRES_LORA ** -0.5),
        "lru_conv_w": nrm((L, CONV_WIDTH, B_WIDTH), CONV_WIDTH ** -0.5),
        "lru_conv_b": nrm((L, B_WIDTH), 0.02),
        "lru_wa": nrm((L, B_BLOCKS, B_BLOCK_DIM, B_BLOCK_DIM), B_BLOCK_DIM ** -0.5),
        "lru_ba": nrm((L, B_WIDTH), 0.02),
        "lru_wx": nrm((L, B_BLOCKS, B_BLOCK_DIM, B_BLOCK_DIM), B_BLOCK_DIM ** -0.5),
        "lru_bx": nrm((L, B_WIDTH), 0.02),
        "lru_lambda": jnp.log(base) - jnp.log1p(-base),
        "ret_gn_g": 1.0 + nrm((L, C_WIDTH), 0.02),
        "w_out": nrm((L, MIX_WIDTH, D_MODEL), MIX_WIDTH ** -0.5),
        "norm2_g": 1.0 + nrm((L, D_MODEL), 0.02),
        "ffn_w_gate": nrm((L, D_MODEL, D_FF), D_MODEL ** -0.5),
        "ffn_w_up": nrm((L, D_MODEL, D_FF), D_MODEL ** -0.5),
        "ffn_w_down": nrm((L, D_FF, D_MODEL), D_FF ** -0.5),
        "final_norm_g": 1.0 + nrm((D_MODEL,), 0.02),
    }


def reference(x, positions, norm1_g, w_in, tshift_mu, rwkv_w0, rwkv_w2, rwkv_a0, rwkv_a2, rwkv_g2,
              rwkv_k_k, rwkv_k_a, rwkv_r_k, rwkv_ln_g, rwkv_ln_b, w_in_vres, tshift_mu_vres, rwkv_v0,
              rwkv_v2, lru_conv_w, lru_conv_b, lru_wa, lru_ba, lru_wx, lru_bx, lru_lambda, ret_gn_g,
              w_out, norm2_g, ffn_w_gate, ffn_w_up, ffn_w_down, final_norm_g):
    cos, sin = rope_tables(positions)
    log_gamma = jnp.log1p(-jnp.exp2(-5.0 - jnp.arange(C_HEADS, dtype=jnp.float32)))
    v_first = None
    for l in range(DEPTH):
        h = rms_norm(x, norm1_g[l])
        w = w_in[l] if l == 0 else jnp.concatenate([w_in[l], w_in_vres[l - 1]], axis=1)
        proj = h @ w
        a_cols = proj[..., :A_COLS]
        b_cols = proj[..., A_COLS:A_COLS + B_COLS]
        c_cols = proj[..., A_COLS + B_COLS:IN_COLS]
        a_cols = a_cols + (token_shift(a_cols) - a_cols) * tshift_mu[l]
        r_a, k_a, v_a, wd, ad, gd = jnp.split(a_cols, A_SPLITS, axis=-1)
        if l == 0:
            vres = None
        else:
            vd = proj[..., IN_COLS:]
            vd = vd + (token_shift(vd) - vd) * tshift_mu_vres[l - 1]
            vres = (vd, rwkv_v0[l - 1], rwkv_v2[l - 1])
        y_a, v_cur = rwkv7_mixer(r_a, k_a, v_a, wd, ad, gd, rwkv_w0[l], rwkv_w2[l], rwkv_a0[l],
                                 rwkv_a2[l], rwkv_g2[l], rwkv_k_k[l], rwkv_k_a[l], rwkv_r_k[l],
                                 rwkv_ln_g[l], rwkv_ln_b[l], v_first, vres)
        if l == 0:
            v_first = v_cur
        gate_b, x_b = jnp.split(b_cols, 2, axis=-1)
        y_b = rglru_mixer(gate_b, x_b, lru_conv_w[l], lru_conv_b[l], lru_wa[l], lru_ba[l],
                          lru_wx[l], lru_bx[l], lru_lambda[l])
        q_c, k_c, v_c, g_c = jnp.split(c_cols, C_SPLITS, axis=-1)
        y_c = retention_mixer(q_c, k_c, v_c, g_c, cos, sin, log_gamma, ret_gn_g[l])
        y = jnp.concatenate([y_a, y_b, y_c], axis=-1).astype(x.dtype)
        x = x + y @ w_out[l]
        h = rms_norm(x, norm2_g[l])
        x = x + (jax.nn.silu(h @ ffn_w_gate[l]) * (h @ ffn_w_up[l])) @ ffn_w_down[l]
    return rms_norm(x, final_norm_g)
```

```python
import math
from contextlib import ExitStack
import numpy as np
import concourse.bass as bass
import concourse.mybir as mybir
from concourse.bass_utils import run_bass_kernel_spmd

F32 = mybir.dt.float32
BF16 = mybir.dt.bfloat16
I32 = mybir.dt.int32
AF = mybir.ActivationFunctionType
ALU = mybir.AluOpType
AX = mybir.AxisListType


class Buf:
    __slots__ = ("name", "last_w", "readers")

    def __init__(self, name=""):
        self.name = name
        self.last_w = None
        self.readers = []


class Prog:
    ENGS = ("sync", "scalar", "vector", "gpsimd", "tensor")

    def __init__(self, nc, n_dma_sems=12, self_sync=True):
        self.nc = nc
        self.self_sync = self_sync
        self.ops = {e: [] for e in self.ENGS}
        self.count = {e: 0 for e in self.ENGS}
        self.waited = {e: {} for e in self.ENGS}
        self.n_dma_sems = n_dma_sems
        self.dma_i = {"sync": 0, "gpsimd": 0, "scalar": 0}
        self.sems = {}
        self._ctx = []

    def _sem(self, key):
        if key not in self.sems:
            cm = self.nc.semaphore("s_" + key)
            self._ctx.append(cm)
            self.sems[key] = cm.__enter__()
        return self.sems[key]

    def close(self):
        for cm in reversed(self._ctx):
            cm.__exit__(None, None, None)

    def _collect(self, eng, reads, writes):
        need = {}
        def req(tok):
            if tok is None:
                return
            k, v = tok
            if need.get(k, 0) < v:
                need[k] = v
        for b in reads:
            req(b.last_w)
        for b in writes:
            req(b.last_w)
            for r in b.readers:
                req(r)
        waits = []
        wd = self.waited[eng]
        for k, v in need.items():
            if k == "e_" + eng and (eng == "tensor" or not self.self_sync):
                continue
            if wd.get(k, 0) >= v:
                continue
            wd[k] = v
            waits.append((k, v))
        return waits

    def op(self, eng, fn, reads=(), writes=()):
        waits = self._collect(eng, reads, writes)
        self.count[eng] += 1
        tok = ("e_" + eng, self.count[eng])
        self.ops[eng].append((waits, fn, tok[0], 1))
        for b in reads:
            b.readers.append(tok)
        for b in writes:
            b.last_w = tok
            b.readers = []
        return tok

    def dma(self, eng, out, in_, reads=(), writes=(), **kw):
        i = self.dma_i[eng]
        self.dma_i[eng] = i + 1
        key = "d_%s_%d" % (eng, i % self.n_dma_sems)
        val = 16 * (i // self.n_dma_sems + 1)
        waits = self._collect(eng, reads, writes)
        if i >= self.n_dma_sems:
            pv = val - 16
            if self.waited[eng].get(key, 0) < pv:
                self.waited[eng][key] = pv
                waits.append((key, pv))
        if eng != "sync":
            self.count[eng] += 0
        fn = lambda e, out=out, in_=in_, kw=kw: e.dma_start(out=out, in_=in_, **kw)
        self.ops[eng].append((waits, fn, key, 16))
        tok = (key, val)
        for b in reads:
            b.readers.append(tok)
        for b in writes:
            b.last_w = tok
            b.readers = []
        return tok

    def final_wait(self, eng, bufs):
        waits = self._collect(eng, bufs, ())
        self.ops[eng].append((waits, None, None, 0))

    def emit(self):
        nc = self.nc
        for e in self.ENGS:
            self._sem("e_" + e)
        for e in ("sync", "gpsimd", "scalar"):
            for i in range(min(self.n_dma_sems, self.dma_i[e])):
                self._sem("d_%s_%d" % (e, i))
        with nc.Block() as block:
            for e in self.ENGS:
                ops = self.ops[e]
                if not ops:
                    continue

                def body(engine, ops=ops):
                    for waits, fn, key, inc in ops:
                        for k, v in waits:
                            engine.wait_ge(self.sems[k], v)
                        if fn is not None:
                            fn(engine).then_inc(self.sems[key], inc)
                getattr(block, e)(body)
        self.close()


D = 2048
KC = 16
TOK = 1024
EPS = 1e-6


def rms_to_bf16(P, nc, es, xs, bxs, g_ap, name, out_tile=None, out_buf=None, scr=None):
    sb = lambda n, shp, dt: es.enter_context(nc.sbuf_tensor(name + n, shp, dt))
    ps_ = lambda n, shp, dt: es.enter_context(nc.psum_tensor(name + n, shp, dt))
    if scr is None:
        scr = {}
    if "ones" not in scr:
        scr["ones"] = sb("ones", [128, 128], F32)
        scr["sq"] = [sb("sq%d" % i, [128, TOK], F32) for i in range(2)]
        scr["rstd"] = sb("rstd", [128, TOK], F32)
        scr["pss"] = [ps_("ss%d" % i, [128, 512], F32) for i in range(2)]
        scr["eps"] = sb("eps", [128, 1], F32)
        scr["b"] = dict(ones=Buf(), rstd=Buf(), sq=[Buf(), Buf()], ps=[Buf(), Buf()], eps=Buf())
        P.op("gpsimd", lambda e: e.memset(scr["ones"][:], 1.0), writes=[scr["b"]["ones"]])
        P.op("gpsimd", lambda e: e.memset(scr["eps"][:], EPS), writes=[scr["b"]["eps"]])
    ones, sq, rstd, pss, epst = scr["ones"], scr["sq"], scr["rstd"], scr["pss"], scr["eps"]
    B = scr["b"]
    bones, brstd, bsq, bps, beps = B["ones"], B["rstd"], B["sq"], B["ps"], B["eps"]
    gs = sb("g", [128, KC], F32)
    hT = out_tile if out_tile is not None else sb("hT", [128, KC, TOK], BF16)
    bg = Buf()
    bh = out_buf if out_buf is not None else Buf()
    P.dma("sync", gs[:], g_ap, writes=[bg])
    for c in range(KC):
        s = sq[c % 2]
        P.op("scalar", lambda e, c=c, s=s: e.activation(out=s[:], in_=xs[:, c, :], func=AF.Square),
             reads=[bxs], writes=[bsq[c % 2]])
        for hf in range(2):
            P.op("tensor", lambda e, c=c, s=s, hf=hf: e.matmul(pss[hf][:], lhsT=ones[:], rhs=s[:, hf * 512:(hf + 1) * 512],
                                                               start=(c == 0), stop=(c == KC - 1)),
                 reads=[bones, bsq[c % 2]], writes=[bps[hf]])
    for hf in range(2):
        sl = slice(hf * 512, (hf + 1) * 512)
        P.op("scalar", lambda e, hf=hf, sl=sl: e.activation(out=rstd[:, sl], in_=pss[hf][:], func=AF.Sqrt,
                                                            scale=1.0 / D, bias=epst[:, 0:1]),
             reads=[bps[hf], beps], writes=[brstd])
        P.op("vector", lambda e, sl=sl: e.reciprocal(out=rstd[:, sl], in_=rstd[:, sl]), reads=[brstd], writes=[brstd])
    for c in range(KC):
        P.op("vector", lambda e, c=c: e.scalar_tensor_tensor(out=hT[:, c, :], in0=xs[:, c, :], scalar=gs[:, c:c + 1],
                                                         in1=rstd[:], op0=ALU.mult, op1=ALU.mult),
             reads=[bxs, bg, brstd], writes=[bh])
    return hT, bh


def build_p1(ncol):
    nc = bass.Bass("TRN2", target_bir_lowering=False)
    xT = nc.dram_tensor("xT", [D, TOK], F32, kind="ExternalInput").ap()
    w = nc.dram_tensor("w", [D, ncol], F32, kind="ExternalInput").ap()
    g = nc.dram_tensor("g", [128, KC], F32, kind="ExternalInput").ap()
    out = nc.dram_tensor("projT", [ncol, TOK], F32, kind="ExternalOutput").ap()
    P = Prog(nc)
    CB = 256
    nblk = (ncol + CB - 1) // CB
    with ExitStack() as es:
        sb = lambda n, shp, dt: es.enter_context(nc.sbuf_tensor(n, shp, dt))
        xs = sb("xs", [128, KC, TOK], F32)
        bxs = Buf()
        xv = xT.rearrange("(c p) t -> p c t", p=128)
        for i in range(4):
            P.dma("sync", xs[:, i * 4:(i + 1) * 4, :], xv[:, i * 4:(i + 1) * 4, :], writes=[bxs])
        hT, bh = rms_to_bf16(P, nc, es, xs, bxs, g, "n1")
        wf = [sb("wf%d" % i, [128, KC, CB], F32) for i in range(2)]
        wb = [sb("wb%d" % i, [128, KC, CB], BF16) for i in range(2)]
        ot = [sb("ot%d" % i, [128, 512], F32) for i in range(4)]
        pt = [es.enter_context(nc.psum_tensor("pt%d" % i, [128, 512], F32)) for i in range(4)]
        bwf, bwb = [Buf(), Buf()], [Buf(), Buf()]
        bot = [Buf() for _ in range(4)]
        bpt = [Buf() for _ in range(4)]
        bout = Buf()
        wv = w.rearrange("(c p) n -> p c n", p=128)
        k = 0
        for j in range(nblk):
            c0 = j * CB
            cw = min(CB, ncol - c0)
            s = j % 2
            P.dma("sync" if j % 2 == 0 else "gpsimd", wf[s][:, :, :cw], wv[:, :, c0:c0 + cw], writes=[bwf[s]])
            half = KC // 2
            P.op("gpsimd", lambda e, s=s, cw=cw: e.tensor_copy(out=wb[s][:, :half, :cw], in_=wf[s][:, :half, :cw]),
                 reads=[bwf[s]], writes=[bwb[s]])
            P.op("scalar", lambda e, s=s, cw=cw: e.copy(out=wb[s][:, half:, :cw], in_=wf[s][:, half:, :cw]),
                 reads=[bwf[s]], writes=[bwb[s]])
            for m0 in range(0, cw, 128):
                mw = min(128, cw - m0)
                for hf in range(2):
                    q = k % 4
                    k += 1
                    for c in range(KC):
                        P.op("tensor", lambda e, s=s, c=c, m0=m0, mw=mw, hf=hf, q=q: e.matmul(
                            pt[q][:mw, :], lhsT=wb[s][:, c, m0:m0 + mw], rhs=hT[:, c, hf * 512:(hf + 1) * 512],
                            start=(c == 0), stop=(c == KC - 1)), reads=[bwb[s], bh], writes=[bpt[q]])
                    if q % 2 == 0:
                        P.op("vector", lambda e, q=q, mw=mw: e.tensor_copy(out=ot[q][:mw, :], in_=pt[q][:mw, :]),
                             reads=[bpt[q]], writes=[bot[q]])
                    else:
                        P.op("scalar", lambda e, q=q, mw=mw: e.copy(out=ot[q][:mw, :], in_=pt[q][:mw, :]),
                             reads=[bpt[q]], writes=[bot[q]])
                    P.dma("sync", out[c0 + m0:c0 + m0 + mw, hf * 512:(hf + 1) * 512], ot[q][:mw, :],
                          reads=[bot[q]], writes=[bout])
        P.final_wait("sync", [bout])
        P.emit()
    return nc


DFF = 5632
NFB = DFF // 128
GRP = 4


def build_p3(final):
    nc = bass.Bass("TRN2", target_bir_lowering=False)
    xT = nc.dram_tensor("xT", [D, TOK], F32, kind="ExternalInput").ap()
    yT = nc.dram_tensor("yT", [D, TOK], F32, kind="ExternalInput").ap()
    wout = nc.dram_tensor("wout", [D, D], F32, kind="ExternalInput").ap()
    g2 = nc.dram_tensor("g2", [128, KC], F32, kind="ExternalInput").ap()
    wg = nc.dram_tensor("wg", [D, DFF], F32, kind="ExternalInput").ap()
    wu = nc.dram_tensor("wu", [D, DFF], F32, kind="ExternalInput").ap()
    wd = nc.dram_tensor("wd", [DFF, D], F32, kind="ExternalInput").ap()
    if final:
        gf = nc.dram_tensor("gf", [128, KC], F32, kind="ExternalInput").ap()
    out = nc.dram_tensor("outT", [D, TOK], F32, kind="ExternalOutput").ap()
    P = Prog(nc)
    with ExitStack() as es:
        sb = lambda n, shp, dt: es.enter_context(nc.sbuf_tensor(n, shp, dt))
        pst = lambda n: es.enter_context(nc.psum_tensor(n, [128, 512], F32))
        xs = sb("xs", [128, KC, TOK], F32)
        hT = sb("hT", [128, KC, TOK], BF16)
        bxs, bh = Buf(), Buf()
        NST = 4
        stg = [sb("stg%d" % i, [128, 2048], F32) for i in range(NST)]
        wbf = [sb("wbf%d" % i, [128, 2048], BF16) for i in range(NST)]
        bstg = [Buf() for _ in range(NST)]
        bwbf = [Buf() for _ in range(NST)]
        act = [sb("act%d" % i, [128, GRP, TOK], BF16) for i in range(2)]
        bact = [Buf(), Buf()]
        sg = [sb("sg%d" % i, [128, 512], F32) for i in range(2)]
        bsg = [Buf(), Buf()]
        pg = [pst("pg%d" % i) for i in range(2)]
        pu = [pst("pu%d" % i) for i in range(2)]
        pd = [pst("pd%d" % i) for i in range(2)]
        bpg, bpu, bpd = [Buf(), Buf()], [Buf(), Buf()], [Buf(), Buf()]
        xv = xT.rearrange("(c p) t -> p c t", p=128)
        yv = yT.rearrange("(c p) t -> p c t", p=128)
        for i in range(4):
            P.dma("sync", xs[:, i * 4:(i + 1) * 4, :], xv[:, i * 4:(i + 1) * 4, :], writes=[bxs])
        st = [0]
        dq = [0]

        def load_cast(src_ap, view3=None):
            s = st[0] % NST
            st[0] += 1
            q = "sync" if dq[0] % 2 == 0 else "gpsimd"
            dq[0] += 1
            dst = stg[s][:] if view3 is None else stg[s][:].rearrange(view3[0], **view3[1])
            P.dma("sync", dst, src_ap, writes=[bstg[s]])
            P.op("gpsimd", lambda e, s=s: e.tensor_copy(out=wbf[s][:], in_=stg[s][:]), reads=[bstg[s]], writes=[bwbf[s]])
            return wbf[s], bwbf[s]

        for i in range(KC // 2):
            t, b = load_cast(yv[:, 2 * i:2 * i + 2, :], ("p (c t) -> p c t", dict(c=2)))
            P.op("scalar", lambda e, i=i, t=t: e.copy(out=hT[:, 2 * i:2 * i + 2, :], in_=t[:].rearrange("p (c t) -> p c t", c=2)),
                 reads=[b], writes=[bh])
        wov = wout.rearrange("(c p) n -> p c n", p=128)
        k = 0
        for m in range(KC):
            t, b = load_cast(wov[:, :, m * 128:(m + 1) * 128], ("p (c n) -> p c n", dict(c=KC)))
            tv = t[:].rearrange("p (c n) -> p c n", c=KC)
            for hf in range(2):
                q = k % 2
                k += 1
                sl = slice(hf * 512, (hf + 1) * 512)
                for c in range(KC):
                    P.op("tensor", lambda e, tv=tv, c=c, q=q, sl=sl: e.matmul(pd[q][:], lhsT=tv[:, c, :], rhs=hT[:, c, sl],
                                                                             start=(c == 0), stop=(c == KC - 1)),
                         reads=[b, bh], writes=[bpd[q]])
                P.op("vector", lambda e, m=m, q=q, sl=sl: e.tensor_tensor(out=xs[:, m, sl], in0=xs[:, m, sl], in1=pd[q][:], op=ALU.add),
                     reads=[bpd[q], bxs], writes=[bxs])
        scr = {}
        hT2, bh2 = rms_to_bf16(P, nc, es, xs, bxs, g2, "n2", out_tile=hT, out_buf=bh, scr=scr)
        wgv = wg.rearrange("(c p) n -> p c n", p=128)
        wuv = wu.rearrange("(c p) n -> p c n", p=128)
        for grp in range(NFB // GRP):
            a = act[grp % 2]
            ba = bact[grp % 2]
            wds = []
            for j in range(GRP):
                fb = grp * GRP + j
                tg, bg_ = load_cast(wgv[:, :, fb * 128:(fb + 1) * 128], ("p (c n) -> p c n", dict(c=KC)))
                tu, bu_ = load_cast(wuv[:, :, fb * 128:(fb + 1) * 128], ("p (c n) -> p c n", dict(c=KC)))
                tgv = tg[:].rearrange("p (c n) -> p c n", c=KC)
                tuv = tu[:].rearrange("p (c n) -> p c n", c=KC)
                for hf in range(2):
                    sl = slice(hf * 512, (hf + 1) * 512)
                    for c in range(KC):
                        P.op("tensor", lambda e, tgv=tgv, c=c, hf=hf, sl=sl: e.matmul(pg[hf][:], lhsT=tgv[:, c, :], rhs=hT[:, c, sl],
                                                                                   start=(c == 0), stop=(c == KC - 1)),
                             reads=[bg_, bh], writes=[bpg[hf]])
                    for c in range(KC):
                        P.op("tensor", lambda e, tuv=tuv, c=c, hf=hf, sl=sl: e.matmul(pu[hf][:], lhsT=tuv[:, c, :], rhs=hT[:, c, sl],
                                                                                   start=(c == 0), stop=(c == KC - 1)),
                             reads=[bu_, bh], writes=[bpu[hf]])
                    P.op("scalar", lambda e, hf=hf: e.activation(out=sg[hf][:], in_=pg[hf][:], func=AF.Silu),
                         reads=[bpg[hf]], writes=[bsg[hf]])
                    P.op("vector", lambda e, hf=hf, a=a, j=j, sl=sl: e.tensor_tensor(out=a[:, j, sl], in0=sg[hf][:], in1=pu[hf][:], op=ALU.mult),
                         reads=[bsg[hf], bpu[hf]], writes=[ba])
            for j in range(GRP):
                fb = grp * GRP + j
                wds.append(load_cast(wd[fb * 128:(fb + 1) * 128, :]))
            for m in range(KC):
                for hf in range(2):
                    q = k % 2
                    k += 1
                    sl = slice(hf * 512, (hf + 1) * 512)
                    for j in range(GRP):
                        td, bd_ = wds[j]
                        P.op("tensor", lambda e, td=td, j=j, m=m, q=q, sl=sl, a=a: e.matmul(pd[q][:], lhsT=td[:, m * 128:(m + 1) * 128], rhs=a[:, j, sl],
                                                                                         start=(j == 0), stop=(j == GRP - 1)),
                             reads=[bd_, ba], writes=[bpd[q]])
                    P.op("vector", lambda e, m=m, q=q, sl=sl: e.tensor_tensor(out=xs[:, m, sl], in0=xs[:, m, sl], in1=pd[q][:], op=ALU.add),
                         reads=[bpd[q], bxs], writes=[bxs])
        bout = Buf()
        ov = out.rearrange("(c p) t -> p c t", p=128)
        if final:
            rms_to_bf16(P, nc, es, xs, bxs, gf, "n3", out_tile=xs, out_buf=bxs, scr=scr)
        for i in range(4):
            P.dma("sync", ov[:, i * 4:(i + 1) * 4, :], xs[:, i * 4:(i + 1) * 4, :], reads=[bxs], writes=[bout])
        P.final_wait("sync", [bout])
        P.emit()
    return nc


S = 4096


def build_p2b():
    nc = bass.Bass("TRN2", target_bir_lowering=False)
    gT = nc.dram_tensor("gT", [128, S], F32, kind="ExternalInput").ap()
    xT = nc.dram_tensor("xT", [128, S], F32, kind="ExternalInput").ap()
    sm = nc.dram_tensor("sm", [128, 8], F32, kind="ExternalInput").ap()
    wa = nc.dram_tensor("wa", [2, 64, 64], F32, kind="ExternalInput").ap()
    wx = nc.dram_tensor("wx", [2, 64, 64], F32, kind="ExternalInput").ap()
    out = nc.dram_tensor("yT", [128, S], F32, kind="ExternalOutput").ap()
    P = Prog(nc)
    with ExitStack() as es:
        sb = lambda n, shp, dt=F32: es.enter_context(nc.sbuf_tensor(n, shp, dt))
        g = sb("g", [128, S]); x = sb("x", [128, S]); xc = sb("xc", [128, S])
        r = sb("r", [128, S]); ii = sb("ii", [128, S]); a = sb("a", [128, S]); u = sb("u", [128, S])
        h = sb("h", [128, S])
        sms = sb("sms", [128, 8]); wab = sb("wab", [128, 128]); wxb = sb("wxb", [128, 128])
        t1 = sb("t1", [128, 8])
        ps = [es.enter_context(nc.psum_tensor("ps%d" % i, [128, 512], F32)) for i in range(4)]
        bps = [Buf() for _ in range(4)]
        bg, bx, bxc, br, bi, ba_, bu, bh, bsm, bwa, bwx, bt1 = (Buf() for _ in range(12))
        for i in range(4):
            sl = slice(i * 1024, (i + 1) * 1024)
            P.dma("sync", x[:, sl], xT[:, sl], writes=[bx])
        for i in range(4):
            sl = slice(i * 1024, (i + 1) * 1024)
            P.dma("gpsimd", g[:, sl], gT[:, sl], writes=[bg])
        P.dma("sync", sms[:], sm, writes=[bsm])
        P.op("gpsimd", lambda e: e.memset(wab[:], 0.0), writes=[bwa])
        P.op("gpsimd", lambda e: e.memset(wxb[:], 0.0), writes=[bwx])
        for bl in range(2):
            P.dma("sync", wab[bl * 64:(bl + 1) * 64, bl * 64:(bl + 1) * 64], wa[bl], writes=[bwa])
            P.dma("sync", wxb[bl * 64:(bl + 1) * 64, bl * 64:(bl + 1) * 64], wx[bl], writes=[bwx])
        P.op("vector", lambda e: e.tensor_scalar(out=xc[:], in0=x[:], scalar1=sms[:, 3:4], scalar2=sms[:, 4:5], op0=ALU.mult, op1=ALU.add),
             reads=[bx, bsm], writes=[bxc])
        for sh in (1, 2, 3):
            P.op("vector", lambda e, sh=sh: e.scalar_tensor_tensor(out=xc[:, sh:], in0=x[:, :S - sh], scalar=sms[:, 3 - sh:4 - sh],
                                                                   in1=xc[:, sh:], op0=ALU.mult, op1=ALU.add),
                 reads=[bx, bsm, bxc], writes=[bxc])
        P.op("scalar", lambda e: e.activation(out=t1[:, 0:1], in_=sms[:, 7:8], func=AF.Exp, scale=-1.0), reads=[bsm], writes=[bt1])
        P.op("vector", lambda e: e.tensor_scalar(out=t1[:, 1:2], in0=t1[:, 0:1], scalar1=2.0, scalar2=None, op0=ALU.add), reads=[bt1], writes=[bt1])
        P.op("vector", lambda e: e.reciprocal(out=t1[:, 1:2], in_=t1[:, 1:2]), reads=[bt1], writes=[bt1])
        P.op("vector", lambda e: e.tensor_tensor(out=t1[:, 2:3], in0=t1[:, 0:1], in1=t1[:, 1:2], op=ALU.mult), reads=[bt1], writes=[bt1])
        P.op("vector", lambda e: e.tensor_tensor(out=t1[:, 3:4], in0=t1[:, 2:3], in1=t1[:, 2:3], op=ALU.mult), reads=[bt1], writes=[bt1])
        P.op("vector", lambda e: e.memset(t1[:, 4:5], 1.0 / 13.0), reads=[bt1], writes=[bt1])
        for cf in (1.0 / 11, 1.0 / 9, 1.0 / 7, 1.0 / 5, 1.0 / 3, 1.0):
            P.op("vector", lambda e, cf=cf: e.tensor_scalar(out=t1[:, 4:5], in0=t1[:, 4:5], scalar1=t1[:, 3:4], scalar2=float(cf), op0=ALU.mult, op1=ALU.add),
                 reads=[bt1], writes=[bt1])
        P.op("vector", lambda e: e.scalar_tensor_tensor(out=t1[:, 5:6], in0=t1[:, 4:5], scalar=-16.0, in1=t1[:, 2:3], op0=ALU.mult, op1=ALU.mult),
             reads=[bt1], writes=[bt1])
        for pc in range(8):
            sl = slice(pc * 512, (pc + 1) * 512)
            q0, q1 = (2 * pc) % 4, (2 * pc + 1) % 4
            P.op("tensor", lambda e, q0=q0, sl=sl: e.matmul(ps[q0][:], lhsT=wab[:], rhs=xc[:, sl], start=True, stop=True),
                 reads=[bwa, bxc], writes=[bps[q0]])
            P.op("tensor", lambda e, q1=q1, sl=sl: e.matmul(ps[q1][:], lhsT=wxb[:], rhs=xc[:, sl], start=True, stop=True),
                 reads=[bwx, bxc], writes=[bps[q1]])
            P.op("scalar", lambda e, q0=q0, sl=sl: e.activation(out=r[:, sl], in_=ps[q0][:], func=AF.Sigmoid, bias=sms[:, 5:6]),
                 reads=[bps[q0], bsm], writes=[br])
            P.op("scalar", lambda e, q1=q1, sl=sl: e.activation(out=ii[:, sl], in_=ps[q1][:], func=AF.Sigmoid, bias=sms[:, 6:7]),
                 reads=[bps[q1], bsm], writes=[bi])
        P.op("scalar", lambda e: e.activation(out=a[:], in_=r[:], func=AF.Exp, scale=t1[:, 5:6]), reads=[br, bt1], writes=[ba_])
        P.op("vector", lambda e: e.tensor_tensor(out=u[:], in0=a[:], in1=a[:], op=ALU.mult), reads=[ba_], writes=[bu])
        P.op("vector", lambda e: e.tensor_scalar(out=u[:], in0=u[:], scalar1=-1.0, scalar2=1.0, op0=ALU.mult, op1=ALU.add), reads=[bu], writes=[bu])
        P.op("scalar", lambda e: e.activation(out=u[:], in_=u[:], func=AF.Sqrt), reads=[bu], writes=[bu])
        P.op("vector", lambda e: e.tensor_tensor(out=ii[:], in0=ii[:], in1=xc[:], op=ALU.mult), reads=[bi, bxc], writes=[bi])
        P.op("vector", lambda e: e.tensor_tensor(out=u[:], in0=u[:], in1=ii[:], op=ALU.mult), reads=[bu, bi], writes=[bu])
        P.op("vector", lambda e: e.tensor_tensor_scan(out=h[:], data0=a[:], data1=u[:], initial=0.0, op0=ALU.mult, op1=ALU.add),
             reads=[ba_, bu], writes=[bh])
        P.op("scalar", lambda e: e.activation(out=r[:], in_=g[:], func=AF.Square), reads=[bg, br], writes=[br])
        P.op("vector", lambda e: e.tensor_scalar(out=r[:], in0=r[:], scalar1=0.044715, scalar2=1.0, op0=ALU.mult, op1=ALU.add), reads=[br], writes=[br])
        P.op("vector", lambda e: e.tensor_tensor(out=r[:], in0=r[:], in1=g[:], op=ALU.mult), reads=[br, bg], writes=[br])
        P.op("scalar", lambda e: e.activation(out=r[:], in_=r[:], func=AF.Sigmoid, scale=1.5957691216057308), reads=[br], writes=[br])
        P.op("vector", lambda e: e.tensor_tensor(out=r[:], in0=r[:], in1=g[:], op=ALU.mult), reads=[br, bg], writes=[br])
        P.op("vector", lambda e: e.tensor_tensor(out=h[:], in0=h[:], in1=r[:], op=ALU.mult), reads=[br, bh], writes=[bh])
        bout = Buf()
        for i in range(4):
            sl = slice(i * 1024, (i + 1) * 1024)
            P.dma("sync", out[:, sl], h[:, sl], reads=[bh], writes=[bout])
        P.final_wait("sync", [bout])
        P.emit()
    return nc


def p2b_inputs(d, l, projT_b, q):
    import numpy as np
    A = 2752
    ch = slice(128 * q, 128 * q + 128)
    gT = projT_b[A + 128 * q: A + 128 * q + 128]
    xT = projT_b[A + 512 + 128 * q: A + 512 + 128 * q + 128]
    sm = np.stack([d['lru_conv_w'][l][0, ch], d['lru_conv_w'][l][1, ch], d['lru_conv_w'][l][2, ch], d['lru_conv_w'][l][3, ch],
                   d['lru_conv_b'][l][ch], d['lru_ba'][l][ch], d['lru_bx'][l][ch], d['lru_lambda'][l][ch]], axis=1)
    return {"gT": np.ascontiguousarray(gT), "xT": np.ascontiguousarray(xT), "sm": np.ascontiguousarray(sm.astype(np.float32)),
            "wa": np.ascontiguousarray(d['lru_wa'][l][2 * q:2 * q + 2]), "wx": np.ascontiguousarray(d['lru_wx'][l][2 * q:2 * q + 2])}


S = 4096
NT = 32
PI = math.pi
C1 = 6.28125
C2 = 2 * math.pi - 6.28125


def build_p2c():
    nc = bass.Bass("TRN2", target_bir_lowering=False)
    dt = lambda n, shp, d=F32: nc.dram_tensor(n, shp, d, kind="ExternalInput").ap()
    q_in = dt("q2", [2, S, 64]); k_in = dt("k2", [2, S, 64]); v_in = dt("v2", [2, S, 128]); g_in = dt("g2", [2, S, 128])
    pos_in = dt("pos", [128, NT], I32); invf_in = dt("invf", [128, 32]); ident_in = dt("ident", [128, 128])
    mask_in = dt("maskT", [2, 128, 128]); qdec_in = dt("qdec", [2, 64, 128]); kdec_in = dt("kdec", [128, 2])
    g128_in = dt("g128", [64, 2]); gng_in = dt("gng", [2, 128, 128])
    out = nc.dram_tensor("o2", [2, S, 128], F32, kind="ExternalOutput").ap()
    P = Prog(nc)
    with ExitStack() as es:
        sb = lambda n, shp, d=F32: es.enter_context(nc.sbuf_tensor("s_" + n, shp, d))
        pst = lambda n, shp: es.enter_context(nc.psum_tensor(n, shp, F32))
        posi = sb("posi", [128, NT], I32); posf = sb("posf", [128, NT]); invf = sb("invf", [128, 32])
        ident = sb("ident", [128, 128])
        ang = sb("ang", [128, NT, 32]); nf = sb("nf", [128, NT, 32]); ni = sb("ni", [128, NT, 32], I32)
        sn = sb("sn", [128, NT, 32]); cs = sb("cs", [128, NT, 32]); tmp = sb("tmp", [128, NT, 32]); tmp2 = sb("tmp2", [128, NT, 32])
        epst = sb("epst", [128, 1])
        bpos, binvf, bident, bang, bnf, bsn, bcs, btmp, btmp2, beps = (Buf() for _ in range(10))
        P.dma("sync", posi[:], pos_in, writes=[bpos])
        P.dma("sync", invf[:], invf_in, writes=[binvf])
        P.dma("sync", ident[:], ident_in, writes=[bident])
        P.op("gpsimd", lambda e: e.memset(epst[:], 1e-5), writes=[beps])
        P.op("vector", lambda e: e.tensor_copy(out=posf[:], in_=posi[:]), reads=[bpos], writes=[bpos])
        for n in range(NT):
            P.op("vector", lambda e, n=n: e.tensor_scalar(out=ang[:, n, :], in0=invf[:], scalar1=posf[:, n:n + 1], scalar2=None, op0=ALU.mult),
                 reads=[bpos, binvf], writes=[bang])
        P.op("vector", lambda e: e.tensor_scalar(out=nf[:], in0=ang[:], scalar1=1.0 / (2 * PI), scalar2=None, op0=ALU.mult), reads=[bang], writes=[bnf])
        P.op("vector", lambda e: e.tensor_copy(out=ni[:], in_=nf[:]), reads=[bnf], writes=[bnf])
        P.op("vector", lambda e: e.tensor_copy(out=nf[:], in_=ni[:]), reads=[bnf], writes=[bnf])
        P.op("vector", lambda e: e.scalar_tensor_tensor(out=ang[:], in0=nf[:], scalar=-C1, in1=ang[:], op0=ALU.mult, op1=ALU.add), reads=[bnf, bang], writes=[bang])
        P.op("vector", lambda e: e.scalar_tensor_tensor(out=ang[:], in0=nf[:], scalar=-C2, in1=ang[:], op0=ALU.mult, op1=ALU.add), reads=[bnf, bang], writes=[bang])

        def wrap(dst, bdst, src, bsrc, shift):
            if shift != 0.0:
                P.op("vector", lambda e: e.tensor_scalar(out=dst[:], in0=src[:], scalar1=float(shift), scalar2=None, op0=ALU.add), reads=[bsrc], writes=[bdst])
                src_, bsrc_ = dst, bdst
            else:
                src_, bsrc_ = src, bsrc
            P.op("vector", lambda e: e.tensor_scalar(out=tmp[:], in0=src_[:], scalar1=PI, scalar2=-2 * PI, op0=ALU.is_gt, op1=ALU.mult), reads=[bsrc_], writes=[btmp])
            P.op("vector", lambda e: e.tensor_scalar(out=tmp2[:], in0=src_[:], scalar1=-PI, scalar2=2 * PI, op0=ALU.is_lt, op1=ALU.mult), reads=[bsrc_], writes=[btmp2])
            P.op("vector", lambda e: e.tensor_tensor(out=dst[:], in0=src_[:], in1=tmp[:], op=ALU.add), reads=[bsrc_, btmp], writes=[bdst])
            P.op("vector", lambda e: e.tensor_tensor(out=dst[:], in0=dst[:], in1=tmp2[:], op=ALU.add), reads=[bdst, btmp2], writes=[bdst])
        wrap(sn, bsn, ang, bang, 0.0)
        wrap(cs, bcs, sn, bsn, PI / 2)
        P.op("scalar", lambda e: e.activation(out=sn[:], in_=sn[:], func=AF.Sin), reads=[bsn], writes=[bsn])
        P.op("scalar", lambda e: e.activation(out=cs[:], in_=cs[:], func=AF.Sin), reads=[bcs], writes=[bcs])

        q = sb("q", [128, NT, 64]); k = sb("k", [128, NT, 64]); qr = sb("qr", [128, NT, 64]); kr = sb("kr", [128, NT, 64])
        kt = sb("kt", [128, NT, 64])
        v = sb("v", [128, NT, 128]); g = sb("g", [128, NT, 128]); o = sb("o", [128, NT, 128])
        maskT = sb("maskT", [128, 128]); qdec = sb("qdec", [64, 128]); kdec = sb("kdec", [128, 2]); g128 = sb("g128", [64, 2])
        gng = sb("gng", [128, 128])
        Sst = sb("Sst", [64, 128])
        qkT = [sb("qkT%d" % i, [64, 2, 128]) for i in range(2)]
        qtT = [sb("qtT%d" % i, [64, 128]) for i in range(2)]
        smk = [sb("smk%d" % i, [128, 128]) for i in range(2)]
        stats = [sb("stats%d" % i, [128, 6]) for i in range(2)]
        mv = [sb("mv%d" % i, [128, 2]) for i in range(2)]
        rs = [sb("rs%d" % i, [128, 1]) for i in range(2)]
        psT = [pst("psT%d" % i, [64, 2, 128]) for i in range(2)]
        pss = [pst("pss%d" % i, [128, 128]) for i in range(2)]
        pso = [pst("pso%d" % i, [128, 128]) for i in range(2)]
        pkv = [pst("pkv%d" % i, [64, 128]) for i in range(2)]
        bq, bk, bqr, bkr, bkt, bv, bg, bo, bmask, bqdec, bkdec, bg128, bgng, bS = (Buf() for _ in range(14))
        bqkT, bqtT, bsmk, bstats, bmv, brs, bpsT, bpss, bpso, bpkv = ([Buf(), Buf()] for _ in range(10))
        P.dma("sync", kdec[:], kdec_in, writes=[bkdec])
        P.dma("sync", g128[:], g128_in, writes=[bg128])
        bout = Buf()
        for sl in range(2):
            vw = lambda ap: ap.rearrange("(n p) d -> p n d", p=128)
            P.dma("sync", q[:], vw(q_in[sl]), writes=[bq])
            P.dma("gpsimd", k[:], vw(k_in[sl]), writes=[bk])
            for hh in range(2):
                P.dma("sync", v[:, hh * 16:(hh + 1) * 16, :], vw(v_in[sl])[:, hh * 16:(hh + 1) * 16, :], writes=[bv])
                P.dma("gpsimd", g[:, hh * 16:(hh + 1) * 16, :], vw(g_in[sl])[:, hh * 16:(hh + 1) * 16, :], writes=[bg])
            P.dma("sync", maskT[:], mask_in[sl], writes=[bmask])
            P.dma("sync", qdec[:], qdec_in[sl], writes=[bqdec])
            P.dma("sync", gng[:], gng_in[sl], writes=[bgng])
            P.op("gpsimd", lambda e: e.memset(Sst[:], 0.0), writes=[bS])
            for (src, bsrc, dst, bdst) in ((q, bq, qr, bqr), (k, bk, kr, bkr)):
                s1, s2 = src[:, :, 0:32], src[:, :, 32:64]
                d1, d2 = dst[:, :, 0:32], dst[:, :, 32:64]
                P.op("vector", lambda e, s1=s1, d1=d1: e.tensor_tensor(out=d1, in0=s1, in1=cs[:], op=ALU.mult), reads=[bsrc, bcs], writes=[bdst])
                P.op("vector", lambda e, s2=s2: e.tensor_tensor(out=tmp[:], in0=s2, in1=sn[:], op=ALU.mult), reads=[bsrc, bsn], writes=[btmp])
                P.op("vector", lambda e, d1=d1: e.tensor_tensor(out=d1, in0=d1, in1=tmp[:], op=ALU.subtract), reads=[btmp, bdst], writes=[bdst])
                P.op("vector", lambda e, s2=s2, d2=d2: e.tensor_tensor(out=d2, in0=s2, in1=cs[:], op=ALU.mult), reads=[bsrc, bcs], writes=[bdst])
                P.op("vector", lambda e, s1=s1: e.tensor_tensor(out=tmp2[:], in0=s1, in1=sn[:], op=ALU.mult), reads=[bsrc, bsn], writes=[btmp2])
                P.op("vector", lambda e, d2=d2: e.tensor_tensor(out=d2, in0=d2, in1=tmp2[:], op=ALU.add), reads=[btmp2, bdst], writes=[bdst])
            P.op("vector", lambda e, sl=sl: e.tensor_scalar(out=kt[:], in0=kr[:], scalar1=kdec[:, sl:sl + 1], scalar2=None, op0=ALU.mult),
                 reads=[bkr, bkdec], writes=[bkt])
            for n in range(NT):
                i = n % 2
                P.op("tensor", lambda e, n=n, i=i: e.transpose(psT[i][:, 0, :], qr[:, n, :], ident[:]), reads=[bqr, bident], writes=[bpsT[i]])
                P.op("tensor", lambda e, n=n, i=i: e.transpose(psT[i][:, 1, :], kr[:, n, :], ident[:]), reads=[bkr, bident], writes=[bpsT[i]])
                P.op("scalar", lambda e, i=i: e.copy(out=qkT[i][:], in_=psT[i][:]), reads=[bpsT[i]], writes=[bqkT[i]])
                P.op("vector", lambda e, i=i: e.tensor_tensor(out=qtT[i][:], in0=psT[i][:, 0, :], in1=qdec[:], op=ALU.mult), reads=[bpsT[i], bqdec], writes=[bqtT[i]])
                P.op("tensor", lambda e, i=i: e.matmul(pss[i][:], lhsT=qkT[i][:, 1, :], rhs=qkT[i][:, 0, :], start=True, stop=True), reads=[bqkT[i]], writes=[bpss[i]])
                P.op("vector", lambda e, i=i: e.tensor_tensor(out=smk[i][:], in0=pss[i][:], in1=maskT[:], op=ALU.mult), reads=[bpss[i], bmask], writes=[bsmk[i]])
                P.op("tensor", lambda e, i=i, n=n: e.matmul(pso[i][:], lhsT=smk[i][:], rhs=v[:, n, :], start=True, stop=False), reads=[bsmk[i], bv], writes=[bpso[i]])
                P.op("tensor", lambda e, i=i: e.matmul(pso[i][:], lhsT=qtT[i][:], rhs=Sst[:], start=False, stop=True), reads=[bqtT[i], bS], writes=[bpso[i]])
                P.op("tensor", lambda e, i=i, n=n: e.matmul(pkv[i][:], lhsT=kt[:, n, :], rhs=v[:, n, :], start=True, stop=True), reads=[bkt, bv], writes=[bpkv[i]])
                P.op("vector", lambda e, i=i, sl=sl: e.scalar_tensor_tensor(out=Sst[:], in0=Sst[:], scalar=g128[:, sl:sl + 1], in1=pkv[i][:], op0=ALU.mult, op1=ALU.add),
                     reads=[bpkv[i], bg128, bS], writes=[bS])
                P.op("vector", lambda e, i=i: e.bn_stats(out=stats[i][:], in_=pso[i][:]), reads=[bpso[i]], writes=[bstats[i]])
                P.op("vector", lambda e, i=i: e.bn_aggr(out=mv[i][:], in_=stats[i][:]), reads=[bstats[i]], writes=[bmv[i]])
                P.op("scalar", lambda e, i=i: e.activation(out=rs[i][:], in_=mv[i][:, 1:2], func=AF.Sqrt, bias=epst[:, 0:1]), reads=[bmv[i], beps], writes=[brs[i]])
                P.op("vector", lambda e, i=i: e.reciprocal(out=rs[i][:], in_=rs[i][:]), reads=[brs[i]], writes=[brs[i]])
                P.op("vector", lambda e, i=i, n=n: e.tensor_scalar(out=o[:, n, :], in0=pso[i][:], scalar1=mv[i][:, 0:1], scalar2=rs[i][:, 0:1], op0=ALU.subtract, op1=ALU.mult),
                     reads=[bpso[i], bmv[i], brs[i]], writes=[bo])
                P.op("gpsimd", lambda e, n=n: e.tensor_tensor(out=o[:, n, :], in0=o[:, n, :], in1=gng[:], op=ALU.mult), reads=[bo, bgng], writes=[bo])
            P.op("scalar", lambda e: e.activation(out=g[:], in_=g[:], func=AF.Silu), reads=[bg], writes=[bg])
            P.op("vector", lambda e: e.tensor_tensor(out=o[:], in0=o[:], in1=g[:], op=ALU.mult), reads=[bo, bg], writes=[bo])
            for hh in range(2):
                P.dma("sync", vw(out[sl])[:, hh * 16:(hh + 1) * 16, :], o[:, hh * 16:(hh + 1) * 16, :], reads=[bo], writes=[bout])
        P.final_wait("sync", [bout])
        P.emit()
    return nc


def p2c_consts(heads):
    maskT = np.zeros((2, 128, 128), np.float32); qdec = np.zeros((2, 64, 128), np.float32)
    kdec = np.zeros((128, 2), np.float32); g128 = np.zeros((64, 2), np.float32)
    idx = np.arange(128)
    for s, hd in enumerate(heads):
        lg = float(np.log1p(-np.exp2(np.float32(-5.0 - hd))).astype(np.float32))
        m = idx[:, None]; c = idx[None, :]
        same = (m // 64) == (c // 64)
        earlier = (m // 64) < (c // 64)
        dec = np.where(same, np.exp(lg * np.abs(c - m)), np.where(earlier, np.exp(lg * (c - m)), 0.0))
        maskT[s] = (dec * 0.125).astype(np.float32)
        qdec[s] = np.exp(lg * (idx + 1.0))[None, :].astype(np.float32)
        kdec[:, s] = (np.exp(lg * (127.0 - idx)) * 0.125).astype(np.float32)
        g128[:, s] = np.float32(np.exp(lg * 128.0))
    invf = (10000.0 ** (-np.arange(0, 64, 2, dtype=np.float32) / 64)).astype(np.float32)
    return dict(maskT=maskT, qdec=qdec, kdec=kdec, g128=g128, invf=np.ascontiguousarray(np.broadcast_to(invf, (128, 32))),
                ident=np.eye(128, dtype=np.float32))


def p2c_inputs(d, l, proj_b, pos_b, heads):
    c0 = 2752 + 1024
    im = p2c_consts(heads)
    q2 = np.stack([proj_b[:, c0 + 64 * h: c0 + 64 * h + 64] for h in heads])
    k2 = np.stack([proj_b[:, c0 + 384 + 64 * h: c0 + 384 + 64 * h + 64] for h in heads])
    v2 = np.stack([proj_b[:, c0 + 768 + 128 * h: c0 + 768 + 128 * h + 128] for h in heads])
    g2 = np.stack([proj_b[:, c0 + 1536 + 128 * h: c0 + 1536 + 128 * h + 128] for h in heads])
    gng = np.stack([np.broadcast_to(d['ret_gn_g'][l][128 * h:128 * h + 128], (128, 128)) for h in heads])
    im.update(q2=np.ascontiguousarray(q2), k2=np.ascontiguousarray(k2), v2=np.ascontiguousarray(v2), g2=np.ascontiguousarray(g2),
              gng=np.ascontiguousarray(gng.astype(np.float32)), pos=np.ascontiguousarray(pos_b.reshape(32, 128).T.astype(np.int32)))
    return im

C_SLOTS = [(0, 1), (2, 3), (4, 5), (4, 5)]


VARA = False

S = 4096
NT = 32
W3 = 192
NEG_EHALF = -math.exp(-0.5)
A_GN_EPS = 64e-5


def build_p2a(has_vres, nt=NT, dbg=9):
    nc = bass.Bass("TRN2", target_bir_lowering=False)
    dt = lambda n, shp, d=F32: nc.dram_tensor(n, shp, d, kind="ExternalInput").ap()
    rkv_in = dt("rkv", [S, 576]); mu_in = dt("mu_rkv", [128, 576])
    wd_in = dt("wdT", [96, S]); ad_in = dt("adT", [96, S]); gd_in = dt("gdT", [256, S])
    lmu_in = dt("lmu", [128, 5])
    w2_in = dt("w2c", [96, W3]); a2_in = dt("a2c", [96, W3]); g2_in = dt("g2c", [256, W3])
    par_in = dt("par", [128, 8, W3])
    cst_in = dt("cst", [128, 4, 128])
    m4_in = dt("mask4", [128, 4, 128])
    if has_vres:
        vd_in = dt("vdT", [64, S]); v2_in = dt("v2c", [64, W3]); vf_in = dt("vfirst", [S, W3])
    y_out = nc.dram_tensor("yA", [S, W3], F32, kind="ExternalOutput").ap()
    v_out = nc.dram_tensor("vout", [S, W3], F32, kind="ExternalOutput").ap()
    P = Prog(nc)
    with ExitStack() as es:
        sb = lambda n, shp, d=F32: es.enter_context(nc.sbuf_tensor("s_" + n, shp, d))
        V = lambda fn, r, w: P.op("vector", fn, reads=r, writes=w)
        A = lambda fn, r, w: P.op("scalar", fn, reads=r, writes=w)
        G = lambda fn, r, w: P.op("gpsimd", fn, reads=r, writes=w)
        T = lambda fn, r, w: P.op("tensor", fn, reads=r, writes=w)
        cst = sb("cst", [128, 4, 128]); m4 = sb("m4", [128, 4, 128]); par = sb("par", [128, 8, W3]); mu = sb("mu", [128, 576])
        lmu = sb("lmu", [128, 5]); w2 = sb("w2", [96, W3]); a2 = sb("a2", [96, W3]); g2 = sb("g2", [128, 2, W3])
        ones = sb("ones", [128, 1]); epsg = sb("epsg", [128, 1])
        bcst, bm4, bpar, bmu, blmu, bw2, ba2, bg2, bones, bepsg = (Buf() for _ in range(10))
        P.dma("sync", cst[:], cst_in, writes=[bcst]); P.dma("sync", m4[:], m4_in, writes=[bm4])
        P.dma("sync", par[:], par_in, writes=[bpar]); P.dma("sync", mu[:], mu_in, writes=[bmu])
        P.dma("sync", lmu[:], lmu_in, writes=[blmu]); P.dma("sync", w2[:], w2_in, writes=[bw2])
        P.dma("sync", a2[:], a2_in, writes=[ba2])
        P.dma("sync", g2[:], g2_in.rearrange("(c p) n -> p c n", p=128), writes=[bg2])
        G(lambda e: e.memset(ones[:], 1.0), [], [bones]); G(lambda e: e.memset(epsg[:], A_GN_EPS), [], [bepsg])
        ident, triu, trirev, maskn = cst[:, 0, :], cst[:, 1, :], cst[:, 2, :], cst[:, 3, :]
        if has_vres:
            v2 = sb("v2", [64, W3]); bv2 = Buf()
            P.dma("sync", v2[:], v2_in, writes=[bv2])
        ltmp = sb("ltmp", [128, S]); bltmp = Buf()
        lo = {}
        specs = [("wd", wd_in, 96, 0, AF.Tanh), ("ad", ad_in, 96, 1, None), ("gd0", gd_in[0:128], 128, 2, AF.Sigmoid),
                 ("gd1", gd_in[128:256], 128, 3, AF.Sigmoid)]
        if has_vres:
            specs.append(("vd", vd_in, 64, 4, None))
        for name, src, rows, mcol, fn in specs:
            t = sb("lo_" + name, [rows, S + 1]); b = Buf()
            G(lambda e, t=t: e.memset(t[:, 0:1], 0.0), [], [b])
            for i in range(4):
                P.dma("sync" if i % 2 == 0 else "gpsimd", t[:, 1 + i * 1024:1 + (i + 1) * 1024], src[:, i * 1024:(i + 1) * 1024], writes=[b])
            V(lambda e, t=t, rows=rows: e.tensor_tensor(out=ltmp[:rows, :], in0=t[:, 0:S], in1=t[:, 1:S + 1], op=ALU.subtract), [b, bltmp], [bltmp])
            V(lambda e, t=t, rows=rows, mcol=mcol: e.scalar_tensor_tensor(out=t[:, 1:S + 1], in0=ltmp[:rows, :], scalar=lmu[:rows, mcol:mcol + 1],
                                                                      in1=t[:, 1:S + 1], op0=ALU.mult, op1=ALU.add), [bltmp, blmu, b], [b])
            if fn is not None:
                A(lambda e, t=t, fn=fn: e.activation(out=t[:, 1:S + 1], in_=t[:, 1:S + 1], func=fn), [b], [b])
            lo[name] = (t, b)
        NS = 2
        cur = [sb("cur%d" % i, [128, 576]) for i in range(NS)]; prv = [sb("prv%d" % i, [128, 576]) for i in range(NS)]
        bcur = [Buf() for _ in range(NS)]; bprv = [Buf() for _ in range(NS)]
        tt = [sb("tt%d" % i, [128, 8, W3]) for i in range(NS)]; btt = [Buf() for _ in range(NS)]
        rk = [sb("rk%d" % i, [128, 3]) for i in range(NS)]; brk = [Buf() for _ in range(NS)]
        tm = sb("tm", [128, 4, W3]); btm = Buf()
        ss = sb("ss", [128, 3]); bss = Buf()
        if has_vres:
            vf = [sb("vf%d" % i, [128, W3]) for i in range(NS)]; bvf = [Buf() for _ in range(NS)]
        ee = sb("ee", [128, 4, W3]); bee = Buf()
        dd = sb("dd", [128, 6, W3]); bdd = Buf()
        wc = sb("wc", [64, 4]); bwc = Buf()
        T4 = sb("T4", [64, 4, 128]); bT4 = Buf()
        M4 = sb("M4", [128, 4, 128]); bM4 = Buf()
        Pk = [sb("Pk%d" % i, [128, 128]) for i in range(2)]; bPk = [Buf(), Buf()]
        Qk = [sb("Qk%d" % i, [128, 128]) for i in range(2)]; bQk = [Buf(), Buf()]
        Z = sb("Z", [128, 128]); bZ = Buf()
        nX = sb("nX", [128, 64]); bnX = Buf()
        U = sb("U", [128, 64]); bU = Buf()
        ST = [sb("ST%d" % h, [64, 64]) for h in range(3)]; bST = [Buf() for _ in range(3)]
        yt = [sb("yt%d" % i, [128, W3]) for i in range(NS)]; byt = [Buf() for _ in range(NS)]
        st6 = sb("st6", [128, 6]); bst6 = Buf(); mv = sb("mv", [128, 2]); bmv = Buf(); rsd = sb("rsd", [128, 1]); brsd = Buf()
        for h in range(3):
            G(lambda e, h=h: e.memset(ST[h][:], 0.0), [], [bST[h]])
        pb = [es.enter_context(nc.psum_tensor("pb%d" % i, [128, 512], F32)) for i in range(8)]
        bpb = [Buf() for _ in range(8)]
        pL1, pL2, pGG, pTT, pPP, pIV, pCH = pb[0], pb[1], pb[2], pb[3], pb[4], pb[5], pb[6]
        bL1, bL2, bGG, bTT, bPP, bIV, bCH = bpb[0], bpb[1], bpb[2], bpb[3], bpb[4], bpb[5], bpb[6]
        rkv_v = rkv_in
        bout = Buf()
        wdT, bwd = lo["wd"]; adT, bad = lo["ad"]; gd0, bgd0 = lo["gd0"]; gd1, bgd1 = lo["gd1"]
        for n in range(nt):
            s = n % NS
            t0 = n * 128
            c_, p_, t_ = cur[s], prv[s], tt[s]
            P.dma("sync", c_[:], rkv_v[t0:t0 + 128, :], writes=[bcur[s]])
            if n == 0:
                G(lambda e, p_=p_: e.memset(p_[:], 0.0), [], [bprv[s]])
                P.dma("gpsimd", p_[1:128, :], rkv_v[0:127, :], writes=[bprv[s]])
            else:
                P.dma("gpsimd", p_[:], rkv_v[t0 - 1:t0 + 127, :], writes=[bprv[s]])
            if has_vres:
                P.dma("sync", vf[s][:], vf_in[t0:t0 + 128, :], writes=[bvf[s]])
            V(lambda e, c_=c_, p_=p_: e.tensor_tensor(out=p_[:], in0=p_[:], in1=c_[:], op=ALU.subtract), [bcur[s], bprv[s]], [bprv[s]])
            G(lambda e, p_=p_: e.tensor_tensor(out=p_[:], in0=p_[:], in1=mu[:], op=ALU.mult), [bprv[s], bmu], [bprv[s]])
            V(lambda e, c_=c_, p_=p_: e.tensor_tensor(out=c_[:], in0=c_[:], in1=p_[:], op=ALU.add), [bcur[s], bprv[s]], [bcur[s]])
            r_, k_, v_ = c_[:, 0:W3], c_[:, W3:2 * W3], c_[:, 2 * W3:3 * W3]
            tsl = slice(1 + t0, 1 + t0 + 128)
            T(lambda e, tsl=tsl: e.matmul(pL1[:, 0:W3], lhsT=wdT[:, tsl], rhs=w2[:], start=True, stop=True), [bwd, bw2], [bL1])
            T(lambda e, tsl=tsl: e.matmul(pL1[:, W3:2 * W3], lhsT=adT[:, tsl], rhs=a2[:], start=True, stop=True), [bad, ba2], [bL1])
            T(lambda e, tsl=tsl: e.matmul(pL2[:, 0:W3], lhsT=gd0[:, tsl], rhs=g2[:, 0, :], start=True, stop=False), [bgd0, bg2], [bL2])
            T(lambda e, tsl=tsl: e.matmul(pL2[:, 0:W3], lhsT=gd1[:, tsl], rhs=g2[:, 1, :], start=False, stop=True), [bgd1, bg2], [bL2])
            if has_vres:
                vdT, bvd = lo["vd"]
                T(lambda e, tsl=tsl: e.matmul(pL2[:, W3:2 * W3], lhsT=vdT[:, tsl], rhs=v2[:], start=True, stop=True), [bvd, bv2], [bL2])
            V(lambda e, t_=t_: e.tensor_tensor(out=t_[:, 5, :], in0=pL1[:, 0:W3], in1=par[:, 0, :], op=ALU.add), [bL1, bpar, btt[s]], [btt[s]])
            A(lambda e, t_=t_: e.activation(out=t_[:, 5, :], in_=t_[:, 5, :], func=AF.Sigmoid), [btt[s]], [btt[s]])
            V(lambda e, t_=t_: e.tensor_scalar(out=t_[:, 5, :], in0=t_[:, 5, :], scalar1=NEG_EHALF, scalar2=None, op0=ALU.mult), [btt[s]], [btt[s]])
            V(lambda e, t_=t_: e.tensor_tensor(out=t_[:, 7, :], in0=pL1[:, W3:2 * W3], in1=par[:, 1, :], op=ALU.add), [bL1, bpar, btt[s]], [btt[s]])
            A(lambda e, t_=t_: e.activation(out=t_[:, 7, :], in_=t_[:, 7, :], func=AF.Sigmoid), [btt[s]], [btt[s]])
            A(lambda e, t_=t_: e.copy(out=t_[:, 6, :], in_=pL2[:, 0:W3]), [bL2, btt[s]], [btt[s]])
            G(lambda e, t_=t_, r_=r_: e.tensor_copy(out=t_[:, 0, :], in_=r_), [bcur[s], btt[s]], [btt[s]])
            if has_vres:
                V(lambda e: e.tensor_tensor(out=tm[:, 0, :], in0=pL2[:, W3:2 * W3], in1=par[:, 2, :], op=ALU.add), [bL2, bpar, btm], [btm])
                A(lambda e: e.activation(out=tm[:, 0, :], in_=tm[:, 0, :], func=AF.Sigmoid), [btm], [btm])
                V(lambda e, v_=v_, s=s: e.tensor_tensor(out=tm[:, 1, :], in0=vf[s][:], in1=v_, op=ALU.subtract), [bvf[s], bcur[s], btm], [btm])
                V(lambda e: e.tensor_tensor(out=tm[:, 1, :], in0=tm[:, 1, :], in1=tm[:, 0, :], op=ALU.mult), [btm], [btm])
                V(lambda e, t_=t_, v_=v_: e.tensor_tensor(out=t_[:, 2, :], in0=tm[:, 1, :], in1=v_, op=ALU.add), [btm, bcur[s], btt[s]], [btt[s]])
            else:
                G(lambda e, t_=t_, v_=v_: e.tensor_copy(out=t_[:, 2, :], in_=v_), [bcur[s], btt[s]], [btt[s]])
            P.dma("sync", v_out[t0:t0 + 128, :], t_[:, 2, :], reads=[btt[s]], writes=[bout])
            V(lambda e, t_=t_, k_=k_: e.tensor_tensor(out=t_[:, 3, :], in0=k_, in1=par[:, 3, :], op=ALU.mult), [bcur[s], bpar, btt[s]], [btt[s]])
            V(lambda e, t_=t_: e.tensor_tensor(out=tm[:, 2, :], in0=t_[:, 3, :], in1=t_[:, 3, :], op=ALU.mult), [btt[s], btm], [btm])
            V(lambda e: e.tensor_reduce(out=ss[:], in_=tm[:, 2, :].rearrange("p (h j) -> p h j", h=3), axis=AX.X, op=ALU.add), [btm, bss], [bss])
            A(lambda e: e.activation(out=ss[:], in_=ss[:], func=AF.Sqrt), [bss], [bss])
            V(lambda e: e.tensor_scalar(out=ss[:], in0=ss[:], scalar1=1e-12, scalar2=None, op0=ALU.max), [bss], [bss])
            V(lambda e: e.reciprocal(out=ss[:], in_=ss[:]), [bss], [bss])
            for h in range(3):
                hs = slice(64 * h, 64 * h + 64)
                V(lambda e, t_=t_, hs=hs, h=h: e.tensor_scalar(out=t_[:, 3, hs], in0=t_[:, 3, hs], scalar1=ss[:, h:h + 1], scalar2=None, op0=ALU.mult),
                  [bss, btt[s]], [btt[s]])
            V(lambda e, t_=t_: e.scalar_tensor_tensor(out=tm[:, 3, :], in0=t_[:, 7, :], scalar=-1.0, in1=par[:, 4, :], op0=ALU.add, op1=ALU.mult),
              [btt[s], bpar, btm], [btm])
            V(lambda e, t_=t_, k_=k_: e.scalar_tensor_tensor(out=t_[:, 1, :], in0=tm[:, 3, :], scalar=1.0, in1=k_, op0=ALU.add, op1=ALU.mult),
              [btm, bcur[s], btt[s]], [btt[s]])
            G(lambda e, t_=t_: e.tensor_tensor(out=t_[:, 4, :], in0=t_[:, 3, :], in1=t_[:, 7, :], op=ALU.mult), [btt[s]], [btt[s]])
            V(lambda e, t_=t_: e.tensor_tensor(out=tm[:, 3, :], in0=t_[:, 0, :], in1=t_[:, 1, :], op=ALU.mult), [btt[s], btm], [btm])
            V(lambda e: e.tensor_tensor(out=tm[:, 3, :], in0=tm[:, 3, :], in1=par[:, 5, :], op=ALU.mult), [btm, bpar], [btm])
            V(lambda e, s=s: e.tensor_reduce(out=rk[s][:], in_=tm[:, 3, :].rearrange("p (h j) -> p h j", h=3), axis=AX.X, op=ALU.add), [btm, brk[s]], [brk[s]])
            if dbg <= 1:
                P.dma("sync", y_out[t0:t0 + 128, :], t_[:, 6, :], reads=[btt[s]], writes=[bout])
                continue
            T(lambda e, t_=t_: e.matmul(pGG[:, 0:W3], lhsT=triu, rhs=t_[:, 5, :], start=True, stop=True), [bcst, btt[s]], [bGG])
            T(lambda e, t_=t_: e.matmul(pGG[:, W3:2 * W3], lhsT=trirev, rhs=t_[:, 5, :], start=True, stop=True), [bcst, btt[s]], [bGG])
            for h in range(3):
                T(lambda e, t_=t_, h=h: e.matmul(pGG[0:64, 400 + h:401 + h], lhsT=t_[:, 5, 64 * h:64 * h + 64], rhs=ones[:], start=True, stop=True),
                  [btt[s], bones], [bGG])
            A(lambda e: e.activation(out=ee[:, 0, :], in_=pGG[:, 0:W3], func=AF.Exp), [bGG, bee], [bee])
            A(lambda e: e.activation(out=ee[:, 1, :], in_=pGG[:, 0:W3], func=AF.Exp, scale=-1.0), [bGG, bee], [bee])
            V(lambda e, t_=t_: e.tensor_tensor(out=ee[:, 2, :], in0=pGG[:, 0:W3], in1=t_[:, 5, :], op=ALU.subtract), [bGG, btt[s], bee], [bee])
            A(lambda e: e.activation(out=ee[:, 2, :], in_=ee[:, 2, :], func=AF.Exp), [bee], [bee])
            A(lambda e: e.activation(out=ee[:, 3, :], in_=pGG[:, W3:2 * W3], func=AF.Exp), [bGG, bee], [bee])
            A(lambda e: e.activation(out=wc[:, 0:3], in_=pGG[0:64, 400:403], func=AF.Exp), [bGG, bwc], [bwc])
            for (di, ti, ei, eng) in ((0, 0, 0, V), (1, 3, 2, G), (2, 1, 1, V), (3, 4, 1, G), (4, 1, 3, V), (5, 4, 3, G)):
                eng(lambda e, t_=t_, di=di, ti=ti, ei=ei: e.tensor_tensor(out=dd[:, di, :], in0=t_[:, ti, :], in1=ee[:, ei, :], op=ALU.mult),
                    [btt[s], bee, bdd], [bdd])
            if dbg <= 2:
                P.dma("sync", y_out[t0:t0 + 128, :], dd[:, 0, :], reads=[bdd], writes=[bout])
                continue
            for h in range(3):
                hs = slice(64 * h, 64 * h + 64)
                for j in range(4):
                    T(lambda e, j=j, hs=hs: e.transpose(pTT[0:64, j * 128:(j + 1) * 128], dd[:, j, hs], ident), [bdd, bcst], [bTT])
                A(lambda e: e.copy(out=T4[:], in_=pTT[0:64, :].rearrange("p (a b) -> p a b", a=4)), [bTT, bT4], [bT4])
                if dbg <= 3:
                    continue
                T(lambda e: e.matmul(pPP[:, 0:256], lhsT=T4[:, 2, :], rhs=T4[:, 0:2, :].rearrange("p a b -> p (a b)"), start=True, stop=True), [bT4], [bPP])
                T(lambda e: e.matmul(pPP[:, 256:512], lhsT=T4[:, 3, :], rhs=T4[:, 0:2, :].rearrange("p a b -> p (a b)"), start=True, stop=True), [bT4], [bPP])
                T(lambda e: e.matmul(pIV[:, 0:128], lhsT=T4[:, 1, :], rhs=T4[:, 3, :], start=True, stop=True), [bT4], [bIV])
                V(lambda e: e.tensor_tensor(out=M4[:].rearrange("p a b -> p (a b)"), in0=pPP[:], in1=m4[:].rearrange("p a b -> p (a b)"), op=ALU.mult),
                  [bPP, bm4, bM4], [bM4])
                V(lambda e: e.tensor_tensor(out=Pk[0][:], in0=pIV[:, 0:128], in1=maskn, op=ALU.mult), [bIV, bcst, bPk[0]], [bPk[0]])
                (V if VARA else G)(lambda e: e.tensor_copy(out=Qk[0][:], in_=M4[:, 3, :]), [bM4, bQk[0]], [bQk[0]])
                (V if VARA else G)(lambda e: e.tensor_tensor(out=Z[:], in0=ident, in1=M4[:, 3, :], op=ALU.subtract), [bM4, bcst, bZ], [bZ])
                if dbg <= 4:
                    continue
                for lv in range(1, 7):
                    pi, ci = (lv - 1) % 2, lv % 2
                    STEP = 9
                    if lv < 6 and STEP >= 1:
                        T(lambda e, pi=pi: e.matmul(pIV[:, 128:256], lhsT=Pk[pi][:], rhs=Qk[pi][:], start=True, stop=True), [bPk[pi], bQk[pi]], [bIV])
                    if STEP >= 2:
                        T(lambda e, pi=pi: e.matmul(pIV[:, 256:384], lhsT=Qk[pi][:], rhs=Pk[pi][:], start=True, stop=True), [bPk[pi], bQk[pi]], [bIV])
                    if lv < 6 and STEP >= 3:
                        A(lambda e, ci=ci: e.copy(out=Qk[ci][:], in_=pIV[:, 128:256]), [bIV, bQk[ci]], [bQk[ci]])
                    if STEP >= 4:
                        A(lambda e, ci=ci: e.copy(out=Pk[ci][:], in_=pIV[:, 256:384]), [bIV, bPk[ci]], [bPk[ci]])
                    if STEP >= 5:
                        T(lambda e, ci=ci: e.matmul(pIV[:, 384:512], lhsT=Pk[ci][:], rhs=Z[:], start=True, stop=True), [bPk[ci], bZ], [bIV])
                    if STEP >= 6:
                        V(lambda e: e.tensor_tensor(out=Z[:], in0=Z[:], in1=pIV[:, 384:512], op=ALU.add), [bIV, bZ], [bZ])
                if dbg <= 5:
                    continue
                vh = t_[:, 2, hs]
                T(lambda e, vh=vh: e.matmul(pCH[:, 0:64], lhsT=M4[:, 1, :], rhs=vh, start=True, stop=False), [bM4, btt[s]], [bCH])
                T(lambda e, h=h: e.matmul(pCH[:, 0:64], lhsT=T4[:, 1, :], rhs=ST[h][:], start=False, stop=True), [bT4, bST[h]], [bCH])
                A(lambda e: e.activation(out=nX[:], in_=pCH[:, 0:64], func=AF.Copy, scale=-1.0), [bCH, bnX], [bnX])
                T(lambda e: e.matmul(pCH[:, 64:128], lhsT=Z[:], rhs=nX[:], start=True, stop=True), [bZ, bnX], [bCH])
                V(lambda e: e.tensor_scalar(out=U[:], in0=pCH[:, 64:128], scalar1=1.0, scalar2=None, op0=ALU.mult), [bCH, bU], [bU])
                T(lambda e, h=h: e.matmul(pCH[:, 128:192], lhsT=T4[:, 0, :], rhs=ST[h][:], start=True, stop=False), [bT4, bST[h]], [bCH])
                T(lambda e: e.matmul(pCH[:, 128:192], lhsT=M4[:, 2, :], rhs=U[:], start=False, stop=False), [bM4, bU], [bCH])
                T(lambda e, vh=vh: e.matmul(pCH[:, 128:192], lhsT=M4[:, 0, :], rhs=vh, start=False, stop=True), [bM4, btt[s]], [bCH])
                T(lambda e, hs=hs: e.matmul(pCH[0:64, 192:256], lhsT=dd[:, 5, hs], rhs=U[:], start=True, stop=False), [bdd, bU], [bCH])
                T(lambda e, hs=hs, vh=vh: e.matmul(pCH[0:64, 192:256], lhsT=dd[:, 4, hs], rhs=vh, start=False, stop=True), [bdd, btt[s]], [bCH])
                V(lambda e, h=h: e.scalar_tensor_tensor(out=ST[h][:], in0=ST[h][:], scalar=wc[:, h:h + 1], in1=pCH[0:64, 192:256], op0=ALU.mult, op1=ALU.add),
                  [bCH, bwc, bST[h]], [bST[h]])
                V(lambda e: e.bn_stats(out=st6[:], in_=pCH[:, 128:192]), [bCH, bst6], [bst6])
                V(lambda e: e.bn_aggr(out=mv[:], in_=st6[:]), [bst6, bmv], [bmv])
                A(lambda e: e.activation(out=rsd[:], in_=mv[:, 1:2], func=AF.Sqrt, bias=epsg[:, 0:1]), [bmv, bepsg, brsd], [brsd])
                V(lambda e: e.reciprocal(out=rsd[:], in_=rsd[:]), [brsd], [brsd])
                V(lambda e, s=s, hs=hs: e.tensor_scalar(out=yt[s][:, hs], in0=pCH[:, 128:192], scalar1=mv[:, 0:1], scalar2=rsd[:, 0:1], op0=ALU.subtract, op1=ALU.mult),
                  [bCH, bmv, brsd, byt[s]], [byt[s]])
            if dbg <= 5:
                P.dma("sync", y_out[t0:t0 + 128, :], dd[:, 0, :], reads=[bdd, bT4, bM4, bZ, bPk[0], bPk[1], bQk[0], bQk[1]], writes=[bout])
                continue
            G(lambda e, s=s: e.tensor_tensor(out=yt[s][:], in0=yt[s][:], in1=par[:, 6, :], op=ALU.mult), [byt[s], bpar], [byt[s]])
            G(lambda e, s=s: e.tensor_tensor(out=yt[s][:], in0=yt[s][:], in1=par[:, 7, :], op=ALU.add), [byt[s], bpar], [byt[s]])
            for h in range(3):
                hs = slice(64 * h, 64 * h + 64)
                V(lambda e, s=s, hs=hs, h=h, t_=t_: e.scalar_tensor_tensor(out=yt[s][:, hs], in0=t_[:, 2, hs], scalar=rk[s][:, h:h + 1], in1=yt[s][:, hs],
                                                                          op0=ALU.mult, op1=ALU.add), [btt[s], brk[s], byt[s]], [byt[s]])
            V(lambda e, s=s, t_=t_: e.tensor_tensor(out=yt[s][:], in0=yt[s][:], in1=t_[:, 6, :], op=ALU.mult), [byt[s], btt[s]], [byt[s]])
            P.dma("sync", y_out[t0:t0 + 128, :], yt[s][:], reads=[byt[s]], writes=[bout])
        P.final_wait("sync", [bout])
        P.emit()
    return nc


def p2a_consts():
    i = np.arange(128)
    row, col = i[:, None], i[None, :]
    cst = np.stack([np.eye(128), (row <= col), (row > col), (row > col)], axis=1).astype(np.float32)
    incl = (row <= col).astype(np.float32); strict = (row < col).astype(np.float32)
    mask4 = np.stack([incl, strict, incl, strict], axis=1).astype(np.float32)
    return dict(cst=np.ascontiguousarray(cst), mask4=np.ascontiguousarray(mask4))


def p2a_inputs(d, l, projT_b, q, vfirst=None):
    im = p2a_consts()
    c = slice(192 * q, 192 * q + 192)
    cols = np.r_[192 * q:192 * q + 192, 768 + 192 * q:768 + 192 * q + 192, 1536 + 192 * q:1536 + 192 * q + 192]
    rep = lambda v: np.ascontiguousarray(np.broadcast_to(v.astype(np.float32), (128, v.shape[-1])))
    mu = d['tshift_mu'][l]
    im["rkv"] = np.ascontiguousarray(projT_b[cols].T)
    im["mu_rkv"] = rep(mu[cols])
    im["wdT"] = np.ascontiguousarray(projT_b[2304:2400]); im["adT"] = np.ascontiguousarray(projT_b[2400:2496])
    im["gdT"] = np.ascontiguousarray(projT_b[2496:2752])
    lmu = np.zeros((128, 5), np.float32)
    lmu[:96, 0] = mu[2304:2400]; lmu[:96, 1] = mu[2400:2496]; lmu[:, 2] = mu[2496:2624]; lmu[:, 3] = mu[2624:2752]
    im["w2c"] = np.ascontiguousarray(d['rwkv_w2'][l][:, c]); im["a2c"] = np.ascontiguousarray(d['rwkv_a2'][l][:, c])
    im["g2c"] = np.ascontiguousarray(d['rwkv_g2'][l][:, c])
    v0 = d['rwkv_v0'][l - 1][c] if l > 0 else np.zeros(192, np.float32)
    par = np.stack([d['rwkv_w0'][l][c], d['rwkv_a0'][l][c], v0, d['rwkv_k_k'][l][c], d['rwkv_k_a'][l][c],
                    d['rwkv_r_k'][l].reshape(-1)[c], d['rwkv_ln_g'][l][c], d['rwkv_ln_b'][l][c]])
    im["par"] = np.ascontiguousarray(np.broadcast_to(par[None].astype(np.float32), (128, 8, 192)))
    if l > 0:
        lmu[:64, 4] = d['tshift_mu_vres'][l - 1]
        im["vdT"] = np.ascontiguousarray(projT_b[6080:6144])
        im["v2c"] = np.ascontiguousarray(d['rwkv_v2'][l - 1][:, c])
        im["vfirst"] = np.ascontiguousarray(vfirst)
    im["lmu"] = lmu
    return im


def _run(nc, in_maps):
    res = run_bass_kernel_spmd(nc, in_maps, core_ids=list(range(8)))
    return res.results


def kernel(**inputs):
    d = {k: np.asarray(v) for k, v in inputs.items()}
    x = d["x"]
    Bn, Sn, Dn = x.shape
    xf = x.reshape(Bn * Sn, Dn)
    lay = lambda g: np.ascontiguousarray(g.reshape(16, 128).T.astype(np.float32))
    xT = [np.ascontiguousarray(xf[c * 1024:(c + 1) * 1024].T) for c in range(8)]
    progs = {}
    vfirst = [None] * 8
    out_sh = None
    for l in range(2):
        w = d["w_in"][l] if l == 0 else np.concatenate([d["w_in"][l], d["w_in_vres"][l - 1]], axis=1)
        w = np.ascontiguousarray(w.astype(np.float32))
        ncol = w.shape[1]
        nc1 = build_p1(ncol)
        g1 = lay(d["norm1_g"][l])
        r1 = _run(nc1, [{"xT": xT[c], "w": w, "g": g1} for c in range(8)])
        projT = np.concatenate([r["projT"] for r in r1], axis=1)
        del r1
        projT_b = [np.ascontiguousarray(projT[:, b * 4096:(b + 1) * 4096]) for b in range(2)]
        proj_b = [np.ascontiguousarray(p.T) for p in projT_b]
        nca = build_p2a(l > 0)
        ra = _run(nca, [p2a_inputs(d, l, projT_b[c // 4], c % 4, vfirst[c]) for c in range(8)])
        if l == 0:
            vfirst = [r["vout"] for r in ra]
        ncb = build_p2b()
        rb = _run(ncb, [p2b_inputs(d, l, projT_b[c // 4], c % 4) for c in range(8)])
        ncc = build_p2c()
        rc = _run(ncc, [p2c_inputs(d, l, proj_b[c // 4], d["positions"][c // 4], C_SLOTS[c % 4]) for c in range(8)])
        yT = np.empty((2048, 8192), np.float32)
        for c in range(8):
            b, q = c // 4, c % 4
            ts = slice(b * 4096, (b + 1) * 4096)
            yT[192 * q:192 * q + 192, ts] = ra[c]["yA"].T
            yT[768 + 128 * q:768 + 128 * q + 128, ts] = rb[c]["yT"]
            if q < 3:
                for s_, h in enumerate(C_SLOTS[q]):
                    yT[1280 + 128 * h:1280 + 128 * h + 128, ts] = rc[c]["o2"][s_].T
        del ra, rb, rc, projT, projT_b, proj_b
        final = (l == 1)
        nc3 = build_p3(final)
        ims = []
        for c in range(8):
            im = {"xT": xT[c], "yT": np.ascontiguousarray(yT[:, c * 1024:(c + 1) * 1024]), "wout": np.ascontiguousarray(d["w_out"][l]),
                  "g2": lay(d["norm2_g"][l]), "wg": np.ascontiguousarray(d["ffn_w_gate"][l]), "wu": np.ascontiguousarray(d["ffn_w_up"][l]),
                  "wd": np.ascontiguousarray(d["ffn_w_down"][l])}
            if final:
                im["gf"] = lay(d["final_norm_g"])
            ims.append(im)
        r3 = _run(nc3, ims)
        xT = [r["outT"] for r in r3]
    out = np.concatenate([t.T for t in xT], axis=0).reshape(Bn, Sn, Dn)
    return np.ascontiguousarray(out.astype(np.float32))
```

```python
import math
from contextlib import ExitStack
import numpy as np
import concourse.bass as bass
import concourse.mybir as mybir
from concourse.bass_utils import run_bass_kernel_spmd

F32 = mybir.dt.float32
BF16 = mybir.dt.bfloat16
I32 = mybir.dt.int32
AF = mybir.ActivationFunctionType
ALU = mybir.AluOpType
AX = mybir.AxisListType


class Buf:
    __slots__ = ("name", "last_w", "readers")

    def __init__(self, name=""):
        self.name = name
        self.last_w = None
        self.readers = []


class SemPool:
    def __init__(self, nc):
        self.nc = nc
        self.sems = {}
        self._ctx = []
        self.count = {e: 0 for e in Prog.ENGS}
        self.dma_i = {"sync": 0, "gpsimd": 0, "scalar": 0}
        self.n_cc = 0

    def sem(self, key):
        if key not in self.sems:
            cm = self.nc.semaphore("s_" + key)
            self._ctx.append(cm)
            self.sems[key] = cm.__enter__()
        return self.sems[key]

    def close(self):
        for cm in reversed(self._ctx):
            cm.__exit__(None, None, None)


_POOL = [None]


class Prog:
    ENGS = ("sync", "scalar", "vector", "gpsimd", "tensor")

    def __init__(self, nc, tag="", n_dma_sems=12, self_sync=True):
        self.nc = nc
        self.tag = tag
        self.self_sync = self_sync
        self.own_pool = _POOL[0] is None
        self.pool = SemPool(nc) if self.own_pool else _POOL[0]
        self.ops = {e: [] for e in self.ENGS}
        self.waited = {e: {} for e in self.ENGS}
        self.n_dma_sems = n_dma_sems

    def _collect(self, eng, reads, writes):
        need = {}
        def req(tok):
            if tok is None:
                return
            k, v = tok
            if need.get(k, 0) < v:
                need[k] = v
        for b in reads:
            req(b.last_w)
        for b in writes:
            req(b.last_w)
            for r in b.readers:
                req(r)
        waits = []
        wd = self.waited[eng]
        for k, v in need.items():
            if k == "e_" + eng and (eng == "tensor" or not self.self_sync):
                continue
            if wd.get(k, 0) >= v:
                continue
            wd[k] = v
            waits.append((k, v))
        return waits

    def _mark(self, tok, reads, writes):
        for b in reads:
            b.readers.append(tok)
        for b in writes:
            b.last_w = tok
            b.readers = []
        return tok

    def op(self, eng, fn, reads=(), writes=()):
        waits = self._collect(eng, reads, writes)
        self.pool.count[eng] += 1
        tok = ("e_" + eng, self.pool.count[eng])
        self.ops[eng].append((waits, fn, tok[0], 1))
        return self._mark(tok, reads, writes)

    def dma(self, eng, out, in_, reads=(), writes=(), **kw):
        i = self.pool.dma_i[eng]
        self.pool.dma_i[eng] = i + 1
        key = "d_%s_%d" % (eng, i % self.n_dma_sems)
        val = 16 * (i // self.n_dma_sems + 1)
        waits = self._collect(eng, reads, writes)
        if i >= self.n_dma_sems:
            pv = val - 16
            if self.waited[eng].get(key, 0) < pv:
                self.waited[eng][key] = pv
                waits.append((key, pv))
        fn = lambda e, out=out, in_=in_, kw=kw: e.dma_start(out=out, in_=in_, **kw)
        self.ops[eng].append((waits, fn, key, 16))
        return self._mark((key, val), reads, writes)

    def cc(self, fn, reads=(), writes=()):
        eng = "gpsimd"
        self.pool.n_cc += 1
        key = "cc_%d" % self.pool.n_cc
        waits = self._collect(eng, reads, writes)
        self.ops[eng].append((waits, fn, key, None))
        return self._mark((key, 1), reads, writes)

    def final_wait(self, eng, bufs):
        waits = self._collect(eng, bufs, ())
        self.ops[eng].append((waits, None, None, 0))

    def emit(self):
        nc = self.nc
        sem = self.pool.sem
        with nc.Block() as block:
            for e in self.ENGS:
                ops = self.ops[e]
                if not ops:
                    continue

                def body(engine, ops=ops):
                    for waits, fn, key, inc in ops:
                        for k, v in waits:
                            engine.wait_ge(sem(k), v)
                        if fn is not None:
                            if inc is None:
                                fn(engine).then_inc(sem(key))
                            else:
                                fn(engine).then_inc(sem(key), inc)
                getattr(block, e)(body)
        if self.own_pool:
            self.pool.close()


D = 2048
KC = 16
TOK = 1024
EPS = 1e-6


def rms_to_bf16(P, nc, es, xs, bxs, g_ap, name, out_tile=None, out_buf=None, scr=None):
    sb = lambda n, shp, dt: es.enter_context(nc.sbuf_tensor(name + n, shp, dt))
    ps_ = lambda n, shp, dt: es.enter_context(nc.psum_tensor(name + n, shp, dt))
    if scr is None:
        scr = {}
    if "ones" not in scr:
        scr["ones"] = sb("ones", [128, 128], F32)
        scr["sq"] = [sb("sq%d" % i, [128, TOK], F32) for i in range(2)]
        scr["rstd"] = sb("rstd", [128, TOK], F32)
        scr["pss"] = [ps_("ss%d" % i, [128, 512], F32) for i in range(2)]
        scr["eps"] = sb("eps", [128, 1], F32)
        scr["b"] = dict(ones=Buf(), rstd=Buf(), sq=[Buf(), Buf()], ps=[Buf(), Buf()], eps=Buf())
        P.op("gpsimd", lambda e: e.memset(scr["ones"][:], 1.0), writes=[scr["b"]["ones"]])
        P.op("gpsimd", lambda e: e.memset(scr["eps"][:], EPS), writes=[scr["b"]["eps"]])
    ones, sq, rstd, pss, epst = scr["ones"], scr["sq"], scr["rstd"], scr["pss"], scr["eps"]
    B = scr["b"]
    bones, brstd, bsq, bps, beps = B["ones"], B["rstd"], B["sq"], B["ps"], B["eps"]
    gs = sb("g", [128, KC], F32)
    hT = out_tile if out_tile is not None else sb("hT", [128, KC, TOK], BF16)
    bg = Buf()
    bh = out_buf if out_buf is not None else Buf()
    P.dma("sync", gs[:], g_ap, writes=[bg])
    for c in range(KC):
        s = sq[c % 2]
        P.op("scalar", lambda e, c=c, s=s: e.activation(out=s[:], in_=xs[:, c, :], func=AF.Square),
             reads=[bxs], writes=[bsq[c % 2]])
        for hf in range(2):
            P.op("tensor", lambda e, c=c, s=s, hf=hf: e.matmul(pss[hf][:], lhsT=ones[:], rhs=s[:, hf * 512:(hf + 1) * 512],
                                                               start=(c == 0), stop=(c == KC - 1)),
                 reads=[bones, bsq[c % 2]], writes=[bps[hf]])
    for hf in range(2):
        sl = slice(hf * 512, (hf + 1) * 512)
        P.op("scalar", lambda e, hf=hf, sl=sl: e.activation(out=rstd[:, sl], in_=pss[hf][:], func=AF.Sqrt,
                                                            scale=1.0 / D, bias=epst[:, 0:1]),
             reads=[bps[hf], beps], writes=[brstd])
        P.op("vector", lambda e, sl=sl: e.reciprocal(out=rstd[:, sl], in_=rstd[:, sl]), reads=[brstd], writes=[brstd])
    for c in range(KC):
        P.op("vector", lambda e, c=c: e.scalar_tensor_tensor(out=hT[:, c, :], in0=xs[:, c, :], scalar=gs[:, c:c + 1],
                                                         in1=rstd[:], op0=ALU.mult, op1=ALU.mult),
             reads=[bxs, bg, brstd], writes=[bh])
    return hT, bh


DFF = 5632
NFB = DFF // 128
GRP = 4


S = 4096


def phase_p2b(nc, io, l):
    tg = "b%d_" % l
    gT = io["fm_all"][512:640, :]; xT = io["fm_all"][640:768, :]
    sm = io["sm%d" % l]; wa = io["wa%d" % l]; wx = io["wx%d" % l]
    out = io["y_src"][192:320, :]
    P = Prog(nc, tg)
    with ExitStack() as es:
        sb = lambda n, shp, dt=F32: es.enter_context(nc.sbuf_tensor(tg + n, shp, dt))
        g = sb("g", [128, S]); x = sb("x", [128, S]); xc = sb("xc", [128, S])
        r = sb("r", [128, S]); ii = sb("ii", [128, S]); a = sb("a", [128, S]); u = sb("u", [128, S])
        h = sb("h", [128, S])
        sms = sb("sms", [128, 8]); wab = sb("wab", [128, 128]); wxb = sb("wxb", [128, 128])
        t1 = sb("t1", [128, 8])
        ps = [es.enter_context(nc.psum_tensor(tg + "ps%d" % i, [128, 512], F32)) for i in range(4)]
        bps = [Buf() for _ in range(4)]
        bg, bx, bxc, br, bi, ba_, bu, bh, bsm, bwa, bwx, bt1 = (Buf() for _ in range(12))
        for i in range(4):
            sl = slice(i * 1024, (i + 1) * 1024)
            P.dma("sync", x[:, sl], xT[:, sl], writes=[bx])
        for i in range(4):
            sl = slice(i * 1024, (i + 1) * 1024)
            P.dma("gpsimd", g[:, sl], gT[:, sl], writes=[bg])
        P.dma("sync", sms[:], sm, writes=[bsm])
        P.op("gpsimd", lambda e: e.memset(wab[:], 0.0), writes=[bwa])
        P.op("gpsimd", lambda e: e.memset(wxb[:], 0.0), writes=[bwx])
        for bl in range(2):
            P.dma("sync", wab[bl * 64:(bl + 1) * 64, bl * 64:(bl + 1) * 64], wa[bl], writes=[bwa])
            P.dma("sync", wxb[bl * 64:(bl + 1) * 64, bl * 64:(bl + 1) * 64], wx[bl], writes=[bwx])
        P.op("vector", lambda e: e.tensor_scalar(out=xc[:], in0=x[:], scalar1=sms[:, 3:4], scalar2=sms[:, 4:5], op0=ALU.mult, op1=ALU.add),
             reads=[bx, bsm], writes=[bxc])
        for sh in (1, 2, 3):
            P.op("vector", lambda e, sh=sh: e.scalar_tensor_tensor(out=xc[:, sh:], in0=x[:, :S - sh], scalar=sms[:, 3 - sh:4 - sh],
                                                                   in1=xc[:, sh:], op0=ALU.mult, op1=ALU.add),
                 reads=[bx, bsm, bxc], writes=[bxc])
        P.op("scalar", lambda e: e.activation(out=t1[:, 0:1], in_=sms[:, 7:8], func=AF.Exp, scale=-1.0), reads=[bsm], writes=[bt1])
        P.op("vector", lambda e: e.tensor_scalar(out=t1[:, 1:2], in0=t1[:, 0:1], scalar1=2.0, scalar2=None, op0=ALU.add), reads=[bt1], writes=[bt1])
        P.op("vector", lambda e: e.reciprocal(out=t1[:, 1:2], in_=t1[:, 1:2]), reads=[bt1], writes=[bt1])
        P.op("vector", lambda e: e.tensor_tensor(out=t1[:, 2:3], in0=t1[:, 0:1], in1=t1[:, 1:2], op=ALU.mult), reads=[bt1], writes=[bt1])
        P.op("vector", lambda e: e.tensor_tensor(out=t1[:, 3:4], in0=t1[:, 2:3], in1=t1[:, 2:3], op=ALU.mult), reads=[bt1], writes=[bt1])
        P.op("vector", lambda e: e.memset(t1[:, 4:5], 1.0 / 13.0), reads=[bt1], writes=[bt1])
        for cf in (1.0 / 11, 1.0 / 9, 1.0 / 7, 1.0 / 5, 1.0 / 3, 1.0):
            P.op("vector", lambda e, cf=cf: e.tensor_scalar(out=t1[:, 4:5], in0=t1[:, 4:5], scalar1=t1[:, 3:4], scalar2=float(cf), op0=ALU.mult, op1=ALU.add),
                 reads=[bt1], writes=[bt1])
        P.op("vector", lambda e: e.scalar_tensor_tensor(out=t1[:, 5:6], in0=t1[:, 4:5], scalar=-16.0, in1=t1[:, 2:3], op0=ALU.mult, op1=ALU.mult),
             reads=[bt1], writes=[bt1])
        for pc in range(8):
            sl = slice(pc * 512, (pc + 1) * 512)
            q0, q1 = (2 * pc) % 4, (2 * pc + 1) % 4
            P.op("tensor", lambda e, q0=q0, sl=sl: e.matmul(ps[q0][:], lhsT=wab[:], rhs=xc[:, sl], start=True, stop=True),
                 reads=[bwa, bxc], writes=[bps[q0]])
            P.op("tensor", lambda e, q1=q1, sl=sl: e.matmul(ps[q1][:], lhsT=wxb[:], rhs=xc[:, sl], start=True, stop=True),
                 reads=[bwx, bxc], writes=[bps[q1]])
            P.op("scalar", lambda e, q0=q0, sl=sl: e.activation(out=r[:, sl], in_=ps[q0][:], func=AF.Sigmoid, bias=sms[:, 5:6]),
                 reads=[bps[q0], bsm], writes=[br])
            P.op("scalar", lambda e, q1=q1, sl=sl: e.activation(out=ii[:, sl], in_=ps[q1][:], func=AF.Sigmoid, bias=sms[:, 6:7]),
                 reads=[bps[q1], bsm], writes=[bi])
        P.op("scalar", lambda e: e.activation(out=a[:], in_=r[:], func=AF.Exp, scale=t1[:, 5:6]), reads=[br, bt1], writes=[ba_])
        P.op("vector", lambda e: e.tensor_tensor(out=u[:], in0=a[:], in1=a[:], op=ALU.mult), reads=[ba_], writes=[bu])
        P.op("vector", lambda e: e.tensor_scalar(out=u[:], in0=u[:], scalar1=-1.0, scalar2=1.0, op0=ALU.mult, op1=ALU.add), reads=[bu], writes=[bu])
        P.op("scalar", lambda e: e.activation(out=u[:], in_=u[:], func=AF.Sqrt), reads=[bu], writes=[bu])
        P.op("vector", lambda e: e.tensor_tensor(out=ii[:], in0=ii[:], in1=xc[:], op=ALU.mult), reads=[bi, bxc], writes=[bi])
        P.op("vector", lambda e: e.tensor_tensor(out=u[:], in0=u[:], in1=ii[:], op=ALU.mult), reads=[bu, bi], writes=[bu])
        P.op("vector", lambda e: e.tensor_tensor_scan(out=h[:], data0=a[:], data1=u[:], initial=0.0, op0=ALU.mult, op1=ALU.add),
             reads=[ba_, bu], writes=[bh])
        P.op("scalar", lambda e: e.activation(out=r[:], in_=g[:], func=AF.Square), reads=[bg, br], writes=[br])
        P.op("vector", lambda e: e.tensor_scalar(out=r[:], in0=r[:], scalar1=0.044715, scalar2=1.0, op0=ALU.mult, op1=ALU.add), reads=[br], writes=[br])
        P.op("vector", lambda e: e.tensor_tensor(out=r[:], in0=r[:], in1=g[:], op=ALU.mult), reads=[br, bg], writes=[br])
        P.op("scalar", lambda e: e.activation(out=r[:], in_=r[:], func=AF.Sigmoid, scale=1.5957691216057308), reads=[br], writes=[br])
        P.op("vector", lambda e: e.tensor_tensor(out=r[:], in0=r[:], in1=g[:], op=ALU.mult), reads=[br, bg], writes=[br])
        P.op("vector", lambda e: e.tensor_tensor(out=h[:], in0=h[:], in1=r[:], op=ALU.mult), reads=[br, bh], writes=[bh])
        bout = Buf()
        hb = sb("hb", [128, S], BF16); bhb = Buf()
        P.op("scalar", lambda e: e.copy(out=hb[:], in_=h[:]), reads=[bh], writes=[bhb])
        for i in range(4):
            sl = slice(i * 1024, (i + 1) * 1024)
            P.dma("sync", out[:, sl], hb[:, sl], reads=[bhb], writes=[bout])
        P.final_wait("sync", [bout]); P.final_wait("gpsimd", [bout])
        P.emit()


S = 4096
NT = 32
PI = math.pi
C1 = 6.28125
C2 = 2 * math.pi - 6.28125


def phase_p2c(nc, io, l):
    tg = "c%d_" % l
    tm = io["tm_all"]
    q_in = [tm[:, 576 + 64 * s:576 + 64 * s + 64] for s in range(2)]
    k_in = [tm[:, 704 + 64 * s:704 + 64 * s + 64] for s in range(2)]
    v_in = [tm[:, 832 + 128 * s:832 + 128 * s + 128] for s in range(2)]
    g_in = [tm[:, 1088 + 128 * s:1088 + 128 * s + 128] for s in range(2)]
    pos_in = io["pos"]; invf_in = io["invf"]; ident_in = io["ident"]
    mask_in = io["maskT"]; qdec_in = io["qdec"]; kdec_in = io["kdec"]; g128_in = io["g128"]; gng_in = io["gng%d" % l]
    ysrc = io["y_src"]
    P = Prog(nc, tg)
    with ExitStack() as es:
        sb = lambda n, shp, d=F32: es.enter_context(nc.sbuf_tensor(tg + n, shp, d))
        pst = lambda n, shp: es.enter_context(nc.psum_tensor(tg + n, shp, F32))
        posi = sb("posi", [128, NT], I32); posf = sb("posf", [128, NT]); invf = sb("invf", [128, 32])
        ident = sb("ident", [128, 128])
        ang = sb("ang", [128, NT, 32]); nf = sb("nf", [128, NT, 32]); ni = sb("ni", [128, NT, 32], I32)
        sn = sb("sn", [128, NT, 32]); cs = sb("cs", [128, NT, 32]); tmp = sb("tmp", [128, NT, 32]); tmp2 = sb("tmp2", [128, NT, 32])
        epst = sb("epst", [128, 1])
        bpos, binvf, bident, bang, bnf, bsn, bcs, btmp, btmp2, beps = (Buf() for _ in range(10))
        P.dma("sync", posi[:], pos_in, writes=[bpos])
        P.dma("sync", invf[:], invf_in, writes=[binvf])
        P.dma("sync", ident[:], ident_in, writes=[bident])
        P.op("gpsimd", lambda e: e.memset(epst[:], 1e-5), writes=[beps])
        P.op("vector", lambda e: e.tensor_copy(out=posf[:], in_=posi[:]), reads=[bpos], writes=[bpos])
        for n in range(NT):
            P.op("vector", lambda e, n=n: e.tensor_scalar(out=ang[:, n, :], in0=invf[:], scalar1=posf[:, n:n + 1], scalar2=None, op0=ALU.mult),
                 reads=[bpos, binvf], writes=[bang])
        P.op("vector", lambda e: e.tensor_scalar(out=nf[:], in0=ang[:], scalar1=1.0 / (2 * PI), scalar2=None, op0=ALU.mult), reads=[bang], writes=[bnf])
        P.op("vector", lambda e: e.tensor_copy(out=ni[:], in_=nf[:]), reads=[bnf], writes=[bnf])
        P.op("vector", lambda e: e.tensor_copy(out=nf[:], in_=ni[:]), reads=[bnf], writes=[bnf])
        P.op("vector", lambda e: e.scalar_tensor_tensor(out=ang[:], in0=nf[:], scalar=-C1, in1=ang[:], op0=ALU.mult, op1=ALU.add), reads=[bnf, bang], writes=[bang])
        P.op("vector", lambda e: e.scalar_tensor_tensor(out=ang[:], in0=nf[:], scalar=-C2, in1=ang[:], op0=ALU.mult, op1=ALU.add), reads=[bnf, bang], writes=[bang])

        def wrap(dst, bdst, src, bsrc, shift):
            if shift != 0.0:
                P.op("vector", lambda e: e.tensor_scalar(out=dst[:], in0=src[:], scalar1=float(shift), scalar2=None, op0=ALU.add), reads=[bsrc], writes=[bdst])
                src_, bsrc_ = dst, bdst
            else:
                src_, bsrc_ = src, bsrc
            P.op("vector", lambda e: e.tensor_scalar(out=tmp[:], in0=src_[:], scalar1=PI, scalar2=-2 * PI, op0=ALU.is_gt, op1=ALU.mult), reads=[bsrc_], writes=[btmp])
            P.op("vector", lambda e: e.tensor_scalar(out=tmp2[:], in0=src_[:], scalar1=-PI, scalar2=2 * PI, op0=ALU.is_lt, op1=ALU.mult), reads=[bsrc_], writes=[btmp2])
            P.op("vector", lambda e: e.tensor_tensor(out=dst[:], in0=src_[:], in1=tmp[:], op=ALU.add), reads=[bsrc_, btmp], writes=[bdst])
            P.op("vector", lambda e: e.tensor_tensor(out=dst[:], in0=dst[:], in1=tmp2[:], op=ALU.add), reads=[bdst, btmp2], writes=[bdst])
        wrap(sn, bsn, ang, bang, 0.0)
        wrap(cs, bcs, sn, bsn, PI / 2)
        P.op("scalar", lambda e: e.activation(out=sn[:], in_=sn[:], func=AF.Sin), reads=[bsn], writes=[bsn])
        P.op("scalar", lambda e: e.activation(out=cs[:], in_=cs[:], func=AF.Sin), reads=[bcs], writes=[bcs])

        q = sb("q", [128, NT, 64]); k = sb("k", [128, NT, 64]); qr = sb("qr", [128, NT, 64]); kr = sb("kr", [128, NT, 64])
        kt = sb("kt", [128, NT, 64])
        v = sb("v", [128, NT, 128]); g = sb("g", [128, NT, 128]); o = sb("o", [128, NT, 128])
        maskT = sb("maskT", [128, 128]); qdec = sb("qdec", [64, 128]); kdec = sb("kdec", [128, 2]); g128 = sb("g128", [64, 2])
        gng = sb("gng", [128, 128])
        Sst = sb("Sst", [64, 128])
        qkT = [sb("qkT%d" % i, [64, 2, 128]) for i in range(2)]
        qtT = [sb("qtT%d" % i, [64, 128]) for i in range(2)]
        smk = [sb("smk%d" % i, [128, 128]) for i in range(2)]
        stats = [sb("stats%d" % i, [128, 6]) for i in range(2)]
        mv = [sb("mv%d" % i, [128, 2]) for i in range(2)]
        rs = [sb("rs%d" % i, [128, 1]) for i in range(2)]
        psT = [pst("psT%d" % i, [64, 2, 128]) for i in range(2)]
        pss = [pst("pss%d" % i, [128, 128]) for i in range(2)]
        pso = [pst("pso%d" % i, [128, 128]) for i in range(2)]
        pkv = [pst("pkv%d" % i, [64, 128]) for i in range(2)]
        bq, bk, bqr, bkr, bkt, bv, bg, bo, bmask, bqdec, bkdec, bg128, bgng, bS = (Buf() for _ in range(14))
        bqkT, bqtT, bsmk, bstats, bmv, brs, bpsT, bpss, bpso, bpkv = ([Buf(), Buf()] for _ in range(10))
        P.dma("sync", kdec[:], kdec_in, writes=[bkdec])
        P.dma("sync", g128[:], g128_in, writes=[bg128])
        bout = Buf()
        oT = sb("oT", [128, S], BF16); boT = Buf()
        for sl in range(2):
            vw = lambda ap: ap.rearrange("(n p) d -> p n d", p=128)
            P.dma("sync", q[:], vw(q_in[sl]), writes=[bq])
            P.dma("gpsimd", k[:], vw(k_in[sl]), writes=[bk])
            for hh in range(2):
                P.dma("sync", v[:, hh * 16:(hh + 1) * 16, :], vw(v_in[sl])[:, hh * 16:(hh + 1) * 16, :], writes=[bv])
                P.dma("gpsimd", g[:, hh * 16:(hh + 1) * 16, :], vw(g_in[sl])[:, hh * 16:(hh + 1) * 16, :], writes=[bg])
            P.dma("sync", maskT[:], mask_in[sl], writes=[bmask])
            P.dma("sync", qdec[:], qdec_in[sl], writes=[bqdec])
            P.dma("sync", gng[:], gng_in[sl], writes=[bgng])
            P.op("gpsimd", lambda e: e.memset(Sst[:], 0.0), writes=[bS])
            for (src, bsrc, dst, bdst) in ((q, bq, qr, bqr), (k, bk, kr, bkr)):
                s1, s2 = src[:, :, 0:32], src[:, :, 32:64]
                d1, d2 = dst[:, :, 0:32], dst[:, :, 32:64]
                P.op("vector", lambda e, s1=s1, d1=d1: e.tensor_tensor(out=d1, in0=s1, in1=cs[:], op=ALU.mult), reads=[bsrc, bcs], writes=[bdst])
                P.op("vector", lambda e, s2=s2: e.tensor_tensor(out=tmp[:], in0=s2, in1=sn[:], op=ALU.mult), reads=[bsrc, bsn], writes=[btmp])
                P.op("vector", lambda e, d1=d1: e.tensor_tensor(out=d1, in0=d1, in1=tmp[:], op=ALU.subtract), reads=[btmp, bdst], writes=[bdst])
                P.op("vector", lambda e, s2=s2, d2=d2: e.tensor_tensor(out=d2, in0=s2, in1=cs[:], op=ALU.mult), reads=[bsrc, bcs], writes=[bdst])
                P.op("vector", lambda e, s1=s1: e.tensor_tensor(out=tmp2[:], in0=s1, in1=sn[:], op=ALU.mult), reads=[bsrc, bsn], writes=[btmp2])
                P.op("vector", lambda e, d2=d2: e.tensor_tensor(out=d2, in0=d2, in1=tmp2[:], op=ALU.add), reads=[btmp2, bdst], writes=[bdst])
            P.op("vector", lambda e, sl=sl: e.tensor_scalar(out=kt[:], in0=kr[:], scalar1=kdec[:, sl:sl + 1], scalar2=None, op0=ALU.mult),
                 reads=[bkr, bkdec], writes=[bkt])
            for n in range(NT):
                i = n % 2
                P.op("tensor", lambda e, n=n, i=i: e.transpose(psT[i][:, 0, :], qr[:, n, :], ident[:]), reads=[bqr, bident], writes=[bpsT[i]])
                P.op("tensor", lambda e, n=n, i=i: e.transpose(psT[i][:, 1, :], kr[:, n, :], ident[:]), reads=[bkr, bident], writes=[bpsT[i]])
                P.op("scalar", lambda e, i=i: e.copy(out=qkT[i][:], in_=psT[i][:]), reads=[bpsT[i]], writes=[bqkT[i]])
                P.op("vector", lambda e, i=i: e.tensor_tensor(out=qtT[i][:], in0=psT[i][:, 0, :], in1=qdec[:], op=ALU.mult), reads=[bpsT[i], bqdec], writes=[bqtT[i]])
                P.op("tensor", lambda e, i=i: e.matmul(pss[i][:], lhsT=qkT[i][:, 1, :], rhs=qkT[i][:, 0, :], start=True, stop=True), reads=[bqkT[i]], writes=[bpss[i]])
                P.op("vector", lambda e, i=i: e.tensor_tensor(out=smk[i][:], in0=pss[i][:], in1=maskT[:], op=ALU.mult), reads=[bpss[i], bmask], writes=[bsmk[i]])
                P.op("tensor", lambda e, i=i, n=n: e.matmul(pso[i][:], lhsT=smk[i][:], rhs=v[:, n, :], start=True, stop=False), reads=[bsmk[i], bv], writes=[bpso[i]])
                P.op("tensor", lambda e, i=i: e.matmul(pso[i][:], lhsT=qtT[i][:], rhs=Sst[:], start=False, stop=True), reads=[bqtT[i], bS], writes=[bpso[i]])
                P.op("tensor", lambda e, i=i, n=n: e.matmul(pkv[i][:], lhsT=kt[:, n, :], rhs=v[:, n, :], start=True, stop=True), reads=[bkt, bv], writes=[bpkv[i]])
                P.op("vector", lambda e, i=i, sl=sl: e.scalar_tensor_tensor(out=Sst[:], in0=Sst[:], scalar=g128[:, sl:sl + 1], in1=pkv[i][:], op0=ALU.mult, op1=ALU.add),
                     reads=[bpkv[i], bg128, bS], writes=[bS])
                P.op("vector", lambda e, i=i: e.bn_stats(out=stats[i][:], in_=pso[i][:]), reads=[bpso[i]], writes=[bstats[i]])
                P.op("vector", lambda e, i=i: e.bn_aggr(out=mv[i][:], in_=stats[i][:]), reads=[bstats[i]], writes=[bmv[i]])
                P.op("scalar", lambda e, i=i: e.activation(out=rs[i][:], in_=mv[i][:, 1:2], func=AF.Sqrt, bias=epst[:, 0:1]), reads=[bmv[i], beps], writes=[brs[i]])
                P.op("vector", lambda e, i=i: e.reciprocal(out=rs[i][:], in_=rs[i][:]), reads=[brs[i]], writes=[brs[i]])
                P.op("vector", lambda e, i=i, n=n: e.tensor_scalar(out=o[:, n, :], in0=pso[i][:], scalar1=mv[i][:, 0:1], scalar2=rs[i][:, 0:1], op0=ALU.subtract, op1=ALU.mult),
                     reads=[bpso[i], bmv[i], brs[i]], writes=[bo])
                P.op("gpsimd", lambda e, n=n: e.tensor_tensor(out=o[:, n, :], in0=o[:, n, :], in1=gng[:], op=ALU.mult), reads=[bo, bgng], writes=[bo])
            P.op("scalar", lambda e: e.activation(out=g[:], in_=g[:], func=AF.Silu), reads=[bg], writes=[bg])
            P.op("vector", lambda e: e.tensor_tensor(out=o[:], in0=o[:], in1=g[:], op=ALU.mult), reads=[bo, bg], writes=[bo])
            for n in range(NT):
                i = n % 2
                P.op("tensor", lambda e, n=n, i=i: e.transpose(pss[i][:], o[:, n, :], ident[:]), reads=[bo, bident], writes=[bpss[i]])
                P.op("scalar", lambda e, n=n, i=i: e.copy(out=oT[:, n * 128:(n + 1) * 128], in_=pss[i][:]), reads=[bpss[i]], writes=[boT])
            for hh in range(4):
                cs_ = slice(hh * 1024, (hh + 1) * 1024)
                P.dma("sync", ysrc[320 + 128 * sl:448 + 128 * sl, cs_], oT[:, cs_], reads=[boT], writes=[bout])
        P.final_wait("sync", [bout]); P.final_wait("gpsimd", [bout])
        P.emit()


def p2c_consts(heads):
    maskT = np.zeros((2, 128, 128), np.float32); qdec = np.zeros((2, 64, 128), np.float32)
    kdec = np.zeros((128, 2), np.float32); g128 = np.zeros((64, 2), np.float32)
    idx = np.arange(128)
    for s, hd in enumerate(heads):
        lg = float(np.log1p(-np.exp2(np.float32(-5.0 - hd))).astype(np.float32))
        m = idx[:, None]; c = idx[None, :]
        same = (m // 64) == (c // 64)
        earlier = (m // 64) < (c // 64)
        dec = np.where(same, np.exp(lg * np.abs(c - m)), np.where(earlier, np.exp(lg * (c - m)), 0.0))
        maskT[s] = (dec * 0.125).astype(np.float32)
        qdec[s] = np.exp(lg * (idx + 1.0))[None, :].astype(np.float32)
        kdec[:, s] = (np.exp(lg * (127.0 - idx)) * 0.125).astype(np.float32)
        g128[:, s] = np.float32(np.exp(lg * 128.0))
    invf = (10000.0 ** (-np.arange(0, 64, 2, dtype=np.float32) / 64)).astype(np.float32)
    return dict(maskT=maskT, qdec=qdec, kdec=kdec, g128=g128, invf=np.ascontiguousarray(np.broadcast_to(invf, (128, 32))),
                ident=np.eye(128, dtype=np.float32))


C_SLOTS = [(0, 1), (2, 3), (4, 5), (4, 5)]


VARA = False

S = 4096
NT = 32
W3 = 192
NEG_EHALF = -math.exp(-0.5)
A_GN_EPS = 64e-5


def phase_p2a(nc, io, l, nt=NT, dbg=9):
    has_vres = l > 0
    tg = "a%d_" % l
    tmA = io["tm_all"]; fm = io["fm_all"]
    rkv_in = tmA[:, 0:576]; mu_in = io["mu_rkv%d" % l]
    wd_in = fm[0:96, :]; ad_in = fm[96:192, :]; gd_in = fm[192:448, :]
    lmu_in = io["lmu%d" % l]; w2_in = io["w2c%d" % l]; a2_in = io["a2c%d" % l]; g2_in = io["g2c%d" % l]
    par_in = io["par%d" % l]; cst_in = io["cst"]; m4_in = io["mask4"]
    if has_vres:
        vd_in = fm[448:512, :]; v2_in = io["v2c"]
    vf_in = io["vfirst"]
    v_out = io["vfirst"]
    ysrc = io["y_src"]
    P = Prog(nc, tg)
    with ExitStack() as es:
        sb = lambda n, shp, d=F32: es.enter_context(nc.sbuf_tensor(tg + n, shp, d))
        V = lambda fn, r, w: P.op("vector", fn, reads=r, writes=w)
        A = lambda fn, r, w: P.op("scalar", fn, reads=r, writes=w)
        G = lambda fn, r, w: P.op("gpsimd", fn, reads=r, writes=w)
        T = lambda fn, r, w: P.op("tensor", fn, reads=r, writes=w)
        cst = sb("cst", [128, 4, 128]); m4 = sb("m4", [128, 4, 128]); par = sb("par", [128, 8, W3]); mu = sb("mu", [128, 576])
        lmu = sb("lmu", [128, 5]); w2 = sb("w2", [96, W3]); a2 = sb("a2", [96, W3]); g2 = sb("g2", [128, 2, W3])
        ones = sb("ones", [128, 1]); epsg = sb("epsg", [128, 1])
        bcst, bm4, bpar, bmu, blmu, bw2, ba2, bg2, bones, bepsg = (Buf() for _ in range(10))
        P.dma("sync", cst[:], cst_in, writes=[bcst]); P.dma("sync", m4[:], m4_in, writes=[bm4])
        P.dma("sync", par[:], par_in, writes=[bpar]); P.dma("sync", mu[:], mu_in, writes=[bmu])
        P.dma("sync", lmu[:], lmu_in, writes=[blmu]); P.dma("sync", w2[:], w2_in, writes=[bw2])
        P.dma("sync", a2[:], a2_in, writes=[ba2])
        P.dma("sync", g2[:], g2_in.rearrange("(c p) n -> p c n", p=128), writes=[bg2])
        G(lambda e: e.memset(ones[:], 1.0), [], [bones]); G(lambda e: e.memset(epsg[:], A_GN_EPS), [], [bepsg])
        ident, triu, trirev, maskn = cst[:, 0, :], cst[:, 1, :], cst[:, 2, :], cst[:, 3, :]
        if has_vres:
            v2 = sb("v2", [64, W3]); bv2 = Buf()
            P.dma("sync", v2[:], v2_in, writes=[bv2])
        ltmp = sb("ltmp", [128, S]); bltmp = Buf()
        lo = {}
        specs = [("wd", wd_in, 96, 0, AF.Tanh), ("ad", ad_in, 96, 1, None), ("gd0", gd_in[0:128], 128, 2, AF.Sigmoid),
                 ("gd1", gd_in[128:256], 128, 3, AF.Sigmoid)]
        if has_vres:
            specs.append(("vd", vd_in, 64, 4, None))
        for name, src, rows, mcol, fn in specs:
            t = sb("lo_" + name, [rows, S + 1]); b = Buf()
            G(lambda e, t=t: e.memset(t[:, 0:1], 0.0), [], [b])
            for i in range(4):
                P.dma("sync" if i % 2 == 0 else "gpsimd", t[:, 1 + i * 1024:1 + (i + 1) * 1024], src[:, i * 1024:(i + 1) * 1024], writes=[b])
            V(lambda e, t=t, rows=rows: e.tensor_tensor(out=ltmp[:rows, :], in0=t[:, 0:S], in1=t[:, 1:S + 1], op=ALU.subtract), [b, bltmp], [bltmp])
            V(lambda e, t=t, rows=rows, mcol=mcol: e.scalar_tensor_tensor(out=t[:, 1:S + 1], in0=ltmp[:rows, :], scalar=lmu[:rows, mcol:mcol + 1],
                                                                      in1=t[:, 1:S + 1], op0=ALU.mult, op1=ALU.add), [bltmp, blmu, b], [b])
            if fn is not None:
                A(lambda e, t=t, fn=fn: e.activation(out=t[:, 1:S + 1], in_=t[:, 1:S + 1], func=fn), [b], [b])
            lo[name] = (t, b)
        NS = 2
        cur = [sb("cur%d" % i, [128, 576]) for i in range(NS)]; prv = [sb("prv%d" % i, [128, 576]) for i in range(NS)]
        bcur = [Buf() for _ in range(NS)]; bprv = [Buf() for _ in range(NS)]
        tt = [sb("tt%d" % i, [128, 8, W3]) for i in range(NS)]; btt = [Buf() for _ in range(NS)]
        rk = [sb("rk%d" % i, [128, 3]) for i in range(NS)]; brk = [Buf() for _ in range(NS)]
        tm = sb("tm", [128, 4, W3]); btm = Buf()
        ss = sb("ss", [128, 3]); bss = Buf()
        if has_vres:
            vf = [sb("vf%d" % i, [128, W3]) for i in range(NS)]; bvf = [Buf() for _ in range(NS)]
        ee = sb("ee", [128, 4, W3]); bee = Buf()
        dd = sb("dd", [128, 6, W3]); bdd = Buf()
        wc = sb("wc", [64, 4]); bwc = Buf()
        T4 = sb("T4", [64, 4, 128]); bT4 = Buf()
        M4 = sb("M4", [128, 4, 128]); bM4 = Buf()
        Pk = [sb("Pk%d" % i, [128, 128]) for i in range(2)]; bPk = [Buf(), Buf()]
        Qk = [sb("Qk%d" % i, [128, 128]) for i in range(2)]; bQk = [Buf(), Buf()]
        Z = sb("Z", [128, 128]); bZ = Buf()
        nX = sb("nX", [128, 64]); bnX = Buf()
        U = sb("U", [128, 64]); bU = Buf()
        ST = [sb("ST%d" % h, [64, 64]) for h in range(3)]; bST = [Buf() for _ in range(3)]
        yt = [sb("yt%d" % i, [128, W3]) for i in range(NS)]; byt = [Buf() for _ in range(NS)]
        st6 = sb("st6", [128, 6]); bst6 = Buf(); mv = sb("mv", [128, 2]); bmv = Buf(); rsd = sb("rsd", [128, 1]); brsd = Buf()
        for h in range(3):
            G(lambda e, h=h: e.memset(ST[h][:], 0.0), [], [bST[h]])
        ytT = [sb("ytT%d" % i, [128, 2, 128], BF16) for i in range(NS)]; bytT = [Buf() for _ in range(NS)]
        pb = [es.enter_context(nc.psum_tensor(tg + "pb%d" % i, [128, 512], F32)) for i in range(8)]
        bpb = [Buf() for _ in range(8)]
        pL1, pL2, pGG, pTT, pPP, pIV, pCH = pb[0], pb[1], pb[2], pb[3], pb[4], pb[5], pb[6]
        bL1, bL2, bGG, bTT, bPP, bIV, bCH = bpb[0], bpb[1], bpb[2], bpb[3], bpb[4], bpb[5], bpb[6]
        rkv_v = rkv_in
        bout = Buf()
        wdT, bwd = lo["wd"]; adT, bad = lo["ad"]; gd0, bgd0 = lo["gd0"]; gd1, bgd1 = lo["gd1"]
        for n in range(nt):
            s = n % NS
            t0 = n * 128
            c_, p_, t_ = cur[s], prv[s], tt[s]
            P.dma("sync", c_[:], rkv_v[t0:t0 + 128, :], writes=[bcur[s]])
            if n == 0:
                G(lambda e, p_=p_: e.memset(p_[:], 0.0), [], [bprv[s]])
                P.dma("gpsimd", p_[1:128, :], rkv_v[0:127, :], writes=[bprv[s]])
            else:
                P.dma("gpsimd", p_[:], rkv_v[t0 - 1:t0 + 127, :], writes=[bprv[s]])
            if has_vres:
                P.dma("sync", vf[s][:], vf_in[t0:t0 + 128, :], writes=[bvf[s]])
            V(lambda e, c_=c_, p_=p_: e.tensor_tensor(out=p_[:], in0=p_[:], in1=c_[:], op=ALU.subtract), [bcur[s], bprv[s]], [bprv[s]])
            G(lambda e, p_=p_: e.tensor_tensor(out=p_[:], in0=p_[:], in1=mu[:], op=ALU.mult), [bprv[s], bmu], [bprv[s]])
            V(lambda e, c_=c_, p_=p_: e.tensor_tensor(out=c_[:], in0=c_[:], in1=p_[:], op=ALU.add), [bcur[s], bprv[s]], [bcur[s]])
            r_, k_, v_ = c_[:, 0:W3], c_[:, W3:2 * W3], c_[:, 2 * W3:3 * W3]
            tsl = slice(1 + t0, 1 + t0 + 128)
            T(lambda e, tsl=tsl: e.matmul(pL1[:, 0:W3], lhsT=wdT[:, tsl], rhs=w2[:], start=True, stop=True), [bwd, bw2], [bL1])
            T(lambda e, tsl=tsl: e.matmul(pL1[:, W3:2 * W3], lhsT=adT[:, tsl], rhs=a2[:], start=True, stop=True), [bad, ba2], [bL1])
            T(lambda e, tsl=tsl: e.matmul(pL2[:, 0:W3], lhsT=gd0[:, tsl], rhs=g2[:, 0, :], start=True, stop=False), [bgd0, bg2], [bL2])
            T(lambda e, tsl=tsl: e.matmul(pL2[:, 0:W3], lhsT=gd1[:, tsl], rhs=g2[:, 1, :], start=False, stop=True), [bgd1, bg2], [bL2])
            if has_vres:
                vdT, bvd = lo["vd"]
                T(lambda e, tsl=tsl: e.matmul(pL2[:, W3:2 * W3], lhsT=vdT[:, tsl], rhs=v2[:], start=True, stop=True), [bvd, bv2], [bL2])
            V(lambda e, t_=t_: e.tensor_tensor(out=t_[:, 5, :], in0=pL1[:, 0:W3], in1=par[:, 0, :], op=ALU.add), [bL1, bpar, btt[s]], [btt[s]])
            A(lambda e, t_=t_: e.activation(out=t_[:, 5, :], in_=t_[:, 5, :], func=AF.Sigmoid), [btt[s]], [btt[s]])
            V(lambda e, t_=t_: e.tensor_scalar(out=t_[:, 5, :], in0=t_[:, 5, :], scalar1=NEG_EHALF, scalar2=None, op0=ALU.mult), [btt[s]], [btt[s]])
            V(lambda e, t_=t_: e.tensor_tensor(out=t_[:, 7, :], in0=pL1[:, W3:2 * W3], in1=par[:, 1, :], op=ALU.add), [bL1, bpar, btt[s]], [btt[s]])
            A(lambda e, t_=t_: e.activation(out=t_[:, 7, :], in_=t_[:, 7, :], func=AF.Sigmoid), [btt[s]], [btt[s]])
            A(lambda e, t_=t_: e.copy(out=t_[:, 6, :], in_=pL2[:, 0:W3]), [bL2, btt[s]], [btt[s]])
            G(lambda e, t_=t_, r_=r_: e.tensor_copy(out=t_[:, 0, :], in_=r_), [bcur[s], btt[s]], [btt[s]])
            if has_vres:
                V(lambda e: e.tensor_tensor(out=tm[:, 0, :], in0=pL2[:, W3:2 * W3], in1=par[:, 2, :], op=ALU.add), [bL2, bpar, btm], [btm])
                A(lambda e: e.activation(out=tm[:, 0, :], in_=tm[:, 0, :], func=AF.Sigmoid), [btm], [btm])
                V(lambda e, v_=v_, s=s: e.tensor_tensor(out=tm[:, 1, :], in0=vf[s][:], in1=v_, op=ALU.subtract), [bvf[s], bcur[s], btm], [btm])
                V(lambda e: e.tensor_tensor(out=tm[:, 1, :], in0=tm[:, 1, :], in1=tm[:, 0, :], op=ALU.mult), [btm], [btm])
                V(lambda e, t_=t_, v_=v_: e.tensor_tensor(out=t_[:, 2, :], in0=tm[:, 1, :], in1=v_, op=ALU.add), [btm, bcur[s], btt[s]], [btt[s]])
            else:
                G(lambda e, t_=t_, v_=v_: e.tensor_copy(out=t_[:, 2, :], in_=v_), [bcur[s], btt[s]], [btt[s]])
            if not has_vres:
                P.dma("sync", v_out[t0:t0 + 128, :], t_[:, 2, :], reads=[btt[s]], writes=[bout])
            V(lambda e, t_=t_, k_=k_: e.tensor_tensor(out=t_[:, 3, :], in0=k_, in1=par[:, 3, :], op=ALU.mult), [bcur[s], bpar, btt[s]], [btt[s]])
            V(lambda e, t_=t_: e.tensor_tensor(out=tm[:, 2, :], in0=t_[:, 3, :], in1=t_[:, 3, :], op=ALU.mult), [btt[s], btm], [btm])
            V(lambda e: e.tensor_reduce(out=ss[:], in_=tm[:, 2, :].rearrange("p (h j) -> p h j", h=3), axis=AX.X, op=ALU.add), [btm, bss], [bss])
            A(lambda e: e.activation(out=ss[:], in_=ss[:], func=AF.Sqrt), [bss], [bss])
            V(lambda e: e.tensor_scalar(out=ss[:], in0=ss[:], scalar1=1e-12, scalar2=None, op0=ALU.max), [bss], [bss])
            V(lambda e: e.reciprocal(out=ss[:], in_=ss[:]), [bss], [bss])
            for h in range(3):
                hs = slice(64 * h, 64 * h + 64)
                V(lambda e, t_=t_, hs=hs, h=h: e.tensor_scalar(out=t_[:, 3, hs], in0=t_[:, 3, hs], scalar1=ss[:, h:h + 1], scalar2=None, op0=ALU.mult),
                  [bss, btt[s]], [btt[s]])
            V(lambda e, t_=t_: e.scalar_tensor_tensor(out=tm[:, 3, :], in0=t_[:, 7, :], scalar=-1.0, in1=par[:, 4, :], op0=ALU.add, op1=ALU.mult),
              [btt[s], bpar, btm], [btm])
            V(lambda e, t_=t_, k_=k_: e.scalar_tensor_tensor(out=t_[:, 1, :], in0=tm[:, 3, :], scalar=1.0, in1=k_, op0=ALU.add, op1=ALU.mult),
              [btm, bcur[s], btt[s]], [btt[s]])
            G(lambda e, t_=t_: e.tensor_tensor(out=t_[:, 4, :], in0=t_[:, 3, :], in1=t_[:, 7, :], op=ALU.mult), [btt[s]], [btt[s]])
            V(lambda e, t_=t_: e.tensor_tensor(out=tm[:, 3, :], in0=t_[:, 0, :], in1=t_[:, 1, :], op=ALU.mult), [btt[s], btm], [btm])
            V(lambda e: e.tensor_tensor(out=tm[:, 3, :], in0=tm[:, 3, :], in1=par[:, 5, :], op=ALU.mult), [btm, bpar], [btm])
            V(lambda e, s=s: e.tensor_reduce(out=rk[s][:], in_=tm[:, 3, :].rearrange("p (h j) -> p h j", h=3), axis=AX.X, op=ALU.add), [btm, brk[s]], [brk[s]])
            if dbg <= 1:
                P.dma("sync", y_out[t0:t0 + 128, :], t_[:, 6, :], reads=[btt[s]], writes=[bout])
                continue
            T(lambda e, t_=t_: e.matmul(pGG[:, 0:W3], lhsT=triu, rhs=t_[:, 5, :], start=True, stop=True), [bcst, btt[s]], [bGG])
            T(lambda e, t_=t_: e.matmul(pGG[:, W3:2 * W3], lhsT=trirev, rhs=t_[:, 5, :], start=True, stop=True), [bcst, btt[s]], [bGG])
            for h in range(3):
                T(lambda e, t_=t_, h=h: e.matmul(pGG[0:64, 400 + h:401 + h], lhsT=t_[:, 5, 64 * h:64 * h + 64], rhs=ones[:], start=True, stop=True),
                  [btt[s], bones], [bGG])
            A(lambda e: e.activation(out=ee[:, 0, :], in_=pGG[:, 0:W3], func=AF.Exp), [bGG, bee], [bee])
            A(lambda e: e.activation(out=ee[:, 1, :], in_=pGG[:, 0:W3], func=AF.Exp, scale=-1.0), [bGG, bee], [bee])
            V(lambda e, t_=t_: e.tensor_tensor(out=ee[:, 2, :], in0=pGG[:, 0:W3], in1=t_[:, 5, :], op=ALU.subtract), [bGG, btt[s], bee], [bee])
            A(lambda e: e.activation(out=ee[:, 2, :], in_=ee[:, 2, :], func=AF.Exp), [bee], [bee])
            A(lambda e: e.activation(out=ee[:, 3, :], in_=pGG[:, W3:2 * W3], func=AF.Exp), [bGG, bee], [bee])
            A(lambda e: e.activation(out=wc[:, 0:3], in_=pGG[0:64, 400:403], func=AF.Exp), [bGG, bwc], [bwc])
            for (di, ti, ei, eng) in ((0, 0, 0, V), (1, 3, 2, G), (2, 1, 1, V), (3, 4, 1, G), (4, 1, 3, V), (5, 4, 3, G)):
                eng(lambda e, t_=t_, di=di, ti=ti, ei=ei: e.tensor_tensor(out=dd[:, di, :], in0=t_[:, ti, :], in1=ee[:, ei, :], op=ALU.mult),
                    [btt[s], bee, bdd], [bdd])
            if dbg <= 2:
                P.dma("sync", y_out[t0:t0 + 128, :], dd[:, 0, :], reads=[bdd], writes=[bout])
                continue
            for h in range(3):
                hs = slice(64 * h, 64 * h + 64)
                for j in range(4):
                    T(lambda e, j=j, hs=hs: e.transpose(pTT[0:64, j * 128:(j + 1) * 128], dd[:, j, hs], ident), [bdd, bcst], [bTT])
                A(lambda e: e.copy(out=T4[:], in_=pTT[0:64, :].rearrange("p (a b) -> p a b", a=4)), [bTT, bT4], [bT4])
                if dbg <= 3:
                    continue
                T(lambda e: e.matmul(pPP[:, 0:256], lhsT=T4[:, 2, :], rhs=T4[:, 0:2, :].rearrange("p a b -> p (a b)"), start=True, stop=True), [bT4], [bPP])
                T(lambda e: e.matmul(pPP[:, 256:512], lhsT=T4[:, 3, :], rhs=T4[:, 0:2, :].rearrange("p a b -> p (a b)"), start=True, stop=True), [bT4], [bPP])
                T(lambda e: e.matmul(pIV[:, 0:128], lhsT=T4[:, 1, :], rhs=T4[:, 3, :], start=True, stop=True), [bT4], [bIV])
                V(lambda e: e.tensor_tensor(out=M4[:].rearrange("p a b -> p (a b)"), in0=pPP[:], in1=m4[:].rearrange("p a b -> p (a b)"), op=ALU.mult),
                  [bPP, bm4, bM4], [bM4])
                V(lambda e: e.tensor_tensor(out=Pk[0][:], in0=pIV[:, 0:128], in1=maskn, op=ALU.mult), [bIV, bcst, bPk[0]], [bPk[0]])
                (V if VARA else G)(lambda e: e.tensor_copy(out=Qk[0][:], in_=M4[:, 3, :]), [bM4, bQk[0]], [bQk[0]])
                (V if VARA else G)(lambda e: e.tensor_tensor(out=Z[:], in0=ident, in1=M4[:, 3, :], op=ALU.subtract), [bM4, bcst, bZ], [bZ])
                if dbg <= 4:
                    continue
                for lv in range(1, 7):
                    pi, ci = (lv - 1) % 2, lv % 2
                    STEP = 9
                    if lv < 6 and STEP >= 1:
                        T(lambda e, pi=pi: e.matmul(pIV[:, 128:256], lhsT=Pk[pi][:], rhs=Qk[pi][:], start=True, stop=True), [bPk[pi], bQk[pi]], [bIV])
                    if STEP >= 2:
                        T(lambda e, pi=pi: e.matmul(pIV[:, 256:384], lhsT=Qk[pi][:], rhs=Pk[pi][:], start=True, stop=True), [bPk[pi], bQk[pi]], [bIV])
                    if lv < 6 and STEP >= 3:
                        A(lambda e, ci=ci: e.copy(out=Qk[ci][:], in_=pIV[:, 128:256]), [bIV, bQk[ci]], [bQk[ci]])
                    if STEP >= 4:
                        A(lambda e, ci=ci: e.copy(out=Pk[ci][:], in_=pIV[:, 256:384]), [bIV, bPk[ci]], [bPk[ci]])
                    if STEP >= 5:
                        T(lambda e, ci=ci: e.matmul(pIV[:, 384:512], lhsT=Pk[ci][:], rhs=Z[:], start=True, stop=True), [bPk[ci], bZ], [bIV])
                    if STEP >= 6:
                        V(lambda e: e.tensor_tensor(out=Z[:], in0=Z[:], in1=pIV[:, 384:512], op=ALU.add), [bIV, bZ], [bZ])
                if dbg <= 5:
                    continue
                vh = t_[:, 2, hs]
                T(lambda e, vh=vh: e.matmul(pCH[:, 0:64], lhsT=M4[:, 1, :], rhs=vh, start=True, stop=False), [bM4, btt[s]], [bCH])
                T(lambda e, h=h: e.matmul(pCH[:, 0:64], lhsT=T4[:, 1, :], rhs=ST[h][:], start=False, stop=True), [bT4, bST[h]], [bCH])
                A(lambda e: e.activation(out=nX[:], in_=pCH[:, 0:64], func=AF.Copy, scale=-1.0), [bCH, bnX], [bnX])
                T(lambda e: e.matmul(pCH[:, 64:128], lhsT=Z[:], rhs=nX[:], start=True, stop=True), [bZ, bnX], [bCH])
                V(lambda e: e.tensor_scalar(out=U[:], in0=pCH[:, 64:128], scalar1=1.0, scalar2=None, op0=ALU.mult), [bCH, bU], [bU])
                T(lambda e, h=h: e.matmul(pCH[:, 128:192], lhsT=T4[:, 0, :], rhs=ST[h][:], start=True, stop=False), [bT4, bST[h]], [bCH])
                T(lambda e: e.matmul(pCH[:, 128:192], lhsT=M4[:, 2, :], rhs=U[:], start=False, stop=False), [bM4, bU], [bCH])
                T(lambda e, vh=vh: e.matmul(pCH[:, 128:192], lhsT=M4[:, 0, :], rhs=vh, start=False, stop=True), [bM4, btt[s]], [bCH])
                T(lambda e, hs=hs: e.matmul(pCH[0:64, 192:256], lhsT=dd[:, 5, hs], rhs=U[:], start=True, stop=False), [bdd, bU], [bCH])
                T(lambda e, hs=hs, vh=vh: e.matmul(pCH[0:64, 192:256], lhsT=dd[:, 4, hs], rhs=vh, start=False, stop=True), [bdd, btt[s]], [bCH])
                V(lambda e, h=h: e.scalar_tensor_tensor(out=ST[h][:], in0=ST[h][:], scalar=wc[:, h:h + 1], in1=pCH[0:64, 192:256], op0=ALU.mult, op1=ALU.add),
                  [bCH, bwc, bST[h]], [bST[h]])
                V(lambda e: e.bn_stats(out=st6[:], in_=pCH[:, 128:192]), [bCH, bst6], [bst6])
                V(lambda e: e.bn_aggr(out=mv[:], in_=st6[:]), [bst6, bmv], [bmv])
                A(lambda e: e.activation(out=rsd[:], in_=mv[:, 1:2], func=AF.Sqrt, bias=epsg[:, 0:1]), [bmv, bepsg, brsd], [brsd])
                V(lambda e: e.reciprocal(out=rsd[:], in_=rsd[:]), [brsd], [brsd])
                V(lambda e, s=s, hs=hs: e.tensor_scalar(out=yt[s][:, hs], in0=pCH[:, 128:192], scalar1=mv[:, 0:1], scalar2=rsd[:, 0:1], op0=ALU.subtract, op1=ALU.mult),
                  [bCH, bmv, brsd, byt[s]], [byt[s]])
            if dbg <= 5:
                P.dma("sync", y_out[t0:t0 + 128, :], dd[:, 0, :], reads=[bdd, bT4, bM4, bZ, bPk[0], bPk[1], bQk[0], bQk[1]], writes=[bout])
                continue
            G(lambda e, s=s: e.tensor_tensor(out=yt[s][:], in0=yt[s][:], in1=par[:, 6, :], op=ALU.mult), [byt[s], bpar], [byt[s]])
            G(lambda e, s=s: e.tensor_tensor(out=yt[s][:], in0=yt[s][:], in1=par[:, 7, :], op=ALU.add), [byt[s], bpar], [byt[s]])
            for h in range(3):
                hs = slice(64 * h, 64 * h + 64)
                V(lambda e, s=s, hs=hs, h=h, t_=t_: e.scalar_tensor_tensor(out=yt[s][:, hs], in0=t_[:, 2, hs], scalar=rk[s][:, h:h + 1], in1=yt[s][:, hs],
                                                                          op0=ALU.mult, op1=ALU.add), [btt[s], brk[s], byt[s]], [byt[s]])
            V(lambda e, s=s, t_=t_: e.tensor_tensor(out=yt[s][:], in0=yt[s][:], in1=t_[:, 6, :], op=ALU.mult), [byt[s], btt[s]], [byt[s]])
            T(lambda e, s=s: e.transpose(pb[7][:, 0:128], yt[s][:, 0:128], ident), [byt[s], bcst], [bpb[7]])
            T(lambda e, s=s: e.transpose(pb[7][0:64, 128:256], yt[s][:, 128:192], ident), [byt[s], bcst], [bpb[7]])
            A(lambda e, s=s: e.copy(out=ytT[s][:, 0, :], in_=pb[7][:, 0:128]), [bpb[7], bytT[s]], [bytT[s]])
            A(lambda e, s=s: e.copy(out=ytT[s][0:64, 1, :], in_=pb[7][0:64, 128:256]), [bpb[7], bytT[s]], [bytT[s]])
            P.dma("sync", ysrc[0:128, t0:t0 + 128], ytT[s][:, 0, :], reads=[bytT[s]], writes=[bout])
            P.dma("sync", ysrc[128:192, t0:t0 + 128], ytT[s][0:64, 1, :], reads=[bytT[s]], writes=[bout])
        P.final_wait("sync", [bout]); P.final_wait("gpsimd", [bout])
        P.emit()


def p2a_consts():
    i = np.arange(128)
    row, col = i[:, None], i[None, :]
    cst = np.stack([np.eye(128), (row <= col), (row > col), (row > col)], axis=1).astype(np.float32)
    incl = (row <= col).astype(np.float32); strict = (row < col).astype(np.float32)
    mask4 = np.stack([incl, strict, incl, strict], axis=1).astype(np.float32)
    return dict(cst=np.ascontiguousarray(cst), mask4=np.ascontiguousarray(mask4))


NTM = 1344
NFM = 768
NY = 576
GROUPS = [[0, 1, 2, 3], [4, 5, 6, 7]]


def phase_norm(nc, io, l, x_ap):
    tg = "n%d_" % l
    P = Prog(nc, tg)
    with ExitStack() as es:
        xs = es.enter_context(nc.sbuf_tensor(tg + "xs", [128, KC, TOK], F32))
        bxs, bhs = Buf(), Buf()
        xv = x_ap.rearrange("(c p) t -> p c t", p=128)
        for i in range(4):
            P.dma("sync", xs[:, i * 4:(i + 1) * 4, :], xv[:, i * 4:(i + 1) * 4, :], writes=[bxs])
        hT, bh = rms_to_bf16(P, nc, es, xs, bxs, io["g1_%d" % l], tg)
        hv = io["h_src"].rearrange("(c p) t -> p c t", p=128)
        for i in range(4):
            P.dma("sync", hv[:, i * 4:(i + 1) * 4, :], hT[:, i * 4:(i + 1) * 4, :], reads=[bh], writes=[bhs])
        P.final_wait("sync", [bhs]); P.final_wait("gpsimd", [bhs])
        P.emit()


def phase_inproj(nc, io, l):
    tg = "p%d_" % l
    has_vres = l > 0
    P = Prog(nc, tg)
    h_src, h_all, tm_all, fm_all = io["h_src_t"], io["h_all_t"], io["tm_all"], io["fm_all"]
    wtm_ap, wfm_ap = io["wtm%d" % l], io["wfm%d" % l]
    with ExitStack() as es:
        sb = lambda n, shp, d=F32: es.enter_context(nc.sbuf_tensor(tg + n, shp, d))
        bhall = Buf()
        for kk in range(4):
            P.cc(lambda e, kk=kk: e.collective_compute("AllGather", mybir.AluOpType.bypass, replica_groups=GROUPS,
                                                       ins=[h_src.ap()[kk * 512:(kk + 1) * 512, :]], outs=[h_all.ap()[kk * 2048:(kk + 1) * 2048, :]]), writes=[bhall])
        wtm = sb("wtm", [128, KC, NTM], BF16); wfm = sb("wfm", [128, KC, NFM], BF16)
        bwtm, bwfm = Buf(), Buf()
        stg = [sb("stg%d" % i, [128, NTM], F32) for i in range(2)]
        bstg = [Buf(), Buf()]
        k = 0
        for (w_ap, wt, bw, n) in ((wtm_ap, wtm, bwtm, NTM), (wfm_ap, wfm, bwfm, NFM)):
            for c in range(KC):
                s = k % 2
                k += 1
                P.dma("sync" if s == 0 else "gpsimd", stg[s][:, :n], w_ap[c * 128:(c + 1) * 128, :], writes=[bstg[s]])
                if s == 0:
                    P.op("gpsimd", lambda e, s=s, wt=wt, c=c, n=n: e.tensor_copy(out=wt[:, c, :], in_=stg[s][:, :n]), reads=[bstg[s]], writes=[bw])
                else:
                    P.op("scalar", lambda e, s=s, wt=wt, c=c, n=n: e.copy(out=wt[:, c, :], in_=stg[s][:, :n]), reads=[bstg[s]], writes=[bw])
        hT = [sb("hT%d" % i, [128, KC, TOK], BF16) for i in range(2)]
        bhT = [Buf(), Buf()]
        otm = [sb("otm%d" % i, [128, NTM], F32) for i in range(2)]
        botm = [Buf(), Buf()]
        ofm = [sb("ofm%d" % i, [128, 512], F32) for i in range(4)]
        bofm = [Buf() for _ in range(4)]
        pt = [es.enter_context(nc.psum_tensor(tg + "pt%d" % i, [128, 512], F32)) for i in range(6)]
        bpt = [Buf() for _ in range(6)]
        hav = h_all.ap()
        btm, bfm = Buf(), Buf()
        fm_blocks = [(0, 96), (96, 192), (192, 320), (320, 448)] + ([(448, 512)] if has_vres else []) + [(512, 640), (640, 768)]
        tm_blocks = [(0, 512), (512, 1024), (1024, NTM)]
        pk = 0
        fk = 0
        for r in range(4):
            hs = r % 2
            for i in range(4):
                hv = hav[i * 2048 + r * 512:i * 2048 + (r + 1) * 512, :].rearrange("(c p) t -> p c t", p=128)
                P.dma("sync" if i % 2 == 0 else "gpsimd", hT[hs][:, i * 4:(i + 1) * 4, :], hv, reads=[bhall], writes=[bhT[hs]])
            for tl in range(8):
                os_ = (r * 8 + tl) % 2
                tsl = slice(tl * 128, (tl + 1) * 128)
                for bi, (c0, c1) in enumerate(tm_blocks):
                    q = pk % 6
                    pk += 1
                    for c in range(KC):
                        P.op("tensor", lambda e, q=q, c=c, hs=hs, tsl=tsl, c0=c0, c1=c1: e.matmul(pt[q][:, 0:c1 - c0], lhsT=hT[hs][:, c, tsl], rhs=wtm[:, c, c0:c1],
                                                                                                  start=(c == 0), stop=(c == KC - 1)),
                             reads=[bhT[hs], bwtm], writes=[bpt[q]])
                    if bi % 2 == 0:
                        P.op("scalar", lambda e, q=q, os_=os_, c0=c0, c1=c1: e.copy(out=otm[os_][:, c0:c1], in_=pt[q][:, 0:c1 - c0]), reads=[bpt[q]], writes=[botm[os_]])
                    else:
                        P.op("vector", lambda e, q=q, os_=os_, c0=c0, c1=c1: e.tensor_scalar(out=otm[os_][:, c0:c1], in0=pt[q][:, 0:c1 - c0], scalar1=1.0, scalar2=None, op0=ALU.mult),
                             reads=[bpt[q]], writes=[botm[os_]])
                t0 = r * 1024 + tl * 128
                P.dma("sync", tm_all[t0:t0 + 128, :], otm[os_][:], reads=[botm[os_]], writes=[btm])
            for (b0, b1) in fm_blocks:
                mw = b1 - b0
                for hf in range(2):
                    q = pk % 6
                    pk += 1
                    fs = fk % 4
                    fk += 1
                    sl = slice(hf * 512, (hf + 1) * 512)
                    for c in range(KC):
                        P.op("tensor", lambda e, q=q, c=c, hs=hs, sl=sl, b0=b0, b1=b1, mw=mw: e.matmul(pt[q][:mw, :], lhsT=wfm[:, c, b0:b1], rhs=hT[hs][:, c, sl],
                                                                                                       start=(c == 0), stop=(c == KC - 1)),
                             reads=[bhT[hs], bwfm], writes=[bpt[q]])
                    if fk % 2 == 0:
                        P.op("scalar", lambda e, q=q, fs=fs, mw=mw: e.copy(out=ofm[fs][:mw, :], in_=pt[q][:mw, :]), reads=[bpt[q]], writes=[bofm[fs]])
                    else:
                        P.op("vector", lambda e, q=q, fs=fs, mw=mw: e.tensor_scalar(out=ofm[fs][:mw, :], in0=pt[q][:mw, :], scalar1=1.0, scalar2=None, op0=ALU.mult),
                             reads=[bpt[q]], writes=[bofm[fs]])
                    P.dma("gpsimd", fm_all[b0:b1, r * 1024 + hf * 512:r * 1024 + (hf + 1) * 512], ofm[fs][:mw, :], reads=[bofm[fs]], writes=[bfm])
        P.final_wait("sync", [btm, bfm]); P.final_wait("gpsimd", [btm, bfm])
        P.emit()


DFF = 5632
NFB = DFF // 128
GRP = 4
KY = 18


def phase_p3(nc, io, l, x_ap, out_ap, final):
    tg = "f%d_" % l
    y_src, y_all = io["y_src_t"], io["y_all_t"]
    wout, g2, wg, wu, wd = io["wout%d" % l], io["g2_%d" % l], io["wg%d" % l], io["wu%d" % l], io["wd%d" % l]
    P = Prog(nc, tg)
    with ExitStack() as es:
        sb = lambda n, shp, dt: es.enter_context(nc.sbuf_tensor(tg + n, shp, dt))
        pst = lambda n: es.enter_context(nc.psum_tensor(tg + n, [128, 512], F32))
        byall = Buf()
        for kk in range(6):
            P.cc(lambda e, kk=kk: e.collective_compute("AllGather", mybir.AluOpType.bypass, replica_groups=GROUPS,
                                                       ins=[y_src.ap()[kk * 96:(kk + 1) * 96, :]], outs=[y_all.ap()[kk * 384:(kk + 1) * 384, :]]), writes=[byall])
        xs = sb("xs", [128, KC, TOK], F32)
        hT = sb("hT", [128, KY, TOK], BF16)
        bxs, bh = Buf(), Buf()
        NST = 4
        stg = [sb("stg%d" % i, [128, 2048], F32) for i in range(NST)]
        wbf = [sb("wbf%d" % i, [128, 2048], BF16) for i in range(NST)]
        bstg = [Buf() for _ in range(NST)]
        bwbf = [Buf() for _ in range(NST)]
        act = [sb("act%d" % i, [128, GRP, TOK], BF16) for i in range(2)]
        bact = [Buf(), Buf()]
        sg = [sb("sg%d" % i, [128, 512], F32) for i in range(2)]
        bsg = [Buf(), Buf()]
        sel = sb("sel", [128, 4], F32); bsel = Buf()
        pg = [pst("pg%d" % i) for i in range(2)]
        pu = [pst("pu%d" % i) for i in range(2)]
        pd = [pst("pd%d" % i) for i in range(2)]
        bpg, bpu, bpd = [Buf(), Buf()], [Buf(), Buf()], [Buf(), Buf()]
        xv = x_ap.rearrange("(c p) t -> p c t", p=128)
        for i in range(4):
            P.dma("sync", xs[:, i * 4:(i + 1) * 4, :], xv[:, i * 4:(i + 1) * 4, :], writes=[bxs])
        P.dma("sync", sel[:], io["sel"], writes=[bsel])
        st = [0]

        def load_cast(src_ap, view3=None):
            s = st[0] % NST
            st[0] += 1
            dst = stg[s][:] if view3 is None else stg[s][:].rearrange(view3[0], **view3[1])
            P.dma("sync", dst, src_ap, writes=[bstg[s]])
            P.op("gpsimd", lambda e, s=s: e.tensor_copy(out=wbf[s][:], in_=stg[s][:]), reads=[bstg[s]], writes=[bwbf[s]])
            return wbf[s], bwbf[s]

        yst = [stg[i][:].bitcast(BF16) for i in range(2)]
        yav = y_all.ap()
        for c in range(KY):
            s = c % 2
            P.dma("sync" if s == 0 else "gpsimd", yst[s], yav[c * 128:(c + 1) * 128, :], reads=[byall], writes=[bstg[s]])
            P.op("vector", lambda e, c=c, s=s: e.tensor_scalar(out=hT[:, c, :], in0=yst[s][:, 0:TOK], scalar1=sel[:, 0:1], scalar2=None, op0=ALU.mult),
                 reads=[bstg[s], bsel], writes=[bh])
            for gi in range(1, 4):
                P.op("vector", lambda e, c=c, s=s, gi=gi: e.scalar_tensor_tensor(out=hT[:, c, :], in0=yst[s][:, gi * TOK:(gi + 1) * TOK], scalar=sel[:, gi:gi + 1],
                                                                                 in1=hT[:, c, :], op0=ALU.mult, op1=ALU.add),
                     reads=[bstg[s], bsel, bh], writes=[bh])
        st[0] = 2
        wov = wout.rearrange("(c p) n -> p c n", p=128)
        k = 0
        for m in range(KC):
            t, b = load_cast(wov[:, 0:KC, m * 128:(m + 1) * 128], ("p (c n) -> p c n", dict(c=KC)))
            tv = t[:].rearrange("p (c n) -> p c n", c=KC)
            s2 = st[0] % NST
            st[0] += 1
            P.dma("sync", stg[s2][:, 0:256].rearrange("p (c n) -> p c n", c=2), wov[:, KC:KY, m * 128:(m + 1) * 128], writes=[bstg[s2]])
            P.op("gpsimd", lambda e, s2=s2: e.tensor_copy(out=wbf[s2][:, 0:256], in_=stg[s2][:, 0:256]), reads=[bstg[s2]], writes=[bwbf[s2]])
            t2v = wbf[s2][:, 0:256].rearrange("p (c n) -> p c n", c=2)
            b2 = bwbf[s2]
            for hf in range(2):
                q = k % 2
                k += 1
                sl = slice(hf * 512, (hf + 1) * 512)
                for c in range(KY):
                    lt = tv[:, c, :] if c < KC else t2v[:, c - KC, :]
                    P.op("tensor", lambda e, lt=lt, c=c, q=q, sl=sl: e.matmul(pd[q][:], lhsT=lt, rhs=hT[:, c, sl], start=(c == 0), stop=(c == KY - 1)),
                         reads=[b, b2, bh], writes=[bpd[q]])
                P.op("vector", lambda e, m=m, q=q, sl=sl: e.tensor_tensor(out=xs[:, m, sl], in0=xs[:, m, sl], in1=pd[q][:], op=ALU.add),
                     reads=[bpd[q], bxs], writes=[bxs])
        scr = {}
        rms_to_bf16(P, nc, es, xs, bxs, g2, tg + "n2", out_tile=hT, out_buf=bh, scr=scr)
        wgv = wg.rearrange("(c p) n -> p c n", p=128)
        wuv = wu.rearrange("(c p) n -> p c n", p=128)
        for grp in range(NFB // GRP):
            a = act[grp % 2]
            ba = bact[grp % 2]
            wds = []
            for j in range(GRP):
                fb = grp * GRP + j
                tg_, bg_ = load_cast(wgv[:, :, fb * 128:(fb + 1) * 128], ("p (c n) -> p c n", dict(c=KC)))
                tu, bu_ = load_cast(wuv[:, :, fb * 128:(fb + 1) * 128], ("p (c n) -> p c n", dict(c=KC)))
                tgv = tg_[:].rearrange("p (c n) -> p c n", c=KC)
                tuv = tu[:].rearrange("p (c n) -> p c n", c=KC)
                for hf in range(2):
                    sl = slice(hf * 512, (hf + 1) * 512)
                    for c in range(KC):
                        P.op("tensor", lambda e, tgv=tgv, c=c, hf=hf, sl=sl: e.matmul(pg[hf][:], lhsT=tgv[:, c, :], rhs=hT[:, c, sl],
                                                                                   start=(c == 0), stop=(c == KC - 1)),
                             reads=[bg_, bh], writes=[bpg[hf]])
                    for c in range(KC):
                        P.op("tensor", lambda e, tuv=tuv, c=c, hf=hf, sl=sl: e.matmul(pu[hf][:], lhsT=tuv[:, c, :], rhs=hT[:, c, sl],
                                                                                   start=(c == 0), stop=(c == KC - 1)),
                             reads=[bu_, bh], writes=[bpu[hf]])
                    P.op("scalar", lambda e, hf=hf: e.activation(out=sg[hf][:], in_=pg[hf][:], func=AF.Silu),
                         reads=[bpg[hf]], writes=[bsg[hf]])
                    P.op("vector", lambda e, hf=hf, a=a, j=j, sl=sl: e.tensor_tensor(out=a[:, j, sl], in0=sg[hf][:], in1=pu[hf][:], op=ALU.mult),
                         reads=[bsg[hf], bpu[hf]], writes=[ba])
            for j in range(GRP):
                fb = grp * GRP + j
                wds.append(load_cast(wd[fb * 128:(fb + 1) * 128, :]))
            for m in range(KC):
                for hf in range(2):
                    q = k % 2
                    k += 1
                    sl = slice(hf * 512, (hf + 1) * 512)
                    for j in range(GRP):
                        td, bd_ = wds[j]
                        P.op("tensor", lambda e, td=td, j=j, m=m, q=q, sl=sl, a=a: e.matmul(pd[q][:], lhsT=td[:, m * 128:(m + 1) * 128], rhs=a[:, j, sl],
                                                                                         start=(j == 0), stop=(j == GRP - 1)),
                             reads=[bd_, ba], writes=[bpd[q]])
                    P.op("vector", lambda e, m=m, q=q, sl=sl: e.tensor_tensor(out=xs[:, m, sl], in0=xs[:, m, sl], in1=pd[q][:], op=ALU.add),
                         reads=[bpd[q], bxs], writes=[bxs])
        bout = Buf()
        ov = out_ap.rearrange("(c p) t -> p c t", p=128)
        if final:
            rms_to_bf16(P, nc, es, xs, bxs, io["gf"], tg + "n3", out_tile=xs, out_buf=bxs, scr=scr)
        for i in range(4):
            P.dma("sync", ov[:, i * 4:(i + 1) * 4, :], xs[:, i * 4:(i + 1) * 4, :], reads=[bxs], writes=[bout])
        P.final_wait("sync", [bout]); P.final_wait("gpsimd", [bout])
        P.emit()


def build_fused(upto=99, dbg=None, only=None):
    nc = bass.Bass("TRN2", target_bir_lowering=False)
    _POOL[0] = SemPool(nc)
    io = {}
    ext = lambda n, shp, d=F32: io.__setitem__(n, nc.dram_tensor(n, shp, d, kind="ExternalInput").ap())
    ext("xT", [D, TOK]); ext("sel", [128, 4]); ext("gf", [128, KC])
    ext("cst", [128, 4, 128]); ext("mask4", [128, 4, 128])
    ext("pos", [128, NT], I32); ext("invf", [128, 32]); ext("ident", [128, 128]); ext("maskT", [2, 128, 128]); ext("qdec", [2, 64, 128])
    ext("kdec", [128, 2]); ext("g128", [64, 2]); ext("v2c", [64, W3])
    for l in range(2):
        ext("g1_%d" % l, [128, KC]); ext("wtm%d" % l, [D, NTM]); ext("wfm%d" % l, [D, NFM])
        ext("mu_rkv%d" % l, [128, 576]); ext("lmu%d" % l, [128, 5]); ext("w2c%d" % l, [96, W3]); ext("a2c%d" % l, [96, W3])
        ext("g2c%d" % l, [256, W3]); ext("par%d" % l, [128, 8, W3])
        ext("sm%d" % l, [128, 8]); ext("wa%d" % l, [2, 64, 64]); ext("wx%d" % l, [2, 64, 64])
        ext("gng%d" % l, [2, 128, 128])
        if upto > 6 * l + 5:
            ext("wout%d" % l, [KY * 128, D]); ext("g2_%d" % l, [128, KC]); ext("wg%d" % l, [D, DFF]); ext("wu%d" % l, [D, DFF]); ext("wd%d" % l, [DFF, D])
    out = nc.dram_tensor("outT", [D, TOK], F32, kind="ExternalOutput").ap()
    io["h_src_t"] = nc.dram_tensor("h_src", [D, TOK], BF16); io["h_src"] = io["h_src_t"].ap()
    io["h_all_t"] = nc.dram_tensor("h_all", [4 * D, TOK], BF16)
    io["tm_all"] = nc.dram_tensor("tm_all", [S, NTM], F32).ap()
    io["fm_all"] = nc.dram_tensor("fm_all", [NFM, S], F32).ap()
    io["y_src_t"] = nc.dram_tensor("y_src", [NY, S], BF16); io["y_src"] = io["y_src_t"].ap()
    io["y_all_t"] = nc.dram_tensor("y_all", [4 * NY, S], BF16)
    io["vfirst"] = nc.dram_tensor("vfirst", [S, W3], F32).ap()
    x_cur = nc.dram_tensor("x_cur", [D, TOK], F32).ap()
    io["x_cur"] = x_cur
    ph = 0
    for l in range(2):
        steps = [lambda l=l: phase_norm(nc, io, l, io["xT"] if l == 0 else x_cur),
                 lambda l=l: phase_inproj(nc, io, l),
                 lambda l=l: phase_p2a(nc, io, l),
                 lambda l=l: phase_p2b(nc, io, l),
                 lambda l=l: phase_p2c(nc, io, l),
                 lambda l=l: phase_p3(nc, io, l, io["xT"] if l == 0 else x_cur, out if l == 1 else x_cur, final=(l == 1))]
        for st_ in steps:
            if ph < upto and (only is None or ph in only):
                st_()
            ph += 1
    if dbg is not None:
        src_ap = io[dbg]
        dout = nc.dram_tensor("dbg", list(src_ap.shape), src_ap.dtype, kind="ExternalOutput").ap()
        P = Prog(nc, "dbg_")
        b = Buf()
        rows = src_ap.shape[0]
        stp = max(1, rows // 8)
        for r0 in range(0, rows, stp):
            P.dma("sync", dout[r0:r0 + stp], src_ap[r0:r0 + stp], writes=[b])
        P.final_wait("sync", [b])
        P.emit()
    _POOL[0].close()
    _POOL[0] = None
    return nc


def core_inputs(d, c):
    b, q = c // 4, c % 4
    f32 = lambda a: np.ascontiguousarray(np.asarray(a, dtype=np.float32))
    lay = lambda g: f32(g.reshape(16, 128).T)
    rep = lambda v: f32(np.broadcast_to(np.asarray(v, np.float32), (128, v.shape[-1])))
    xf = d["x"].reshape(-1, D)
    im = {"xT": f32(xf[c * 1024:(c + 1) * 1024].T), "gf": lay(d["final_norm_g"])}
    sel = np.zeros((128, 4), np.float32); sel[:, q] = 1.0
    im["sel"] = sel
    im.update(p2a_consts())
    heads = C_SLOTS[q]
    im.update(p2c_consts(heads))
    im["pos"] = np.ascontiguousarray(d["positions"][b].reshape(32, 128).T.astype(np.int32))
    cA = slice(192 * q, 192 * q + 192)
    cols_rkv = np.r_[192 * q:192 * q + 192, 768 + 192 * q:768 + 192 * q + 192, 1536 + 192 * q:1536 + 192 * q + 192]
    c0 = 2752 + 1024
    cols_c = np.concatenate([np.r_[c0 + 64 * h:c0 + 64 * h + 64] for h in heads] + [np.r_[c0 + 384 + 64 * h:c0 + 384 + 64 * h + 64] for h in heads]
                            + [np.r_[c0 + 768 + 128 * h:c0 + 768 + 128 * h + 128] for h in heads]
                            + [np.r_[c0 + 1536 + 128 * h:c0 + 1536 + 128 * h + 128] for h in heads])
    im["v2c"] = f32(d["rwkv_v2"][0][:, cA])
    perm = np.full(KY * 128, -1, np.int64)
    for r in range(4):
        loc = np.full(NY, -1, np.int64)
        loc[0:192] = np.r_[192 * r:192 * r + 192]
        loc[192:320] = 768 + np.r_[128 * r:128 * r + 128]
        if r < 3:
            for s_, h in enumerate(C_SLOTS[r]):
                loc[320 + 128 * s_:448 + 128 * s_] = 1280 + np.r_[128 * h:128 * h + 128]
        for kk in range(6):
            perm[kk * 384 + r * 96:kk * 384 + (r + 1) * 96] = loc[96 * kk:96 * (kk + 1)]
    for l in range(2):
        w = d["w_in"][l]
        mu = d["tshift_mu"][l]
        im["g1_%d" % l] = lay(d["norm1_g"][l])
        im["wtm%d" % l] = f32(w[:, np.concatenate([cols_rkv, cols_c])])
        wfm = np.zeros((D, NFM), np.float32)
        wfm[:, 0:448] = w[:, 2304:2752]
        if l > 0:
            wfm[:, 448:512] = d["w_in_vres"][l - 1]
        A0 = 2752
        wfm[:, 512:640] = w[:, A0 + 128 * q:A0 + 128 * q + 128]
        wfm[:, 640:768] = w[:, A0 + 512 + 128 * q:A0 + 512 + 128 * q + 128]
        im["wfm%d" % l] = wfm
        im["mu_rkv%d" % l] = rep(mu[cols_rkv])
        lmu = np.zeros((128, 5), np.float32)
        lmu[:96, 0] = mu[2304:2400]; lmu[:96, 1] = mu[2400:2496]; lmu[:, 2] = mu[2496:2624]; lmu[:, 3] = mu[2624:2752]
        if l > 0:
            lmu[:64, 4] = d["tshift_mu_vres"][l - 1]
        im["lmu%d" % l] = lmu
        im["w2c%d" % l] = f32(d["rwkv_w2"][l][:, cA]); im["a2c%d" % l] = f32(d["rwkv_a2"][l][:, cA]); im["g2c%d" % l] = f32(d["rwkv_g2"][l][:, cA])
        v0 = d["rwkv_v0"][l - 1][cA] if l > 0 else np.zeros(192, np.float32)
        par = np.stack([d["rwkv_w0"][l][cA], d["rwkv_a0"][l][cA], v0, d["rwkv_k_k"][l][cA], d["rwkv_k_a"][l][cA],
                        d["rwkv_r_k"][l].reshape(-1)[cA], d["rwkv_ln_g"][l][cA], d["rwkv_ln_b"][l][cA]])
        im["par%d" % l] = f32(np.broadcast_to(par[None].astype(np.float32), (128, 8, 192)))
        ch = slice(128 * q, 128 * q + 128)
        cw = d["lru_conv_w"][l]
        im["sm%d" % l] = f32(np.stack([cw[0, ch], cw[1, ch], cw[2, ch], cw[3, ch], d["lru_conv_b"][l][ch], d["lru_ba"][l][ch],
                                       d["lru_bx"][l][ch], d["lru_lambda"][l][ch]], axis=1))
        im["wa%d" % l] = f32(d["lru_wa"][l][2 * q:2 * q + 2]); im["wx%d" % l] = f32(d["lru_wx"][l][2 * q:2 * q + 2])
        im["gng%d" % l] = f32(np.stack([np.broadcast_to(d["ret_gn_g"][l][128 * h:128 * h + 128], (128, 128)) for h in heads]))
        wo = np.zeros((KY * 128, D), np.float32)
        ok = perm >= 0
        wo[ok] = d["w_out"][l][perm[ok]]
        im["wout%d" % l] = wo
        im["g2_%d" % l] = lay(d["norm2_g"][l])
        im["wg%d" % l] = f32(d["ffn_w_gate"][l]); im["wu%d" % l] = f32(d["ffn_w_up"][l]); im["wd%d" % l] = f32(d["ffn_w_down"][l])
    return im


def kernel(**inputs):
    d = {k: np.asarray(v) for k, v in inputs.items()}
    Bn, Sn, Dn = d["x"].shape
    nc = build_fused()
    in_maps = [core_inputs(d, c) for c in range(8)]
    res = run_bass_kernel_spmd(nc, in_maps, core_ids=list(range(8)))
    out = np.concatenate([r["outT"].T for r in res.results], axis=0).reshape(Bn, Sn, Dn)
    return np.ascontiguousarray(out.astype(np.float32))
```

```python
import math
from contextlib import ExitStack
import numpy as np
import concourse.bass as bass
import concourse.mybir as mybir
from concourse.bass_utils import run_bass_kernel_spmd

F32 = mybir.dt.float32
BF16 = mybir.dt.bfloat16
I32 = mybir.dt.int32
AF = mybir.ActivationFunctionType
ALU = mybir.AluOpType
AX = mybir.AxisListType


class Buf:
    __slots__ = ("name", "last_w", "readers")

    def __init__(self, name=""):
        self.name = name
        self.last_w = None
        self.readers = []


class SemPool:
    def __init__(self, nc):
        self.nc = nc
        self.sems = {}
        self._ctx = []
        self.count = {e: 0 for e in Prog.ENGS}
        self.dma_i = {"sync": 0, "gpsimd": 0, "scalar": 0}
        self.n_cc = 0

    def sem(self, key):
        if key not in self.sems:
            cm = self.nc.semaphore("s_" + key)
            self._ctx.append(cm)
            self.sems[key] = cm.__enter__()
        return self.sems[key]

    def close(self):
        for cm in reversed(self._ctx):
            cm.__exit__(None, None, None)


_POOL = [None]


class Prog:
    ENGS = ("sync", "scalar", "vector", "gpsimd", "tensor")

    def __init__(self, nc, tag="", n_dma_sems=12, self_sync=True):
        self.nc = nc
        self.tag = tag
        self.self_sync = self_sync
        self.own_pool = _POOL[0] is None
        self.pool = SemPool(nc) if self.own_pool else _POOL[0]
        self.ops = {e: [] for e in self.ENGS}
        self.waited = {e: {} for e in self.ENGS}
        self.n_dma_sems = n_dma_sems

    def _collect(self, eng, reads, writes):
        need = {}
        def req(tok):
            if tok is None:
                return
            k, v = tok
            if need.get(k, 0) < v:
                need[k] = v
        for b in reads:
            req(b.last_w)
        for b in writes:
            req(b.last_w)
            for r in b.readers:
                req(r)
        waits = []
        wd = self.waited[eng]
        for k, v in need.items():
            if k == "e_" + eng and (eng == "tensor" or not self.self_sync):
                continue
            if wd.get(k, 0) >= v:
                continue
            wd[k] = v
            waits.append((k, v))
        return waits

    def _mark(self, tok, reads, writes):
        for b in reads:
            b.readers.append(tok)
        for b in writes:
            b.last_w = tok
            b.readers = []
        return tok

    def op(self, eng, fn, reads=(), writes=()):
        waits = self._collect(eng, reads, writes)
        self.pool.count[eng] += 1
        tok = ("e_" + eng, self.pool.count[eng])
        self.ops[eng].append((waits, fn, tok[0], 1))
        return self._mark(tok, reads, writes)

    def dma(self, eng, out, in_, reads=(), writes=(), **kw):
        i = self.pool.dma_i[eng]
        self.pool.dma_i[eng] = i + 1
        key = "d_%s_%d" % (eng, i % self.n_dma_sems)
        val = 16 * (i // self.n_dma_sems + 1)
        waits = self._collect(eng, reads, writes)
        if i >= self.n_dma_sems:
            pv = val - 16
            if self.waited[eng].get(key, 0) < pv:
                self.waited[eng][key] = pv
                waits.append((key, pv))
        fn = lambda e, out=out, in_=in_, kw=kw: e.dma_start(out=out, in_=in_, **kw)
        self.ops[eng].append((waits, fn, key, 16))
        return self._mark((key, val), reads, writes)

    def cc(self, fn, reads=(), writes=()):
        eng = "gpsimd"
        self.pool.n_cc += 1
        key = "cc_%d" % self.pool.n_cc
        waits = self._collect(eng, reads, writes)
        self.ops[eng].append((waits, fn, key, None))
        return self._mark((key, 1), reads, writes)

    def final_wait(self, eng, bufs):
        waits = self._collect(eng, bufs, ())
        self.ops[eng].append((waits, None, None, 0))

    def emit(self):
        nc = self.nc
        sem = self.pool.sem
        with nc.Block() as block:
            for e in self.ENGS:
                ops = self.ops[e]
                if not ops:
                    continue

                def body(engine, ops=ops):
                    for waits, fn, key, inc in ops:
                        for k, v in waits:
                            engine.wait_ge(sem(k), v)
                        if fn is not None:
                            if inc is None:
                                fn(engine).then_inc(sem(key))
                            else:
                                fn(engine).then_inc(sem(key), inc)
                getattr(block, e)(body)
        if self.own_pool:
            self.pool.close()


D = 2048
KC = 16
TOK = 1024
EPS = 1e-6


def rms_to_bf16(P, nc, es, xs, bxs, g_ap, name, out_tile=None, out_buf=None, scr=None):
    sb = lambda n, shp, dt: es.enter_context(nc.sbuf_tensor(name + n, shp, dt))
    ps_ = lambda n, shp, dt: es.enter_context(nc.psum_tensor(name + n, shp, dt))
    if scr is None:
        scr = {}
    if "ones" not in scr:
        scr["ones"] = sb("ones", [128, 128], F32)
        scr["sq"] = [sb("sq%d" % i, [128, TOK], F32) for i in range(2)]
        scr["rstd"] = sb("rstd", [128, TOK], F32)
        scr["pss"] = [ps_("ss%d" % i, [128, 512], F32) for i in range(2)]
        scr["eps"] = sb("eps", [128, 1], F32)
        scr["b"] = dict(ones=Buf(), rstd=Buf(), sq=[Buf(), Buf()], ps=[Buf(), Buf()], eps=Buf())
        P.op("gpsimd", lambda e: e.memset(scr["ones"][:], 1.0), writes=[scr["b"]["ones"]])
        P.op("gpsimd", lambda e: e.memset(scr["eps"][:], EPS), writes=[scr["b"]["eps"]])
    ones, sq, rstd, pss, epst = scr["ones"], scr["sq"], scr["rstd"], scr["pss"], scr["eps"]
    B = scr["b"]
    bones, brstd, bsq, bps, beps = B["ones"], B["rstd"], B["sq"], B["ps"], B["eps"]
    gs = sb("g", [128, KC], F32)
    hT = out_tile if out_tile is not None else sb("hT", [128, KC, TOK], BF16)
    bg = Buf()
    bh = out_buf if out_buf is not None else Buf()
    P.dma("sync", gs[:], g_ap, writes=[bg])
    for c in range(KC):
        s = sq[c % 2]
        P.op("scalar", lambda e, c=c, s=s: e.activation(out=s[:], in_=xs[:, c, :], func=AF.Square),
             reads=[bxs], writes=[bsq[c % 2]])
        for hf in range(2):
            P.op("tensor", lambda e, c=c, s=s, hf=hf: e.matmul(pss[hf][:], lhsT=ones[:], rhs=s[:, hf * 512:(hf + 1) * 512],
                                                               start=(c == 0), stop=(c == KC - 1)),
                 reads=[bones, bsq[c % 2]], writes=[bps[hf]])
    for hf in range(2):
        sl = slice(hf * 512, (hf + 1) * 512)
        P.op("scalar", lambda e, hf=hf, sl=sl: e.activation(out=rstd[:, sl], in_=pss[hf][:], func=AF.Sqrt,
                                                            scale=1.0 / D, bias=epst[:, 0:1]),
             reads=[bps[hf], beps], writes=[brstd])
        P.op("vector", lambda e, sl=sl: e.reciprocal(out=rstd[:, sl], in_=rstd[:, sl]), reads=[brstd], writes=[brstd])
    for c in range(KC):
        P.op("vector", lambda e, c=c: e.scalar_tensor_tensor(out=hT[:, c, :], in0=xs[:, c, :], scalar=gs[:, c:c + 1],
                                                         in1=rstd[:], op0=ALU.mult, op1=ALU.mult),
             reads=[bxs, bg, brstd], writes=[bh])
    return hT, bh


DFF = 5632
NFB = DFF // 128
GRP = 4


S = 4096


def phase_p2b(nc, io, l):
    tg = "b%d_" % l
    gT = io["fm_all"][512:640, :]; xT = io["fm_all"][640:768, :]
    sm = io["sm%d" % l]; wa = io["wa%d" % l]; wx = io["wx%d" % l]
    out = io["y_src"][192:320, :]
    P = Prog(nc, tg)
    with ExitStack() as es:
        sb = lambda n, shp, dt=F32: es.enter_context(nc.sbuf_tensor(tg + n, shp, dt))
        g = sb("g", [128, S]); x = sb("x", [128, S]); xc = sb("xc", [128, S])
        r = sb("r", [128, S]); ii = sb("ii", [128, S]); a = sb("a", [128, S]); u = sb("u", [128, S])
        h = sb("h", [128, S])
        sms = sb("sms", [128, 8]); wab = sb("wab", [128, 128]); wxb = sb("wxb", [128, 128])
        t1 = sb("t1", [128, 8])
        ps = [es.enter_context(nc.psum_tensor(tg + "ps%d" % i, [128, 512], F32)) for i in range(4)]
        bps = [Buf() for _ in range(4)]
        bg, bx, bxc, br, bi, ba_, bu, bh, bsm, bwa, bwx, bt1 = (Buf() for _ in range(12))
        for i in range(4):
            sl = slice(i * 1024, (i + 1) * 1024)
            P.dma("sync", x[:, sl], xT[:, sl], writes=[bx])
        for i in range(4):
            sl = slice(i * 1024, (i + 1) * 1024)
            P.dma("gpsimd", g[:, sl], gT[:, sl], writes=[bg])
        P.dma("sync", sms[:], sm, writes=[bsm])
        P.op("gpsimd", lambda e: e.memset(wab[:], 0.0), writes=[bwa])
        P.op("gpsimd", lambda e: e.memset(wxb[:], 0.0), writes=[bwx])
        for bl in range(2):
            P.dma("sync", wab[bl * 64:(bl + 1) * 64, bl * 64:(bl + 1) * 64], wa[bl], writes=[bwa])
            P.dma("sync", wxb[bl * 64:(bl + 1) * 64, bl * 64:(bl + 1) * 64], wx[bl], writes=[bwx])
        P.op("vector", lambda e: e.tensor_scalar(out=xc[:], in0=x[:], scalar1=sms[:, 3:4], scalar2=sms[:, 4:5], op0=ALU.mult, op1=ALU.add),
             reads=[bx, bsm], writes=[bxc])
        for sh in (1, 2, 3):
            P.op("vector", lambda e, sh=sh: e.scalar_tensor_tensor(out=xc[:, sh:], in0=x[:, :S - sh], scalar=sms[:, 3 - sh:4 - sh],
                                                                   in1=xc[:, sh:], op0=ALU.mult, op1=ALU.add),
                 reads=[bx, bsm, bxc], writes=[bxc])
        P.op("scalar", lambda e: e.activation(out=t1[:, 0:1], in_=sms[:, 7:8], func=AF.Exp, scale=-1.0), reads=[bsm], writes=[bt1])
        P.op("vector", lambda e: e.tensor_scalar(out=t1[:, 1:2], in0=t1[:, 0:1], scalar1=2.0, scalar2=None, op0=ALU.add), reads=[bt1], writes=[bt1])
        P.op("vector", lambda e: e.reciprocal(out=t1[:, 1:2], in_=t1[:, 1:2]), reads=[bt1], writes=[bt1])
        P.op("vector", lambda e: e.tensor_tensor(out=t1[:, 2:3], in0=t1[:, 0:1], in1=t1[:, 1:2], op=ALU.mult), reads=[bt1], writes=[bt1])
        P.op("vector", lambda e: e.tensor_tensor(out=t1[:, 3:4], in0=t1[:, 2:3], in1=t1[:, 2:3], op=ALU.mult), reads=[bt1], writes=[bt1])
        P.op("vector", lambda e: e.memset(t1[:, 4:5], 1.0 / 13.0), reads=[bt1], writes=[bt1])
        for cf in (1.0 / 11, 1.0 / 9, 1.0 / 7, 1.0 / 5, 1.0 / 3, 1.0):
            P.op("vector", lambda e, cf=cf: e.tensor_scalar(out=t1[:, 4:5], in0=t1[:, 4:5], scalar1=t1[:, 3:4], scalar2=float(cf), op0=ALU.mult, op1=ALU.add),
                 reads=[bt1], writes=[bt1])
        P.op("vector", lambda e: e.scalar_tensor_tensor(out=t1[:, 5:6], in0=t1[:, 4:5], scalar=-16.0, in1=t1[:, 2:3], op0=ALU.mult, op1=ALU.mult),
             reads=[bt1], writes=[bt1])
        for pc in range(8):
            sl = slice(pc * 512, (pc + 1) * 512)
            q0, q1 = (2 * pc) % 4, (2 * pc + 1) % 4
            P.op("tensor", lambda e, q0=q0, sl=sl: e.matmul(ps[q0][:], lhsT=wab[:], rhs=xc[:, sl], start=True, stop=True),
                 reads=[bwa, bxc], writes=[bps[q0]])
            P.op("tensor", lambda e, q1=q1, sl=sl: e.matmul(ps[q1][:], lhsT=wxb[:], rhs=xc[:, sl], start=True, stop=True),
                 reads=[bwx, bxc], writes=[bps[q1]])
            P.op("scalar", lambda e, q0=q0, sl=sl: e.activation(out=r[:, sl], in_=ps[q0][:], func=AF.Sigmoid, bias=sms[:, 5:6]),
                 reads=[bps[q0], bsm], writes=[br])
            P.op("scalar", lambda e, q1=q1, sl=sl: e.activation(out=ii[:, sl], in_=ps[q1][:], func=AF.Sigmoid, bias=sms[:, 6:7]),
                 reads=[bps[q1], bsm], writes=[bi])
        P.op("scalar", lambda e: e.activation(out=a[:], in_=r[:], func=AF.Exp, scale=t1[:, 5:6]), reads=[br, bt1], writes=[ba_])
        P.op("vector", lambda e: e.tensor_tensor(out=u[:], in0=a[:], in1=a[:], op=ALU.mult), reads=[ba_], writes=[bu])
        P.op("vector", lambda e: e.tensor_scalar(out=u[:], in0=u[:], scalar1=-1.0, scalar2=1.0, op0=ALU.mult, op1=ALU.add), reads=[bu], writes=[bu])
        P.op("scalar", lambda e: e.activation(out=u[:], in_=u[:], func=AF.Sqrt), reads=[bu], writes=[bu])
        P.op("vector", lambda e: e.tensor_tensor(out=ii[:], in0=ii[:], in1=xc[:], op=ALU.mult), reads=[bi, bxc], writes=[bi])
        P.op("vector", lambda e: e.tensor_tensor(out=u[:], in0=u[:], in1=ii[:], op=ALU.mult), reads=[bu, bi], writes=[bu])
        P.op("vector", lambda e: e.tensor_tensor_scan(out=h[:], data0=a[:], data1=u[:], initial=0.0, op0=ALU.mult, op1=ALU.add),
             reads=[ba_, bu], writes=[bh])
        P.op("scalar", lambda e: e.activation(out=r[:], in_=g[:], func=AF.Square), reads=[bg, br], writes=[br])
        P.op("vector", lambda e: e.tensor_scalar(out=r[:], in0=r[:], scalar1=0.044715, scalar2=1.0, op0=ALU.mult, op1=ALU.add), reads=[br], writes=[br])
        P.op("vector", lambda e: e.tensor_tensor(out=r[:], in0=r[:], in1=g[:], op=ALU.mult), reads=[br, bg], writes=[br])
        P.op("scalar", lambda e: e.activation(out=r[:], in_=r[:], func=AF.Sigmoid, scale=1.5957691216057308), reads=[br], writes=[br])
        P.op("vector", lambda e: e.tensor_tensor(out=r[:], in0=r[:], in1=g[:], op=ALU.mult), reads=[br, bg], writes=[br])
        P.op("vector", lambda e: e.tensor_tensor(out=h[:], in0=h[:], in1=r[:], op=ALU.mult), reads=[br, bh], writes=[bh])
        bout = Buf()
        hb = sb("hb", [128, S], BF16); bhb = Buf()
        P.op("scalar", lambda e: e.copy(out=hb[:], in_=h[:]), reads=[bh], writes=[bhb])
        for i in range(4):
            sl = slice(i * 1024, (i + 1) * 1024)
            P.dma("sync", out[:, sl], hb[:, sl], reads=[bhb], writes=[bout])
        P.final_wait("sync", [bout]); P.final_wait("gpsimd", [bout])
        P.emit()


S = 4096
NT = 32
PI = math.pi
C1 = 6.28125
C2 = 2 * math.pi - 6.28125


def phase_p2c(nc, io, l):
    tg = "c%d_" % l
    tm = io["tm_all"]
    q_in = [tm[:, 576 + 64 * s:576 + 64 * s + 64] for s in range(2)]
    k_in = [tm[:, 704 + 64 * s:704 + 64 * s + 64] for s in range(2)]
    v_in = [tm[:, 832 + 128 * s:832 + 128 * s + 128] for s in range(2)]
    g_in = [tm[:, 1088 + 128 * s:1088 + 128 * s + 128] for s in range(2)]
    pos_in = io["pos"]; invf_in = io["invf"]; ident_in = io["ident"]
    mask_in = io["maskT"]; qdec_in = io["qdec"]; kdec_in = io["kdec"]; g128_in = io["g128"]; gng_in = io["gng%d" % l]
    ysrc = io["y_src"]
    P = Prog(nc, tg)
    with ExitStack() as es:
        sb = lambda n, shp, d=F32: es.enter_context(nc.sbuf_tensor(tg + n, shp, d))
        pst = lambda n, shp: es.enter_context(nc.psum_tensor(tg + n, shp, F32))
        posi = sb("posi", [128, NT], I32); posf = sb("posf", [128, NT]); invf = sb("invf", [128, 32])
        ident = sb("ident", [128, 128])
        ang = sb("ang", [128, NT, 32]); nf = sb("nf", [128, NT, 32]); ni = sb("ni", [128, NT, 32], I32)
        sn = sb("sn", [128, NT, 32]); cs = sb("cs", [128, NT, 32]); tmp = sb("tmp", [128, NT, 32]); tmp2 = sb("tmp2", [128, NT, 32])
        epst = sb("epst", [128, 1])
        bpos, binvf, bident, bang, bnf, bsn, bcs, btmp, btmp2, beps = (Buf() for _ in range(10))
        P.dma("sync", posi[:], pos_in, writes=[bpos])
        P.dma("sync", invf[:], invf_in, writes=[binvf])
        P.dma("sync", ident[:], ident_in, writes=[bident])
        P.op("gpsimd", lambda e: e.memset(epst[:], 1e-5), writes=[beps])
        P.op("vector", lambda e: e.tensor_copy(out=posf[:], in_=posi[:]), reads=[bpos], writes=[bpos])
        for n in range(NT):
            P.op("vector", lambda e, n=n: e.tensor_scalar(out=ang[:, n, :], in0=invf[:], scalar1=posf[:, n:n + 1], scalar2=None, op0=ALU.mult),
                 reads=[bpos, binvf], writes=[bang])
        P.op("vector", lambda e: e.tensor_scalar(out=nf[:], in0=ang[:], scalar1=1.0 / (2 * PI), scalar2=None, op0=ALU.mult), reads=[bang], writes=[bnf])
        P.op("vector", lambda e: e.tensor_copy(out=ni[:], in_=nf[:]), reads=[bnf], writes=[bnf])
        P.op("vector", lambda e: e.tensor_copy(out=nf[:], in_=ni[:]), reads=[bnf], writes=[bnf])
        P.op("vector", lambda e: e.scalar_tensor_tensor(out=ang[:], in0=nf[:], scalar=-C1, in1=ang[:], op0=ALU.mult, op1=ALU.add), reads=[bnf, bang], writes=[bang])
        P.op("vector", lambda e: e.scalar_tensor_tensor(out=ang[:], in0=nf[:], scalar=-C2, in1=ang[:], op0=ALU.mult, op1=ALU.add), reads=[bnf, bang], writes=[bang])

        def wrap(dst, bdst, src, bsrc, shift):
            if shift != 0.0:
                P.op("vector", lambda e: e.tensor_scalar(out=dst[:], in0=src[:], scalar1=float(shift), scalar2=None, op0=ALU.add), reads=[bsrc], writes=[bdst])
                src_, bsrc_ = dst, bdst
            else:
                src_, bsrc_ = src, bsrc
            P.op("vector", lambda e: e.tensor_scalar(out=tmp[:], in0=src_[:], scalar1=PI, scalar2=-2 * PI, op0=ALU.is_gt, op1=ALU.mult), reads=[bsrc_], writes=[btmp])
            P.op("vector", lambda e: e.tensor_scalar(out=tmp2[:], in0=src_[:], scalar1=-PI, scalar2=2 * PI, op0=ALU.is_lt, op1=ALU.mult), reads=[bsrc_], writes=[btmp2])
            P.op("vector", lambda e: e.tensor_tensor(out=dst[:], in0=src_[:], in1=tmp[:], op=ALU.add), reads=[bsrc_, btmp], writes=[bdst])
            P.op("vector", lambda e: e.tensor_tensor(out=dst[:], in0=dst[:], in1=tmp2[:], op=ALU.add), reads=[bdst, btmp2], writes=[bdst])
        wrap(sn, bsn, ang, bang, 0.0)
        wrap(cs, bcs, sn, bsn, PI / 2)
        P.op("scalar", lambda e: e.activation(out=sn[:], in_=sn[:], func=AF.Sin), reads=[bsn], writes=[bsn])
        P.op("scalar", lambda e: e.activation(out=cs[:], in_=cs[:], func=AF.Sin), reads=[bcs], writes=[bcs])

        q = sb("q", [128, NT, 64]); k = sb("k", [128, NT, 64]); qr = sb("qr", [128, NT, 64]); kr = sb("kr", [128, NT, 64])
        kt = sb("kt", [128, NT, 64])
        v = sb("v", [128, NT, 128]); g = sb("g", [128, NT, 128]); o = sb("o", [128, NT, 128])
        maskT = sb("maskT", [128, 128]); qdec = sb("qdec", [64, 128]); kdec = sb("kdec", [128, 2]); g128 = sb("g128", [64, 2])
        gng = sb("gng", [128, 128])
        Sst = sb("Sst", [64, 128])
        qkT = [sb("qkT%d" % i, [64, 2, 128]) for i in range(2)]
        qtT = [sb("qtT%d" % i, [64, 128]) for i in range(2)]
        smk = [sb("smk%d" % i, [128, 128]) for i in range(2)]
        stats = [sb("stats%d" % i, [128, 6]) for i in range(2)]
        mv = [sb("mv%d" % i, [128, 2]) for i in range(2)]
        rs = [sb("rs%d" % i, [128, 1]) for i in range(2)]
        psT = [pst("psT%d" % i, [64, 2, 128]) for i in range(2)]
        pss = [pst("pss%d" % i, [128, 128]) for i in range(2)]
        pso = [pst("pso%d" % i, [128, 128]) for i in range(2)]
        pkv = [pst("pkv%d" % i, [64, 128]) for i in range(2)]
        bq, bk, bqr, bkr, bkt, bv, bg, bo, bmask, bqdec, bkdec, bg128, bgng, bS = (Buf() for _ in range(14))
        bqkT, bqtT, bsmk, bstats, bmv, brs, bpsT, bpss, bpso, bpkv = ([Buf(), Buf()] for _ in range(10))
        P.dma("sync", kdec[:], kdec_in, writes=[bkdec])
        P.dma("sync", g128[:], g128_in, writes=[bg128])
        bout = Buf()
        oT = sb("oT", [128, S], BF16); boT = Buf()
        for sl in range(2):
            vw = lambda ap: ap.rearrange("(n p) d -> p n d", p=128)
            P.dma("sync", q[:], vw(q_in[sl]), writes=[bq])
            P.dma("gpsimd", k[:], vw(k_in[sl]), writes=[bk])
            for hh in range(2):
                P.dma("sync", v[:, hh * 16:(hh + 1) * 16, :], vw(v_in[sl])[:, hh * 16:(hh + 1) * 16, :], writes=[bv])
                P.dma("gpsimd", g[:, hh * 16:(hh + 1) * 16, :], vw(g_in[sl])[:, hh * 16:(hh + 1) * 16, :], writes=[bg])
            P.dma("sync", maskT[:], mask_in[sl], writes=[bmask])
            P.dma("sync", qdec[:], qdec_in[sl], writes=[bqdec])
            P.dma("sync", gng[:], gng_in[sl], writes=[bgng])
            P.op("gpsimd", lambda e: e.memset(Sst[:], 0.0), writes=[bS])
            for (src, bsrc, dst, bdst) in ((q, bq, qr, bqr), (k, bk, kr, bkr)):
                s1, s2 = src[:, :, 0:32], src[:, :, 32:64]
                d1, d2 = dst[:, :, 0:32], dst[:, :, 32:64]
                P.op("vector", lambda e, s1=s1, d1=d1: e.tensor_tensor(out=d1, in0=s1, in1=cs[:], op=ALU.mult), reads=[bsrc, bcs], writes=[bdst])
                P.op("vector", lambda e, s2=s2: e.tensor_tensor(out=tmp[:], in0=s2, in1=sn[:], op=ALU.mult), reads=[bsrc, bsn], writes=[btmp])
                P.op("vector", lambda e, d1=d1: e.tensor_tensor(out=d1, in0=d1, in1=tmp[:], op=ALU.subtract), reads=[btmp, bdst], writes=[bdst])
                P.op("vector", lambda e, s2=s2, d2=d2: e.tensor_tensor(out=d2, in0=s2, in1=cs[:], op=ALU.mult), reads=[bsrc, bcs], writes=[bdst])
                P.op("vector", lambda e, s1=s1: e.tensor_tensor(out=tmp2[:], in0=s1, in1=sn[:], op=ALU.mult), reads=[bsrc, bsn], writes=[btmp2])
                P.op("vector", lambda e, d2=d2: e.tensor_tensor(out=d2, in0=d2, in1=tmp2[:], op=ALU.add), reads=[btmp2, bdst], writes=[bdst])
            P.op("vector", lambda e, sl=sl: e.tensor_scalar(out=kt[:], in0=kr[:], scalar1=kdec[:, sl:sl + 1], scalar2=None, op0=ALU.mult),
                 reads=[bkr, bkdec], writes=[bkt])
            for n in range(NT):
                i = n % 2
                P.op("tensor", lambda e, n=n, i=i: e.transpose(psT[i][:, 0, :], qr[:, n, :], ident[:]), reads=[bqr, bident], writes=[bpsT[i]])
                P.op("tensor", lambda e, n=n, i=i: e.transpose(psT[i][:, 1, :], kr[:, n, :], ident[:]), reads=[bkr, bident], writes=[bpsT[i]])
                P.op("scalar", lambda e, i=i: e.copy(out=qkT[i][:], in_=psT[i][:]), reads=[bpsT[i]], writes=[bqkT[i]])
                P.op("vector", lambda e, i=i: e.tensor_tensor(out=qtT[i][:], in0=psT[i][:, 0, :], in1=qdec[:], op=ALU.mult), reads=[bpsT[i], bqdec], writes=[bqtT[i]])
                P.op("tensor", lambda e, i=i: e.matmul(pss[i][:], lhsT=qkT[i][:, 1, :], rhs=qkT[i][:, 0, :], start=True, stop=True), reads=[bqkT[i]], writes=[bpss[i]])
                P.op("vector", lambda e, i=i: e.tensor_tensor(out=smk[i][:], in0=pss[i][:], in1=maskT[:], op=ALU.mult), reads=[bpss[i], bmask], writes=[bsmk[i]])
                P.op("tensor", lambda e, i=i, n=n: e.matmul(pso[i][:], lhsT=smk[i][:], rhs=v[:, n, :], start=True, stop=False), reads=[bsmk[i], bv], writes=[bpso[i]])
                P.op("tensor", lambda e, i=i: e.matmul(pso[i][:], lhsT=qtT[i][:], rhs=Sst[:], start=False, stop=True), reads=[bqtT[i], bS], writes=[bpso[i]])
                P.op("tensor", lambda e, i=i, n=n: e.matmul(pkv[i][:], lhsT=kt[:, n, :], rhs=v[:, n, :], start=True, stop=True), reads=[bkt, bv], writes=[bpkv[i]])
                P.op("vector", lambda e, i=i, sl=sl: e.scalar_tensor_tensor(out=Sst[:], in0=Sst[:], scalar=g128[:, sl:sl + 1], in1=pkv[i][:], op0=ALU.mult, op1=ALU.add),
                     reads=[bpkv[i], bg128, bS], writes=[bS])
                P.op("vector", lambda e, i=i: e.bn_stats(out=stats[i][:], in_=pso[i][:]), reads=[bpso[i]], writes=[bstats[i]])
                P.op("vector", lambda e, i=i: e.bn_aggr(out=mv[i][:], in_=stats[i][:]), reads=[bstats[i]], writes=[bmv[i]])
                P.op("scalar", lambda e, i=i: e.activation(out=rs[i][:], in_=mv[i][:, 1:2], func=AF.Sqrt, bias=epst[:, 0:1]), reads=[bmv[i], beps], writes=[brs[i]])
                P.op("vector", lambda e, i=i: e.reciprocal(out=rs[i][:], in_=rs[i][:]), reads=[brs[i]], writes=[brs[i]])
                P.op("vector", lambda e, i=i, n=n: e.tensor_scalar(out=o[:, n, :], in0=pso[i][:], scalar1=mv[i][:, 0:1], scalar2=rs[i][:, 0:1], op0=ALU.subtract, op1=ALU.mult),
                     reads=[bpso[i], bmv[i], brs[i]], writes=[bo])
                P.op("gpsimd", lambda e, n=n: e.tensor_tensor(out=o[:, n, :], in0=o[:, n, :], in1=gng[:], op=ALU.mult), reads=[bo, bgng], writes=[bo])
            P.op("scalar", lambda e: e.activation(out=g[:], in_=g[:], func=AF.Silu), reads=[bg], writes=[bg])
            P.op("vector", lambda e: e.tensor_tensor(out=o[:], in0=o[:], in1=g[:], op=ALU.mult), reads=[bo, bg], writes=[bo])
            for n in range(NT):
                i = n % 2
                P.op("tensor", lambda e, n=n, i=i: e.transpose(pss[i][:], o[:, n, :], ident[:]), reads=[bo, bident], writes=[bpss[i]])
                P.op("scalar", lambda e, n=n, i=i: e.copy(out=oT[:, n * 128:(n + 1) * 128], in_=pss[i][:]), reads=[bpss[i]], writes=[boT])
            for hh in range(4):
                cs_ = slice(hh * 1024, (hh + 1) * 1024)
                P.dma("sync", ysrc[320 + 128 * sl:448 + 128 * sl, cs_], oT[:, cs_], reads=[boT], writes=[bout])
        P.final_wait("sync", [bout]); P.final_wait("gpsimd", [bout])
        P.emit()


def p2c_consts(heads):
    maskT = np.zeros((2, 128, 128), np.float32); qdec = np.zeros((2, 64, 128), np.float32)
    kdec = np.zeros((128, 2), np.float32); g128 = np.zeros((64, 2), np.float32)
    idx = np.arange(128)
    for s, hd in enumerate(heads):
        lg = float(np.log1p(-np.exp2(np.float32(-5.0 - hd))).astype(np.float32))
        m = idx[:, None]; c = idx[None, :]
        same = (m // 64) == (c // 64)
        earlier = (m // 64) < (c // 64)
        dec = np.where(same, np.exp(lg * np.abs(c - m)), np.where(earlier, np.exp(lg * (c - m)), 0.0))
        maskT[s] = (dec * 0.125).astype(np.float32)
        qdec[s] = np.exp(lg * (idx + 1.0))[None, :].astype(np.float32)
        kdec[:, s] = (np.exp(lg * (127.0 - idx)) * 0.125).astype(np.float32)
        g128[:, s] = np.float32(np.exp(lg * 128.0))
    invf = (10000.0 ** (-np.arange(0, 64, 2, dtype=np.float32) / 64)).astype(np.float32)
    return dict(maskT=maskT, qdec=qdec, kdec=kdec, g128=g128, invf=np.ascontiguousarray(np.broadcast_to(invf, (128, 32))),
                ident=np.eye(128, dtype=np.float32))


C_SLOTS = [(0, 1), (2, 3), (4, 5), (4, 5)]


VARA = False

S = 4096
NT = 32
W3 = 192
NEG_EHALF = -math.exp(-0.5)
A_GN_EPS = 64e-5


def phase_p2a(nc, io, l, nt=NT, dbg=9):
    has_vres = l > 0
    tg = "a%d_" % l
    tmA = io["tm_all"]; fm = io["fm_all"]
    rkv_in = tmA[:, 0:576]; mu_in = io["mu_rkv%d" % l]
    wd_in = fm[0:96, :]; ad_in = fm[96:192, :]; gd_in = fm[192:448, :]
    lmu_in = io["lmu%d" % l]; w2_in = io["w2c%d" % l]; a2_in = io["a2c%d" % l]; g2_in = io["g2c%d" % l]
    par_in = io["par%d" % l]; cst_in = io["cst"]; m4_in = io["mask4"]
    if has_vres:
        vd_in = fm[448:512, :]; v2_in = io["v2c"]
    vf_in = io["vfirst"]
    v_out = io["vfirst"]
    ysrc = io["y_src"]
    P = Prog(nc, tg)
    with ExitStack() as es:
        sb = lambda n, shp, d=F32: es.enter_context(nc.sbuf_tensor(tg + n, shp, d))
        V = lambda fn, r, w: P.op("vector", fn, reads=r, writes=w)
        A = lambda fn, r, w: P.op("scalar", fn, reads=r, writes=w)
        G = lambda fn, r, w: P.op("gpsimd", fn, reads=r, writes=w)
        T = lambda fn, r, w: P.op("tensor", fn, reads=r, writes=w)
        cst = sb("cst", [128, 4, 128]); m4 = sb("m4", [128, 4, 128]); par = sb("par", [128, 8, W3]); mu = sb("mu", [128, 576])
        lmu = sb("lmu", [128, 5]); w2 = sb("w2", [96, W3]); a2 = sb("a2", [96, W3]); g2 = sb("g2", [128, 2, W3])
        ones = sb("ones", [128, 1]); epsg = sb("epsg", [128, 1])
        bcst, bm4, bpar, bmu, blmu, bw2, ba2, bg2, bones, bepsg = (Buf() for _ in range(10))
        P.dma("sync", cst[:], cst_in, writes=[bcst]); P.dma("sync", m4[:], m4_in, writes=[bm4])
        P.dma("sync", par[:], par_in, writes=[bpar]); P.dma("sync", mu[:], mu_in, writes=[bmu])
        P.dma("sync", lmu[:], lmu_in, writes=[blmu]); P.dma("sync", w2[:], w2_in, writes=[bw2])
        P.dma("sync", a2[:], a2_in, writes=[ba2])
        P.dma("sync", g2[:], g2_in.rearrange("(c p) n -> p c n", p=128), writes=[bg2])
        G(lambda e: e.memset(ones[:], 1.0), [], [bones]); G(lambda e: e.memset(epsg[:], A_GN_EPS), [], [bepsg])
        ident, triu, trirev, maskn = cst[:, 0, :], cst[:, 1, :], cst[:, 2, :], cst[:, 3, :]
        if has_vres:
            v2 = sb("v2", [64, W3]); bv2 = Buf()
            P.dma("sync", v2[:], v2_in, writes=[bv2])
        ltmp = sb("ltmp", [128, S]); bltmp = Buf()
        lo = {}
        specs = [("wd", wd_in, 96, 0, AF.Tanh), ("ad", ad_in, 96, 1, None), ("gd0", gd_in[0:128], 128, 2, AF.Sigmoid),
                 ("gd1", gd_in[128:256], 128, 3, AF.Sigmoid)]
        if has_vres:
            specs.append(("vd", vd_in, 64, 4, None))
        for name, src, rows, mcol, fn in specs:
            t = sb("lo_" + name, [rows, S + 1]); b = Buf()
            G(lambda e, t=t: e.memset(t[:, 0:1], 0.0), [], [b])
            for i in range(4):
                P.dma("sync" if i % 2 == 0 else "gpsimd", t[:, 1 + i * 1024:1 + (i + 1) * 1024], src[:, i * 1024:(i + 1) * 1024], writes=[b])
            V(lambda e, t=t, rows=rows: e.tensor_tensor(out=ltmp[:rows, :], in0=t[:, 0:S], in1=t[:, 1:S + 1], op=ALU.subtract), [b, bltmp], [bltmp])
            V(lambda e, t=t, rows=rows, mcol=mcol: e.scalar_tensor_tensor(out=t[:, 1:S + 1], in0=ltmp[:rows, :], scalar=lmu[:rows, mcol:mcol + 1],
                                                                      in1=t[:, 1:S + 1], op0=ALU.mult, op1=ALU.add), [bltmp, blmu, b], [b])
            if fn is not None:
                A(lambda e, t=t, fn=fn: e.activation(out=t[:, 1:S + 1], in_=t[:, 1:S + 1], func=fn), [b], [b])
            lo[name] = (t, b)
        NS = 2
        cur = [sb("cur%d" % i, [128, 576]) for i in range(NS)]; prv = [sb("prv%d" % i, [128, 576]) for i in range(NS)]
        bcur = [Buf() for _ in range(NS)]; bprv = [Buf() for _ in range(NS)]
        tt = [sb("tt%d" % i, [128, 8, W3]) for i in range(NS)]; btt = [Buf() for _ in range(NS)]
        rk = [sb("rk%d" % i, [128, 3]) for i in range(NS)]; brk = [Buf() for _ in range(NS)]
        tm = sb("tm", [128, 4, W3]); btm = Buf()
        ss = sb("ss", [128, 3]); bss = Buf()
        if has_vres:
            vf = [sb("vf%d" % i, [128, W3]) for i in range(NS)]; bvf = [Buf() for _ in range(NS)]
        ee = sb("ee", [128, 4, W3]); bee = Buf()
        dd = sb("dd", [128, 6, W3]); bdd = Buf()
        wc = sb("wc", [64, 4]); bwc = Buf()
        T4 = sb("T4", [64, 4, 128]); bT4 = Buf()
        M4 = sb("M4", [128, 4, 128]); bM4 = Buf()
        Pk = [sb("Pk%d" % i, [128, 128]) for i in range(2)]; bPk = [Buf(), Buf()]
        Qk = [sb("Qk%d" % i, [128, 128]) for i in range(2)]; bQk = [Buf(), Buf()]
        Z = sb("Z", [128, 128]); bZ = Buf()
        nX = sb("nX", [128, 64]); bnX = Buf()
        U = sb("U", [128, 64]); bU = Buf()
        ST = [sb("ST%d" % h, [64, 64]) for h in range(3)]; bST = [Buf() for _ in range(3)]
        yt = [sb("yt%d" % i, [128, W3]) for i in range(NS)]; byt = [Buf() for _ in range(NS)]
        st6 = sb("st6", [128, 6]); bst6 = Buf(); mv = sb("mv", [128, 2]); bmv = Buf(); rsd = sb("rsd", [128, 1]); brsd = Buf()
        for h in range(3):
            G(lambda e, h=h: e.memset(ST[h][:], 0.0), [], [bST[h]])
        ytT = [sb("ytT%d" % i, [128, 2, 128], BF16) for i in range(NS)]; bytT = [Buf() for _ in range(NS)]
        pb = [es.enter_context(nc.psum_tensor(tg + "pb%d" % i, [128, 512], F32)) for i in range(8)]
        bpb = [Buf() for _ in range(8)]
        pL1, pL2, pGG, pTT, pPP, pIV, pCH = pb[0], pb[1], pb[2], pb[3], pb[4], pb[5], pb[6]
        bL1, bL2, bGG, bTT, bPP, bIV, bCH = bpb[0], bpb[1], bpb[2], bpb[3], bpb[4], bpb[5], bpb[6]
        rkv_v = rkv_in
        bout = Buf()
        wdT, bwd = lo["wd"]; adT, bad = lo["ad"]; gd0, bgd0 = lo["gd0"]; gd1, bgd1 = lo["gd1"]
        for n in range(nt):
            s = n % NS
            t0 = n * 128
            c_, p_, t_ = cur[s], prv[s], tt[s]
            P.dma("sync", c_[:], rkv_v[t0:t0 + 128, :], writes=[bcur[s]])
            if n == 0:
                G(lambda e, p_=p_: e.memset(p_[:], 0.0), [], [bprv[s]])
                P.dma("gpsimd", p_[1:128, :], rkv_v[0:127, :], writes=[bprv[s]])
            else:
                P.dma("gpsimd", p_[:], rkv_v[t0 - 1:t0 + 127, :], writes=[bprv[s]])
            if has_vres:
                P.dma("sync", vf[s][:], vf_in[t0:t0 + 128, :], writes=[bvf[s]])
            V(lambda e, c_=c_, p_=p_: e.tensor_tensor(out=p_[:], in0=p_[:], in1=c_[:], op=ALU.subtract), [bcur[s], bprv[s]], [bprv[s]])
            G(lambda e, p_=p_: e.tensor_tensor(out=p_[:], in0=p_[:], in1=mu[:], op=ALU.mult), [bprv[s], bmu], [bprv[s]])
            V(lambda e, c_=c_, p_=p_: e.tensor_tensor(out=c_[:], in0=c_[:], in1=p_[:], op=ALU.add), [bcur[s], bprv[s]], [bcur[s]])
            r_, k_, v_ = c_[:, 0:W3], c_[:, W3:2 * W3], c_[:, 2 * W3:3 * W3]
            tsl = slice(1 + t0, 1 + t0 + 128)
            T(lambda e, tsl=tsl: e.matmul(pL1[:, 0:W3], lhsT=wdT[:, tsl], rhs=w2[:], start=True, stop=True), [bwd, bw2], [bL1])
            T(lambda e, tsl=tsl: e.matmul(pL1[:, W3:2 * W3], lhsT=adT[:, tsl], rhs=a2[:], start=True, stop=True), [bad, ba2], [bL1])
            T(lambda e, tsl=tsl: e.matmul(pL2[:, 0:W3], lhsT=gd0[:, tsl], rhs=g2[:, 0, :], start=True, stop=False), [bgd0, bg2], [bL2])
            T(lambda e, tsl=tsl: e.matmul(pL2[:, 0:W3], lhsT=gd1[:, tsl], rhs=g2[:, 1, :], start=False, stop=True), [bgd1, bg2], [bL2])
            if has_vres:
                vdT, bvd = lo["vd"]
                T(lambda e, tsl=tsl: e.matmul(pL2[:, W3:2 * W3], lhsT=vdT[:, tsl], rhs=v2[:], start=True, stop=True), [bvd, bv2], [bL2])
            V(lambda e, t_=t_: e.tensor_tensor(out=t_[:, 5, :], in0=pL1[:, 0:W3], in1=par[:, 0, :], op=ALU.add), [bL1, bpar, btt[s]], [btt[s]])
            A(lambda e, t_=t_: e.activation(out=t_[:, 5, :], in_=t_[:, 5, :], func=AF.Sigmoid), [btt[s]], [btt[s]])
            V(lambda e, t_=t_: e.tensor_scalar(out=t_[:, 5, :], in0=t_[:, 5, :], scalar1=NEG_EHALF, scalar2=None, op0=ALU.mult), [btt[s]], [btt[s]])
            V(lambda e, t_=t_: e.tensor_tensor(out=t_[:, 7, :], in0=pL1[:, W3:2 * W3], in1=par[:, 1, :], op=ALU.add), [bL1, bpar, btt[s]], [btt[s]])
            A(lambda e, t_=t_: e.activation(out=t_[:, 7, :], in_=t_[:, 7, :], func=AF.Sigmoid), [btt[s]], [btt[s]])
            A(lambda e, t_=t_: e.copy(out=t_[:, 6, :], in_=pL2[:, 0:W3]), [bL2, btt[s]], [btt[s]])
            G(lambda e, t_=t_, r_=r_: e.tensor_copy(out=t_[:, 0, :], in_=r_), [bcur[s], btt[s]], [btt[s]])
            if has_vres:
                V(lambda e: e.tensor_tensor(out=tm[:, 0, :], in0=pL2[:, W3:2 * W3], in1=par[:, 2, :], op=ALU.add), [bL2, bpar, btm], [btm])
                A(lambda e: e.activation(out=tm[:, 0, :], in_=tm[:, 0, :], func=AF.Sigmoid), [btm], [btm])
                V(lambda e, v_=v_, s=s: e.tensor_tensor(out=tm[:, 1, :], in0=vf[s][:], in1=v_, op=ALU.subtract), [bvf[s], bcur[s], btm], [btm])
                V(lambda e: e.tensor_tensor(out=tm[:, 1, :], in0=tm[:, 1, :], in1=tm[:, 0, :], op=ALU.mult), [btm], [btm])
                V(lambda e, t_=t_, v_=v_: e.tensor_tensor(out=t_[:, 2, :], in0=tm[:, 1, :], in1=v_, op=ALU.add), [btm, bcur[s], btt[s]], [btt[s]])
            else:
                G(lambda e, t_=t_, v_=v_: e.tensor_copy(out=t_[:, 2, :], in_=v_), [bcur[s], btt[s]], [btt[s]])
            if not has_vres:
                P.dma("sync", v_out[t0:t0 + 128, :], t_[:, 2, :], reads=[btt[s]], writes=[bout])
            V(lambda e, t_=t_, k_=k_: e.tensor_tensor(out=t_[:, 3, :], in0=k_, in1=par[:, 3, :], op=ALU.mult), [bcur[s], bpar, btt[s]], [btt[s]])
            V(lambda e, t_=t_: e.tensor_tensor(out=tm[:, 2, :], in0=t_[:, 3, :], in1=t_[:, 3, :], op=ALU.mult), [btt[s], btm], [btm])
            V(lambda e: e.tensor_reduce(out=ss[:], in_=tm[:, 2, :].rearrange("p (h j) -> p h j", h=3), axis=AX.X, op=ALU.add), [btm, bss], [bss])
            A(lambda e: e.activation(out=ss[:], in_=ss[:], func=AF.Sqrt), [bss], [bss])
            V(lambda e: e.tensor_scalar(out=ss[:], in0=ss[:], scalar1=1e-12, scalar2=None, op0=ALU.max), [bss], [bss])
            V(lambda e: e.reciprocal(out=ss[:], in_=ss[:]), [bss], [bss])
            for h in range(3):
                hs = slice(64 * h, 64 * h + 64)
                V(lambda e, t_=t_, hs=hs, h=h: e.tensor_scalar(out=t_[:, 3, hs], in0=t_[:, 3, hs], scalar1=ss[:, h:h + 1], scalar2=None, op0=ALU.mult),
                  [bss, btt[s]], [btt[s]])
            V(lambda e, t_=t_: e.scalar_tensor_tensor(out=tm[:, 3, :], in0=t_[:, 7, :], scalar=-1.0, in1=par[:, 4, :], op0=ALU.add, op1=ALU.mult),
              [btt[s], bpar, btm], [btm])
            V(lambda e, t_=t_, k_=k_: e.scalar_tensor_tensor(out=t_[:, 1, :], in0=tm[:, 3, :], scalar=1.0, in1=k_, op0=ALU.add, op1=ALU.mult),
              [btm, bcur[s], btt[s]], [btt[s]])
            G(lambda e, t_=t_: e.tensor_tensor(out=t_[:, 4, :], in0=t_[:, 3, :], in1=t_[:, 7, :], op=ALU.mult), [btt[s]], [btt[s]])
            V(lambda e, t_=t_: e.tensor_tensor(out=tm[:, 3, :], in0=t_[:, 0, :], in1=t_[:, 1, :], op=ALU.mult), [btt[s], btm], [btm])
            V(lambda e: e.tensor_tensor(out=tm[:, 3, :], in0=tm[:, 3, :], in1=par[:, 5, :], op=ALU.mult), [btm, bpar], [btm])
            V(lambda e, s=s: e.tensor_reduce(out=rk[s][:], in_=tm[:, 3, :].rearrange("p (h j) -> p h j", h=3), axis=AX.X, op=ALU.add), [btm, brk[s]], [brk[s]])
            if dbg <= 1:
                P.dma("sync", y_out[t0:t0 + 128, :], t_[:, 6, :], reads=[btt[s]], writes=[bout])
                continue
            T(lambda e, t_=t_: e.matmul(pGG[:, 0:W3], lhsT=triu, rhs=t_[:, 5, :], start=True, stop=True), [bcst, btt[s]], [bGG])
            T(lambda e, t_=t_: e.matmul(pGG[:, W3:2 * W3], lhsT=trirev, rhs=t_[:, 5, :], start=True, stop=True), [bcst, btt[s]], [bGG])
            for h in range(3):
                T(lambda e, t_=t_, h=h: e.matmul(pGG[0:64, 400 + h:401 + h], lhsT=t_[:, 5, 64 * h:64 * h + 64], rhs=ones[:], start=True, stop=True),
                  [btt[s], bones], [bGG])
            A(lambda e: e.activation(out=ee[:, 0, :], in_=pGG[:, 0:W3], func=AF.Exp), [bGG, bee], [bee])
            A(lambda e: e.activation(out=ee[:, 1, :], in_=pGG[:, 0:W3], func=AF.Exp, scale=-1.0), [bGG, bee], [bee])
            V(lambda e, t_=t_: e.tensor_tensor(out=ee[:, 2, :], in0=pGG[:, 0:W3], in1=t_[:, 5, :], op=ALU.subtract), [bGG, btt[s], bee], [bee])
            A(lambda e: e.activation(out=ee[:, 2, :], in_=ee[:, 2, :], func=AF.Exp), [bee], [bee])
            A(lambda e: e.activation(out=ee[:, 3, :], in_=pGG[:, W3:2 * W3], func=AF.Exp), [bGG, bee], [bee])
            A(lambda e: e.activation(out=wc[:, 0:3], in_=pGG[0:64, 400:403], func=AF.Exp), [bGG, bwc], [bwc])
            for (di, ti, ei, eng) in ((0, 0, 0, V), (1, 3, 2, G), (2, 1, 1, V), (3, 4, 1, G), (4, 1, 3, V), (5, 4, 3, G)):
                eng(lambda e, t_=t_, di=di, ti=ti, ei=ei: e.tensor_tensor(out=dd[:, di, :], in0=t_[:, ti, :], in1=ee[:, ei, :], op=ALU.mult),
                    [btt[s], bee, bdd], [bdd])
            if dbg <= 2:
                P.dma("sync", y_out[t0:t0 + 128, :], dd[:, 0, :], reads=[bdd], writes=[bout])
                continue
            for h in range(3):
                hs = slice(64 * h, 64 * h + 64)
                for j in range(4):
                    T(lambda e, j=j, hs=hs: e.transpose(pTT[0:64, j * 128:(j + 1) * 128], dd[:, j, hs], ident), [bdd, bcst], [bTT])
                A(lambda e: e.copy(out=T4[:], in_=pTT[0:64, :].rearrange("p (a b) -> p a b", a=4)), [bTT, bT4], [bT4])
                if dbg <= 3:
                    continue
                T(lambda e: e.matmul(pPP[:, 0:256], lhsT=T4[:, 2, :], rhs=T4[:, 0:2, :].rearrange("p a b -> p (a b)"), start=True, stop=True), [bT4], [bPP])
                T(lambda e: e.matmul(pPP[:, 256:512], lhsT=T4[:, 3, :], rhs=T4[:, 0:2, :].rearrange("p a b -> p (a b)"), start=True, stop=True), [bT4], [bPP])
                T(lambda e: e.matmul(pIV[:, 0:128], lhsT=T4[:, 1, :], rhs=T4[:, 3, :], start=True, stop=True), [bT4], [bIV])
                V(lambda e: e.tensor_tensor(out=M4[:].rearrange("p a b -> p (a b)"), in0=pPP[:], in1=m4[:].rearrange("p a b -> p (a b)"), op=ALU.mult),
                  [bPP, bm4, bM4], [bM4])
                V(lambda e: e.tensor_tensor(out=Pk[0][:], in0=pIV[:, 0:128], in1=maskn, op=ALU.mult), [bIV, bcst, bPk[0]], [bPk[0]])
                (V if VARA else G)(lambda e: e.tensor_copy(out=Qk[0][:], in_=M4[:, 3, :]), [bM4, bQk[0]], [bQk[0]])
                (V if VARA else G)(lambda e: e.tensor_tensor(out=Z[:], in0=ident, in1=M4[:, 3, :], op=ALU.subtract), [bM4, bcst, bZ], [bZ])
                if dbg <= 4:
                    continue
                for lv in range(1, 7):
                    pi, ci = (lv - 1) % 2, lv % 2
                    STEP = 9
                    if lv < 6 and STEP >= 1:
                        T(lambda e, pi=pi: e.matmul(pIV[:, 128:256], lhsT=Pk[pi][:], rhs=Qk[pi][:], start=True, stop=True), [bPk[pi], bQk[pi]], [bIV])
                    if STEP >= 2:
                        T(lambda e, pi=pi: e.matmul(pIV[:, 256:384], lhsT=Qk[pi][:], rhs=Pk[pi][:], start=True, stop=True), [bPk[pi], bQk[pi]], [bIV])
                    if lv < 6 and STEP >= 3:
                        A(lambda e, ci=ci: e.copy(out=Qk[ci][:], in_=pIV[:, 128:256]), [bIV, bQk[ci]], [bQk[ci]])
                    if STEP >= 4:
                        A(lambda e, ci=ci: e.copy(out=Pk[ci][:], in_=pIV[:, 256:384]), [bIV, bPk[ci]], [bPk[ci]])
                    if STEP >= 5:
                        T(lambda e, ci=ci: e.matmul(pIV[:, 384:512], lhsT=Pk[ci][:], rhs=Z[:], start=True, stop=True), [bPk[ci], bZ], [bIV])
                    if STEP >= 6:
                        V(lambda e: e.tensor_tensor(out=Z[:], in0=Z[:], in1=pIV[:, 384:512], op=ALU.add), [bIV, bZ], [bZ])
                if dbg <= 5:
                    continue
                vh = t_[:, 2, hs]
                T(lambda e, vh=vh: e.matmul(pCH[:, 0:64], lhsT=M4[:, 1, :], rhs=vh, start=True, stop=False), [bM4, btt[s]], [bCH])
                T(lambda e, h=h: e.matmul(pCH[:, 0:64], lhsT=T4[:, 1, :], rhs=ST[h][:], start=False, stop=True), [bT4, bST[h]], [bCH])
                A(lambda e: e.activation(out=nX[:], in_=pCH[:, 0:64], func=AF.Copy, scale=-1.0), [bCH, bnX], [bnX])
                T(lambda e: e.matmul(pCH[:, 64:128], lhsT=Z[:], rhs=nX[:], start=True, stop=True), [bZ, bnX], [bCH])
                V(lambda e: e.tensor_scalar(out=U[:], in0=pCH[:, 64:128], scalar1=1.0, scalar2=None, op0=ALU.mult), [bCH, bU], [bU])
                T(lambda e, h=h: e.matmul(pCH[:, 128:192], lhsT=T4[:, 0, :], rhs=ST[h][:], start=True, stop=False), [bT4, bST[h]], [bCH])
                T(lambda e: e.matmul(pCH[:, 128:192], lhsT=M4[:, 2, :], rhs=U[:], start=False, stop=False), [bM4, bU], [bCH])
                T(lambda e, vh=vh: e.matmul(pCH[:, 128:192], lhsT=M4[:, 0, :], rhs=vh, start=False, stop=True), [bM4, btt[s]], [bCH])
                T(lambda e, hs=hs: e.matmul(pCH[0:64, 192:256], lhsT=dd[:, 5, hs], rhs=U[:], start=True, stop=False), [bdd, bU], [bCH])
                T(lambda e, hs=hs, vh=vh: e.matmul(pCH[0:64, 192:256], lhsT=dd[:, 4, hs], rhs=vh, start=False, stop=True), [bdd, btt[s]], [bCH])
                V(lambda e, h=h: e.scalar_tensor_tensor(out=ST[h][:], in0=ST[h][:], scalar=wc[:, h:h + 1], in1=pCH[0:64, 192:256], op0=ALU.mult, op1=ALU.add),
                  [bCH, bwc, bST[h]], [bST[h]])
                V(lambda e: e.bn_stats(out=st6[:], in_=pCH[:, 128:192]), [bCH, bst6], [bst6])
                V(lambda e: e.bn_aggr(out=mv[:], in_=st6[:]), [bst6, bmv], [bmv])
                A(lambda e: e.activation(out=rsd[:], in_=mv[:, 1:2], func=AF.Sqrt, bias=epsg[:, 0:1]), [bmv, bepsg, brsd], [brsd])
                V(lambda e: e.reciprocal(out=rsd[:], in_=rsd[:]), [brsd], [brsd])
                V(lambda e, s=s, hs=hs: e.tensor_scalar(out=yt[s][:, hs], in0=pCH[:, 128:192], scalar1=mv[:, 0:1], scalar2=rsd[:, 0:1], op0=ALU.subtract, op1=ALU.mult),
                  [bCH, bmv, brsd, byt[s]], [byt[s]])
            if dbg <= 5:
                P.dma("sync", y_out[t0:t0 + 128, :], dd[:, 0, :], reads=[bdd, bT4, bM4, bZ, bPk[0], bPk[1], bQk[0], bQk[1]], writes=[bout])
                continue
            G(lambda e, s=s: e.tensor_tensor(out=yt[s][:], in0=yt[s][:], in1=par[:, 6, :], op=ALU.mult), [byt[s], bpar], [byt[s]])
            G(lambda e, s=s: e.tensor_tensor(out=yt[s][:], in0=yt[s][:], in1=par[:, 7, :], op=ALU.add), [byt[s], bpar], [byt[s]])
            for h in range(3):
                hs = slice(64 * h, 64 * h + 64)
                V(lambda e, s=s, hs=hs, h=h, t_=t_: e.scalar_tensor_tensor(out=yt[s][:, hs], in0=t_[:, 2, hs], scalar=rk[s][:, h:h + 1], in1=yt[s][:, hs],
                                                                          op0=ALU.mult, op1=ALU.add), [btt[s], brk[s], byt[s]], [byt[s]])
            V(lambda e, s=s, t_=t_: e.tensor_tensor(out=yt[s][:], in0=yt[s][:], in1=t_[:, 6, :], op=ALU.mult), [byt[s], btt[s]], [byt[s]])
            T(lambda e, s=s: e.transpose(pb[7][:, 0:128], yt[s][:, 0:128], ident), [byt[s], bcst], [bpb[7]])
            T(lambda e, s=s: e.transpose(pb[7][0:64, 128:256], yt[s][:, 128:192], ident), [byt[s], bcst], [bpb[7]])
            A(lambda e, s=s: e.copy(out=ytT[s][:, 0, :], in_=pb[7][:, 0:128]), [bpb[7], bytT[s]], [bytT[s]])
            A(lambda e, s=s: e.copy(out=ytT[s][0:64, 1, :], in_=pb[7][0:64, 128:256]), [bpb[7], bytT[s]], [bytT[s]])
            P.dma("sync", ysrc[0:128, t0:t0 + 128], ytT[s][:, 0, :], reads=[bytT[s]], writes=[bout])
            P.dma("sync", ysrc[128:192, t0:t0 + 128], ytT[s][0:64, 1, :], reads=[bytT[s]], writes=[bout])
        P.final_wait("sync", [bout]); P.final_wait("gpsimd", [bout])
        P.emit()


def phase_p2a_il(nc, io, l, nt=NT, dbg=9):
    has_vres = l > 0
    tg = "a%d_" % l
    tmA = io["tm_all"]; fm = io["fm_all"]
    rkv_in = tmA[:, 0:576]; mu_in = io["mu_rkv%d" % l]
    wd_in = fm[0:96, :]; ad_in = fm[96:192, :]; gd_in = fm[192:448, :]
    lmu_in = io["lmu%d" % l]; w2_in = io["w2c%d" % l]; a2_in = io["a2c%d" % l]; g2_in = io["g2c%d" % l]
    par_in = io["par%d" % l]; cst_in = io["cst"]; m4_in = io["mask4"]
    if has_vres:
        vd_in = fm[448:512, :]; v2_in = io["v2c"]
    vf_in = io["vfirst"]
    v_out = io["vfirst"]
    ysrc = io["y_src"]
    P = Prog(nc, tg)
    with ExitStack() as es:
        sb = lambda n, shp, d=F32: es.enter_context(nc.sbuf_tensor(tg + n, shp, d))
        V = lambda fn, r, w: P.op("vector", fn, reads=r, writes=w)
        A = lambda fn, r, w: P.op("scalar", fn, reads=r, writes=w)
        G = lambda fn, r, w: P.op("gpsimd", fn, reads=r, writes=w)
        T = lambda fn, r, w: P.op("tensor", fn, reads=r, writes=w)
        cst = sb("cst", [128, 4, 128]); m4 = sb("m4", [128, 4, 128]); par = sb("par", [128, 8, W3]); mu = sb("mu", [128, 576])
        lmu = sb("lmu", [128, 5]); w2 = sb("w2", [96, W3]); a2 = sb("a2", [96, W3]); g2 = sb("g2", [128, 2, W3])
        ones = sb("ones", [128, 1]); epsg = sb("epsg", [128, 1])
        bcst, bm4, bpar, bmu, blmu, bw2, ba2, bg2, bones, bepsg = (Buf() for _ in range(10))
        P.dma("sync", cst[:], cst_in, writes=[bcst]); P.dma("sync", m4[:], m4_in, writes=[bm4])
        P.dma("sync", par[:], par_in, writes=[bpar]); P.dma("sync", mu[:], mu_in, writes=[bmu])
        P.dma("sync", lmu[:], lmu_in, writes=[blmu]); P.dma("sync", w2[:], w2_in, writes=[bw2])
        P.dma("sync", a2[:], a2_in, writes=[ba2])
        P.dma("sync", g2[:], g2_in.rearrange("(c p) n -> p c n", p=128), writes=[bg2])
        G(lambda e: e.memset(ones[:], 1.0), [], [bones]); G(lambda e: e.memset(epsg[:], A_GN_EPS), [], [bepsg])
        ident, triu, trirev, maskn = cst[:, 0, :], cst[:, 1, :], cst[:, 2, :], cst[:, 3, :]
        if has_vres:
            v2 = sb("v2", [64, W3]); bv2 = Buf()
            P.dma("sync", v2[:], v2_in, writes=[bv2])
        ltmp = sb("ltmp", [128, S]); bltmp = Buf()
        lo = {}
        specs = [("wd", wd_in, 96, 0, AF.Tanh), ("ad", ad_in, 96, 1, None), ("gd0", gd_in[0:128], 128, 2, AF.Sigmoid),
                 ("gd1", gd_in[128:256], 128, 3, AF.Sigmoid)]
        if has_vres:
            specs.append(("vd", vd_in, 64, 4, None))
        for name, src, rows, mcol, fn in specs:
            t = sb("lo_" + name, [rows, S + 1]); b = Buf()
            G(lambda e, t=t: e.memset(t[:, 0:1], 0.0), [], [b])
            for i in range(4):
                P.dma("sync" if i % 2 == 0 else "gpsimd", t[:, 1 + i * 1024:1 + (i + 1) * 1024], src[:, i * 1024:(i + 1) * 1024], writes=[b])
            V(lambda e, t=t, rows=rows: e.tensor_tensor(out=ltmp[:rows, :], in0=t[:, 0:S], in1=t[:, 1:S + 1], op=ALU.subtract), [b, bltmp], [bltmp])
            V(lambda e, t=t, rows=rows, mcol=mcol: e.scalar_tensor_tensor(out=t[:, 1:S + 1], in0=ltmp[:rows, :], scalar=lmu[:rows, mcol:mcol + 1],
                                                                      in1=t[:, 1:S + 1], op0=ALU.mult, op1=ALU.add), [bltmp, blmu, b], [b])
            if fn is not None:
                A(lambda e, t=t, fn=fn: e.activation(out=t[:, 1:S + 1], in_=t[:, 1:S + 1], func=fn), [b], [b])
            lo[name] = (t, b)
        NS = 2
        cur = [sb("cur%d" % i, [128, 576]) for i in range(NS)]; prv = [sb("prv%d" % i, [128, 576]) for i in range(NS)]
        bcur = [Buf() for _ in range(NS)]; bprv = [Buf() for _ in range(NS)]
        tt = [sb("tt%d" % i, [128, 8, W3]) for i in range(NS)]; btt = [Buf() for _ in range(NS)]
        rk = [sb("rk%d" % i, [128, 3]) for i in range(NS)]; brk = [Buf() for _ in range(NS)]
        tm = sb("tm", [128, 4, W3]); btm = Buf()
        ss = sb("ss", [128, 3]); bss = Buf()
        if has_vres:
            vf = [sb("vf%d" % i, [128, W3]) for i in range(NS)]; bvf = [Buf() for _ in range(NS)]
        ee = [sb("ee%d" % i, [128, 4, W3]) for i in range(NS)]; bee = [Buf() for _ in range(NS)]
        dd = [sb("dd%d" % i, [128, 6, W3]) for i in range(NS)]; bdd = [Buf() for _ in range(NS)]
        wc = [sb("wc%d" % i, [64, 4]) for i in range(NS)]; bwc = [Buf() for _ in range(NS)]
        yt = [sb("yt%d" % i, [128, W3]) for i in range(NS)]; byt = [Buf() for _ in range(NS)]
        ytT = [sb("ytT%d" % i, [128, 2, 128], BF16) for i in range(NS)]; bytT = [Buf() for _ in range(NS)]
        NZ = 2
        T4 = [sb("T4_%d" % z, [64, 4, 128]) for z in range(NZ)]; bT4 = [Buf() for _ in range(NZ)]
        M4 = [sb("M4_%d" % z, [128, 4, 128]) for z in range(NZ)]; bM4 = [Buf() for _ in range(NZ)]
        Pk = [[sb("Pk%d_%d" % (z, i), [128, 128]) for i in range(2)] for z in range(NZ)]; bPk = [[Buf(), Buf()] for _ in range(NZ)]
        Qk = [[sb("Qk%d_%d" % (z, i), [128, 128]) for i in range(2)] for z in range(NZ)]; bQk = [[Buf(), Buf()] for _ in range(NZ)]
        Zt = [sb("Z%d" % z, [128, 128]) for z in range(NZ)]; bZ = [Buf() for _ in range(NZ)]
        nX = [sb("nX%d" % z, [128, 64]) for z in range(NZ)]; bnX = [Buf() for _ in range(NZ)]
        Ut = [sb("U%d" % z, [128, 64]) for z in range(NZ)]; bU = [Buf() for _ in range(NZ)]
        st6 = [sb("st6_%d" % z, [128, 6]) for z in range(NZ)]; bst6 = [Buf() for _ in range(NZ)]
        mv = [sb("mv%d" % z, [128, 2]) for z in range(NZ)]; bmv = [Buf() for _ in range(NZ)]
        rsd = [sb("rsd%d" % z, [128, 1]) for z in range(NZ)]; brsd = [Buf() for _ in range(NZ)]
        ST = [sb("ST%d" % h, [64, 64]) for h in range(3)]; bST = [Buf() for _ in range(3)]
        for h in range(3):
            G(lambda e, h=h: e.memset(ST[h][:], 0.0), [], [bST[h]])
        pb = [es.enter_context(nc.psum_tensor(tg + "pb%d" % i, [128, 512], F32)) for i in range(8)]
        pL1, pL2, pGG = pb[0], pb[0], pb[1]
        bL1 = Buf(); bL2 = bL1; byT1 = bL1; byT2 = bL1; bGG = Buf()
        pX = [pb[2], pb[3]]; bX = [Buf(), Buf()]
        pIVs = [pb[4], pb[5]]; bIVs = [Buf(), Buf()]
        pCHs = [pb[6], pb[7]]; bCHs = [Buf(), Buf()]
        rkv_v = rkv_in
        bout = Buf()
        wdT, bwd = lo["wd"]; adT, bad = lo["ad"]; gd0, bgd0 = lo["gd0"]; gd1, bgd1 = lo["gd1"]

        def prep(n):
            s = n % NS
            t0 = n * 128
            c_, p_, t_ = cur[s], prv[s], tt[s]
            P.dma("sync", c_[:], rkv_v[t0:t0 + 128, :], writes=[bcur[s]])
            if n == 0:
                G(lambda e: e.memset(p_[:], 0.0), [], [bprv[s]])
                P.dma("gpsimd", p_[1:128, :], rkv_v[0:127, :], writes=[bprv[s]])
            else:
                P.dma("gpsimd", p_[:], rkv_v[t0 - 1:t0 + 127, :], writes=[bprv[s]])
            if has_vres:
                P.dma("sync", vf[s][:], vf_in[t0:t0 + 128, :], writes=[bvf[s]])
            V(lambda e: e.tensor_tensor(out=p_[:], in0=p_[:], in1=c_[:], op=ALU.subtract), [bcur[s], bprv[s]], [bprv[s]])
            G(lambda e: e.tensor_tensor(out=p_[:], in0=p_[:], in1=mu[:], op=ALU.mult), [bprv[s], bmu], [bprv[s]])
            V(lambda e: e.tensor_tensor(out=c_[:], in0=c_[:], in1=p_[:], op=ALU.add), [bcur[s], bprv[s]], [bcur[s]])
            r_, k_, v_ = c_[:, 0:W3], c_[:, W3:2 * W3], c_[:, 2 * W3:3 * W3]
            tsl = slice(1 + t0, 1 + t0 + 128)
            T(lambda e: e.matmul(pL1[:, 0:W3], lhsT=wdT[:, tsl], rhs=w2[:], start=True, stop=True), [bwd, bw2], [bL1])
            T(lambda e: e.matmul(pL1[:, W3:2 * W3], lhsT=adT[:, tsl], rhs=a2[:], start=True, stop=True), [bad, ba2], [bL1])
            V(lambda e: e.tensor_tensor(out=t_[:, 5, :], in0=pL1[:, 0:W3], in1=par[:, 0, :], op=ALU.add), [bL1, bpar, btt[s]], [btt[s]])
            A(lambda e: e.activation(out=t_[:, 5, :], in_=t_[:, 5, :], func=AF.Sigmoid), [btt[s]], [btt[s]])
            V(lambda e: e.tensor_scalar(out=t_[:, 5, :], in0=t_[:, 5, :], scalar1=NEG_EHALF, scalar2=None, op0=ALU.mult), [btt[s]], [btt[s]])
            V(lambda e: e.tensor_tensor(out=t_[:, 7, :], in0=pL1[:, W3:2 * W3], in1=par[:, 1, :], op=ALU.add), [bL1, bpar, btt[s]], [btt[s]])
            A(lambda e: e.activation(out=t_[:, 7, :], in_=t_[:, 7, :], func=AF.Sigmoid), [btt[s]], [btt[s]])
            T(lambda e: e.matmul(pL2[:, 0:W3], lhsT=gd0[:, tsl], rhs=g2[:, 0, :], start=True, stop=False), [bgd0, bg2], [bL2])
            T(lambda e: e.matmul(pL2[:, 0:W3], lhsT=gd1[:, tsl], rhs=g2[:, 1, :], start=False, stop=True), [bgd1, bg2], [bL2])
            if has_vres:
                vdT, bvd = lo["vd"]
                T(lambda e: e.matmul(pL2[:, W3:2 * W3], lhsT=vdT[:, tsl], rhs=v2[:], start=True, stop=True), [bvd, bv2], [bL2])
            A(lambda e: e.copy(out=t_[:, 6, :], in_=pL2[:, 0:W3]), [bL2, btt[s]], [btt[s]])
            G(lambda e: e.tensor_copy(out=t_[:, 0, :], in_=r_), [bcur[s], btt[s]], [btt[s]])
            if has_vres:
                V(lambda e: e.tensor_tensor(out=tm[:, 0, :], in0=pL2[:, W3:2 * W3], in1=par[:, 2, :], op=ALU.add), [bL2, bpar, btm], [btm])
                A(lambda e: e.activation(out=tm[:, 0, :], in_=tm[:, 0, :], func=AF.Sigmoid), [btm], [btm])
                V(lambda e: e.tensor_tensor(out=tm[:, 1, :], in0=vf[s][:], in1=v_, op=ALU.subtract), [bvf[s], bcur[s], btm], [btm])
                V(lambda e: e.tensor_tensor(out=tm[:, 1, :], in0=tm[:, 1, :], in1=tm[:, 0, :], op=ALU.mult), [btm], [btm])
                V(lambda e: e.tensor_tensor(out=t_[:, 2, :], in0=tm[:, 1, :], in1=v_, op=ALU.add), [btm, bcur[s], btt[s]], [btt[s]])
            else:
                G(lambda e: e.tensor_copy(out=t_[:, 2, :], in_=v_), [bcur[s], btt[s]], [btt[s]])
                P.dma("sync", v_out[t0:t0 + 128, :], t_[:, 2, :], reads=[btt[s]], writes=[bout])
            V(lambda e: e.tensor_tensor(out=t_[:, 3, :], in0=k_, in1=par[:, 3, :], op=ALU.mult), [bcur[s], bpar, btt[s]], [btt[s]])
            V(lambda e: e.tensor_tensor(out=tm[:, 2, :], in0=t_[:, 3, :], in1=t_[:, 3, :], op=ALU.mult), [btt[s], btm], [btm])
            V(lambda e: e.tensor_reduce(out=ss[:], in_=tm[:, 2, :].rearrange("p (h j) -> p h j", h=3), axis=AX.X, op=ALU.add), [btm, bss], [bss])
            A(lambda e: e.activation(out=ss[:], in_=ss[:], func=AF.Sqrt), [bss], [bss])
            V(lambda e: e.tensor_scalar(out=ss[:], in0=ss[:], scalar1=1e-12, scalar2=None, op0=ALU.max), [bss], [bss])
            V(lambda e: e.reciprocal(out=ss[:], in_=ss[:]), [bss], [bss])
            for h in range(3):
                hs = slice(64 * h, 64 * h + 64)
                V(lambda e, hs=hs, h=h: e.tensor_scalar(out=t_[:, 3, hs], in0=t_[:, 3, hs], scalar1=ss[:, h:h + 1], scalar2=None, op0=ALU.mult),
                  [bss, btt[s]], [btt[s]])
            V(lambda e: e.scalar_tensor_tensor(out=tm[:, 3, :], in0=t_[:, 7, :], scalar=-1.0, in1=par[:, 4, :], op0=ALU.add, op1=ALU.mult),
              [btt[s], bpar, btm], [btm])
            V(lambda e: e.scalar_tensor_tensor(out=t_[:, 1, :], in0=tm[:, 3, :], scalar=1.0, in1=k_, op0=ALU.add, op1=ALU.mult),
              [btm, bcur[s], btt[s]], [btt[s]])
            G(lambda e: e.tensor_tensor(out=t_[:, 4, :], in0=t_[:, 3, :], in1=t_[:, 7, :], op=ALU.mult), [btt[s]], [btt[s]])
            V(lambda e: e.tensor_tensor(out=tm[:, 3, :], in0=t_[:, 0, :], in1=t_[:, 1, :], op=ALU.mult), [btt[s], btm], [btm])
            V(lambda e: e.tensor_tensor(out=tm[:, 3, :], in0=tm[:, 3, :], in1=par[:, 5, :], op=ALU.mult), [btm, bpar], [btm])
            V(lambda e: e.tensor_reduce(out=rk[s][:], in_=tm[:, 3, :].rearrange("p (h j) -> p h j", h=3), axis=AX.X, op=ALU.add), [btm, brk[s]], [brk[s]])
            T(lambda e: e.matmul(pGG[:, 0:W3], lhsT=triu, rhs=t_[:, 5, :], start=True, stop=True), [bcst, btt[s]], [bGG])
            T(lambda e: e.matmul(pGG[:, W3:2 * W3], lhsT=trirev, rhs=t_[:, 5, :], start=True, stop=True), [bcst, btt[s]], [bGG])
            for h in range(3):
                T(lambda e, h=h: e.matmul(pGG[0:64, 400 + h:401 + h], lhsT=t_[:, 5, 64 * h:64 * h + 64], rhs=ones[:], start=True, stop=True),
                  [btt[s], bones], [bGG])
            e_, d_ = ee[s], dd[s]
            A(lambda e: e.activation(out=e_[:, 0, :], in_=pGG[:, 0:W3], func=AF.Exp), [bGG, bee[s]], [bee[s]])
            A(lambda e: e.activation(out=e_[:, 1, :], in_=pGG[:, 0:W3], func=AF.Exp, scale=-1.0), [bGG, bee[s]], [bee[s]])
            V(lambda e: e.tensor_tensor(out=e_[:, 2, :], in0=pGG[:, 0:W3], in1=t_[:, 5, :], op=ALU.subtract), [bGG, btt[s], bee[s]], [bee[s]])
            A(lambda e: e.activation(out=e_[:, 2, :], in_=e_[:, 2, :], func=AF.Exp), [bee[s]], [bee[s]])
            A(lambda e: e.activation(out=e_[:, 3, :], in_=pGG[:, W3:2 * W3], func=AF.Exp), [bGG, bee[s]], [bee[s]])
            A(lambda e: e.activation(out=wc[s][:, 0:3], in_=pGG[0:64, 400:403], func=AF.Exp), [bGG, bwc[s]], [bwc[s]])
            for (di, ti, ei, eng) in ((0, 0, 0, V), (1, 3, 2, G), (2, 1, 1, V), (3, 4, 1, G), (4, 1, 3, V), (5, 4, 3, G)):
                eng(lambda e, di=di, ti=ti, ei=ei: e.tensor_tensor(out=d_[:, di, :], in0=t_[:, ti, :], in1=e_[:, ei, :], op=ALU.mult),
                    [btt[s], bee[s], bdd[s]], [bdd[s]])

        def head_gen(n, h, z):
            s = n % NS
            t_, d_ = tt[s], dd[s]
            hs = slice(64 * h, 64 * h + 64)
            X, IV = pX[z], pIVs[z]
            CH = pCHs[z][:, 0:256]
            CH64 = pCHs[z][0:64, 0:256]
            bXz, bIV, bCH = bX[z], bIVs[z], bCHs[z]
            t4, m4_, Z, nx, U = T4[z], M4[z], Zt[z], nX[z], Ut[z]
            for j in range(4):
                T(lambda e, j=j: e.transpose(X[0:64, j * 128:(j + 1) * 128], d_[:, j, hs], ident), [bdd[s], bcst], [bXz])
                yield
            A(lambda e: e.copy(out=t4[:], in_=X[0:64, :].rearrange("p (a b) -> p a b", a=4)), [bXz, bT4[z]], [bT4[z]])
            yield
            T(lambda e: e.matmul(X[:, 0:256], lhsT=t4[:, 2, :], rhs=t4[:, 0:2, :].rearrange("p a b -> p (a b)"), start=True, stop=True), [bT4[z]], [bXz])
            T(lambda e: e.matmul(X[:, 256:512], lhsT=t4[:, 3, :], rhs=t4[:, 0:2, :].rearrange("p a b -> p (a b)"), start=True, stop=True), [bT4[z]], [bXz])
            T(lambda e: e.matmul(IV[:, 0:128], lhsT=t4[:, 1, :], rhs=t4[:, 3, :], start=True, stop=True), [bT4[z]], [bIV])
            yield
            V(lambda e: e.tensor_tensor(out=m4_[:].rearrange("p a b -> p (a b)"), in0=X[:], in1=m4[:].rearrange("p a b -> p (a b)"), op=ALU.mult),
              [bXz, bm4, bM4[z]], [bM4[z]])
            V(lambda e: e.tensor_tensor(out=Pk[z][0][:], in0=IV[:, 0:128], in1=maskn, op=ALU.mult), [bIV, bcst, bPk[z][0]], [bPk[z][0]])
            yield
            G(lambda e: e.tensor_copy(out=Qk[z][0][:], in_=m4_[:, 3, :]), [bM4[z], bQk[z][0]], [bQk[z][0]])
            G(lambda e: e.tensor_tensor(out=Z[:], in0=ident, in1=m4_[:, 3, :], op=ALU.subtract), [bM4[z], bcst, bZ[z]], [bZ[z]])
            yield
            for lv in range(1, 7):
                pi, ci = (lv - 1) % 2, lv % 2
                if lv < 6:
                    T(lambda e, pi=pi: e.matmul(IV[:, 128:256], lhsT=Pk[z][pi][:], rhs=Qk[z][pi][:], start=True, stop=True), [bPk[z][pi], bQk[z][pi]], [bIV])
                T(lambda e, pi=pi: e.matmul(IV[:, 256:384], lhsT=Qk[z][pi][:], rhs=Pk[z][pi][:], start=True, stop=True), [bPk[z][pi], bQk[z][pi]], [bIV])
                yield
                if lv < 6:
                    A(lambda e, ci=ci: e.copy(out=Qk[z][ci][:], in_=IV[:, 128:256]), [bIV, bQk[z][ci]], [bQk[z][ci]])
                A(lambda e, ci=ci: e.copy(out=Pk[z][ci][:], in_=IV[:, 256:384]), [bIV, bPk[z][ci]], [bPk[z][ci]])
                yield
                T(lambda e, ci=ci: e.matmul(IV[:, 384:512], lhsT=Pk[z][ci][:], rhs=Z[:], start=True, stop=True), [bPk[z][ci], bZ[z]], [bIV])
                yield
                V(lambda e: e.tensor_tensor(out=Z[:], in0=Z[:], in1=IV[:, 384:512], op=ALU.add), [bIV, bZ[z]], [bZ[z]])
                yield
            vh = t_[:, 2, hs]
            T(lambda e: e.matmul(CH[:, 0:64], lhsT=m4_[:, 1, :], rhs=vh, start=True, stop=False), [bM4[z], btt[s]], [bCH])
            T(lambda e: e.matmul(CH[:, 0:64], lhsT=t4[:, 1, :], rhs=ST[h][:], start=False, stop=True), [bT4[z], bST[h]], [bCH])
            yield
            A(lambda e: e.activation(out=nx[:], in_=CH[:, 0:64], func=AF.Copy, scale=-1.0), [bCH, bnX[z]], [bnX[z]])
            yield
            T(lambda e: e.matmul(CH[:, 64:128], lhsT=Z[:], rhs=nx[:], start=True, stop=True), [bZ[z], bnX[z]], [bCH])
            yield
            V(lambda e: e.tensor_scalar(out=U[:], in0=CH[:, 64:128], scalar1=1.0, scalar2=None, op0=ALU.mult), [bCH, bU[z]], [bU[z]])
            yield
            T(lambda e: e.matmul(CH[:, 128:192], lhsT=t4[:, 0, :], rhs=ST[h][:], start=True, stop=False), [bT4[z], bST[h]], [bCH])
            T(lambda e: e.matmul(CH[:, 128:192], lhsT=m4_[:, 2, :], rhs=U[:], start=False, stop=False), [bM4[z], bU[z]], [bCH])
            T(lambda e: e.matmul(CH[:, 128:192], lhsT=m4_[:, 0, :], rhs=vh, start=False, stop=True), [bM4[z], btt[s]], [bCH])
            T(lambda e: e.matmul(CH64[:, 192:256], lhsT=d_[:, 5, hs], rhs=U[:], start=True, stop=False), [bdd[s], bU[z]], [bCH])
            T(lambda e: e.matmul(CH64[:, 192:256], lhsT=d_[:, 4, hs], rhs=vh, start=False, stop=True), [bdd[s], btt[s]], [bCH])
            yield
            V(lambda e: e.scalar_tensor_tensor(out=ST[h][:], in0=ST[h][:], scalar=wc[s][:, h:h + 1], in1=CH64[:, 192:256], op0=ALU.mult, op1=ALU.add),
              [bCH, bwc[s], bST[h]], [bST[h]])
            V(lambda e: e.bn_stats(out=st6[z][:], in_=CH[:, 128:192]), [bCH, bst6[z]], [bst6[z]])
            yield
            V(lambda e: e.bn_aggr(out=mv[z][:], in_=st6[z][:]), [bst6[z], bmv[z]], [bmv[z]])
            yield
            A(lambda e: e.activation(out=rsd[z][:], in_=mv[z][:, 1:2], func=AF.Sqrt, bias=epsg[:, 0:1]), [bmv[z], bepsg, brsd[z]], [brsd[z]])
            yield
            V(lambda e: e.reciprocal(out=rsd[z][:], in_=rsd[z][:]), [brsd[z]], [brsd[z]])
            yield
            V(lambda e: e.tensor_scalar(out=yt[s][:, hs], in0=CH[:, 128:192], scalar1=mv[z][:, 0:1], scalar2=rsd[z][:, 0:1], op0=ALU.subtract, op1=ALU.mult),
              [bCH, bmv[z], brsd[z], byt[s]], [byt[s]])
            yield

        def post(n):
            s = n % NS
            t0 = n * 128
            t_ = tt[s]
            G(lambda e: e.tensor_tensor(out=yt[s][:], in0=yt[s][:], in1=par[:, 6, :], op=ALU.mult), [byt[s], bpar], [byt[s]])
            G(lambda e: e.tensor_tensor(out=yt[s][:], in0=yt[s][:], in1=par[:, 7, :], op=ALU.add), [byt[s], bpar], [byt[s]])
            for h in range(3):
                hs = slice(64 * h, 64 * h + 64)
                V(lambda e, hs=hs, h=h: e.scalar_tensor_tensor(out=yt[s][:, hs], in0=t_[:, 2, hs], scalar=rk[s][:, h:h + 1], in1=yt[s][:, hs],
                                                                op0=ALU.mult, op1=ALU.add), [btt[s], brk[s], byt[s]], [byt[s]])
            V(lambda e: e.tensor_tensor(out=yt[s][:], in0=yt[s][:], in1=t_[:, 6, :], op=ALU.mult), [byt[s], btt[s]], [byt[s]])
            T(lambda e: e.transpose(pL1[:, 384:512], yt[s][:, 0:128], ident), [byt[s], bcst], [byT1])
            A(lambda e: e.copy(out=ytT[s][:, 0, :], in_=pL1[:, 384:512]), [byT1, bytT[s]], [bytT[s]])
            T(lambda e: e.transpose(pL1[0:64, 384:512], yt[s][:, 128:192], ident), [byt[s], bcst], [byT1])
            A(lambda e: e.copy(out=ytT[s][0:64, 1, :], in_=pL1[0:64, 384:512]), [byT1, bytT[s]], [bytT[s]])
            P.dma("sync", ysrc[0:128, t0:t0 + 128], ytT[s][:, 0, :], reads=[bytT[s]], writes=[bout])
            P.dma("sync", ysrc[128:192, t0:t0 + 128], ytT[s][0:64, 1, :], reads=[bytT[s]], writes=[bout])

        jobs = [(n, h) for n in range(nt) for h in range(3)]
        prepped = [-1]
        done = {}
        for p0 in range(0, len(jobs), NZ):
            pair = jobs[p0:p0 + NZ]
            for (n, h) in pair:
                while prepped[0] < n:
                    prepped[0] += 1
                    prep(prepped[0])
            gens = [head_gen(n, h, z) for z, (n, h) in enumerate(pair)]
            while gens:
                for g_ in list(gens):
                    try:
                        next(g_)
                    except StopIteration:
                        gens.remove(g_)
            for (n, h) in pair:
                done[n] = done.get(n, 0) + 1
                if done[n] == 3:
                    post(n)
        P.final_wait("sync", [bout]); P.final_wait("gpsimd", [bout])
        P.emit()


def p2a_consts():
    i = np.arange(128)
    row, col = i[:, None], i[None, :]
    cst = np.stack([np.eye(128), (row <= col), (row > col), (row > col)], axis=1).astype(np.float32)
    incl = (row <= col).astype(np.float32); strict = (row < col).astype(np.float32)
    mask4 = np.stack([incl, strict, incl, strict], axis=1).astype(np.float32)
    return dict(cst=np.ascontiguousarray(cst), mask4=np.ascontiguousarray(mask4))


NTM = 1344
NFM = 768
NY = 576
GROUPS = [[0, 1, 2, 3], [4, 5, 6, 7]]


def phase_norm(nc, io, l, x_ap):
    tg = "n%d_" % l
    P = Prog(nc, tg)
    with ExitStack() as es:
        xs = es.enter_context(nc.sbuf_tensor(tg + "xs", [128, KC, TOK], F32))
        bxs, bhs = Buf(), Buf()
        xv = x_ap.rearrange("(c p) t -> p c t", p=128)
        for i in range(4):
            P.dma("sync", xs[:, i * 4:(i + 1) * 4, :], xv[:, i * 4:(i + 1) * 4, :], writes=[bxs])
        hT, bh = rms_to_bf16(P, nc, es, xs, bxs, io["g1_%d" % l], tg)
        hv = io["h_src"].rearrange("(c p) t -> p c t", p=128)
        for i in range(4):
            P.dma("sync", hv[:, i * 4:(i + 1) * 4, :], hT[:, i * 4:(i + 1) * 4, :], reads=[bh], writes=[bhs])
        P.final_wait("sync", [bhs]); P.final_wait("gpsimd", [bhs])
        P.emit()


def phase_inproj(nc, io, l):
    tg = "p%d_" % l
    has_vres = l > 0
    P = Prog(nc, tg)
    h_src, h_all, tm_all, fm_all = io["h_src_t"], io["h_all_t"], io["tm_all"], io["fm_all"]
    wtm_ap, wfm_ap = io["wtm%d" % l], io["wfm%d" % l]
    with ExitStack() as es:
        sb = lambda n, shp, d=F32: es.enter_context(nc.sbuf_tensor(tg + n, shp, d))
        bhall = Buf()
        for kk in range(4):
            P.cc(lambda e, kk=kk: e.collective_compute("AllGather", mybir.AluOpType.bypass, replica_groups=GROUPS,
                                                       ins=[h_src.ap()[kk * 512:(kk + 1) * 512, :]], outs=[h_all.ap()[kk * 2048:(kk + 1) * 2048, :]]), writes=[bhall])
        wtm = sb("wtm", [128, KC, NTM], BF16); wfm = sb("wfm", [128, KC, NFM], BF16)
        bwtm, bwfm = Buf(), Buf()
        stg = [sb("stg%d" % i, [128, NTM], F32) for i in range(2)]
        bstg = [Buf(), Buf()]
        k = 0
        for (w_ap, wt, bw, n) in ((wtm_ap, wtm, bwtm, NTM), (wfm_ap, wfm, bwfm, NFM)):
            for c in range(KC):
                s = k % 2
                k += 1
                P.dma("sync" if s == 0 else "gpsimd", stg[s][:, :n], w_ap[c * 128:(c + 1) * 128, :], writes=[bstg[s]])
                if s == 0:
                    P.op("gpsimd", lambda e, s=s, wt=wt, c=c, n=n: e.tensor_copy(out=wt[:, c, :], in_=stg[s][:, :n]), reads=[bstg[s]], writes=[bw])
                else:
                    P.op("scalar", lambda e, s=s, wt=wt, c=c, n=n: e.copy(out=wt[:, c, :], in_=stg[s][:, :n]), reads=[bstg[s]], writes=[bw])
        hT = [sb("hT%d" % i, [128, KC, TOK], BF16) for i in range(2)]
        bhT = [Buf(), Buf()]
        otm = [sb("otm%d" % i, [128, NTM], F32) for i in range(2)]
        botm = [Buf(), Buf()]
        ofm = [sb("ofm%d" % i, [128, 512], F32) for i in range(4)]
        bofm = [Buf() for _ in range(4)]
        pt = [es.enter_context(nc.psum_tensor(tg + "pt%d" % i, [128, 512], F32)) for i in range(6)]
        bpt = [Buf() for _ in range(6)]
        hav = h_all.ap()
        btm, bfm = Buf(), Buf()
        fm_blocks = [(0, 96), (96, 192), (192, 320), (320, 448)] + ([(448, 512)] if has_vres else []) + [(512, 640), (640, 768)]
        tm_blocks = [(0, 512), (512, 1024), (1024, NTM)]
        pk = 0
        fk = 0
        for r in range(4):
            hs = r % 2
            for i in range(4):
                hv = hav[i * 2048 + r * 512:i * 2048 + (r + 1) * 512, :].rearrange("(c p) t -> p c t", p=128)
                P.dma("sync" if i % 2 == 0 else "gpsimd", hT[hs][:, i * 4:(i + 1) * 4, :], hv, reads=[bhall], writes=[bhT[hs]])
            for tl in range(8):
                os_ = (r * 8 + tl) % 2
                tsl = slice(tl * 128, (tl + 1) * 128)
                for bi, (c0, c1) in enumerate(tm_blocks):
                    q = pk % 6
                    pk += 1
                    for c in range(KC):
                        P.op("tensor", lambda e, q=q, c=c, hs=hs, tsl=tsl, c0=c0, c1=c1: e.matmul(pt[q][:, 0:c1 - c0], lhsT=hT[hs][:, c, tsl], rhs=wtm[:, c, c0:c1],
                                                                                                  start=(c == 0), stop=(c == KC - 1)),
                             reads=[bhT[hs], bwtm], writes=[bpt[q]])
                    if bi % 2 == 0:
                        P.op("scalar", lambda e, q=q, os_=os_, c0=c0, c1=c1: e.copy(out=otm[os_][:, c0:c1], in_=pt[q][:, 0:c1 - c0]), reads=[bpt[q]], writes=[botm[os_]])
                    else:
                        P.op("vector", lambda e, q=q, os_=os_, c0=c0, c1=c1: e.tensor_scalar(out=otm[os_][:, c0:c1], in0=pt[q][:, 0:c1 - c0], scalar1=1.0, scalar2=None, op0=ALU.mult),
                             reads=[bpt[q]], writes=[botm[os_]])
                t0 = r * 1024 + tl * 128
                P.dma("sync", tm_all[t0:t0 + 128, :], otm[os_][:], reads=[botm[os_]], writes=[btm])
            for (b0, b1) in fm_blocks:
                mw = b1 - b0
                for hf in range(2):
                    q = pk % 6
                    pk += 1
                    fs = fk % 4
                    fk += 1
                    sl = slice(hf * 512, (hf + 1) * 512)
                    for c in range(KC):
                        P.op("tensor", lambda e, q=q, c=c, hs=hs, sl=sl, b0=b0, b1=b1, mw=mw: e.matmul(pt[q][:mw, :], lhsT=wfm[:, c, b0:b1], rhs=hT[hs][:, c, sl],
                                                                                                       start=(c == 0), stop=(c == KC - 1)),
                             reads=[bhT[hs], bwfm], writes=[bpt[q]])
                    if fk % 2 == 0:
                        P.op("scalar", lambda e, q=q, fs=fs, mw=mw: e.copy(out=ofm[fs][:mw, :], in_=pt[q][:mw, :]), reads=[bpt[q]], writes=[bofm[fs]])
                    else:
                        P.op("vector", lambda e, q=q, fs=fs, mw=mw: e.tensor_scalar(out=ofm[fs][:mw, :], in0=pt[q][:mw, :], scalar1=1.0, scalar2=None, op0=ALU.mult),
                             reads=[bpt[q]], writes=[bofm[fs]])
                    P.dma("gpsimd", fm_all[b0:b1, r * 1024 + hf * 512:r * 1024 + (hf + 1) * 512], ofm[fs][:mw, :], reads=[bofm[fs]], writes=[bfm])
        P.final_wait("sync", [btm, bfm]); P.final_wait("gpsimd", [btm, bfm])
        P.emit()


DFF = 5632
NFB = DFF // 128
GRP = 4
KY = 18


def phase_p3(nc, io, l, x_ap, out_ap, final):
    tg = "f%d_" % l
    y_src, y_all = io["y_src_t"], io["y_all_t"]
    wout, g2, wg, wu, wd = io["wout%d" % l], io["g2_%d" % l], io["wg%d" % l], io["wu%d" % l], io["wd%d" % l]
    P = Prog(nc, tg)
    with ExitStack() as es:
        sb = lambda n, shp, dt: es.enter_context(nc.sbuf_tensor(tg + n, shp, dt))
        pst = lambda n: es.enter_context(nc.psum_tensor(tg + n, [128, 512], F32))
        byall = Buf()
        for kk in range(6):
            P.cc(lambda e, kk=kk: e.collective_compute("AllGather", mybir.AluOpType.bypass, replica_groups=GROUPS,
                                                       ins=[y_src.ap()[kk * 96:(kk + 1) * 96, :]], outs=[y_all.ap()[kk * 384:(kk + 1) * 384, :]]), writes=[byall])
        xs = sb("xs", [128, KC, TOK], F32)
        hT = sb("hT", [128, KY, TOK], BF16)
        bxs, bh = Buf(), Buf()
        NST = 4
        stg = [sb("stg%d" % i, [128, 2048], F32) for i in range(NST)]
        wbf = [sb("wbf%d" % i, [128, 2048], BF16) for i in range(NST)]
        bstg = [Buf() for _ in range(NST)]
        bwbf = [Buf() for _ in range(NST)]
        act = [sb("act%d" % i, [128, GRP, TOK], BF16) for i in range(2)]
        bact = [Buf(), Buf()]
        sg = [sb("sg%d" % i, [128, 512], F32) for i in range(2)]
        bsg = [Buf(), Buf()]
        sel = sb("sel", [128, 4], F32); bsel = Buf()
        pg = [pst("pg%d" % i) for i in range(2)]
        pu = [pst("pu%d" % i) for i in range(2)]
        pd = [pst("pd%d" % i) for i in range(2)]
        bpg, bpu, bpd = [Buf(), Buf()], [Buf(), Buf()], [Buf(), Buf()]
        xv = x_ap.rearrange("(c p) t -> p c t", p=128)
        for i in range(4):
            P.dma("sync", xs[:, i * 4:(i + 1) * 4, :], xv[:, i * 4:(i + 1) * 4, :], writes=[bxs])
        P.dma("sync", sel[:], io["sel"], writes=[bsel])
        st = [0]

        def load_cast(src_ap, view3=None):
            s = st[0] % NST
            st[0] += 1
            dst = stg[s][:] if view3 is None else stg[s][:].rearrange(view3[0], **view3[1])
            P.dma("sync", dst, src_ap, writes=[bstg[s]])
            P.op("gpsimd", lambda e, s=s: e.tensor_copy(out=wbf[s][:], in_=stg[s][:]), reads=[bstg[s]], writes=[bwbf[s]])
            return wbf[s], bwbf[s]

        yst = [stg[i][:].bitcast(BF16) for i in range(2)]
        yav = y_all.ap()
        for c in range(KY):
            s = c % 2
            P.dma("sync" if s == 0 else "gpsimd", yst[s], yav[c * 128:(c + 1) * 128, :], reads=[byall], writes=[bstg[s]])
            P.op("vector", lambda e, c=c, s=s: e.tensor_scalar(out=hT[:, c, :], in0=yst[s][:, 0:TOK], scalar1=sel[:, 0:1], scalar2=None, op0=ALU.mult),
                 reads=[bstg[s], bsel], writes=[bh])
            for gi in range(1, 4):
                P.op("vector", lambda e, c=c, s=s, gi=gi: e.scalar_tensor_tensor(out=hT[:, c, :], in0=yst[s][:, gi * TOK:(gi + 1) * TOK], scalar=sel[:, gi:gi + 1],
                                                                                 in1=hT[:, c, :], op0=ALU.mult, op1=ALU.add),
                     reads=[bstg[s], bsel, bh], writes=[bh])
        st[0] = 2
        wov = wout.rearrange("(c p) n -> p c n", p=128)
        k = 0
        for m in range(KC):
            t, b = load_cast(wov[:, 0:KC, m * 128:(m + 1) * 128], ("p (c n) -> p c n", dict(c=KC)))
            tv = t[:].rearrange("p (c n) -> p c n", c=KC)
            s2 = st[0] % NST
            st[0] += 1
            P.dma("sync", stg[s2][:, 0:256].rearrange("p (c n) -> p c n", c=2), wov[:, KC:KY, m * 128:(m + 1) * 128], writes=[bstg[s2]])
            P.op("gpsimd", lambda e, s2=s2: e.tensor_copy(out=wbf[s2][:, 0:256], in_=stg[s2][:, 0:256]), reads=[bstg[s2]], writes=[bwbf[s2]])
            t2v = wbf[s2][:, 0:256].rearrange("p (c n) -> p c n", c=2)
            b2 = bwbf[s2]
            for hf in range(2):
                q = k % 2
                k += 1
                sl = slice(hf * 512, (hf + 1) * 512)
                for c in range(KY):
                    lt = tv[:, c, :] if c < KC else t2v[:, c - KC, :]
                    P.op("tensor", lambda e, lt=lt, c=c, q=q, sl=sl: e.matmul(pd[q][:], lhsT=lt, rhs=hT[:, c, sl], start=(c == 0), stop=(c == KY - 1)),
                         reads=[b, b2, bh], writes=[bpd[q]])
                P.op("vector", lambda e, m=m, q=q, sl=sl: e.tensor_tensor(out=xs[:, m, sl], in0=xs[:, m, sl], in1=pd[q][:], op=ALU.add),
                     reads=[bpd[q], bxs], writes=[bxs])
        scr = {}
        rms_to_bf16(P, nc, es, xs, bxs, g2, tg + "n2", out_tile=hT, out_buf=bh, scr=scr)
        wgv = wg.rearrange("(c p) n -> p c n", p=128)
        wuv = wu.rearrange("(c p) n -> p c n", p=128)
        for grp in range(NFB // GRP):
            a = act[grp % 2]
            ba = bact[grp % 2]
            wds = []
            for j in range(GRP):
                fb = grp * GRP + j
                tg_, bg_ = load_cast(wgv[:, :, fb * 128:(fb + 1) * 128], ("p (c n) -> p c n", dict(c=KC)))
                tu, bu_ = load_cast(wuv[:, :, fb * 128:(fb + 1) * 128], ("p (c n) -> p c n", dict(c=KC)))
                tgv = tg_[:].rearrange("p (c n) -> p c n", c=KC)
                tuv = tu[:].rearrange("p (c n) -> p c n", c=KC)
                for hf in range(2):
                    sl = slice(hf * 512, (hf + 1) * 512)
                    for c in range(KC):
                        P.op("tensor", lambda e, tgv=tgv, c=c, hf=hf, sl=sl: e.matmul(pg[hf][:], lhsT=tgv[:, c, :], rhs=hT[:, c, sl],
                                                                                   start=(c == 0), stop=(c == KC - 1)),
                             reads=[bg_, bh], writes=[bpg[hf]])
                    for c in range(KC):
                        P.op("tensor", lambda e, tuv=tuv, c=c, hf=hf, sl=sl: e.matmul(pu[hf][:], lhsT=tuv[:, c, :], rhs=hT[:, c, sl],
                                                                                   start=(c == 0), stop=(c == KC - 1)),
                             reads=[bu_, bh], writes=[bpu[hf]])
                    P.op("scalar", lambda e, hf=hf: e.activation(out=sg[hf][:], in_=pg[hf][:], func=AF.Silu),
                         reads=[bpg[hf]], writes=[bsg[hf]])
                    P.op("vector", lambda e, hf=hf, a=a, j=j, sl=sl: e.tensor_tensor(out=a[:, j, sl], in0=sg[hf][:], in1=pu[hf][:], op=ALU.mult),
                         reads=[bsg[hf], bpu[hf]], writes=[ba])
            for j in range(GRP):
                fb = grp * GRP + j
                wds.append(load_cast(wd[fb * 128:(fb + 1) * 128, :]))
            for m in range(KC):
                for hf in range(2):
                    q = k % 2
                    k += 1
                    sl = slice(hf * 512, (hf + 1) * 512)
                    for j in range(GRP):
                        td, bd_ = wds[j]
                        P.op("tensor", lambda e, td=td, j=j, m=m, q=q, sl=sl, a=a: e.matmul(pd[q][:], lhsT=td[:, m * 128:(m + 1) * 128], rhs=a[:, j, sl],
                                                                                         start=(j == 0), stop=(j == GRP - 1)),
                             reads=[bd_, ba], writes=[bpd[q]])
                    P.op("vector", lambda e, m=m, q=q, sl=sl: e.tensor_tensor(out=xs[:, m, sl], in0=xs[:, m, sl], in1=pd[q][:], op=ALU.add),
                         reads=[bpd[q], bxs], writes=[bxs])
        bout = Buf()
        ov = out_ap.rearrange("(c p) t -> p c t", p=128)
        if final:
            rms_to_bf16(P, nc, es, xs, bxs, io["gf"], tg + "n3", out_tile=xs, out_buf=bxs, scr=scr)
        for i in range(4):
            P.dma("sync", ov[:, i * 4:(i + 1) * 4, :], xs[:, i * 4:(i + 1) * 4, :], reads=[bxs], writes=[bout])
        P.final_wait("sync", [bout]); P.final_wait("gpsimd", [bout])
        P.emit()


def build_fused(upto=99, dbg=None, only=None):
    nc = bass.Bass("TRN2", target_bir_lowering=False)
    _POOL[0] = SemPool(nc)
    io = {}
    ext = lambda n, shp, d=F32: io.__setitem__(n, nc.dram_tensor(n, shp, d, kind="ExternalInput").ap())
    ext("xT", [D, TOK]); ext("sel", [128, 4]); ext("gf", [128, KC])
    ext("cst", [128, 4, 128]); ext("mask4", [128, 4, 128])
    ext("pos", [128, NT], I32); ext("invf", [128, 32]); ext("ident", [128, 128]); ext("maskT", [2, 128, 128]); ext("qdec", [2, 64, 128])
    ext("kdec", [128, 2]); ext("g128", [64, 2]); ext("v2c", [64, W3])
    for l in range(2):
        ext("g1_%d" % l, [128, KC]); ext("wtm%d" % l, [D, NTM]); ext("wfm%d" % l, [D, NFM])
        ext("mu_rkv%d" % l, [128, 576]); ext("lmu%d" % l, [128, 5]); ext("w2c%d" % l, [96, W3]); ext("a2c%d" % l, [96, W3])
        ext("g2c%d" % l, [256, W3]); ext("par%d" % l, [128, 8, W3])
        ext("sm%d" % l, [128, 8]); ext("wa%d" % l, [2, 64, 64]); ext("wx%d" % l, [2, 64, 64])
        ext("gng%d" % l, [2, 128, 128])
        if upto > 6 * l + 5:
            ext("wout%d" % l, [KY * 128, D]); ext("g2_%d" % l, [128, KC]); ext("wg%d" % l, [D, DFF]); ext("wu%d" % l, [D, DFF]); ext("wd%d" % l, [DFF, D])
    out = nc.dram_tensor("outT", [D, TOK], F32, kind="ExternalOutput").ap()
    io["h_src_t"] = nc.dram_tensor("h_src", [D, TOK], BF16); io["h_src"] = io["h_src_t"].ap()
    io["h_all_t"] = nc.dram_tensor("h_all", [4 * D, TOK], BF16)
    io["tm_all"] = nc.dram_tensor("tm_all", [S, NTM], F32).ap()
    io["fm_all"] = nc.dram_tensor("fm_all", [NFM, S], F32).ap()
    io["y_src_t"] = nc.dram_tensor("y_src", [NY, S], BF16); io["y_src"] = io["y_src_t"].ap()
    io["y_all_t"] = nc.dram_tensor("y_all", [4 * NY, S], BF16)
    io["vfirst"] = nc.dram_tensor("vfirst", [S, W3], F32).ap()
    x_cur = nc.dram_tensor("x_cur", [D, TOK], F32).ap()
    io["x_cur"] = x_cur
    ph = 0
    for l in range(2):
        steps = [lambda l=l: phase_norm(nc, io, l, io["xT"] if l == 0 else x_cur),
                 lambda l=l: phase_inproj(nc, io, l),
                 lambda l=l: (phase_p2a_il(nc, io, l) if l == 0 else phase_p2a(nc, io, l)),
                 lambda l=l: phase_p2b(nc, io, l),
                 lambda l=l: phase_p2c(nc, io, l),
                 lambda l=l: phase_p3(nc, io, l, io["xT"] if l == 0 else x_cur, out if l == 1 else x_cur, final=(l == 1))]
        for st_ in steps:
            if ph < upto and (only is None or ph in only):
                st_()
            ph += 1
    if dbg is not None:
        src_ap = io[dbg]
        dout = nc.dram_tensor("dbg", list(src_ap.shape), src_ap.dtype, kind="ExternalOutput").ap()
        P = Prog(nc, "dbg_")
        b = Buf()
        rows = src_ap.shape[0]
        stp = max(1, rows // 8)
        for r0 in range(0, rows, stp):
            P.dma("sync", dout[r0:r0 + stp], src_ap[r0:r0 + stp], writes=[b])
        P.final_wait("sync", [b])
        P.emit()
    _POOL[0].close()
    _POOL[0] = None
    return nc


def core_inputs(d, c):
    b, q = c // 4, c % 4
    f32 = lambda a: np.ascontiguousarray(np.asarray(a, dtype=np.float32))
    lay = lambda g: f32(g.reshape(16, 128).T)
    rep = lambda v: f32(np.broadcast_to(np.asarray(v, np.float32), (128, v.shape[-1])))
    xf = d["x"].reshape(-1, D)
    im = {"xT": f32(xf[c * 1024:(c + 1) * 1024].T), "gf": lay(d["final_norm_g"])}
    sel = np.zeros((128, 4), np.float32); sel[:, q] = 1.0
    im["sel"] = sel
    im.update(p2a_consts())
    heads = C_SLOTS[q]
    im.update(p2c_consts(heads))
    im["pos"] = np.ascontiguousarray(d["positions"][b].reshape(32, 128).T.astype(np.int32))
    cA = slice(192 * q, 192 * q + 192)
    cols_rkv = np.r_[192 * q:192 * q + 192, 768 + 192 * q:768 + 192 * q + 192, 1536 + 192 * q:1536 + 192 * q + 192]
    c0 = 2752 + 1024
    cols_c = np.concatenate([np.r_[c0 + 64 * h:c0 + 64 * h + 64] for h in heads] + [np.r_[c0 + 384 + 64 * h:c0 + 384 + 64 * h + 64] for h in heads]
                            + [np.r_[c0 + 768 + 128 * h:c0 + 768 + 128 * h + 128] for h in heads]
                            + [np.r_[c0 + 1536 + 128 * h:c0 + 1536 + 128 * h + 128] for h in heads])
    im["v2c"] = f32(d["rwkv_v2"][0][:, cA])
    perm = np.full(KY * 128, -1, np.int64)
    for r in range(4):
        loc = np.full(NY, -1, np.int64)
        loc[0:192] = np.r_[192 * r:192 * r + 192]
        loc[192:320] = 768 + np.r_[128 * r:128 * r + 128]
        if r < 3:
            for s_, h in enumerate(C_SLOTS[r]):
                loc[320 + 128 * s_:448 + 128 * s_] = 1280 + np.r_[128 * h:128 * h + 128]
        for kk in range(6):
            perm[kk * 384 + r * 96:kk * 384 + (r + 1) * 96] = loc[96 * kk:96 * (kk + 1)]
    for l in range(2):
        w = d["w_in"][l]
        mu = d["tshift_mu"][l]
        im["g1_%d" % l] = lay(d["norm1_g"][l])
        im["wtm%d" % l] = f32(w[:, np.concatenate([cols_rkv, cols_c])])
        wfm = np.zeros((D, NFM), np.float32)
        wfm[:, 0:448] = w[:, 2304:2752]
        if l > 0:
            wfm[:, 448:512] = d["w_in_vres"][l - 1]
        A0 = 2752
        wfm[:, 512:640] = w[:, A0 + 128 * q:A0 + 128 * q + 128]
        wfm[:, 640:768] = w[:, A0 + 512 + 128 * q:A0 + 512 + 128 * q + 128]
        im["wfm%d" % l] = wfm
        im["mu_rkv%d" % l] = rep(mu[cols_rkv])
        lmu = np.zeros((128, 5), np.float32)
        lmu[:96, 0] = mu[2304:2400]; lmu[:96, 1] = mu[2400:2496]; lmu[:, 2] = mu[2496:2624]; lmu[:, 3] = mu[2624:2752]
        if l > 0:
            lmu[:64, 4] = d["tshift_mu_vres"][l - 1]
        im["lmu%d" % l] = lmu
        im["w2c%d" % l] = f32(d["rwkv_w2"][l][:, cA]); im["a2c%d" % l] = f32(d["rwkv_a2"][l][:, cA]); im["g2c%d" % l] = f32(d["rwkv_g2"][l][:, cA])
        v0 = d["rwkv_v0"][l - 1][cA] if l > 0 else np.zeros(192, np.float32)
        par = np.stack([d["rwkv_w0"][l][cA], d["rwkv_a0"][l][cA], v0, d["rwkv_k_k"][l][cA], d["rwkv_k_a"][l][cA],
                        d["rwkv_r_k"][l].reshape(-1)[cA], d["rwkv_ln_g"][l][cA], d["rwkv_ln_b"][l][cA]])
        im["par%d" % l] = f32(np.broadcast_to(par[None].astype(np.float32), (128, 8, 192)))
        ch = slice(128 * q, 128 * q + 128)
        cw = d["lru_conv_w"][l]
        im["sm%d" % l] = f32(np.stack([cw[0, ch], cw[1, ch], cw[2, ch], cw[3, ch], d["lru_conv_b"][l][ch], d["lru_ba"][l][ch],
                                       d["lru_bx"][l][ch], d["lru_lambda"][l][ch]], axis=1))
        im["wa%d" % l] = f32(d["lru_wa"][l][2 * q:2 * q + 2]); im["wx%d" % l] = f32(d["lru_wx"][l][2 * q:2 * q + 2])
        im["gng%d" % l] = f32(np.stack([np.broadcast_to(d["ret_gn_g"][l][128 * h:128 * h + 128], (128, 128)) for h in heads]))
        wo = np.zeros((KY * 128, D), np.float32)
        ok = perm >= 0
        wo[ok] = d["w_out"][l][perm[ok]]
        im["wout%d" % l] = wo
        im["g2_%d" % l] = lay(d["norm2_g"][l])
        im["wg%d" % l] = f32(d["ffn_w_gate"][l]); im["wu%d" % l] = f32(d["ffn_w_up"][l]); im["wd%d" % l] = f32(d["ffn_w_down"][l])
    return im


def kernel(**inputs):
    d = {k: np.asarray(v) for k, v in inputs.items()}
    Bn, Sn, Dn = d["x"].shape
    nc = build_fused()
    in_maps = [core_inputs(d, c) for c in range(8)]
    res = run_bass_kernel_spmd(nc, in_maps, core_ids=list(range(8)))
    out = np.concatenate([r["outT"].T for r in res.results], axis=0).reshape(Bn, Sn, Dn)
    return np.ascontiguousarray(out.astype(np.float32))
```

```python
import math
from contextlib import ExitStack
import numpy as np
import concourse.bass as bass
import concourse.mybir as mybir
from concourse.bass_utils import run_bass_kernel_spmd

F32 = mybir.dt.float32
BF16 = mybir.dt.bfloat16
I32 = mybir.dt.int32
AF = mybir.ActivationFunctionType
ALU = mybir.AluOpType
AX = mybir.AxisListType


class Buf:
    __slots__ = ("name", "last_w", "readers")

    def __init__(self, name=""):
        self.name = name
        self.last_w = None
        self.readers = []


class SemPool:
    def __init__(self, nc):
        self.nc = nc
        self.sems = {}
        self._ctx = []
        self.count = {e: 0 for e in Prog.ENGS}
        self.dma_i = {"sync": 0, "gpsimd": 0, "scalar": 0}
        self.n_cc = 0

    def sem(self, key):
        if key not in self.sems:
            cm = self.nc.semaphore("s_" + key)
            self._ctx.append(cm)
            self.sems[key] = cm.__enter__()
        return self.sems[key]

    def close(self):
        for cm in reversed(self._ctx):
            cm.__exit__(None, None, None)


_POOL = [None]


class Prog:
    ENGS = ("sync", "scalar", "vector", "gpsimd", "tensor")

    def __init__(self, nc, tag="", n_dma_sems=12, self_sync=True):
        self.nc = nc
        self.tag = tag
        self.self_sync = self_sync
        self.own_pool = _POOL[0] is None
        self.pool = SemPool(nc) if self.own_pool else _POOL[0]
        self.ops = {e: [] for e in self.ENGS}
        self.waited = {e: {} for e in self.ENGS}
        self.n_dma_sems = n_dma_sems

    def _collect(self, eng, reads, writes):
        need = {}
        def req(tok):
            if tok is None:
                return
            k, v = tok
            if need.get(k, 0) < v:
                need[k] = v
        for b in reads:
            req(b.last_w)
        for b in writes:
            req(b.last_w)
            for r in b.readers:
                req(r)
        waits = []
        wd = self.waited[eng]
        for k, v in need.items():
            if k == "e_" + eng and (eng == "tensor" or not self.self_sync):
                continue
            if wd.get(k, 0) >= v:
                continue
            wd[k] = v
            waits.append((k, v))
        return waits

    def _mark(self, tok, reads, writes):
        for b in reads:
            b.readers.append(tok)
        for b in writes:
            b.last_w = tok
            b.readers = []
        return tok

    def op(self, eng, fn, reads=(), writes=()):
        waits = self._collect(eng, reads, writes)
        self.pool.count[eng] += 1
        tok = ("e_" + eng, self.pool.count[eng])
        self.ops[eng].append((waits, fn, tok[0], 1))
        return self._mark(tok, reads, writes)

    def dma(self, eng, out, in_, reads=(), writes=(), **kw):
        i = self.pool.dma_i[eng]
        self.pool.dma_i[eng] = i + 1
        key = "d_%s_%d" % (eng, i % self.n_dma_sems)
        val = 16 * (i // self.n_dma_sems + 1)
        waits = self._collect(eng, reads, writes)
        if i >= self.n_dma_sems:
            pv = val - 16
            if self.waited[eng].get(key, 0) < pv:
                self.waited[eng][key] = pv
                waits.append((key, pv))
        fn = lambda e, out=out, in_=in_, kw=kw: e.dma_start(out=out, in_=in_, **kw)
        self.ops[eng].append((waits, fn, key, 16))
        return self._mark((key, val), reads, writes)

    def cc(self, fn, reads=(), writes=()):
        eng = "gpsimd"
        self.pool.n_cc += 1
        key = "cc_%d" % self.pool.n_cc
        waits = self._collect(eng, reads, writes)
        self.ops[eng].append((waits, fn, key, None))
        return self._mark((key, 1), reads, writes)

    def final_wait(self, eng, bufs):
        waits = self._collect(eng, bufs, ())
        self.ops[eng].append((waits, None, None, 0))

    def emit(self):
        nc = self.nc
        sem = self.pool.sem
        with nc.Block() as block:
            for e in self.ENGS:
                ops = self.ops[e]
                if not ops:
                    continue

                def body(engine, ops=ops):
                    for waits, fn, key, inc in ops:
                        for k, v in waits:
                            engine.wait_ge(sem(k), v)
                        if fn is not None:
                            if inc is None:
                                fn(engine).then_inc(sem(key))
                            else:
                                fn(engine).then_inc(sem(key), inc)
                getattr(block, e)(body)
        if self.own_pool:
            self.pool.close()


D = 2048
KC = 16
TOK = 1024
EPS = 1e-6


def rms_to_bf16(P, nc, es, xs, bxs, g_ap, name, out_tile=None, out_buf=None, scr=None):
    sb = lambda n, shp, dt: es.enter_context(nc.sbuf_tensor(name + n, shp, dt))
    ps_ = lambda n, shp, dt: es.enter_context(nc.psum_tensor(name + n, shp, dt))
    if scr is None:
        scr = {}
    if "ones" not in scr:
        scr["ones"] = sb("ones", [128, 128], F32)
        scr["sq"] = [sb("sq%d" % i, [128, TOK], F32) for i in range(2)]
        scr["rstd"] = sb("rstd", [128, TOK], F32)
        scr["pss"] = [ps_("ss%d" % i, [128, 512], F32) for i in range(2)]
        scr["eps"] = sb("eps", [128, 1], F32)
        scr["b"] = dict(ones=Buf(), rstd=Buf(), sq=[Buf(), Buf()], ps=[Buf(), Buf()], eps=Buf())
        P.op("gpsimd", lambda e: e.memset(scr["ones"][:], 1.0), writes=[scr["b"]["ones"]])
        P.op("gpsimd", lambda e: e.memset(scr["eps"][:], EPS), writes=[scr["b"]["eps"]])
    ones, sq, rstd, pss, epst = scr["ones"], scr["sq"], scr["rstd"], scr["pss"], scr["eps"]
    B = scr["b"]
    bones, brstd, bsq, bps, beps = B["ones"], B["rstd"], B["sq"], B["ps"], B["eps"]
    gs = sb("g", [128, KC], F32)
    hT = out_tile if out_tile is not None else sb("hT", [128, KC, TOK], BF16)
    bg = Buf()
    bh = out_buf if out_buf is not None else Buf()
    P.dma("sync", gs[:], g_ap, writes=[bg])
    for c in range(KC):
        s = sq[c % 2]
        P.op("scalar", lambda e, c=c, s=s: e.activation(out=s[:], in_=xs[:, c, :], func=AF.Square),
             reads=[bxs], writes=[bsq[c % 2]])
        for hf in range(2):
            P.op("tensor", lambda e, c=c, s=s, hf=hf: e.matmul(pss[hf][:], lhsT=ones[:], rhs=s[:, hf * 512:(hf + 1) * 512],
                                                               start=(c == 0), stop=(c == KC - 1)),
                 reads=[bones, bsq[c % 2]], writes=[bps[hf]])
    for hf in range(2):
        sl = slice(hf * 512, (hf + 1) * 512)
        P.op("scalar", lambda e, hf=hf, sl=sl: e.activation(out=rstd[:, sl], in_=pss[hf][:], func=AF.Sqrt,
                                                            scale=1.0 / D, bias=epst[:, 0:1]),
             reads=[bps[hf], beps], writes=[brstd])
        P.op("vector", lambda e, sl=sl: e.reciprocal(out=rstd[:, sl], in_=rstd[:, sl]), reads=[brstd], writes=[brstd])
    for c in range(KC):
        P.op("vector", lambda e, c=c: e.scalar_tensor_tensor(out=hT[:, c, :], in0=xs[:, c, :], scalar=gs[:, c:c + 1],
                                                         in1=rstd[:], op0=ALU.mult, op1=ALU.mult),
             reads=[bxs, bg, brstd], writes=[bh])
    return hT, bh


DFF = 5632
NFB = DFF // 128
GRP = 4


S = 4096


def phase_p2b(nc, io, l):
    tg = "b%d_" % l
    gT = io["fm_all"][512:640, :]; xT = io["fm_all"][640:768, :]
    sm = io["sm%d" % l]; wa = io["wa%d" % l]; wx = io["wx%d" % l]
    out = io["y_src"][192:320, :]
    P = Prog(nc, tg)
    with ExitStack() as es:
        sb = lambda n, shp, dt=F32: es.enter_context(nc.sbuf_tensor(tg + n, shp, dt))
        g = sb("g", [128, S]); x = sb("x", [128, S]); xc = sb("xc", [128, S])
        r = sb("r", [128, S]); ii = sb("ii", [128, S]); a = sb("a", [128, S]); u = sb("u", [128, S])
        h = sb("h", [128, S])
        sms = sb("sms", [128, 8]); wab = sb("wab", [128, 128]); wxb = sb("wxb", [128, 128])
        t1 = sb("t1", [128, 8])
        ps = [es.enter_context(nc.psum_tensor(tg + "ps%d" % i, [128, 512], F32)) for i in range(4)]
        bps = [Buf() for _ in range(4)]
        bg, bx, bxc, br, bi, ba_, bu, bh, bsm, bwa, bwx, bt1 = (Buf() for _ in range(12))
        for i in range(4):
            sl = slice(i * 1024, (i + 1) * 1024)
            P.dma("sync", x[:, sl], xT[:, sl], writes=[bx])
        for i in range(4):
            sl = slice(i * 1024, (i + 1) * 1024)
            P.dma("gpsimd", g[:, sl], gT[:, sl], writes=[bg])
        P.dma("sync", sms[:], sm, writes=[bsm])
        P.op("gpsimd", lambda e: e.memset(wab[:], 0.0), writes=[bwa])
        P.op("gpsimd", lambda e: e.memset(wxb[:], 0.0), writes=[bwx])
        for bl in range(2):
            P.dma("sync", wab[bl * 64:(bl + 1) * 64, bl * 64:(bl + 1) * 64], wa[bl], writes=[bwa])
            P.dma("sync", wxb[bl * 64:(bl + 1) * 64, bl * 64:(bl + 1) * 64], wx[bl], writes=[bwx])
        P.op("vector", lambda e: e.tensor_scalar(out=xc[:], in0=x[:], scalar1=sms[:, 3:4], scalar2=sms[:, 4:5], op0=ALU.mult, op1=ALU.add),
             reads=[bx, bsm], writes=[bxc])
        for sh in (1, 2, 3):
            P.op("vector", lambda e, sh=sh: e.scalar_tensor_tensor(out=xc[:, sh:], in0=x[:, :S - sh], scalar=sms[:, 3 - sh:4 - sh],
                                                                   in1=xc[:, sh:], op0=ALU.mult, op1=ALU.add),
                 reads=[bx, bsm, bxc], writes=[bxc])
        P.op("scalar", lambda e: e.activation(out=t1[:, 0:1], in_=sms[:, 7:8], func=AF.Exp, scale=-1.0), reads=[bsm], writes=[bt1])
        P.op("vector", lambda e: e.tensor_scalar(out=t1[:, 1:2], in0=t1[:, 0:1], scalar1=2.0, scalar2=None, op0=ALU.add), reads=[bt1], writes=[bt1])
        P.op("vector", lambda e: e.reciprocal(out=t1[:, 1:2], in_=t1[:, 1:2]), reads=[bt1], writes=[bt1])
        P.op("vector", lambda e: e.tensor_tensor(out=t1[:, 2:3], in0=t1[:, 0:1], in1=t1[:, 1:2], op=ALU.mult), reads=[bt1], writes=[bt1])
        P.op("vector", lambda e: e.tensor_tensor(out=t1[:, 3:4], in0=t1[:, 2:3], in1=t1[:, 2:3], op=ALU.mult), reads=[bt1], writes=[bt1])
        P.op("vector", lambda e: e.memset(t1[:, 4:5], 1.0 / 13.0), reads=[bt1], writes=[bt1])
        for cf in (1.0 / 11, 1.0 / 9, 1.0 / 7, 1.0 / 5, 1.0 / 3, 1.0):
            P.op("vector", lambda e, cf=cf: e.tensor_scalar(out=t1[:, 4:5], in0=t1[:, 4:5], scalar1=t1[:, 3:4], scalar2=float(cf), op0=ALU.mult, op1=ALU.add),
                 reads=[bt1], writes=[bt1])
        P.op("vector", lambda e: e.scalar_tensor_tensor(out=t1[:, 5:6], in0=t1[:, 4:5], scalar=-16.0, in1=t1[:, 2:3], op0=ALU.mult, op1=ALU.mult),
             reads=[bt1], writes=[bt1])
        for pc in range(8):
            sl = slice(pc * 512, (pc + 1) * 512)
            q0, q1 = (2 * pc) % 4, (2 * pc + 1) % 4
            P.op("tensor", lambda e, q0=q0, sl=sl: e.matmul(ps[q0][:], lhsT=wab[:], rhs=xc[:, sl], start=True, stop=True),
                 reads=[bwa, bxc], writes=[bps[q0]])
            P.op("tensor", lambda e, q1=q1, sl=sl: e.matmul(ps[q1][:], lhsT=wxb[:], rhs=xc[:, sl], start=True, stop=True),
                 reads=[bwx, bxc], writes=[bps[q1]])
            P.op("scalar", lambda e, q0=q0, sl=sl: e.activation(out=r[:, sl], in_=ps[q0][:], func=AF.Sigmoid, bias=sms[:, 5:6]),
                 reads=[bps[q0], bsm], writes=[br])
            P.op("scalar", lambda e, q1=q1, sl=sl: e.activation(out=ii[:, sl], in_=ps[q1][:], func=AF.Sigmoid, bias=sms[:, 6:7]),
                 reads=[bps[q1], bsm], writes=[bi])
        P.op("scalar", lambda e: e.activation(out=a[:], in_=r[:], func=AF.Exp, scale=t1[:, 5:6]), reads=[br, bt1], writes=[ba_])
        P.op("vector", lambda e: e.tensor_tensor(out=u[:], in0=a[:], in1=a[:], op=ALU.mult), reads=[ba_], writes=[bu])
        P.op("vector", lambda e: e.tensor_scalar(out=u[:], in0=u[:], scalar1=-1.0, scalar2=1.0, op0=ALU.mult, op1=ALU.add), reads=[bu], writes=[bu])
        P.op("scalar", lambda e: e.activation(out=u[:], in_=u[:], func=AF.Sqrt), reads=[bu], writes=[bu])
        P.op("vector", lambda e: e.tensor_tensor(out=ii[:], in0=ii[:], in1=xc[:], op=ALU.mult), reads=[bi, bxc], writes=[bi])
        P.op("vector", lambda e: e.tensor_tensor(out=u[:], in0=u[:], in1=ii[:], op=ALU.mult), reads=[bu, bi], writes=[bu])
        P.op("vector", lambda e: e.tensor_tensor_scan(out=h[:], data0=a[:], data1=u[:], initial=0.0, op0=ALU.mult, op1=ALU.add),
             reads=[ba_, bu], writes=[bh])
        P.op("scalar", lambda e: e.activation(out=r[:], in_=g[:], func=AF.Square), reads=[bg, br], writes=[br])
        P.op("vector", lambda e: e.tensor_scalar(out=r[:], in0=r[:], scalar1=0.044715, scalar2=1.0, op0=ALU.mult, op1=ALU.add), reads=[br], writes=[br])
        P.op("vector", lambda e: e.tensor_tensor(out=r[:], in0=r[:], in1=g[:], op=ALU.mult), reads=[br, bg], writes=[br])
        P.op("scalar", lambda e: e.activation(out=r[:], in_=r[:], func=AF.Sigmoid, scale=1.5957691216057308), reads=[br], writes=[br])
        P.op("vector", lambda e: e.tensor_tensor(out=r[:], in0=r[:], in1=g[:], op=ALU.mult), reads=[br, bg], writes=[br])
        P.op("vector", lambda e: e.tensor_tensor(out=h[:], in0=h[:], in1=r[:], op=ALU.mult), reads=[br, bh], writes=[bh])
        bout = Buf()
        hb = sb("hb", [128, S], BF16); bhb = Buf()
        P.op("scalar", lambda e: e.copy(out=hb[:], in_=h[:]), reads=[bh], writes=[bhb])
        for i in range(4):
            sl = slice(i * 1024, (i + 1) * 1024)
            P.dma("sync", out[:, sl], hb[:, sl], reads=[bhb], writes=[bout])
        P.final_wait("sync", [bout]); P.final_wait("gpsimd", [bout])
        P.emit()


S = 4096
NT = 32
PI = math.pi
C1 = 6.28125
C2 = 2 * math.pi - 6.28125


def phase_p2c(nc, io, l):
    tg = "c%d_" % l
    tm = io["tm_all"]
    q_in = [tm[:, 576 + 64 * s:576 + 64 * s + 64] for s in range(2)]
    k_in = [tm[:, 704 + 64 * s:704 + 64 * s + 64] for s in range(2)]
    v_in = [tm[:, 832 + 128 * s:832 + 128 * s + 128] for s in range(2)]
    g_in = [tm[:, 1088 + 128 * s:1088 + 128 * s + 128] for s in range(2)]
    pos_in = io["pos"]; invf_in = io["invf"]; ident_in = io["ident"]
    mask_in = io["maskT"]; qdec_in = io["qdec"]; kdec_in = io["kdec"]; g128_in = io["g128"]; gng_in = io["gng%d" % l]
    ysrc = io["y_src"]
    P = Prog(nc, tg)
    with ExitStack() as es:
        sb = lambda n, shp, d=F32: es.enter_context(nc.sbuf_tensor(tg + n, shp, d))
        pst = lambda n, shp: es.enter_context(nc.psum_tensor(tg + n, shp, F32))
        posi = sb("posi", [128, NT], I32); posf = sb("posf", [128, NT]); invf = sb("invf", [128, 32])
        ident = sb("ident", [128, 128])
        ang = sb("ang", [128, NT, 32]); nf = sb("nf", [128, NT, 32]); ni = sb("ni", [128, NT, 32], I32)
        sn = sb("sn", [128, NT, 32]); cs = sb("cs", [128, NT, 32]); tmp = sb("tmp", [128, NT, 32]); tmp2 = sb("tmp2", [128, NT, 32])
        epst = sb("epst", [128, 1])
        bpos, binvf, bident, bang, bnf, bsn, bcs, btmp, btmp2, beps = (Buf() for _ in range(10))
        P.dma("sync", posi[:], pos_in, writes=[bpos])
        P.dma("sync", invf[:], invf_in, writes=[binvf])
        P.dma("sync", ident[:], ident_in, writes=[bident])
        P.op("gpsimd", lambda e: e.memset(epst[:], 1e-5), writes=[beps])
        P.op("vector", lambda e: e.tensor_copy(out=posf[:], in_=posi[:]), reads=[bpos], writes=[bpos])
        for n in range(NT):
            P.op("vector", lambda e, n=n: e.tensor_scalar(out=ang[:, n, :], in0=invf[:], scalar1=posf[:, n:n + 1], scalar2=None, op0=ALU.mult),
                 reads=[bpos, binvf], writes=[bang])
        P.op("vector", lambda e: e.tensor_scalar(out=nf[:], in0=ang[:], scalar1=1.0 / (2 * PI), scalar2=None, op0=ALU.mult), reads=[bang], writes=[bnf])
        P.op("vector", lambda e: e.tensor_copy(out=ni[:], in_=nf[:]), reads=[bnf], writes=[bnf])
        P.op("vector", lambda e: e.tensor_copy(out=nf[:], in_=ni[:]), reads=[bnf], writes=[bnf])
        P.op("vector", lambda e: e.scalar_tensor_tensor(out=ang[:], in0=nf[:], scalar=-C1, in1=ang[:], op0=ALU.mult, op1=ALU.add), reads=[bnf, bang], writes=[bang])
        P.op("vector", lambda e: e.scalar_tensor_tensor(out=ang[:], in0=nf[:], scalar=-C2, in1=ang[:], op0=ALU.mult, op1=ALU.add), reads=[bnf, bang], writes=[bang])

        def wrap(dst, bdst, src, bsrc, shift):
            if shift != 0.0:
                P.op("vector", lambda e: e.tensor_scalar(out=dst[:], in0=src[:], scalar1=float(shift), scalar2=None, op0=ALU.add), reads=[bsrc], writes=[bdst])
                src_, bsrc_ = dst, bdst
            else:
                src_, bsrc_ = src, bsrc
            P.op("vector", lambda e: e.tensor_scalar(out=tmp[:], in0=src_[:], scalar1=PI, scalar2=-2 * PI, op0=ALU.is_gt, op1=ALU.mult), reads=[bsrc_], writes=[btmp])
            P.op("vector", lambda e: e.tensor_scalar(out=tmp2[:], in0=src_[:], scalar1=-PI, scalar2=2 * PI, op0=ALU.is_lt, op1=ALU.mult), reads=[bsrc_], writes=[btmp2])
            P.op("vector", lambda e: e.tensor_tensor(out=dst[:], in0=src_[:], in1=tmp[:], op=ALU.add), reads=[bsrc_, btmp], writes=[bdst])
            P.op("vector", lambda e: e.tensor_tensor(out=dst[:], in0=dst[:], in1=tmp2[:], op=ALU.add), reads=[bdst, btmp2], writes=[bdst])
        wrap(sn, bsn, ang, bang, 0.0)
        wrap(cs, bcs, sn, bsn, PI / 2)
        P.op("scalar", lambda e: e.activation(out=sn[:], in_=sn[:], func=AF.Sin), reads=[bsn], writes=[bsn])
        P.op("scalar", lambda e: e.activation(out=cs[:], in_=cs[:], func=AF.Sin), reads=[bcs], writes=[bcs])

        q = sb("q", [128, NT, 64]); k = sb("k", [128, NT, 64]); qr = sb("qr", [128, NT, 64]); kr = sb("kr", [128, NT, 64])
        kt = sb("kt", [128, NT, 64])
        v = sb("v", [128, NT, 128]); g = sb("g", [128, NT, 128]); o = sb("o", [128, NT, 128])
        maskT = sb("maskT", [128, 128]); qdec = sb("qdec", [64, 128]); kdec = sb("kdec", [128, 2]); g128 = sb("g128", [64, 2])
        gng = sb("gng", [128, 128])
        Sst = sb("Sst", [64, 128])
        qkT = [sb("qkT%d" % i, [64, 2, 128]) for i in range(2)]
        qtT = [sb("qtT%d" % i, [64, 128]) for i in range(2)]
        smk = [sb("smk%d" % i, [128, 128]) for i in range(2)]
        stats = [sb("stats%d" % i, [128, 6]) for i in range(2)]
        mv = [sb("mv%d" % i, [128, 2]) for i in range(2)]
        rs = [sb("rs%d" % i, [128, 1]) for i in range(2)]
        psT = [pst("psT%d" % i, [64, 2, 128]) for i in range(2)]
        pss = [pst("pss%d" % i, [128, 128]) for i in range(2)]
        pso = [pst("pso%d" % i, [128, 128]) for i in range(2)]
        pkv = [pst("pkv%d" % i, [64, 128]) for i in range(2)]
        bq, bk, bqr, bkr, bkt, bv, bg, bo, bmask, bqdec, bkdec, bg128, bgng, bS = (Buf() for _ in range(14))
        bqkT, bqtT, bsmk, bstats, bmv, brs, bpsT, bpss, bpso, bpkv = ([Buf(), Buf()] for _ in range(10))
        P.dma("sync", kdec[:], kdec_in, writes=[bkdec])
        P.dma("sync", g128[:], g128_in, writes=[bg128])
        bout = Buf()
        oT = sb("oT", [128, S], BF16); boT = Buf()
        for sl in range(2):
            vw = lambda ap: ap.rearrange("(n p) d -> p n d", p=128)
            P.dma("sync", q[:], vw(q_in[sl]), writes=[bq])
            P.dma("gpsimd", k[:], vw(k_in[sl]), writes=[bk])
            for hh in range(2):
                P.dma("sync", v[:, hh * 16:(hh + 1) * 16, :], vw(v_in[sl])[:, hh * 16:(hh + 1) * 16, :], writes=[bv])
                P.dma("gpsimd", g[:, hh * 16:(hh + 1) * 16, :], vw(g_in[sl])[:, hh * 16:(hh + 1) * 16, :], writes=[bg])
            P.dma("sync", maskT[:], mask_in[sl], writes=[bmask])
            P.dma("sync", qdec[:], qdec_in[sl], writes=[bqdec])
            P.dma("sync", gng[:], gng_in[sl], writes=[bgng])
            P.op("gpsimd", lambda e: e.memset(Sst[:], 0.0), writes=[bS])
            for (src, bsrc, dst, bdst) in ((q, bq, qr, bqr), (k, bk, kr, bkr)):
                s1, s2 = src[:, :, 0:32], src[:, :, 32:64]
                d1, d2 = dst[:, :, 0:32], dst[:, :, 32:64]
                P.op("vector", lambda e, s1=s1, d1=d1: e.tensor_tensor(out=d1, in0=s1, in1=cs[:], op=ALU.mult), reads=[bsrc, bcs], writes=[bdst])
                P.op("vector", lambda e, s2=s2: e.tensor_tensor(out=tmp[:], in0=s2, in1=sn[:], op=ALU.mult), reads=[bsrc, bsn], writes=[btmp])
                P.op("vector", lambda e, d1=d1: e.tensor_tensor(out=d1, in0=d1, in1=tmp[:], op=ALU.subtract), reads=[btmp, bdst], writes=[bdst])
                P.op("vector", lambda e, s2=s2, d2=d2: e.tensor_tensor(out=d2, in0=s2, in1=cs[:], op=ALU.mult), reads=[bsrc, bcs], writes=[bdst])
                P.op("vector", lambda e, s1=s1: e.tensor_tensor(out=tmp2[:], in0=s1, in1=sn[:], op=ALU.mult), reads=[bsrc, bsn], writes=[btmp2])
                P.op("vector", lambda e, d2=d2: e.tensor_tensor(out=d2, in0=d2, in1=tmp2[:], op=ALU.add), reads=[btmp2, bdst], writes=[bdst])
            P.op("vector", lambda e, sl=sl: e.tensor_scalar(out=kt[:], in0=kr[:], scalar1=kdec[:, sl:sl + 1], scalar2=None, op0=ALU.mult),
                 reads=[bkr, bkdec], writes=[bkt])
            for n in range(NT):
                i = n % 2
                P.op("tensor", lambda e, n=n, i=i: e.transpose(psT[i][:, 0, :], qr[:, n, :], ident[:]), reads=[bqr, bident], writes=[bpsT[i]])
                P.op("tensor", lambda e, n=n, i=i: e.transpose(psT[i][:, 1, :], kr[:, n, :], ident[:]), reads=[bkr, bident], writes=[bpsT[i]])
                P.op("scalar", lambda e, i=i: e.copy(out=qkT[i][:], in_=psT[i][:]), reads=[bpsT[i]], writes=[bqkT[i]])
                P.op("vector", lambda e, i=i: e.tensor_tensor(out=qtT[i][:], in0=psT[i][:, 0, :], in1=qdec[:], op=ALU.mult), reads=[bpsT[i], bqdec], writes=[bqtT[i]])
                P.op("tensor", lambda e, i=i: e.matmul(pss[i][:], lhsT=qkT[i][:, 1, :], rhs=qkT[i][:, 0, :], start=True, stop=True), reads=[bqkT[i]], writes=[bpss[i]])
                P.op("vector", lambda e, i=i: e.tensor_tensor(out=smk[i][:], in0=pss[i][:], in1=maskT[:], op=ALU.mult), reads=[bpss[i], bmask], writes=[bsmk[i]])
                P.op("tensor", lambda e, i=i, n=n: e.matmul(pso[i][:], lhsT=smk[i][:], rhs=v[:, n, :], start=True, stop=False), reads=[bsmk[i], bv], writes=[bpso[i]])
                P.op("tensor", lambda e, i=i: e.matmul(pso[i][:], lhsT=qtT[i][:], rhs=Sst[:], start=False, stop=True), reads=[bqtT[i], bS], writes=[bpso[i]])
                P.op("tensor", lambda e, i=i, n=n: e.matmul(pkv[i][:], lhsT=kt[:, n, :], rhs=v[:, n, :], start=True, stop=True), reads=[bkt, bv], writes=[bpkv[i]])
                P.op("vector", lambda e, i=i, sl=sl: e.scalar_tensor_tensor(out=Sst[:], in0=Sst[:], scalar=g128[:, sl:sl + 1], in1=pkv[i][:], op0=ALU.mult, op1=ALU.add),
                     reads=[bpkv[i], bg128, bS], writes=[bS])
                P.op("vector", lambda e, i=i: e.bn_stats(out=stats[i][:], in_=pso[i][:]), reads=[bpso[i]], writes=[bstats[i]])
                P.op("vector", lambda e, i=i: e.bn_aggr(out=mv[i][:], in_=stats[i][:]), reads=[bstats[i]], writes=[bmv[i]])
                P.op("scalar", lambda e, i=i: e.activation(out=rs[i][:], in_=mv[i][:, 1:2], func=AF.Sqrt, bias=epst[:, 0:1]), reads=[bmv[i], beps], writes=[brs[i]])
                P.op("vector", lambda e, i=i: e.reciprocal(out=rs[i][:], in_=rs[i][:]), reads=[brs[i]], writes=[brs[i]])
                P.op("vector", lambda e, i=i, n=n: e.tensor_scalar(out=o[:, n, :], in0=pso[i][:], scalar1=mv[i][:, 0:1], scalar2=rs[i][:, 0:1], op0=ALU.subtract, op1=ALU.mult),
                     reads=[bpso[i], bmv[i], brs[i]], writes=[bo])
                P.op("gpsimd", lambda e, n=n: e.tensor_tensor(out=o[:, n, :], in0=o[:, n, :], in1=gng[:], op=ALU.mult), reads=[bo, bgng], writes=[bo])
            P.op("scalar", lambda e: e.activation(out=g[:], in_=g[:], func=AF.Silu), reads=[bg], writes=[bg])
            P.op("vector", lambda e: e.tensor_tensor(out=o[:], in0=o[:], in1=g[:], op=ALU.mult), reads=[bo, bg], writes=[bo])
            for n in range(NT):
                i = n % 2
                P.op("tensor", lambda e, n=n, i=i: e.transpose(pss[i][:], o[:, n, :], ident[:]), reads=[bo, bident], writes=[bpss[i]])
                P.op("scalar", lambda e, n=n, i=i: e.copy(out=oT[:, n * 128:(n + 1) * 128], in_=pss[i][:]), reads=[bpss[i]], writes=[boT])
            for hh in range(4):
                cs_ = slice(hh * 1024, (hh + 1) * 1024)
                P.dma("sync", ysrc[320 + 128 * sl:448 + 128 * sl, cs_], oT[:, cs_], reads=[boT], writes=[bout])
        P.final_wait("sync", [bout]); P.final_wait("gpsimd", [bout])
        P.emit()


def p2c_consts(heads):
    maskT = np.zeros((2, 128, 128), np.float32); qdec = np.zeros((2, 64, 128), np.float32)
    kdec = np.zeros((128, 2), np.float32); g128 = np.zeros((64, 2), np.float32)
    idx = np.arange(128)
    for s, hd in enumerate(heads):
        lg = float(np.log1p(-np.exp2(np.float32(-5.0 - hd))).astype(np.float32))
        m = idx[:, None]; c = idx[None, :]
        same = (m // 64) == (c // 64)
        earlier = (m // 64) < (c // 64)
        dec = np.where(same, np.exp(lg * np.abs(c - m)), np.where(earlier, np.exp(lg * (c - m)), 0.0))
        maskT[s] = (dec * 0.125).astype(np.float32)
        qdec[s] = np.exp(lg * (idx + 1.0))[None, :].astype(np.float32)
        kdec[:, s] = (np.exp(lg * (127.0 - idx)) * 0.125).astype(np.float32)
        g128[:, s] = np.float32(np.exp(lg * 128.0))
    invf = (10000.0 ** (-np.arange(0, 64, 2, dtype=np.float32) / 64)).astype(np.float32)
    return dict(maskT=maskT, qdec=qdec, kdec=kdec, g128=g128, invf=np.ascontiguousarray(np.broadcast_to(invf, (128, 32))),
                ident=np.eye(128, dtype=np.float32))


C_SLOTS = [(0, 1), (2, 3), (4, 5), (4, 5)]


VARA = False

S = 4096
NT = 32
W3 = 192
NEG_EHALF = -math.exp(-0.5)
A_GN_EPS = 64e-5


def phase_p2a(nc, io, l, nt=NT, dbg=9):
    has_vres = l > 0
    tg = "a%d_" % l
    tmA = io["tm_all"]; fm = io["fm_all"]
    rkv_in = tmA[:, 0:576]; mu_in = io["mu_rkv%d" % l]
    wd_in = fm[0:96, :]; ad_in = fm[96:192, :]; gd_in = fm[192:448, :]
    lmu_in = io["lmu%d" % l]; w2_in = io["w2c%d" % l]; a2_in = io["a2c%d" % l]; g2_in = io["g2c%d" % l]
    par_in = io["par%d" % l]; cst_in = io["cst"]; m4_in = io["mask4"]
    if has_vres:
        vd_in = fm[448:512, :]; v2_in = io["v2c"]
    vf_in = io["vfirst"]
    v_out = io["vfirst"]
    ysrc = io["y_src"]
    P = Prog(nc, tg)
    with ExitStack() as es:
        sb = lambda n, shp, d=F32: es.enter_context(nc.sbuf_tensor(tg + n, shp, d))
        V = lambda fn, r, w: P.op("vector", fn, reads=r, writes=w)
        A = lambda fn, r, w: P.op("scalar", fn, reads=r, writes=w)
        G = lambda fn, r, w: P.op("gpsimd", fn, reads=r, writes=w)
        T = lambda fn, r, w: P.op("tensor", fn, reads=r, writes=w)
        cst = sb("cst", [128, 4, 128]); m4 = sb("m4", [128, 4, 128]); par = sb("par", [128, 8, W3]); mu = sb("mu", [128, 576])
        lmu = sb("lmu", [128, 5]); w2 = sb("w2", [96, W3]); a2 = sb("a2", [96, W3]); g2 = sb("g2", [128, 2, W3])
        ones = sb("ones", [128, 1]); epsg = sb("epsg", [128, 1])
        bcst, bm4, bpar, bmu, blmu, bw2, ba2, bg2, bones, bepsg = (Buf() for _ in range(10))
        P.dma("sync", cst[:], cst_in, writes=[bcst]); P.dma("sync", m4[:], m4_in, writes=[bm4])
        P.dma("sync", par[:], par_in, writes=[bpar]); P.dma("sync", mu[:], mu_in, writes=[bmu])
        P.dma("sync", lmu[:], lmu_in, writes=[blmu]); P.dma("sync", w2[:], w2_in, writes=[bw2])
        P.dma("sync", a2[:], a2_in, writes=[ba2])
        P.dma("sync", g2[:], g2_in.rearrange("(c p) n -> p c n", p=128), writes=[bg2])
        G(lambda e: e.memset(ones[:], 1.0), [], [bones]); G(lambda e: e.memset(epsg[:], A_GN_EPS), [], [bepsg])
        ident, triu, trirev, maskn = cst[:, 0, :], cst[:, 1, :], cst[:, 2, :], cst[:, 3, :]
        if has_vres:
            v2 = sb("v2", [64, W3]); bv2 = Buf()
            P.dma("sync", v2[:], v2_in, writes=[bv2])
        ltmp = sb("ltmp", [128, S]); bltmp = Buf()
        lo = {}
        specs = [("wd", wd_in, 96, 0, AF.Tanh), ("ad", ad_in, 96, 1, None), ("gd0", gd_in[0:128], 128, 2, AF.Sigmoid),
                 ("gd1", gd_in[128:256], 128, 3, AF.Sigmoid)]
        if has_vres:
            specs.append(("vd", vd_in, 64, 4, None))
        for name, src, rows, mcol, fn in specs:
            t = sb("lo_" + name, [rows, S + 1]); b = Buf()
            G(lambda e, t=t: e.memset(t[:, 0:1], 0.0), [], [b])
            for i in range(4):
                P.dma("sync" if i % 2 == 0 else "gpsimd", t[:, 1 + i * 1024:1 + (i + 1) * 1024], src[:, i * 1024:(i + 1) * 1024], writes=[b])
            V(lambda e, t=t, rows=rows: e.tensor_tensor(out=ltmp[:rows, :], in0=t[:, 0:S], in1=t[:, 1:S + 1], op=ALU.subtract), [b, bltmp], [bltmp])
            V(lambda e, t=t, rows=rows, mcol=mcol: e.scalar_tensor_tensor(out=t[:, 1:S + 1], in0=ltmp[:rows, :], scalar=lmu[:rows, mcol:mcol + 1],
                                                                      in1=t[:, 1:S + 1], op0=ALU.mult, op1=ALU.add), [bltmp, blmu, b], [b])
            if fn is not None:
                A(lambda e, t=t, fn=fn: e.activation(out=t[:, 1:S + 1], in_=t[:, 1:S + 1], func=fn), [b], [b])
            lo[name] = (t, b)
        NS = 2
        cur = [sb("cur%d" % i, [128, 576]) for i in range(NS)]; prv = [sb("prv%d" % i, [128, 576]) for i in range(NS)]
        bcur = [Buf() for _ in range(NS)]; bprv = [Buf() for _ in range(NS)]
        tt = [sb("tt%d" % i, [128, 8, W3]) for i in range(NS)]; btt = [Buf() for _ in range(NS)]
        rk = [sb("rk%d" % i, [128, 3]) for i in range(NS)]; brk = [Buf() for _ in range(NS)]
        tm = sb("tm", [128, 4, W3]); btm = Buf()
        ss = sb("ss", [128, 3]); bss = Buf()
        if has_vres:
            vf = [sb("vf%d" % i, [128, W3]) for i in range(NS)]; bvf = [Buf() for _ in range(NS)]
        ee = sb("ee", [128, 4, W3]); bee = Buf()
        dd = sb("dd", [128, 6, W3]); bdd = Buf()
        wc = sb("wc", [64, 4]); bwc = Buf()
        T4 = sb("T4", [64, 4, 128]); bT4 = Buf()
        M4 = sb("M4", [128, 4, 128]); bM4 = Buf()
        Pk = [sb("Pk%d" % i, [128, 128]) for i in range(2)]; bPk = [Buf(), Buf()]
        Qk = [sb("Qk%d" % i, [128, 128]) for i in range(2)]; bQk = [Buf(), Buf()]
        Z = sb("Z", [128, 128]); bZ = Buf()
        nX = sb("nX", [128, 64]); bnX = Buf()
        U = sb("U", [128, 64]); bU = Buf()
        ST = [sb("ST%d" % h, [64, 64]) for h in range(3)]; bST = [Buf() for _ in range(3)]
        yt = [sb("yt%d" % i, [128, W3]) for i in range(NS)]; byt = [Buf() for _ in range(NS)]
        st6 = sb("st6", [128, 6]); bst6 = Buf(); mv = sb("mv", [128, 2]); bmv = Buf(); rsd = sb("rsd", [128, 1]); brsd = Buf()
        for h in range(3):
            G(lambda e, h=h: e.memset(ST[h][:], 0.0), [], [bST[h]])
        ytT = [sb("ytT%d" % i, [128, 2, 128], BF16) for i in range(NS)]; bytT = [Buf() for _ in range(NS)]
        pb = [es.enter_context(nc.psum_tensor(tg + "pb%d" % i, [128, 512], F32)) for i in range(8)]
        bpb = [Buf() for _ in range(8)]
        pL1, pL2, pGG, pTT, pPP, pIV, pCH = pb[0], pb[1], pb[2], pb[3], pb[4], pb[5], pb[6]
        bL1, bL2, bGG, bTT, bPP, bIV, bCH = bpb[0], bpb[1], bpb[2], bpb[3], bpb[4], bpb[5], bpb[6]
        rkv_v = rkv_in
        bout = Buf()
        wdT, bwd = lo["wd"]; adT, bad = lo["ad"]; gd0, bgd0 = lo["gd0"]; gd1, bgd1 = lo["gd1"]
        for n in range(nt):
            s = n % NS
            t0 = n * 128
            c_, p_, t_ = cur[s], prv[s], tt[s]
            P.dma("sync", c_[:], rkv_v[t0:t0 + 128, :], writes=[bcur[s]])
            if n == 0:
                G(lambda e, p_=p_: e.memset(p_[:], 0.0), [], [bprv[s]])
                P.dma("gpsimd", p_[1:128, :], rkv_v[0:127, :], writes=[bprv[s]])
            else:
                P.dma("gpsimd", p_[:], rkv_v[t0 - 1:t0 + 127, :], writes=[bprv[s]])
            if has_vres:
                P.dma("sync", vf[s][:], vf_in[t0:t0 + 128, :], writes=[bvf[s]])
            V(lambda e, c_=c_, p_=p_: e.tensor_tensor(out=p_[:], in0=p_[:], in1=c_[:], op=ALU.subtract), [bcur[s], bprv[s]], [bprv[s]])
            G(lambda e, p_=p_: e.tensor_tensor(out=p_[:], in0=p_[:], in1=mu[:], op=ALU.mult), [bprv[s], bmu], [bprv[s]])
            V(lambda e, c_=c_, p_=p_: e.tensor_tensor(out=c_[:], in0=c_[:], in1=p_[:], op=ALU.add), [bcur[s], bprv[s]], [bcur[s]])
            r_, k_, v_ = c_[:, 0:W3], c_[:, W3:2 * W3], c_[:, 2 * W3:3 * W3]
            tsl = slice(1 + t0, 1 + t0 + 128)
            T(lambda e, tsl=tsl: e.matmul(pL1[:, 0:W3], lhsT=wdT[:, tsl], rhs=w2[:], start=True, stop=True), [bwd, bw2], [bL1])
            T(lambda e, tsl=tsl: e.matmul(pL1[:, W3:2 * W3], lhsT=adT[:, tsl], rhs=a2[:], start=True, stop=True), [bad, ba2], [bL1])
            T(lambda e, tsl=tsl: e.matmul(pL2[:, 0:W3], lhsT=gd0[:, tsl], rhs=g2[:, 0, :], start=True, stop=False), [bgd0, bg2], [bL2])
            T(lambda e, tsl=tsl: e.matmul(pL2[:, 0:W3], lhsT=gd1[:, tsl], rhs=g2[:, 1, :], start=False, stop=True), [bgd1, bg2], [bL2])
            if has_vres:
                vdT, bvd = lo["vd"]
                T(lambda e, tsl=tsl: e.matmul(pL2[:, W3:2 * W3], lhsT=vdT[:, tsl], rhs=v2[:], start=True, stop=True), [bvd, bv2], [bL2])
            V(lambda e, t_=t_: e.tensor_tensor(out=t_[:, 5, :], in0=pL1[:, 0:W3], in1=par[:, 0, :], op=ALU.add), [bL1, bpar, btt[s]], [btt[s]])
            A(lambda e, t_=t_: e.activation(out=t_[:, 5, :], in_=t_[:, 5, :], func=AF.Sigmoid), [btt[s]], [btt[s]])
            V(lambda e, t_=t_: e.tensor_scalar(out=t_[:, 5, :], in0=t_[:, 5, :], scalar1=NEG_EHALF, scalar2=None, op0=ALU.mult), [btt[s]], [btt[s]])
            V(lambda e, t_=t_: e.tensor_tensor(out=t_[:, 7, :], in0=pL1[:, W3:2 * W3], in1=par[:, 1, :], op=ALU.add), [bL1, bpar, btt[s]], [btt[s]])
            A(lambda e, t_=t_: e.activation(out=t_[:, 7, :], in_=t_[:, 7, :], func=AF.Sigmoid), [btt[s]], [btt[s]])
            A(lambda e, t_=t_: e.copy(out=t_[:, 6, :], in_=pL2[:, 0:W3]), [bL2, btt[s]], [btt[s]])
            G(lambda e, t_=t_, r_=r_: e.tensor_copy(out=t_[:, 0, :], in_=r_), [bcur[s], btt[s]], [btt[s]])
            if has_vres:
                V(lambda e: e.tensor_tensor(out=tm[:, 0, :], in0=pL2[:, W3:2 * W3], in1=par[:, 2, :], op=ALU.add), [bL2, bpar, btm], [btm])
                A(lambda e: e.activation(out=tm[:, 0, :], in_=tm[:, 0, :], func=AF.Sigmoid), [btm], [btm])
                V(lambda e, v_=v_, s=s: e.tensor_tensor(out=tm[:, 1, :], in0=vf[s][:], in1=v_, op=ALU.subtract), [bvf[s], bcur[s], btm], [btm])
                V(lambda e: e.tensor_tensor(out=tm[:, 1, :], in0=tm[:, 1, :], in1=tm[:, 0, :], op=ALU.mult), [btm], [btm])
                V(lambda e, t_=t_, v_=v_: e.tensor_tensor(out=t_[:, 2, :], in0=tm[:, 1, :], in1=v_, op=ALU.add), [btm, bcur[s], btt[s]], [btt[s]])
            else:
                G(lambda e, t_=t_, v_=v_: e.tensor_copy(out=t_[:, 2, :], in_=v_), [bcur[s], btt[s]], [btt[s]])
            if not has_vres:
                P.dma("sync", v_out[t0:t0 + 128, :], t_[:, 2, :], reads=[btt[s]], writes=[bout])
            V(lambda e, t_=t_, k_=k_: e.tensor_tensor(out=t_[:, 3, :], in0=k_, in1=par[:, 3, :], op=ALU.mult), [bcur[s], bpar, btt[s]], [btt[s]])
            V(lambda e, t_=t_: e.tensor_tensor(out=tm[:, 2, :], in0=t_[:, 3, :], in1=t_[:, 3, :], op=ALU.mult), [btt[s], btm], [btm])
            V(lambda e: e.tensor_reduce(out=ss[:], in_=tm[:, 2, :].rearrange("p (h j) -> p h j", h=3), axis=AX.X, op=ALU.add), [btm, bss], [bss])
            A(lambda e: e.activation(out=ss[:], in_=ss[:], func=AF.Sqrt), [bss], [bss])
            V(lambda e: e.tensor_scalar(out=ss[:], in0=ss[:], scalar1=1e-12, scalar2=None, op0=ALU.max), [bss], [bss])
            V(lambda e: e.reciprocal(out=ss[:], in_=ss[:]), [bss], [bss])
            for h in range(3):
                hs = slice(64 * h, 64 * h + 64)
                V(lambda e, t_=t_, hs=hs, h=h: e.tensor_scalar(out=t_[:, 3, hs], in0=t_[:, 3, hs], scalar1=ss[:, h:h + 1], scalar2=None, op0=ALU.mult),
                  [bss, btt[s]], [btt[s]])
            V(lambda e, t_=t_: e.scalar_tensor_tensor(out=tm[:, 3, :], in0=t_[:, 7, :], scalar=-1.0, in1=par[:, 4, :], op0=ALU.add, op1=ALU.mult),
              [btt[s], bpar, btm], [btm])
            V(lambda e, t_=t_, k_=k_: e.scalar_tensor_tensor(out=t_[:, 1, :], in0=tm[:, 3, :], scalar=1.0, in1=k_, op0=ALU.add, op1=ALU.mult),
              [btm, bcur[s], btt[s]], [btt[s]])
            G(lambda e, t_=t_: e.tensor_tensor(out=t_[:, 4, :], in0=t_[:, 3, :], in1=t_[:, 7, :], op=ALU.mult), [btt[s]], [btt[s]])
            V(lambda e, t_=t_: e.tensor_tensor(out=tm[:, 3, :], in0=t_[:, 0, :], in1=t_[:, 1, :], op=ALU.mult), [btt[s], btm], [btm])
            V(lambda e: e.tensor_tensor(out=tm[:, 3, :], in0=tm[:, 3, :], in1=par[:, 5, :], op=ALU.mult), [btm, bpar], [btm])
            V(lambda e, s=s: e.tensor_reduce(out=rk[s][:], in_=tm[:, 3, :].rearrange("p (h j) -> p h j", h=3), axis=AX.X, op=ALU.add), [btm, brk[s]], [brk[s]])
            if dbg <= 1:
                P.dma("sync", y_out[t0:t0 + 128, :], t_[:, 6, :], reads=[btt[s]], writes=[bout])
                continue
            T(lambda e, t_=t_: e.matmul(pGG[:, 0:W3], lhsT=triu, rhs=t_[:, 5, :], start=True, stop=True), [bcst, btt[s]], [bGG])
            T(lambda e, t_=t_: e.matmul(pGG[:, W3:2 * W3], lhsT=trirev, rhs=t_[:, 5, :], start=True, stop=True), [bcst, btt[s]], [bGG])
            for h in range(3):
                T(lambda e, t_=t_, h=h: e.matmul(pGG[0:64, 400 + h:401 + h], lhsT=t_[:, 5, 64 * h:64 * h + 64], rhs=ones[:], start=True, stop=True),
                  [btt[s], bones], [bGG])
            A(lambda e: e.activation(out=ee[:, 0, :], in_=pGG[:, 0:W3], func=AF.Exp), [bGG, bee], [bee])
            A(lambda e: e.activation(out=ee[:, 1, :], in_=pGG[:, 0:W3], func=AF.Exp, scale=-1.0), [bGG, bee], [bee])
            V(lambda e, t_=t_: e.tensor_tensor(out=ee[:, 2, :], in0=pGG[:, 0:W3], in1=t_[:, 5, :], op=ALU.subtract), [bGG, btt[s], bee], [bee])
            A(lambda e: e.activation(out=ee[:, 2, :], in_=ee[:, 2, :], func=AF.Exp), [bee], [bee])
            A(lambda e: e.activation(out=ee[:, 3, :], in_=pGG[:, W3:2 * W3], func=AF.Exp), [bGG, bee], [bee])
            A(lambda e: e.activation(out=wc[:, 0:3], in_=pGG[0:64, 400:403], func=AF.Exp), [bGG, bwc], [bwc])
            for (di, ti, ei, eng) in ((0, 0, 0, V), (1, 3, 2, G), (2, 1, 1, V), (3, 4, 1, G), (4, 1, 3, V), (5, 4, 3, G)):
                eng(lambda e, t_=t_, di=di, ti=ti, ei=ei: e.tensor_tensor(out=dd[:, di, :], in0=t_[:, ti, :], in1=ee[:, ei, :], op=ALU.mult),
                    [btt[s], bee, bdd], [bdd])
            if dbg <= 2:
                P.dma("sync", y_out[t0:t0 + 128, :], dd[:, 0, :], reads=[bdd], writes=[bout])
                continue
            for h in range(3):
                hs = slice(64 * h, 64 * h + 64)
                for j in range(4):
                    T(lambda e, j=j, hs=hs: e.transpose(pTT[0:64, j * 128:(j + 1) * 128], dd[:, j, hs], ident), [bdd, bcst], [bTT])
                A(lambda e: e.copy(out=T4[:], in_=pTT[0:64, :].rearrange("p (a b) -> p a b", a=4)), [bTT, bT4], [bT4])
                if dbg <= 3:
                    continue
                T(lambda e: e.matmul(pPP[:, 0:256], lhsT=T4[:, 2, :], rhs=T4[:, 0:2, :].rearrange("p a b -> p (a b)"), start=True, stop=True), [bT4], [bPP])
                T(lambda e: e.matmul(pPP[:, 256:512], lhsT=T4[:, 3, :], rhs=T4[:, 0:2, :].rearrange("p a b -> p (a b)"), start=True, stop=True), [bT4], [bPP])
                T(lambda e: e.matmul(pIV[:, 0:128], lhsT=T4[:, 1, :], rhs=T4[:, 3, :], start=True, stop=True), [bT4], [bIV])
                V(lambda e: e.tensor_tensor(out=M4[:].rearrange("p a b -> p (a b)"), in0=pPP[:], in1=m4[:].rearrange("p a b -> p (a b)"), op=ALU.mult),
                  [bPP, bm4, bM4], [bM4])
                V(lambda e: e.tensor_tensor(out=Pk[0][:], in0=pIV[:, 0:128], in1=maskn, op=ALU.mult), [bIV, bcst, bPk[0]], [bPk[0]])
                (V if VARA else G)(lambda e: e.tensor_copy(out=Qk[0][:], in_=M4[:, 3, :]), [bM4, bQk[0]], [bQk[0]])
                (V if VARA else G)(lambda e: e.tensor_tensor(out=Z[:], in0=ident, in1=M4[:, 3, :], op=ALU.subtract), [bM4, bcst, bZ], [bZ])
                if dbg <= 4:
                    continue
                for lv in range(1, 7):
                    pi, ci = (lv - 1) % 2, lv % 2
                    STEP = 9
                    if lv < 6 and STEP >= 1:
                        T(lambda e, pi=pi: e.matmul(pIV[:, 128:256], lhsT=Pk[pi][:], rhs=Qk[pi][:], start=True, stop=True), [bPk[pi], bQk[pi]], [bIV])
                    if STEP >= 2:
                        T(lambda e, pi=pi: e.matmul(pIV[:, 256:384], lhsT=Qk[pi][:], rhs=Pk[pi][:], start=True, stop=True), [bPk[pi], bQk[pi]], [bIV])
                    if lv < 6 and STEP >= 3:
                        A(lambda e, ci=ci: e.copy(out=Qk[ci][:], in_=pIV[:, 128:256]), [bIV, bQk[ci]], [bQk[ci]])
                    if STEP >= 4:
                        A(lambda e, ci=ci: e.copy(out=Pk[ci][:], in_=pIV[:, 256:384]), [bIV, bPk[ci]], [bPk[ci]])
                    if STEP >= 5:
                        T(lambda e, ci=ci: e.matmul(pIV[:, 384:512], lhsT=Pk[ci][:], rhs=Z[:], start=True, stop=True), [bPk[ci], bZ], [bIV])
                    if STEP >= 6:
                        V(lambda e: e.tensor_tensor(out=Z[:], in0=Z[:], in1=pIV[:, 384:512], op=ALU.add), [bIV, bZ], [bZ])
                if dbg <= 5:
                    continue
                vh = t_[:, 2, hs]
                T(lambda e, vh=vh: e.matmul(pCH[:, 0:64], lhsT=M4[:, 1, :], rhs=vh, start=True, stop=False), [bM4, btt[s]], [bCH])
                T(lambda e, h=h: e.matmul(pCH[:, 0:64], lhsT=T4[:, 1, :], rhs=ST[h][:], start=False, stop=True), [bT4, bST[h]], [bCH])
                A(lambda e: e.activation(out=nX[:], in_=pCH[:, 0:64], func=AF.Copy, scale=-1.0), [bCH, bnX], [bnX])
                T(lambda e: e.matmul(pCH[:, 64:128], lhsT=Z[:], rhs=nX[:], start=True, stop=True), [bZ, bnX], [bCH])
                V(lambda e: e.tensor_scalar(out=U[:], in0=pCH[:, 64:128], scalar1=1.0, scalar2=None, op0=ALU.mult), [bCH, bU], [bU])
                T(lambda e, h=h: e.matmul(pCH[:, 128:192], lhsT=T4[:, 0, :], rhs=ST[h][:], start=True, stop=False), [bT4, bST[h]], [bCH])
                T(lambda e: e.matmul(pCH[:, 128:192], lhsT=M4[:, 2, :], rhs=U[:], start=False, stop=False), [bM4, bU], [bCH])
                T(lambda e, vh=vh: e.matmul(pCH[:, 128:192], lhsT=M4[:, 0, :], rhs=vh, start=False, stop=True), [bM4, btt[s]], [bCH])
                T(lambda e, hs=hs: e.matmul(pCH[0:64, 192:256], lhsT=dd[:, 5, hs], rhs=U[:], start=True, stop=False), [bdd, bU], [bCH])
                T(lambda e, hs=hs, vh=vh: e.matmul(pCH[0:64, 192:256], lhsT=dd[:, 4, hs], rhs=vh, start=False, stop=True), [bdd, btt[s]], [bCH])
                V(lambda e, h=h: e.scalar_tensor_tensor(out=ST[h][:], in0=ST[h][:], scalar=wc[:, h:h + 1], in1=pCH[0:64, 192:256], op0=ALU.mult, op1=ALU.add),
                  [bCH, bwc, bST[h]], [bST[h]])
                V(lambda e: e.bn_stats(out=st6[:], in_=pCH[:, 128:192]), [bCH, bst6], [bst6])
                V(lambda e: e.bn_aggr(out=mv[:], in_=st6[:]), [bst6, bmv], [bmv])
                A(lambda e: e.activation(out=rsd[:], in_=mv[:, 1:2], func=AF.Sqrt, bias=epsg[:, 0:1]), [bmv, bepsg, brsd], [brsd])
                V(lambda e: e.reciprocal(out=rsd[:], in_=rsd[:]), [brsd], [brsd])
                V(lambda e, s=s, hs=hs: e.tensor_scalar(out=yt[s][:, hs], in0=pCH[:, 128:192], scalar1=mv[:, 0:1], scalar2=rsd[:, 0:1], op0=ALU.subtract, op1=ALU.mult),
                  [bCH, bmv, brsd, byt[s]], [byt[s]])
            if dbg <= 5:
                P.dma("sync", y_out[t0:t0 + 128, :], dd[:, 0, :], reads=[bdd, bT4, bM4, bZ, bPk[0], bPk[1], bQk[0], bQk[1]], writes=[bout])
                continue
            G(lambda e, s=s: e.tensor_tensor(out=yt[s][:], in0=yt[s][:], in1=par[:, 6, :], op=ALU.mult), [byt[s], bpar], [byt[s]])
            G(lambda e, s=s: e.tensor_tensor(out=yt[s][:], in0=yt[s][:], in1=par[:, 7, :], op=ALU.add), [byt[s], bpar], [byt[s]])
            for h in range(3):
                hs = slice(64 * h, 64 * h + 64)
                V(lambda e, s=s, hs=hs, h=h, t_=t_: e.scalar_tensor_tensor(out=yt[s][:, hs], in0=t_[:, 2, hs], scalar=rk[s][:, h:h + 1], in1=yt[s][:, hs],
                                                                          op0=ALU.mult, op1=ALU.add), [btt[s], brk[s], byt[s]], [byt[s]])
            V(lambda e, s=s, t_=t_: e.tensor_tensor(out=yt[s][:], in0=yt[s][:], in1=t_[:, 6, :], op=ALU.mult), [byt[s], btt[s]], [byt[s]])
            T(lambda e, s=s: e.transpose(pb[7][:, 0:128], yt[s][:, 0:128], ident), [byt[s], bcst], [bpb[7]])
            T(lambda e, s=s: e.transpose(pb[7][0:64, 128:256], yt[s][:, 128:192], ident), [byt[s], bcst], [bpb[7]])
            A(lambda e, s=s: e.copy(out=ytT[s][:, 0, :], in_=pb[7][:, 0:128]), [bpb[7], bytT[s]], [bytT[s]])
            A(lambda e, s=s: e.copy(out=ytT[s][0:64, 1, :], in_=pb[7][0:64, 128:256]), [bpb[7], bytT[s]], [bytT[s]])
            P.dma("sync", ysrc[0:128, t0:t0 + 128], ytT[s][:, 0, :], reads=[bytT[s]], writes=[bout])
            P.dma("sync", ysrc[128:192, t0:t0 + 128], ytT[s][0:64, 1, :], reads=[bytT[s]], writes=[bout])
        P.final_wait("sync", [bout]); P.final_wait("gpsimd", [bout])
        P.emit()


def phase_p2a_il(nc, io, l, nt=NT, dbg=9):
    has_vres = l > 0
    tg = "a%d_" % l
    tmA = io["tm_all"]; fm = io["fm_all"]
    rkv_in = tmA[:, 0:576]; mu_in = io["mu_rkv%d" % l]
    wd_in = fm[0:96, :]; ad_in = fm[96:192, :]; gd_in = fm[192:448, :]
    lmu_in = io["lmu%d" % l]; w2_in = io["w2c%d" % l]; a2_in = io["a2c%d" % l]; g2_in = io["g2c%d" % l]
    par_in = io["par%d" % l]; cst_in = io["cst"]; m4_in = io["mask4"]
    if has_vres:
        vd_in = fm[448:512, :]; v2_in = io["v2c"]
    vf_in = io["vfirst"]
    v_out = io["vfirst"]
    ysrc = io["y_src"]
    P = Prog(nc, tg)
    with ExitStack() as es:
        sb = lambda n, shp, d=F32: es.enter_context(nc.sbuf_tensor(tg + n, shp, d))
        V = lambda fn, r, w: P.op("vector", fn, reads=r, writes=w)
        A = lambda fn, r, w: P.op("scalar", fn, reads=r, writes=w)
        G = lambda fn, r, w: P.op("gpsimd", fn, reads=r, writes=w)
        T = lambda fn, r, w: P.op("tensor", fn, reads=r, writes=w)
        cst = sb("cst", [128, 4, 128]); m4 = sb("m4", [128, 4, 128]); par = sb("par", [128, 8, W3]); mu = sb("mu", [128, 576])
        lmu = sb("lmu", [128, 5]); w2 = sb("w2", [96, W3]); a2 = sb("a2", [96, W3]); g2 = sb("g2", [128, 2, W3])
        ones = sb("ones", [128, 1]); epsg = sb("epsg", [128, 1])
        bcst, bm4, bpar, bmu, blmu, bw2, ba2, bg2, bones, bepsg = (Buf() for _ in range(10))
        P.dma("sync", cst[:], cst_in, writes=[bcst]); P.dma("sync", m4[:], m4_in, writes=[bm4])
        P.dma("sync", par[:], par_in, writes=[bpar]); P.dma("sync", mu[:], mu_in, writes=[bmu])
        P.dma("sync", lmu[:], lmu_in, writes=[blmu]); P.dma("sync", w2[:], w2_in, writes=[bw2])
        P.dma("sync", a2[:], a2_in, writes=[ba2])
        P.dma("sync", g2[:], g2_in.rearrange("(c p) n -> p c n", p=128), writes=[bg2])
        G(lambda e: e.memset(ones[:], 1.0), [], [bones]); G(lambda e: e.memset(epsg[:], A_GN_EPS), [], [bepsg])
        ident, triu, trirev, maskn = cst[:, 0, :], cst[:, 1, :], cst[:, 2, :], cst[:, 3, :]
        if has_vres:
            v2 = sb("v2", [64, W3]); bv2 = Buf()
            P.dma("sync", v2[:], v2_in, writes=[bv2])
        ltmp = sb("ltmp", [128, S]); bltmp = Buf()
        lo = {}
        specs = [("wd", wd_in, 96, 0, AF.Tanh), ("ad", ad_in, 96, 1, None), ("gd0", gd_in[0:128], 128, 2, AF.Sigmoid),
                 ("gd1", gd_in[128:256], 128, 3, AF.Sigmoid)]
        if has_vres:
            specs.append(("vd", vd_in, 64, 4, None))
        for name, src, rows, mcol, fn in specs:
            t = sb("lo_" + name, [rows, S + 1]); b = Buf()
            G(lambda e, t=t: e.memset(t[:, 0:1], 0.0), [], [b])
            for i in range(4):
                P.dma("sync" if i % 2 == 0 else "gpsimd", t[:, 1 + i * 1024:1 + (i + 1) * 1024], src[:, i * 1024:(i + 1) * 1024], writes=[b])
            V(lambda e, t=t, rows=rows: e.tensor_tensor(out=ltmp[:rows, :], in0=t[:, 0:S], in1=t[:, 1:S + 1], op=ALU.subtract), [b, bltmp], [bltmp])
            V(lambda e, t=t, rows=rows, mcol=mcol: e.scalar_tensor_tensor(out=t[:, 1:S + 1], in0=ltmp[:rows, :], scalar=lmu[:rows, mcol:mcol + 1],
                                                                      in1=t[:, 1:S + 1], op0=ALU.mult, op1=ALU.add), [bltmp, blmu, b], [b])
            if fn is not None:
                A(lambda e, t=t, fn=fn: e.activation(out=t[:, 1:S + 1], in_=t[:, 1:S + 1], func=fn), [b], [b])
            lo[name] = (t, b)
        NS = 2
        cur = [sb("cur%d" % i, [128, 576]) for i in range(NS)]; prv = [sb("prv%d" % i, [128, 576]) for i in range(NS)]
        bcur = [Buf() for _ in range(NS)]; bprv = [Buf() for _ in range(NS)]
        tt = [sb("tt%d" % i, [128, 8, W3]) for i in range(NS)]; btt = [Buf() for _ in range(NS)]
        rk = [sb("rk%d" % i, [128, 3]) for i in range(NS)]; brk = [Buf() for _ in range(NS)]
        tm = sb("tm", [128, 4, W3]); btm = Buf()
        ss = sb("ss", [128, 3]); bss = Buf()
        if has_vres:
            vf = [sb("vf%d" % i, [128, W3]) for i in range(NS)]; bvf = [Buf() for _ in range(NS)]
        ee = [sb("ee%d" % i, [128, 4, W3]) for i in range(NS)]; bee = [Buf() for _ in range(NS)]
        dd = [sb("dd%d" % i, [128, 6, W3]) for i in range(NS)]; bdd = [Buf() for _ in range(NS)]
        wc = [sb("wc%d" % i, [64, 4]) for i in range(NS)]; bwc = [Buf() for _ in range(NS)]
        yt = [sb("yt%d" % i, [128, W3]) for i in range(NS)]; byt = [Buf() for _ in range(NS)]
        ytT = [sb("ytT%d" % i, [128, 2, 128], BF16) for i in range(NS)]; bytT = [Buf() for _ in range(NS)]
        NZ = 2
        T4 = [sb("T4_%d" % z, [64, 4, 128]) for z in range(NZ)]; bT4 = [Buf() for _ in range(NZ)]
        M4 = [sb("M4_%d" % z, [128, 4, 128]) for z in range(NZ)]; bM4 = [Buf() for _ in range(NZ)]
        Pk = [[sb("Pk%d_%d" % (z, i), [128, 128]) for i in range(2)] for z in range(NZ)]; bPk = [[Buf(), Buf()] for _ in range(NZ)]
        Qk = [[sb("Qk%d_%d" % (z, i), [128, 128]) for i in range(2)] for z in range(NZ)]; bQk = [[Buf(), Buf()] for _ in range(NZ)]
        Zt = [sb("Z%d" % z, [128, 128]) for z in range(NZ)]; bZ = [Buf() for _ in range(NZ)]
        nX = [sb("nX%d" % z, [128, 64]) for z in range(NZ)]; bnX = [Buf() for _ in range(NZ)]
        Ut = [sb("U%d" % z, [128, 64]) for z in range(NZ)]; bU = [Buf() for _ in range(NZ)]
        st6 = [sb("st6_%d" % z, [128, 6]) for z in range(NZ)]; bst6 = [Buf() for _ in range(NZ)]
        mv = [sb("mv%d" % z, [128, 2]) for z in range(NZ)]; bmv = [Buf() for _ in range(NZ)]
        rsd = [sb("rsd%d" % z, [128, 1]) for z in range(NZ)]; brsd = [Buf() for _ in range(NZ)]
        ST = [sb("ST%d" % h, [64, 64]) for h in range(3)]; bST = [Buf() for _ in range(3)]
        for h in range(3):
            G(lambda e, h=h: e.memset(ST[h][:], 0.0), [], [bST[h]])
        pb = [es.enter_context(nc.psum_tensor(tg + "pb%d" % i, [128, 512], F32)) for i in range(8)]
        pL1, pL2, pGG = pb[0], pb[0], pb[1]
        bL1 = Buf(); bL2 = bL1; byT1 = bL1; byT2 = bL1; bGG = Buf()
        pX = [pb[2], pb[3]]; bX = [Buf(), Buf()]
        pIVs = [pb[4], pb[5]]; bIVs = [Buf(), Buf()]
        pCHs = [pb[6], pb[7]]; bCHs = [Buf(), Buf()]
        rkv_v = rkv_in
        bout = Buf()
        wdT, bwd = lo["wd"]; adT, bad = lo["ad"]; gd0, bgd0 = lo["gd0"]; gd1, bgd1 = lo["gd1"]

        def prep(n):
            s = n % NS
            t0 = n * 128
            c_, p_, t_ = cur[s], prv[s], tt[s]
            P.dma("sync", c_[:], rkv_v[t0:t0 + 128, :], writes=[bcur[s]])
            if n == 0:
                G(lambda e: e.memset(p_[:], 0.0), [], [bprv[s]])
                P.dma("gpsimd", p_[1:128, :], rkv_v[0:127, :], writes=[bprv[s]])
            else:
                P.dma("gpsimd", p_[:], rkv_v[t0 - 1:t0 + 127, :], writes=[bprv[s]])
            if has_vres:
                P.dma("sync", vf[s][:], vf_in[t0:t0 + 128, :], writes=[bvf[s]])
            V(lambda e: e.tensor_tensor(out=p_[:], in0=p_[:], in1=c_[:], op=ALU.subtract), [bcur[s], bprv[s]], [bprv[s]])
            G(lambda e: e.tensor_tensor(out=p_[:], in0=p_[:], in1=mu[:], op=ALU.mult), [bprv[s], bmu], [bprv[s]])
            V(lambda e: e.tensor_tensor(out=c_[:], in0=c_[:], in1=p_[:], op=ALU.add), [bcur[s], bprv[s]], [bcur[s]])
            r_, k_, v_ = c_[:, 0:W3], c_[:, W3:2 * W3], c_[:, 2 * W3:3 * W3]
            tsl = slice(1 + t0, 1 + t0 + 128)
            T(lambda e: e.matmul(pL1[:, 0:W3], lhsT=wdT[:, tsl], rhs=w2[:], start=True, stop=True), [bwd, bw2], [bL1])
            T(lambda e: e.matmul(pL1[:, W3:2 * W3], lhsT=adT[:, tsl], rhs=a2[:], start=True, stop=True), [bad, ba2], [bL1])
            V(lambda e: e.tensor_tensor(out=t_[:, 5, :], in0=pL1[:, 0:W3], in1=par[:, 0, :], op=ALU.add), [bL1, bpar, btt[s]], [btt[s]])
            A(lambda e: e.activation(out=t_[:, 5, :], in_=t_[:, 5, :], func=AF.Sigmoid), [btt[s]], [btt[s]])
            V(lambda e: e.tensor_scalar(out=t_[:, 5, :], in0=t_[:, 5, :], scalar1=NEG_EHALF, scalar2=None, op0=ALU.mult), [btt[s]], [btt[s]])
            V(lambda e: e.tensor_tensor(out=t_[:, 7, :], in0=pL1[:, W3:2 * W3], in1=par[:, 1, :], op=ALU.add), [bL1, bpar, btt[s]], [btt[s]])
            A(lambda e: e.activation(out=t_[:, 7, :], in_=t_[:, 7, :], func=AF.Sigmoid), [btt[s]], [btt[s]])
            T(lambda e: e.matmul(pL2[:, 0:W3], lhsT=gd0[:, tsl], rhs=g2[:, 0, :], start=True, stop=False), [bgd0, bg2], [bL2])
            T(lambda e: e.matmul(pL2[:, 0:W3], lhsT=gd1[:, tsl], rhs=g2[:, 1, :], start=False, stop=True), [bgd1, bg2], [bL2])
            if has_vres:
                vdT, bvd = lo["vd"]
                T(lambda e: e.matmul(pL2[:, W3:2 * W3], lhsT=vdT[:, tsl], rhs=v2[:], start=True, stop=True), [bvd, bv2], [bL2])
            A(lambda e: e.copy(out=t_[:, 6, :], in_=pL2[:, 0:W3]), [bL2, btt[s]], [btt[s]])
            G(lambda e: e.tensor_copy(out=t_[:, 0, :], in_=r_), [bcur[s], btt[s]], [btt[s]])
            if has_vres:
                V(lambda e: e.tensor_tensor(out=tm[:, 0, :], in0=pL2[:, W3:2 * W3], in1=par[:, 2, :], op=ALU.add), [bL2, bpar, btm], [btm])
                A(lambda e: e.activation(out=tm[:, 0, :], in_=tm[:, 0, :], func=AF.Sigmoid), [btm], [btm])
                V(lambda e: e.tensor_tensor(out=tm[:, 1, :], in0=vf[s][:], in1=v_, op=ALU.subtract), [bvf[s], bcur[s], btm], [btm])
                V(lambda e: e.tensor_tensor(out=tm[:, 1, :], in0=tm[:, 1, :], in1=tm[:, 0, :], op=ALU.mult), [btm], [btm])
                V(lambda e: e.tensor_tensor(out=t_[:, 2, :], in0=tm[:, 1, :], in1=v_, op=ALU.add), [btm, bcur[s], btt[s]], [btt[s]])
            else:
                G(lambda e: e.tensor_copy(out=t_[:, 2, :], in_=v_), [bcur[s], btt[s]], [btt[s]])
                P.dma("sync", v_out[t0:t0 + 128, :], t_[:, 2, :], reads=[btt[s]], writes=[bout])
            V(lambda e: e.tensor_tensor(out=t_[:, 3, :], in0=k_, in1=par[:, 3, :], op=ALU.mult), [bcur[s], bpar, btt[s]], [btt[s]])
            V(lambda e: e.tensor_tensor(out=tm[:, 2, :], in0=t_[:, 3, :], in1=t_[:, 3, :], op=ALU.mult), [btt[s], btm], [btm])
            V(lambda e: e.tensor_reduce(out=ss[:], in_=tm[:, 2, :].rearrange("p (h j) -> p h j", h=3), axis=AX.X, op=ALU.add), [btm, bss], [bss])
            A(lambda e: e.activation(out=ss[:], in_=ss[:], func=AF.Sqrt), [bss], [bss])
            V(lambda e: e.tensor_scalar(out=ss[:], in0=ss[:], scalar1=1e-12, scalar2=None, op0=ALU.max), [bss], [bss])
            V(lambda e: e.reciprocal(out=ss[:], in_=ss[:]), [bss], [bss])
            for h in range(3):
                hs = slice(64 * h, 64 * h + 64)
                V(lambda e, hs=hs, h=h: e.tensor_scalar(out=t_[:, 3, hs], in0=t_[:, 3, hs], scalar1=ss[:, h:h + 1], scalar2=None, op0=ALU.mult),
                  [bss, btt[s]], [btt[s]])
            V(lambda e: e.scalar_tensor_tensor(out=tm[:, 3, :], in0=t_[:, 7, :], scalar=-1.0, in1=par[:, 4, :], op0=ALU.add, op1=ALU.mult),
              [btt[s], bpar, btm], [btm])
            V(lambda e: e.scalar_tensor_tensor(out=t_[:, 1, :], in0=tm[:, 3, :], scalar=1.0, in1=k_, op0=ALU.add, op1=ALU.mult),
              [btm, bcur[s], btt[s]], [btt[s]])
            G(lambda e: e.tensor_tensor(out=t_[:, 4, :], in0=t_[:, 3, :], in1=t_[:, 7, :], op=ALU.mult), [btt[s]], [btt[s]])
            V(lambda e: e.tensor_tensor(out=tm[:, 3, :], in0=t_[:, 0, :], in1=t_[:, 1, :], op=ALU.mult), [btt[s], btm], [btm])
            V(lambda e: e.tensor_tensor(out=tm[:, 3, :], in0=tm[:, 3, :], in1=par[:, 5, :], op=ALU.mult), [btm, bpar], [btm])
            V(lambda e: e.tensor_reduce(out=rk[s][:], in_=tm[:, 3, :].rearrange("p (h j) -> p h j", h=3), axis=AX.X, op=ALU.add), [btm, brk[s]], [brk[s]])
            T(lambda e: e.matmul(pGG[:, 0:W3], lhsT=triu, rhs=t_[:, 5, :], start=True, stop=True), [bcst, btt[s]], [bGG])
            T(lambda e: e.matmul(pGG[:, W3:2 * W3], lhsT=trirev, rhs=t_[:, 5, :], start=True, stop=True), [bcst, btt[s]], [bGG])
            for h in range(3):
                T(lambda e, h=h: e.matmul(pGG[0:64, 400 + h:401 + h], lhsT=t_[:, 5, 64 * h:64 * h + 64], rhs=ones[:], start=True, stop=True),
                  [btt[s], bones], [bGG])
            e_, d_ = ee[s], dd[s]
            A(lambda e: e.activation(out=e_[:, 0, :], in_=pGG[:, 0:W3], func=AF.Exp), [bGG, bee[s]], [bee[s]])
            A(lambda e: e.activation(out=e_[:, 1, :], in_=pGG[:, 0:W3], func=AF.Exp, scale=-1.0), [bGG, bee[s]], [bee[s]])
            V(lambda e: e.tensor_tensor(out=e_[:, 2, :], in0=pGG[:, 0:W3], in1=t_[:, 5, :], op=ALU.subtract), [bGG, btt[s], bee[s]], [bee[s]])
            A(lambda e: e.activation(out=e_[:, 2, :], in_=e_[:, 2, :], func=AF.Exp), [bee[s]], [bee[s]])
            A(lambda e: e.activation(out=e_[:, 3, :], in_=pGG[:, W3:2 * W3], func=AF.Exp), [bGG, bee[s]], [bee[s]])
            A(lambda e: e.activation(out=wc[s][:, 0:3], in_=pGG[0:64, 400:403], func=AF.Exp), [bGG, bwc[s]], [bwc[s]])
            for (di, ti, ei, eng) in ((0, 0, 0, V), (1, 3, 2, G), (2, 1, 1, V), (3, 4, 1, G), (4, 1, 3, V), (5, 4, 3, G)):
                eng(lambda e, di=di, ti=ti, ei=ei: e.tensor_tensor(out=d_[:, di, :], in0=t_[:, ti, :], in1=e_[:, ei, :], op=ALU.mult),
                    [btt[s], bee[s], bdd[s]], [bdd[s]])

        def head_gen(n, h, z):
            s = n % NS
            t_, d_ = tt[s], dd[s]
            hs = slice(64 * h, 64 * h + 64)
            X, IV = pX[z], pIVs[z]
            CH = pCHs[z][:, 0:256]
            CH64 = pCHs[z][0:64, 0:256]
            bXz, bIV, bCH = bX[z], bIVs[z], bCHs[z]
            t4, m4_, Z, nx, U = T4[z], M4[z], Zt[z], nX[z], Ut[z]
            for j in range(4):
                T(lambda e, j=j: e.transpose(X[0:64, j * 128:(j + 1) * 128], d_[:, j, hs], ident), [bdd[s], bcst], [bXz])
                yield
            A(lambda e: e.copy(out=t4[:], in_=X[0:64, :].rearrange("p (a b) -> p a b", a=4)), [bXz, bT4[z]], [bT4[z]])
            yield
            T(lambda e: e.matmul(X[:, 0:256], lhsT=t4[:, 2, :], rhs=t4[:, 0:2, :].rearrange("p a b -> p (a b)"), start=True, stop=True), [bT4[z]], [bXz])
            T(lambda e: e.matmul(X[:, 256:512], lhsT=t4[:, 3, :], rhs=t4[:, 0:2, :].rearrange("p a b -> p (a b)"), start=True, stop=True), [bT4[z]], [bXz])
            T(lambda e: e.matmul(IV[:, 0:128], lhsT=t4[:, 1, :], rhs=t4[:, 3, :], start=True, stop=True), [bT4[z]], [bIV])
            yield
            V(lambda e: e.tensor_tensor(out=m4_[:].rearrange("p a b -> p (a b)"), in0=X[:], in1=m4[:].rearrange("p a b -> p (a b)"), op=ALU.mult),
              [bXz, bm4, bM4[z]], [bM4[z]])
            V(lambda e: e.tensor_tensor(out=Pk[z][0][:], in0=IV[:, 0:128], in1=maskn, op=ALU.mult), [bIV, bcst, bPk[z][0]], [bPk[z][0]])
            yield
            A(lambda e: e.copy(out=Qk[z][0][:], in_=m4_[:, 3, :]), [bM4[z], bQk[z][0]], [bQk[z][0]])
            V(lambda e: e.tensor_tensor(out=Z[:], in0=ident, in1=m4_[:, 3, :], op=ALU.subtract), [bM4[z], bcst, bZ[z]], [bZ[z]])
            yield
            for lv in range(1, 7):
                pi, ci = (lv - 1) % 2, lv % 2
                if lv < 6:
                    T(lambda e, pi=pi: e.matmul(IV[:, 128:256], lhsT=Pk[z][pi][:], rhs=Qk[z][pi][:], start=True, stop=True), [bPk[z][pi], bQk[z][pi]], [bIV])
                T(lambda e, pi=pi: e.matmul(IV[:, 256:384], lhsT=Qk[z][pi][:], rhs=Pk[z][pi][:], start=True, stop=True), [bPk[z][pi], bQk[z][pi]], [bIV])
                yield
                if lv < 6:
                    A(lambda e, ci=ci: e.copy(out=Qk[z][ci][:], in_=IV[:, 128:256]), [bIV, bQk[z][ci]], [bQk[z][ci]])
                A(lambda e, ci=ci: e.copy(out=Pk[z][ci][:], in_=IV[:, 256:384]), [bIV, bPk[z][ci]], [bPk[z][ci]])
                yield
                T(lambda e, ci=ci: e.matmul(IV[:, 384:512], lhsT=Pk[z][ci][:], rhs=Z[:], start=True, stop=True), [bPk[z][ci], bZ[z]], [bIV])
                yield
                V(lambda e: e.tensor_tensor(out=Z[:], in0=Z[:], in1=IV[:, 384:512], op=ALU.add), [bIV, bZ[z]], [bZ[z]])
                yield
            vh = t_[:, 2, hs]
            T(lambda e: e.matmul(CH[:, 0:64], lhsT=m4_[:, 1, :], rhs=vh, start=True, stop=False), [bM4[z], btt[s]], [bCH])
            T(lambda e: e.matmul(CH[:, 0:64], lhsT=t4[:, 1, :], rhs=ST[h][:], start=False, stop=True), [bT4[z], bST[h]], [bCH])
            yield
            A(lambda e: e.activation(out=nx[:], in_=CH[:, 0:64], func=AF.Copy, scale=-1.0), [bCH, bnX[z]], [bnX[z]])
            yield
            T(lambda e: e.matmul(CH[:, 64:128], lhsT=Z[:], rhs=nx[:], start=True, stop=True), [bZ[z], bnX[z]], [bCH])
            yield
            V(lambda e: e.tensor_scalar(out=U[:], in0=CH[:, 64:128], scalar1=1.0, scalar2=None, op0=ALU.mult), [bCH, bU[z]], [bU[z]])
            yield
            T(lambda e: e.matmul(CH[:, 128:192], lhsT=t4[:, 0, :], rhs=ST[h][:], start=True, stop=False), [bT4[z], bST[h]], [bCH])
            T(lambda e: e.matmul(CH[:, 128:192], lhsT=m4_[:, 2, :], rhs=U[:], start=False, stop=False), [bM4[z], bU[z]], [bCH])
            T(lambda e: e.matmul(CH[:, 128:192], lhsT=m4_[:, 0, :], rhs=vh, start=False, stop=True), [bM4[z], btt[s]], [bCH])
            T(lambda e: e.matmul(CH64[:, 192:256], lhsT=d_[:, 5, hs], rhs=U[:], start=True, stop=False), [bdd[s], bU[z]], [bCH])
            T(lambda e: e.matmul(CH64[:, 192:256], lhsT=d_[:, 4, hs], rhs=vh, start=False, stop=True), [bdd[s], btt[s]], [bCH])
            yield
            V(lambda e: e.scalar_tensor_tensor(out=ST[h][:], in0=ST[h][:], scalar=wc[s][:, h:h + 1], in1=CH64[:, 192:256], op0=ALU.mult, op1=ALU.add),
              [bCH, bwc[s], bST[h]], [bST[h]])
            V(lambda e: e.bn_stats(out=st6[z][:], in_=CH[:, 128:192]), [bCH, bst6[z]], [bst6[z]])
            yield
            V(lambda e: e.bn_aggr(out=mv[z][:], in_=st6[z][:]), [bst6[z], bmv[z]], [bmv[z]])
            yield
            A(lambda e: e.activation(out=rsd[z][:], in_=mv[z][:, 1:2], func=AF.Sqrt, bias=epsg[:, 0:1]), [bmv[z], bepsg, brsd[z]], [brsd[z]])
            yield
            V(lambda e: e.reciprocal(out=rsd[z][:], in_=rsd[z][:]), [brsd[z]], [brsd[z]])
            yield
            V(lambda e: e.tensor_scalar(out=yt[s][:, hs], in0=CH[:, 128:192], scalar1=mv[z][:, 0:1], scalar2=rsd[z][:, 0:1], op0=ALU.subtract, op1=ALU.mult),
              [bCH, bmv[z], brsd[z], byt[s]], [byt[s]])
            yield

        def post(n):
            s = n % NS
            t0 = n * 128
            t_ = tt[s]
            G(lambda e: e.tensor_tensor(out=yt[s][:], in0=yt[s][:], in1=par[:, 6, :], op=ALU.mult), [byt[s], bpar], [byt[s]])
            G(lambda e: e.tensor_tensor(out=yt[s][:], in0=yt[s][:], in1=par[:, 7, :], op=ALU.add), [byt[s], bpar], [byt[s]])
            for h in range(3):
                hs = slice(64 * h, 64 * h + 64)
                V(lambda e, hs=hs, h=h: e.scalar_tensor_tensor(out=yt[s][:, hs], in0=t_[:, 2, hs], scalar=rk[s][:, h:h + 1], in1=yt[s][:, hs],
                                                                op0=ALU.mult, op1=ALU.add), [btt[s], brk[s], byt[s]], [byt[s]])
            V(lambda e: e.tensor_tensor(out=yt[s][:], in0=yt[s][:], in1=t_[:, 6, :], op=ALU.mult), [byt[s], btt[s]], [byt[s]])
            T(lambda e: e.transpose(pL1[:, 384:512], yt[s][:, 0:128], ident), [byt[s], bcst], [byT1])
            A(lambda e: e.copy(out=ytT[s][:, 0, :], in_=pL1[:, 384:512]), [byT1, bytT[s]], [bytT[s]])
            T(lambda e: e.transpose(pL1[0:64, 384:512], yt[s][:, 128:192], ident), [byt[s], bcst], [byT1])
            A(lambda e: e.copy(out=ytT[s][0:64, 1, :], in_=pL1[0:64, 384:512]), [byT1, bytT[s]], [bytT[s]])
            P.dma("sync", ysrc[0:128, t0:t0 + 128], ytT[s][:, 0, :], reads=[bytT[s]], writes=[bout])
            P.dma("sync", ysrc[128:192, t0:t0 + 128], ytT[s][0:64, 1, :], reads=[bytT[s]], writes=[bout])

        jobs = [(n, h) for n in range(nt) for h in range(3)]
        prepped = [-1]
        done = {}
        for p0 in range(0, len(jobs), NZ):
            pair = jobs[p0:p0 + NZ]
            for (n, h) in pair:
                while prepped[0] < n:
                    prepped[0] += 1
                    prep(prepped[0])
            gens = [head_gen(n, h, z) for z, (n, h) in enumerate(pair)]
            while gens:
                for g_ in list(gens):
                    try:
                        next(g_)
                    except StopIteration:
                        gens.remove(g_)
            for (n, h) in pair:
                done[n] = done.get(n, 0) + 1
                if done[n] == 3:
                    post(n)
        P.final_wait("sync", [bout]); P.final_wait("gpsimd", [bout])
        P.emit()


def p2a_consts():
    i = np.arange(128)
    row, col = i[:, None], i[None, :]
    cst = np.stack([np.eye(128), (row <= col), (row > col), (row > col)], axis=1).astype(np.float32)
    incl = (row <= col).astype(np.float32); strict = (row < col).astype(np.float32)
    mask4 = np.stack([incl, strict, incl, strict], axis=1).astype(np.float32)
    return dict(cst=np.ascontiguousarray(cst), mask4=np.ascontiguousarray(mask4))


NTM = 1344
NFM = 768
NY = 576
GROUPS = [[0, 1, 2, 3], [4, 5, 6, 7]]


def phase_norm(nc, io, l, x_ap):
    tg = "n%d_" % l
    P = Prog(nc, tg)
    with ExitStack() as es:
        xs = es.enter_context(nc.sbuf_tensor(tg + "xs", [128, KC, TOK], F32))
        bxs, bhs = Buf(), Buf()
        xv = x_ap.rearrange("(c p) t -> p c t", p=128)
        for i in range(4):
            P.dma("sync", xs[:, i * 4:(i + 1) * 4, :], xv[:, i * 4:(i + 1) * 4, :], writes=[bxs])
        hT, bh = rms_to_bf16(P, nc, es, xs, bxs, io["g1_%d" % l], tg)
        hv = io["h_src"].rearrange("(c p) t -> p c t", p=128)
        for i in range(4):
            P.dma("sync", hv[:, i * 4:(i + 1) * 4, :], hT[:, i * 4:(i + 1) * 4, :], reads=[bh], writes=[bhs])
        P.final_wait("sync", [bhs]); P.final_wait("gpsimd", [bhs])
        P.emit()


def phase_inproj(nc, io, l):
    tg = "p%d_" % l
    has_vres = l > 0
    P = Prog(nc, tg)
    h_src, h_all, tm_all, fm_all = io["h_src_t"], io["h_all_t"], io["tm_all"], io["fm_all"]
    wtm_ap, wfm_ap = io["wtm%d" % l], io["wfm%d" % l]
    with ExitStack() as es:
        sb = lambda n, shp, d=F32: es.enter_context(nc.sbuf_tensor(tg + n, shp, d))
        bhall = Buf()
        for kk in range(4):
            P.cc(lambda e, kk=kk: e.collective_compute("AllGather", mybir.AluOpType.bypass, replica_groups=GROUPS,
                                                       ins=[h_src.ap()[kk * 512:(kk + 1) * 512, :]], outs=[h_all.ap()[kk * 2048:(kk + 1) * 2048, :]]), writes=[bhall])
        wtm = sb("wtm", [128, KC, NTM], BF16); wfm = sb("wfm", [128, KC, NFM], BF16)
        bwtm, bwfm = Buf(), Buf()
        stg = [sb("stg%d" % i, [128, NTM], F32) for i in range(2)]
        bstg = [Buf(), Buf()]
        k = 0
        for (w_ap, wt, bw, n) in ((wtm_ap, wtm, bwtm, NTM), (wfm_ap, wfm, bwfm, NFM)):
            for c in range(KC):
                s = k % 2
                k += 1
                P.dma("sync" if s == 0 else "gpsimd", stg[s][:, :n], w_ap[c * 128:(c + 1) * 128, :], writes=[bstg[s]])
                if s == 0:
                    P.op("gpsimd", lambda e, s=s, wt=wt, c=c, n=n: e.tensor_copy(out=wt[:, c, :], in_=stg[s][:, :n]), reads=[bstg[s]], writes=[bw])
                else:
                    P.op("scalar", lambda e, s=s, wt=wt, c=c, n=n: e.copy(out=wt[:, c, :], in_=stg[s][:, :n]), reads=[bstg[s]], writes=[bw])
        hT = [sb("hT%d" % i, [128, KC, TOK], BF16) for i in range(2)]
        bhT = [Buf(), Buf()]
        otm = [sb("otm%d" % i, [128, NTM], F32) for i in range(2)]
        botm = [Buf(), Buf()]
        ofm = [sb("ofm%d" % i, [128, 512], F32) for i in range(4)]
        bofm = [Buf() for _ in range(4)]
        pt = [es.enter_context(nc.psum_tensor(tg + "pt%d" % i, [128, 512], F32)) for i in range(6)]
        bpt = [Buf() for _ in range(6)]
        hav = h_all.ap()
        btm, bfm = Buf(), Buf()
        fm_blocks = [(0, 96), (96, 192), (192, 320), (320, 448)] + ([(448, 512)] if has_vres else []) + [(512, 640), (640, 768)]
        tm_blocks = [(0, 512), (512, 1024), (1024, NTM)]
        pk = 0
        fk = 0
        for r in range(4):
            hs = r % 2
            for i in range(4):
                hv = hav[i * 2048 + r * 512:i * 2048 + (r + 1) * 512, :].rearrange("(c p) t -> p c t", p=128)
                P.dma("sync" if i % 2 == 0 else "gpsimd", hT[hs][:, i * 4:(i + 1) * 4, :], hv, reads=[bhall], writes=[bhT[hs]])
            for tl in range(8):
                os_ = (r * 8 + tl) % 2
                tsl = slice(tl * 128, (tl + 1) * 128)
                for bi, (c0, c1) in enumerate(tm_blocks):
                    q = pk % 6
                    pk += 1
                    for c in range(KC):
                        P.op("tensor", lambda e, q=q, c=c, hs=hs, tsl=tsl, c0=c0, c1=c1: e.matmul(pt[q][:, 0:c1 - c0], lhsT=hT[hs][:, c, tsl], rhs=wtm[:, c, c0:c1],
                                                                                                  start=(c == 0), stop=(c == KC - 1)),
                             reads=[bhT[hs], bwtm], writes=[bpt[q]])
                    if bi % 2 == 0:
                        P.op("scalar", lambda e, q=q, os_=os_, c0=c0, c1=c1: e.copy(out=otm[os_][:, c0:c1], in_=pt[q][:, 0:c1 - c0]), reads=[bpt[q]], writes=[botm[os_]])
                    else:
                        P.op("vector", lambda e, q=q, os_=os_, c0=c0, c1=c1: e.tensor_scalar(out=otm[os_][:, c0:c1], in0=pt[q][:, 0:c1 - c0], scalar1=1.0, scalar2=None, op0=ALU.mult),
                             reads=[bpt[q]], writes=[botm[os_]])
                t0 = r * 1024 + tl * 128
                P.dma("sync", tm_all[t0:t0 + 128, :], otm[os_][:], reads=[botm[os_]], writes=[btm])
            for (b0, b1) in fm_blocks:
                mw = b1 - b0
                for hf in range(2):
                    q = pk % 6
                    pk += 1
                    fs = fk % 4
                    fk += 1
                    sl = slice(hf * 512, (hf + 1) * 512)
                    for c in range(KC):
                        P.op("tensor", lambda e, q=q, c=c, hs=hs, sl=sl, b0=b0, b1=b1, mw=mw: e.matmul(pt[q][:mw, :], lhsT=wfm[:, c, b0:b1], rhs=hT[hs][:, c, sl],
                                                                                                       start=(c == 0), stop=(c == KC - 1)),
                             reads=[bhT[hs], bwfm], writes=[bpt[q]])
                    if fk % 2 == 0:
                        P.op("scalar", lambda e, q=q, fs=fs, mw=mw: e.copy(out=ofm[fs][:mw, :], in_=pt[q][:mw, :]), reads=[bpt[q]], writes=[bofm[fs]])
                    else:
                        P.op("vector", lambda e, q=q, fs=fs, mw=mw: e.tensor_scalar(out=ofm[fs][:mw, :], in0=pt[q][:mw, :], scalar1=1.0, scalar2=None, op0=ALU.mult),
                             reads=[bpt[q]], writes=[bofm[fs]])
                    P.dma("gpsimd", fm_all[b0:b1, r * 1024 + hf * 512:r * 1024 + (hf + 1) * 512], ofm[fs][:mw, :], reads=[bofm[fs]], writes=[bfm])
        P.final_wait("sync", [btm, bfm]); P.final_wait("gpsimd", [btm, bfm])
        P.emit()


DFF = 5632
NFB = DFF // 128
GRP = 4
KY = 18


def phase_p3(nc, io, l, x_ap, out_ap, final):
    tg = "f%d_" % l
    y_src, y_all = io["y_src_t"], io["y_all_t"]
    wout, g2, wg, wu, wd = io["wout%d" % l], io["g2_%d" % l], io["wg%d" % l], io["wu%d" % l], io["wd%d" % l]
    P = Prog(nc, tg)
    with ExitStack() as es:
        sb = lambda n, shp, dt: es.enter_context(nc.sbuf_tensor(tg + n, shp, dt))
        pst = lambda n: es.enter_context(nc.psum_tensor(tg + n, [128, 512], F32))
        byall = Buf()
        for kk in range(6):
            P.cc(lambda e, kk=kk: e.collective_compute("AllGather", mybir.AluOpType.bypass, replica_groups=GROUPS,
                                                       ins=[y_src.ap()[kk * 96:(kk + 1) * 96, :]], outs=[y_all.ap()[kk * 384:(kk + 1) * 384, :]]), writes=[byall])
        xs = sb("xs", [128, KC, TOK], F32)
        hT = sb("hT", [128, KY, TOK], BF16)
        bxs, bh = Buf(), Buf()
        NST = 4
        stg = [sb("stg%d" % i, [128, 2048], F32) for i in range(NST)]
        wbf = [sb("wbf%d" % i, [128, 2048], BF16) for i in range(NST)]
        bstg = [Buf() for _ in range(NST)]
        bwbf = [Buf() for _ in range(NST)]
        act = [sb("act%d" % i, [128, GRP, TOK], BF16) for i in range(2)]
        bact = [Buf(), Buf()]
        sg = [sb("sg%d" % i, [128, 512], F32) for i in range(2)]
        bsg = [Buf(), Buf()]
        sel = sb("sel", [128, 4], F32); bsel = Buf()
        pg = [pst("pg%d" % i) for i in range(2)]
        pu = [pst("pu%d" % i) for i in range(2)]
        pd = [pst("pd%d" % i) for i in range(2)]
        bpg, bpu, bpd = [Buf(), Buf()], [Buf(), Buf()], [Buf(), Buf()]
        xv = x_ap.rearrange("(c p) t -> p c t", p=128)
        for i in range(4):
            P.dma("sync", xs[:, i * 4:(i + 1) * 4, :], xv[:, i * 4:(i + 1) * 4, :], writes=[bxs])
        P.dma("sync", sel[:], io["sel"], writes=[bsel])
        st = [0]

        def load_cast(src_ap, view3=None):
            s = st[0] % NST
            st[0] += 1
            dst = stg[s][:] if view3 is None else stg[s][:].rearrange(view3[0], **view3[1])
            P.dma("sync", dst, src_ap, writes=[bstg[s]])
            P.op("gpsimd", lambda e, s=s: e.tensor_copy(out=wbf[s][:], in_=stg[s][:]), reads=[bstg[s]], writes=[bwbf[s]])
            return wbf[s], bwbf[s]

        yst = [stg[i][:].bitcast(BF16) for i in range(2)]
        yav = y_all.ap()
        for c in range(KY):
            s = c % 2
            P.dma("sync" if s == 0 else "gpsimd", yst[s], yav[c * 128:(c + 1) * 128, :], reads=[byall], writes=[bstg[s]])
            P.op("vector", lambda e, c=c, s=s: e.tensor_scalar(out=hT[:, c, :], in0=yst[s][:, 0:TOK], scalar1=sel[:, 0:1], scalar2=None, op0=ALU.mult),
                 reads=[bstg[s], bsel], writes=[bh])
            for gi in range(1, 4):
                P.op("vector", lambda e, c=c, s=s, gi=gi: e.scalar_tensor_tensor(out=hT[:, c, :], in0=yst[s][:, gi * TOK:(gi + 1) * TOK], scalar=sel[:, gi:gi + 1],
                                                                                 in1=hT[:, c, :], op0=ALU.mult, op1=ALU.add),
                     reads=[bstg[s], bsel, bh], writes=[bh])
        st[0] = 2
        wov = wout.rearrange("(c p) n -> p c n", p=128)
        k = 0
        for m in range(KC):
            t, b = load_cast(wov[:, 0:KC, m * 128:(m + 1) * 128], ("p (c n) -> p c n", dict(c=KC)))
            tv = t[:].rearrange("p (c n) -> p c n", c=KC)
            s2 = st[0] % NST
            st[0] += 1
            P.dma("sync", stg[s2][:, 0:256].rearrange("p (c n) -> p c n", c=2), wov[:, KC:KY, m * 128:(m + 1) * 128], writes=[bstg[s2]])
            P.op("gpsimd", lambda e, s2=s2: e.tensor_copy(out=wbf[s2][:, 0:256], in_=stg[s2][:, 0:256]), reads=[bstg[s2]], writes=[bwbf[s2]])
            t2v = wbf[s2][:, 0:256].rearrange("p (c n) -> p c n", c=2)
            b2 = bwbf[s2]
            for hf in range(2):
                q = k % 2
                k += 1
                sl = slice(hf * 512, (hf + 1) * 512)
                for c in range(KY):
                    lt = tv[:, c, :] if c < KC else t2v[:, c - KC, :]
                    P.op("tensor", lambda e, lt=lt, c=c, q=q, sl=sl: e.matmul(pd[q][:], lhsT=lt, rhs=hT[:, c, sl], start=(c == 0), stop=(c == KY - 1)),
                         reads=[b, b2, bh], writes=[bpd[q]])
                P.op("vector", lambda e, m=m, q=q, sl=sl: e.tensor_tensor(out=xs[:, m, sl], in0=xs[:, m, sl], in1=pd[q][:], op=ALU.add),
                     reads=[bpd[q], bxs], writes=[bxs])
        scr = {}
        rms_to_bf16(P, nc, es, xs, bxs, g2, tg + "n2", out_tile=hT, out_buf=bh, scr=scr)
        wgv = wg.rearrange("(c p) n -> p c n", p=128)
        wuv = wu.rearrange("(c p) n -> p c n", p=128)
        for grp in range(NFB // GRP):
            a = act[grp % 2]
            ba = bact[grp % 2]
            wds = []
            for j in range(GRP):
                fb = grp * GRP + j
                tg_, bg_ = load_cast(wgv[:, :, fb * 128:(fb + 1) * 128], ("p (c n) -> p c n", dict(c=KC)))
                tu, bu_ = load_cast(wuv[:, :, fb * 128:(fb + 1) * 128], ("p (c n) -> p c n", dict(c=KC)))
                tgv = tg_[:].rearrange("p (c n) -> p c n", c=KC)
                tuv = tu[:].rearrange("p (c n) -> p c n", c=KC)
                for hf in range(2):
                    sl = slice(hf * 512, (hf + 1) * 512)
                    for c in range(KC):
                        P.op("tensor", lambda e, tgv=tgv, c=c, hf=hf, sl=sl: e.matmul(pg[hf][:], lhsT=tgv[:, c, :], rhs=hT[:, c, sl],
                                                                                   start=(c == 0), stop=(c == KC - 1)),
                             reads=[bg_, bh], writes=[bpg[hf]])
                    for c in range(KC):
                        P.op("tensor", lambda e, tuv=tuv, c=c, hf=hf, sl=sl: e.matmul(pu[hf][:], lhsT=tuv[:, c, :], rhs=hT[:, c, sl],
                                                                                   start=(c == 0), stop=(c == KC - 1)),
                             reads=[bu_, bh], writes=[bpu[hf]])
                    P.op("scalar", lambda e, hf=hf: e.activation(out=sg[hf][:], in_=pg[hf][:], func=AF.Silu),
                         reads=[bpg[hf]], writes=[bsg[hf]])
                    P.op("vector", lambda e, hf=hf, a=a, j=j, sl=sl: e.tensor_tensor(out=a[:, j, sl], in0=sg[hf][:], in1=pu[hf][:], op=ALU.mult),
                         reads=[bsg[hf], bpu[hf]], writes=[ba])
            for j in range(GRP):
                fb = grp * GRP + j
                wds.append(load_cast(wd[fb * 128:(fb + 1) * 128, :]))
            for m in range(KC):
                for hf in range(2):
                    q = k % 2
                    k += 1
                    sl = slice(hf * 512, (hf + 1) * 512)
                    for j in range(GRP):
                        td, bd_ = wds[j]
                        P.op("tensor", lambda e, td=td, j=j, m=m, q=q, sl=sl, a=a: e.matmul(pd[q][:], lhsT=td[:, m * 128:(m + 1) * 128], rhs=a[:, j, sl],
                                                                                         start=(j == 0), stop=(j == GRP - 1)),
                             reads=[bd_, ba], writes=[bpd[q]])
                    P.op("vector", lambda e, m=m, q=q, sl=sl: e.tensor_tensor(out=xs[:, m, sl], in0=xs[:, m, sl], in1=pd[q][:], op=ALU.add),
                         reads=[bpd[q], bxs], writes=[bxs])
        bout = Buf()
        ov = out_ap.rearrange("(c p) t -> p c t", p=128)
        if final:
            rms_to_bf16(P, nc, es, xs, bxs, io["gf"], tg + "n3", out_tile=xs, out_buf=bxs, scr=scr)
        for i in range(4):
            P.dma("sync", ov[:, i * 4:(i + 1) * 4, :], xs[:, i * 4:(i + 1) * 4, :], reads=[bxs], writes=[bout])
        P.final_wait("sync", [bout]); P.final_wait("gpsimd", [bout])
        P.emit()


def build_fused(upto=99, dbg=None, only=None):
    nc = bass.Bass("TRN2", target_bir_lowering=False)
    _POOL[0] = SemPool(nc)
    io = {}
    ext = lambda n, shp, d=F32: io.__setitem__(n, nc.dram_tensor(n, shp, d, kind="ExternalInput").ap())
    ext("xT", [D, TOK]); ext("sel", [128, 4]); ext("gf", [128, KC])
    ext("cst", [128, 4, 128]); ext("mask4", [128, 4, 128])
    ext("pos", [128, NT], I32); ext("invf", [128, 32]); ext("ident", [128, 128]); ext("maskT", [2, 128, 128]); ext("qdec", [2, 64, 128])
    ext("kdec", [128, 2]); ext("g128", [64, 2]); ext("v2c", [64, W3])
    for l in range(2):
        ext("g1_%d" % l, [128, KC]); ext("wtm%d" % l, [D, NTM]); ext("wfm%d" % l, [D, NFM])
        ext("mu_rkv%d" % l, [128, 576]); ext("lmu%d" % l, [128, 5]); ext("w2c%d" % l, [96, W3]); ext("a2c%d" % l, [96, W3])
        ext("g2c%d" % l, [256, W3]); ext("par%d" % l, [128, 8, W3])
        ext("sm%d" % l, [128, 8]); ext("wa%d" % l, [2, 64, 64]); ext("wx%d" % l, [2, 64, 64])
        ext("gng%d" % l, [2, 128, 128])
        if upto > 6 * l + 5:
            ext("wout%d" % l, [KY * 128, D]); ext("g2_%d" % l, [128, KC]); ext("wg%d" % l, [D, DFF]); ext("wu%d" % l, [D, DFF]); ext("wd%d" % l, [DFF, D])
    out = nc.dram_tensor("outT", [D, TOK], F32, kind="ExternalOutput").ap()
    io["h_src_t"] = nc.dram_tensor("h_src", [D, TOK], BF16); io["h_src"] = io["h_src_t"].ap()
    io["h_all_t"] = nc.dram_tensor("h_all", [4 * D, TOK], BF16)
    io["tm_all"] = nc.dram_tensor("tm_all", [S, NTM], F32).ap()
    io["fm_all"] = nc.dram_tensor("fm_all", [NFM, S], F32).ap()
    io["y_src_t"] = nc.dram_tensor("y_src", [NY, S], BF16); io["y_src"] = io["y_src_t"].ap()
    io["y_all_t"] = nc.dram_tensor("y_all", [4 * NY, S], BF16)
    io["vfirst"] = nc.dram_tensor("vfirst", [S, W3], F32).ap()
    x_cur = nc.dram_tensor("x_cur", [D, TOK], F32).ap()
    io["x_cur"] = x_cur
    ph = 0
    for l in range(2):
        steps = [lambda l=l: phase_norm(nc, io, l, io["xT"] if l == 0 else x_cur),
                 lambda l=l: phase_inproj(nc, io, l),
                 lambda l=l: (phase_p2a_il(nc, io, l) if l == 0 else phase_p2a(nc, io, l)),
                 lambda l=l: phase_p2b(nc, io, l),
                 lambda l=l: phase_p2c(nc, io, l),
                 lambda l=l: phase_p3(nc, io, l, io["xT"] if l == 0 else x_cur, out if l == 1 else x_cur, final=(l == 1))]
        for st_ in steps:
            if ph < upto and (only is None or ph in only):
                st_()
            ph += 1
    if dbg is not None:
        src_ap = io[dbg]
        dout = nc.dram_tensor("dbg", list(src_ap.shape), src_ap.dtype, kind="ExternalOutput").ap()
        P = Prog(nc, "dbg_")
        b = Buf()
        rows = src_ap.shape[0]
        stp = max(1, rows // 8)
        for r0 in range(0, rows, stp):
            P.dma("sync", dout[r0:r0 + stp], src_ap[r0:r0 + stp], writes=[b])
        P.final_wait("sync", [b])
        P.emit()
    _POOL[0].close()
    _POOL[0] = None
    return nc


def core_inputs(d, c):
    b, q = c // 4, c % 4
    f32 = lambda a: np.ascontiguousarray(np.asarray(a, dtype=np.float32))
    lay = lambda g: f32(g.reshape(16, 128).T)
    rep = lambda v: f32(np.broadcast_to(np.asarray(v, np.float32), (128, v.shape[-1])))
    xf = d["x"].reshape(-1, D)
    im = {"xT": f32(xf[c * 1024:(c + 1) * 1024].T), "gf": lay(d["final_norm_g"])}
    sel = np.zeros((128, 4), np.float32); sel[:, q] = 1.0
    im["sel"] = sel
    im.update(p2a_consts())
    heads = C_SLOTS[q]
    im.update(p2c_consts(heads))
    im["pos"] = np.ascontiguousarray(d["positions"][b].reshape(32, 128).T.astype(np.int32))
    cA = slice(192 * q, 192 * q + 192)
    cols_rkv = np.r_[192 * q:192 * q + 192, 768 + 192 * q:768 + 192 * q + 192, 1536 + 192 * q:1536 + 192 * q + 192]
    c0 = 2752 + 1024
    cols_c = np.concatenate([np.r_[c0 + 64 * h:c0 + 64 * h + 64] for h in heads] + [np.r_[c0 + 384 + 64 * h:c0 + 384 + 64 * h + 64] for h in heads]
                            + [np.r_[c0 + 768 + 128 * h:c0 + 768 + 128 * h + 128] for h in heads]
                            + [np.r_[c0 + 1536 + 128 * h:c0 + 1536 + 128 * h + 128] for h in heads])
    im["v2c"] = f32(d["rwkv_v2"][0][:, cA])
    perm = np.full(KY * 128, -1, np.int64)
    for r in range(4):
        loc = np.full(NY, -1, np.int64)
        loc[0:192] = np.r_[192 * r:192 * r + 192]
        loc[192:320] = 768 + np.r_[128 * r:128 * r + 128]
        if r < 3:
            for s_, h in enumerate(C_SLOTS[r]):
                loc[320 + 128 * s_:448 + 128 * s_] = 1280 + np.r_[128 * h:128 * h + 128]
        for kk in range(6):
            perm[kk * 384 + r * 96:kk * 384 + (r + 1) * 96] = loc[96 * kk:96 * (kk + 1)]
    for l in range(2):
        w = d["w_in"][l]
        mu = d["tshift_mu"][l]
        im["g1_%d" % l] = lay(d["norm1_g"][l])
        im["wtm%d" % l] = f32(w[:, np.concatenate([cols_rkv, cols_c])])
        wfm = np.zeros((D, NFM), np.float32)
        wfm[:, 0:448] = w[:, 2304:2752]
        if l > 0:
            wfm[:, 448:512] = d["w_in_vres"][l - 1]
        A0 = 2752
        wfm[:, 512:640] = w[:, A0 + 128 * q:A0 + 128 * q + 128]
        wfm[:, 640:768] = w[:, A0 + 512 + 128 * q:A0 + 512 + 128 * q + 128]
        im["wfm%d" % l] = wfm
        im["mu_rkv%d" % l] = rep(mu[cols_rkv])
        lmu = np.zeros((128, 5), np.float32)
        lmu[:96, 0] = mu[2304:2400]; lmu[:96, 1] = mu[2400:2496]; lmu[:, 2] = mu[2496:2624]; lmu[:, 3] = mu[2624:2752]
        if l > 0:
            lmu[:64, 4] = d["tshift_mu_vres"][l - 1]
        im["lmu%d" % l] = lmu
        im["w2c%d" % l] = f32(d["rwkv_w2"][l][:, cA]); im["a2c%d" % l] = f32(d["rwkv_a2"][l][:, cA]); im["g2c%d" % l] = f32(d["rwkv_g2"][l][:, cA])
        v0 = d["rwkv_v0"][l - 1][cA] if l > 0 else np.zeros(192, np.float32)
        par = np.stack([d["rwkv_w0"][l][cA], d["rwkv_a0"][l][cA], v0, d["rwkv_k_k"][l][cA], d["rwkv_k_a"][l][cA],
                        d["rwkv_r_k"][l].reshape(-1)[cA], d["rwkv_ln_g"][l][cA], d["rwkv_ln_b"][l][cA]])
        im["par%d" % l] = f32(np.broadcast_to(par[None].astype(np.float32), (128, 8, 192)))
        ch = slice(128 * q, 128 * q + 128)
        cw = d["lru_conv_w"][l]
        im["sm%d" % l] = f32(np.stack([cw[0, ch], cw[1, ch], cw[2, ch], cw[3, ch], d["lru_conv_b"][l][ch], d["lru_ba"][l][ch],
                                       d["lru_bx"][l][ch], d["lru_lambda"][l][ch]], axis=1))
        im["wa%d" % l] = f32(d["lru_wa"][l][2 * q:2 * q + 2]); im["wx%d" % l] = f32(d["lru_wx"][l][2 * q:2 * q + 2])
        im["gng%d" % l] = f32(np.stack([np.broadcast_to(d["ret_gn_g"][l][128 * h:128 * h + 128], (128, 128)) for h in heads]))
        wo = np.zeros((KY * 128, D), np.float32)
        ok = perm >= 0
        wo[ok] = d["w_out"][l][perm[ok]]
        im["wout%d" % l] = wo
        im["g2_%d" % l] = lay(d["norm2_g"][l])
        im["wg%d" % l] = f32(d["ffn_w_gate"][l]); im["wu%d" % l] = f32(d["ffn_w_up"][l]); im["wd%d" % l] = f32(d["ffn_w_down"][l])
    return im


def kernel(**inputs):
    d = {k: np.asarray(v) for k, v in inputs.items()}
    Bn, Sn, Dn = d["x"].shape
    nc = build_fused()
    in_maps = [core_inputs(d, c) for c in range(8)]
    res = run_bass_kernel_spmd(nc, in_maps, core_ids=list(range(8)))
    out = np.concatenate([r["outT"].T for r in res.results], axis=0).reshape(Bn, Sn, Dn)
    return np.ascontiguousarray(out.astype(np.float32))
```

```python
import math
from contextlib import ExitStack
import numpy as np
import concourse.bass as bass
import concourse.mybir as mybir
from concourse.bass_utils import run_bass_kernel_spmd

F32 = mybir.dt.float32
BF16 = mybir.dt.bfloat16
I32 = mybir.dt.int32
AF = mybir.ActivationFunctionType
ALU = mybir.AluOpType
AX = mybir.AxisListType


class Buf:
    __slots__ = ("name", "last_w", "readers")

    def __init__(self, name=""):
        self.name = name
        self.last_w = None
        self.readers = []


class SemPool:
    def __init__(self, nc):
        self.nc = nc
        self.sems = {}
        self._ctx = []
        self.count = {e: 0 for e in Prog.ENGS}
        self.dma_i = {"sync": 0, "gpsimd": 0, "scalar": 0}
        self.n_cc = 0

    def sem(self, key):
        if key not in self.sems:
            cm = self.nc.semaphore("s_" + key)
            self._ctx.append(cm)
            self.sems[key] = cm.__enter__()
        return self.sems[key]

    def close(self):
        for cm in reversed(self._ctx):
            cm.__exit__(None, None, None)


_POOL = [None]


class Prog:
    ENGS = ("sync", "scalar", "vector", "gpsimd", "tensor")

    def __init__(self, nc, tag="", n_dma_sems=12, self_sync=True):
        self.nc = nc
        self.tag = tag
        self.self_sync = self_sync
        self.own_pool = _POOL[0] is None
        self.pool = SemPool(nc) if self.own_pool else _POOL[0]
        self.ops = {e: [] for e in self.ENGS}
        self.waited = {e: {} for e in self.ENGS}
        self.n_dma_sems = n_dma_sems

    def _collect(self, eng, reads, writes):
        need = {}
        def req(tok):
            if tok is None:
                return
            k, v = tok
            if need.get(k, 0) < v:
                need[k] = v
        for b in reads:
            req(b.last_w)
        for b in writes:
            req(b.last_w)
            for r in b.readers:
                req(r)
        waits = []
        wd = self.waited[eng]
        for k, v in need.items():
            if k == "e_" + eng and (eng == "tensor" or not self.self_sync):
                continue
            if wd.get(k, 0) >= v:
                continue
            wd[k] = v
            waits.append((k, v))
        return waits

    def _mark(self, tok, reads, writes):
        for b in reads:
            b.readers.append(tok)
        for b in writes:
            b.last_w = tok
            b.readers = []
        return tok

    def op(self, eng, fn, reads=(), writes=()):
        waits = self._collect(eng, reads, writes)
        self.pool.count[eng] += 1
        tok = ("e_" + eng, self.pool.count[eng])
        self.ops[eng].append((waits, fn, tok[0], 1))
        return self._mark(tok, reads, writes)

    def dma(self, eng, out, in_, reads=(), writes=(), **kw):
        i = self.pool.dma_i[eng]
        self.pool.dma_i[eng] = i + 1
        key = "d_%s_%d" % (eng, i % self.n_dma_sems)
        val = 16 * (i // self.n_dma_sems + 1)
        waits = self._collect(eng, reads, writes)
        if i >= self.n_dma_sems:
            pv = val - 16
            if self.waited[eng].get(key, 0) < pv:
                self.waited[eng][key] = pv
                waits.append((key, pv))
        fn = lambda e, out=out, in_=in_, kw=kw: e.dma_start(out=out, in_=in_, **kw)
        self.ops[eng].append((waits, fn, key, 16))
        return self._mark((key, val), reads, writes)

    def cc(self, fn, reads=(), writes=()):
        eng = "gpsimd"
        self.pool.n_cc += 1
        key = "cc_%d" % self.pool.n_cc
        waits = self._collect(eng, reads, writes)
        self.ops[eng].append((waits, fn, key, None))
        return self._mark((key, 1), reads, writes)

    def final_wait(self, eng, bufs):
        waits = self._collect(eng, bufs, ())
        self.ops[eng].append((waits, None, None, 0))

    def emit(self):
        nc = self.nc
        sem = self.pool.sem
        with nc.Block() as block:
            for e in self.ENGS:
                ops = self.ops[e]
                if not ops:
                    continue

                def body(engine, ops=ops):
                    for waits, fn, key, inc in ops:
                        for k, v in waits:
                            engine.wait_ge(sem(k), v)
                        if fn is not None:
                            if inc is None:
                                fn(engine).then_inc(sem(key))
                            else:
                                fn(engine).then_inc(sem(key), inc)
                getattr(block, e)(body)
        if self.own_pool:
            self.pool.close()


D = 2048
KC = 16
TOK = 1024
EPS = 1e-6


def rms_to_bf16(P, nc, es, xs, bxs, g_ap, name, out_tile=None, out_buf=None, scr=None):
    sb = lambda n, shp, dt: es.enter_context(nc.sbuf_tensor(name + n, shp, dt))
    ps_ = lambda n, shp, dt: es.enter_context(nc.psum_tensor(name + n, shp, dt))
    if scr is None:
        scr = {}
    if "ones" not in scr:
        scr["ones"] = sb("ones", [128, 128], F32)
        scr["sq"] = [sb("sq%d" % i, [128, TOK], F32) for i in range(2)]
        scr["rstd"] = sb("rstd", [128, TOK], F32)
        scr["pss"] = [ps_("ss%d" % i, [128, 512], F32) for i in range(2)]
        scr["eps"] = sb("eps", [128, 1], F32)
        scr["b"] = dict(ones=Buf(), rstd=Buf(), sq=[Buf(), Buf()], ps=[Buf(), Buf()], eps=Buf())
        P.op("gpsimd", lambda e: e.memset(scr["ones"][:], 1.0), writes=[scr["b"]["ones"]])
        P.op("gpsimd", lambda e: e.memset(scr["eps"][:], EPS), writes=[scr["b"]["eps"]])
    ones, sq, rstd, pss, epst = scr["ones"], scr["sq"], scr["rstd"], scr["pss"], scr["eps"]
    B = scr["b"]
    bones, brstd, bsq, bps, beps = B["ones"], B["rstd"], B["sq"], B["ps"], B["eps"]
    gs = sb("g", [128, KC], F32)
    hT = out_tile if out_tile is not None else sb("hT", [128, KC, TOK], BF16)
    bg = Buf()
    bh = out_buf if out_buf is not None else Buf()
    P.dma("sync", gs[:], g_ap, writes=[bg])
    for c in range(KC):
        s = sq[c % 2]
        P.op("scalar", lambda e, c=c, s=s: e.activation(out=s[:], in_=xs[:, c, :], func=AF.Square),
             reads=[bxs], writes=[bsq[c % 2]])
        for hf in range(2):
            P.op("tensor", lambda e, c=c, s=s, hf=hf: e.matmul(pss[hf][:], lhsT=ones[:], rhs=s[:, hf * 512:(hf + 1) * 512],
                                                               start=(c == 0), stop=(c == KC - 1)),
                 reads=[bones, bsq[c % 2]], writes=[bps[hf]])
    for hf in range(2):
        sl = slice(hf * 512, (hf + 1) * 512)
        P.op("scalar", lambda e, hf=hf, sl=sl: e.activation(out=rstd[:, sl], in_=pss[hf][:], func=AF.Sqrt,
                                                            scale=1.0 / D, bias=epst[:, 0:1]),
             reads=[bps[hf], beps], writes=[brstd])
        P.op("vector", lambda e, sl=sl: e.reciprocal(out=rstd[:, sl], in_=rstd[:, sl]), reads=[brstd], writes=[brstd])
    for c in range(KC):
        P.op("vector", lambda e, c=c: e.scalar_tensor_tensor(out=hT[:, c, :], in0=xs[:, c, :], scalar=gs[:, c:c + 1],
                                                         in1=rstd[:], op0=ALU.mult, op1=ALU.mult),
             reads=[bxs, bg, brstd], writes=[bh])
    return hT, bh


DFF = 5632
NFB = DFF // 128
GRP = 4


S = 4096


def phase_p2b(nc, io, l):
    tg = "b%d_" % l
    gT = io["fm_all"][512:640, :]; xT = io["fm_all"][640:768, :]
    sm = io["sm%d" % l]; wa = io["wa%d" % l]; wx = io["wx%d" % l]
    out = io["y_src"][192:320, :]
    P = Prog(nc, tg)
    with ExitStack() as es:
        sb = lambda n, shp, dt=F32: es.enter_context(nc.sbuf_tensor(tg + n, shp, dt))
        g = sb("g", [128, S]); x = sb("x", [128, S]); xc = sb("xc", [128, S])
        r = sb("r", [128, S]); ii = sb("ii", [128, S]); a = sb("a", [128, S]); u = sb("u", [128, S])
        h = sb("h", [128, S])
        sms = sb("sms", [128, 8]); wab = sb("wab", [128, 128]); wxb = sb("wxb", [128, 128])
        t1 = sb("t1", [128, 8])
        ps = [es.enter_context(nc.psum_tensor(tg + "ps%d" % i, [128, 512], F32)) for i in range(4)]
        bps = [Buf() for _ in range(4)]
        bg, bx, bxc, br, bi, ba_, bu, bh, bsm, bwa, bwx, bt1 = (Buf() for _ in range(12))
        for i in range(4):
            sl = slice(i * 1024, (i + 1) * 1024)
            P.dma("sync", x[:, sl], xT[:, sl], writes=[bx])
        for i in range(4):
            sl = slice(i * 1024, (i + 1) * 1024)
            P.dma("gpsimd", g[:, sl], gT[:, sl], writes=[bg])
        P.dma("sync", sms[:], sm, writes=[bsm])
        P.op("gpsimd", lambda e: e.memset(wab[:], 0.0), writes=[bwa])
        P.op("gpsimd", lambda e: e.memset(wxb[:], 0.0), writes=[bwx])
        for bl in range(2):
            P.dma("sync", wab[bl * 64:(bl + 1) * 64, bl * 64:(bl + 1) * 64], wa[bl], writes=[bwa])
            P.dma("sync", wxb[bl * 64:(bl + 1) * 64, bl * 64:(bl + 1) * 64], wx[bl], writes=[bwx])
        P.op("vector", lambda e: e.tensor_scalar(out=xc[:], in0=x[:], scalar1=sms[:, 3:4], scalar2=sms[:, 4:5], op0=ALU.mult, op1=ALU.add),
             reads=[bx, bsm], writes=[bxc])
        for sh in (1, 2, 3):
            P.op("vector", lambda e, sh=sh: e.scalar_tensor_tensor(out=xc[:, sh:], in0=x[:, :S - sh], scalar=sms[:, 3 - sh:4 - sh],
                                                                   in1=xc[:, sh:], op0=ALU.mult, op1=ALU.add),
                 reads=[bx, bsm, bxc], writes=[bxc])
        P.op("scalar", lambda e: e.activation(out=t1[:, 0:1], in_=sms[:, 7:8], func=AF.Exp, scale=-1.0), reads=[bsm], writes=[bt1])
        P.op("vector", lambda e: e.tensor_scalar(out=t1[:, 1:2], in0=t1[:, 0:1], scalar1=2.0, scalar2=None, op0=ALU.add), reads=[bt1], writes=[bt1])
        P.op("vector", lambda e: e.reciprocal(out=t1[:, 1:2], in_=t1[:, 1:2]), reads=[bt1], writes=[bt1])
        P.op("vector", lambda e: e.tensor_tensor(out=t1[:, 2:3], in0=t1[:, 0:1], in1=t1[:, 1:2], op=ALU.mult), reads=[bt1], writes=[bt1])
        P.op("vector", lambda e: e.tensor_tensor(out=t1[:, 3:4], in0=t1[:, 2:3], in1=t1[:, 2:3], op=ALU.mult), reads=[bt1], writes=[bt1])
        P.op("vector", lambda e: e.memset(t1[:, 4:5], 1.0 / 13.0), reads=[bt1], writes=[bt1])
        for cf in (1.0 / 11, 1.0 / 9, 1.0 / 7, 1.0 / 5, 1.0 / 3, 1.0):
            P.op("vector", lambda e, cf=cf: e.tensor_scalar(out=t1[:, 4:5], in0=t1[:, 4:5], scalar1=t1[:, 3:4], scalar2=float(cf), op0=ALU.mult, op1=ALU.add),
                 reads=[bt1], writes=[bt1])
        P.op("vector", lambda e: e.scalar_tensor_tensor(out=t1[:, 5:6], in0=t1[:, 4:5], scalar=-16.0, in1=t1[:, 2:3], op0=ALU.mult, op1=ALU.mult),
             reads=[bt1], writes=[bt1])
        for pc in range(8):
            sl = slice(pc * 512, (pc + 1) * 512)
            q0, q1 = (2 * pc) % 4, (2 * pc + 1) % 4
            P.op("tensor", lambda e, q0=q0, sl=sl: e.matmul(ps[q0][:], lhsT=wab[:], rhs=xc[:, sl], start=True, stop=True),
                 reads=[bwa, bxc], writes=[bps[q0]])
            P.op("tensor", lambda e, q1=q1, sl=sl: e.matmul(ps[q1][:], lhsT=wxb[:], rhs=xc[:, sl], start=True, stop=True),
                 reads=[bwx, bxc], writes=[bps[q1]])
            P.op("scalar", lambda e, q0=q0, sl=sl: e.activation(out=r[:, sl], in_=ps[q0][:], func=AF.Sigmoid, bias=sms[:, 5:6]),
                 reads=[bps[q0], bsm], writes=[br])
            P.op("scalar", lambda e, q1=q1, sl=sl: e.activation(out=ii[:, sl], in_=ps[q1][:], func=AF.Sigmoid, bias=sms[:, 6:7]),
                 reads=[bps[q1], bsm], writes=[bi])
        P.op("scalar", lambda e: e.activation(out=a[:], in_=r[:], func=AF.Exp, scale=t1[:, 5:6]), reads=[br, bt1], writes=[ba_])
        P.op("vector", lambda e: e.tensor_tensor(out=u[:], in0=a[:], in1=a[:], op=ALU.mult), reads=[ba_], writes=[bu])
        P.op("vector", lambda e: e.tensor_scalar(out=u[:], in0=u[:], scalar1=-1.0, scalar2=1.0, op0=ALU.mult, op1=ALU.add), reads=[bu], writes=[bu])
        P.op("scalar", lambda e: e.activation(out=u[:], in_=u[:], func=AF.Sqrt), reads=[bu], writes=[bu])
        P.op("vector", lambda e: e.tensor_tensor(out=ii[:], in0=ii[:], in1=xc[:], op=ALU.mult), reads=[bi, bxc], writes=[bi])
        P.op("vector", lambda e: e.tensor_tensor(out=u[:], in0=u[:], in1=ii[:], op=ALU.mult), reads=[bu, bi], writes=[bu])
        P.op("vector", lambda e: e.tensor_tensor_scan(out=h[:], data0=a[:], data1=u[:], initial=0.0, op0=ALU.mult, op1=ALU.add),
             reads=[ba_, bu], writes=[bh])
        P.op("scalar", lambda e: e.activation(out=r[:], in_=g[:], func=AF.Square), reads=[bg, br], writes=[br])
        P.op("vector", lambda e: e.tensor_scalar(out=r[:], in0=r[:], scalar1=0.044715, scalar2=1.0, op0=ALU.mult, op1=ALU.add), reads=[br], writes=[br])
        P.op("vector", lambda e: e.tensor_tensor(out=r[:], in0=r[:], in1=g[:], op=ALU.mult), reads=[br, bg], writes=[br])
        P.op("scalar", lambda e: e.activation(out=r[:], in_=r[:], func=AF.Sigmoid, scale=1.5957691216057308), reads=[br], writes=[br])
        P.op("vector", lambda e: e.tensor_tensor(out=r[:], in0=r[:], in1=g[:], op=ALU.mult), reads=[br, bg], writes=[br])
        P.op("vector", lambda e: e.tensor_tensor(out=h[:], in0=h[:], in1=r[:], op=ALU.mult), reads=[br, bh], writes=[bh])
        bout = Buf()
        hb = sb("hb", [128, S], BF16); bhb = Buf()
        P.op("scalar", lambda e: e.copy(out=hb[:], in_=h[:]), reads=[bh], writes=[bhb])
        for i in range(4):
            sl = slice(i * 1024, (i + 1) * 1024)
            P.dma("sync", out[:, sl], hb[:, sl], reads=[bhb], writes=[bout])
        P.final_wait("sync", [bout]); P.final_wait("gpsimd", [bout])
        P.emit()


S = 4096
NT = 32
PI = math.pi
C1 = 6.28125
C2 = 2 * math.pi - 6.28125


def phase_p2c(nc, io, l):
    tg = "c%d_" % l
    tm = io["tm_all"]
    q_in = [tm[:, 576 + 64 * s:576 + 64 * s + 64] for s in range(2)]
    k_in = [tm[:, 704 + 64 * s:704 + 64 * s + 64] for s in range(2)]
    v_in = [tm[:, 832 + 128 * s:832 + 128 * s + 128] for s in range(2)]
    g_in = [tm[:, 1088 + 128 * s:1088 + 128 * s + 128] for s in range(2)]
    pos_in = io["pos"]; invf_in = io["invf"]; ident_in = io["ident"]
    mask_in = io["maskT"]; qdec_in = io["qdec"]; kdec_in = io["kdec"]; g128_in = io["g128"]; gng_in = io["gng%d" % l]
    ysrc = io["y_src"]
    P = Prog(nc, tg)
    with ExitStack() as es:
        sb = lambda n, shp, d=F32: es.enter_context(nc.sbuf_tensor(tg + n, shp, d))
        pst = lambda n, shp: es.enter_context(nc.psum_tensor(tg + n, shp, F32))
        posi = sb("posi", [128, NT], I32); posf = sb("posf", [128, NT]); invf = sb("invf", [128, 32])
        ident = sb("ident", [128, 128])
        ang = sb("ang", [128, NT, 32]); nf = sb("nf", [128, NT, 32]); ni = sb("ni", [128, NT, 32], I32)
        sn = sb("sn", [128, NT, 32]); cs = sb("cs", [128, NT, 32]); tmp = sb("tmp", [128, NT, 32]); tmp2 = sb("tmp2", [128, NT, 32])
        epst = sb("epst", [128, 1])
        bpos, binvf, bident, bang, bnf, bsn, bcs, btmp, btmp2, beps = (Buf() for _ in range(10))
        P.dma("sync", posi[:], pos_in, writes=[bpos])
        P.dma("sync", invf[:], invf_in, writes=[binvf])
        P.dma("sync", ident[:], ident_in, writes=[bident])
        P.op("gpsimd", lambda e: e.memset(epst[:], 1e-5), writes=[beps])
        P.op("vector", lambda e: e.tensor_copy(out=posf[:], in_=posi[:]), reads=[bpos], writes=[bpos])
        for n in range(NT):
            P.op("vector", lambda e, n=n: e.tensor_scalar(out=ang[:, n, :], in0=invf[:], scalar1=posf[:, n:n + 1], scalar2=None, op0=ALU.mult),
                 reads=[bpos, binvf], writes=[bang])
        P.op("vector", lambda e: e.tensor_scalar(out=nf[:], in0=ang[:], scalar1=1.0 / (2 * PI), scalar2=None, op0=ALU.mult), reads=[bang], writes=[bnf])
        P.op("vector", lambda e: e.tensor_copy(out=ni[:], in_=nf[:]), reads=[bnf], writes=[bnf])
        P.op("vector", lambda e: e.tensor_copy(out=nf[:], in_=ni[:]), reads=[bnf], writes=[bnf])
        P.op("vector", lambda e: e.scalar_tensor_tensor(out=ang[:], in0=nf[:], scalar=-C1, in1=ang[:], op0=ALU.mult, op1=ALU.add), reads=[bnf, bang], writes=[bang])
        P.op("vector", lambda e: e.scalar_tensor_tensor(out=ang[:], in0=nf[:], scalar=-C2, in1=ang[:], op0=ALU.mult, op1=ALU.add), reads=[bnf, bang], writes=[bang])

        def wrap(dst, bdst, src, bsrc, shift):
            if shift != 0.0:
                P.op("vector", lambda e: e.tensor_scalar(out=dst[:], in0=src[:], scalar1=float(shift), scalar2=None, op0=ALU.add), reads=[bsrc], writes=[bdst])
                src_, bsrc_ = dst, bdst
            else:
                src_, bsrc_ = src, bsrc
            P.op("vector", lambda e: e.tensor_scalar(out=tmp[:], in0=src_[:], scalar1=PI, scalar2=-2 * PI, op0=ALU.is_gt, op1=ALU.mult), reads=[bsrc_], writes=[btmp])
            P.op("vector", lambda e: e.tensor_scalar(out=tmp2[:], in0=src_[:], scalar1=-PI, scalar2=2 * PI, op0=ALU.is_lt, op1=ALU.mult), reads=[bsrc_], writes=[btmp2])
            P.op("vector", lambda e: e.tensor_tensor(out=dst[:], in0=src_[:], in1=tmp[:], op=ALU.add), reads=[bsrc_, btmp], writes=[bdst])
            P.op("vector", lambda e: e.tensor_tensor(out=dst[:], in0=dst[:], in1=tmp2[:], op=ALU.add), reads=[bdst, btmp2], writes=[bdst])
        wrap(sn, bsn, ang, bang, 0.0)
        wrap(cs, bcs, sn, bsn, PI / 2)
        P.op("scalar", lambda e: e.activation(out=sn[:], in_=sn[:], func=AF.Sin), reads=[bsn], writes=[bsn])
        P.op("scalar", lambda e: e.activation(out=cs[:], in_=cs[:], func=AF.Sin), reads=[bcs], writes=[bcs])

        q = sb("q", [128, NT, 64]); k = sb("k", [128, NT, 64]); qr = sb("qr", [128, NT, 64]); kr = sb("kr", [128, NT, 64])
        kt = sb("kt", [128, NT, 64])
        v = sb("v", [128, NT, 128]); g = sb("g", [128, NT, 128]); o = sb("o", [128, NT, 128])
        maskT = sb("maskT", [128, 128]); qdec = sb("qdec", [64, 128]); kdec = sb("kdec", [128, 2]); g128 = sb("g128", [64, 2])
        gng = sb("gng", [128, 128])
        Sst = sb("Sst", [64, 128])
        qkT = [sb("qkT%d" % i, [64, 2, 128]) for i in range(2)]
        qtT = [sb("qtT%d" % i, [64, 128]) for i in range(2)]
        smk = [sb("smk%d" % i, [128, 128]) for i in range(2)]
        stats = [sb("stats%d" % i, [128, 6]) for i in range(2)]
        mv = [sb("mv%d" % i, [128, 2]) for i in range(2)]
        rs = [sb("rs%d" % i, [128, 1]) for i in range(2)]
        psT = [pst("psT%d" % i, [64, 2, 128]) for i in range(2)]
        pss = [pst("pss%d" % i, [128, 128]) for i in range(2)]
        pso = [pst("pso%d" % i, [128, 128]) for i in range(2)]
        pkv = [pst("pkv%d" % i, [64, 128]) for i in range(2)]
        bq, bk, bqr, bkr, bkt, bv, bg, bo, bmask, bqdec, bkdec, bg128, bgng, bS = (Buf() for _ in range(14))
        bqkT, bqtT, bsmk, bstats, bmv, brs, bpsT, bpss, bpso, bpkv = ([Buf(), Buf()] for _ in range(10))
        P.dma("sync", kdec[:], kdec_in, writes=[bkdec])
        P.dma("sync", g128[:], g128_in, writes=[bg128])
        bout = Buf()
        oT = sb("oT", [128, S], BF16); boT = Buf()
        for sl in range(2):
            vw = lambda ap: ap.rearrange("(n p) d -> p n d", p=128)
            P.dma("sync", q[:], vw(q_in[sl]), writes=[bq])
            P.dma("gpsimd", k[:], vw(k_in[sl]), writes=[bk])
            for hh in range(2):
                P.dma("sync", v[:, hh * 16:(hh + 1) * 16, :], vw(v_in[sl])[:, hh * 16:(hh + 1) * 16, :], writes=[bv])
                P.dma("gpsimd", g[:, hh * 16:(hh + 1) * 16, :], vw(g_in[sl])[:, hh * 16:(hh + 1) * 16, :], writes=[bg])
            P.dma("sync", maskT[:], mask_in[sl], writes=[bmask])
            P.dma("sync", qdec[:], qdec_in[sl], writes=[bqdec])
            P.dma("sync", gng[:], gng_in[sl], writes=[bgng])
            P.op("gpsimd", lambda e: e.memset(Sst[:], 0.0), writes=[bS])
            for (src, bsrc, dst, bdst) in ((q, bq, qr, bqr), (k, bk, kr, bkr)):
                s1, s2 = src[:, :, 0:32], src[:, :, 32:64]
                d1, d2 = dst[:, :, 0:32], dst[:, :, 32:64]
                P.op("vector", lambda e, s1=s1, d1=d1: e.tensor_tensor(out=d1, in0=s1, in1=cs[:], op=ALU.mult), reads=[bsrc, bcs], writes=[bdst])
                P.op("vector", lambda e, s2=s2: e.tensor_tensor(out=tmp[:], in0=s2, in1=sn[:], op=ALU.mult), reads=[bsrc, bsn], writes=[btmp])
                P.op("vector", lambda e, d1=d1: e.tensor_tensor(out=d1, in0=d1, in1=tmp[:], op=ALU.subtract), reads=[btmp, bdst], writes=[bdst])
                P.op("vector", lambda e, s2=s2, d2=d2: e.tensor_tensor(out=d2, in0=s2, in1=cs[:], op=ALU.mult), reads=[bsrc, bcs], writes=[bdst])
                P.op("vector", lambda e, s1=s1: e.tensor_tensor(out=tmp2[:], in0=s1, in1=sn[:], op=ALU.mult), reads=[bsrc, bsn], writes=[btmp2])
                P.op("vector", lambda e, d2=d2: e.tensor_tensor(out=d2, in0=d2, in1=tmp2[:], op=ALU.add), reads=[btmp2, bdst], writes=[bdst])
            P.op("vector", lambda e, sl=sl: e.tensor_scalar(out=kt[:], in0=kr[:], scalar1=kdec[:, sl:sl + 1], scalar2=None, op0=ALU.mult),
                 reads=[bkr, bkdec], writes=[bkt])
            for n in range(NT):
                i = n % 2
                P.op("tensor", lambda e, n=n, i=i: e.transpose(psT[i][:, 0, :], qr[:, n, :], ident[:]), reads=[bqr, bident], writes=[bpsT[i]])
                P.op("tensor", lambda e, n=n, i=i: e.transpose(psT[i][:, 1, :], kr[:, n, :], ident[:]), reads=[bkr, bident], writes=[bpsT[i]])
                P.op("scalar", lambda e, i=i: e.copy(out=qkT[i][:], in_=psT[i][:]), reads=[bpsT[i]], writes=[bqkT[i]])
                P.op("vector", lambda e, i=i: e.tensor_tensor(out=qtT[i][:], in0=psT[i][:, 0, :], in1=qdec[:], op=ALU.mult), reads=[bpsT[i], bqdec], writes=[bqtT[i]])
                P.op("tensor", lambda e, i=i: e.matmul(pss[i][:], lhsT=qkT[i][:, 1, :], rhs=qkT[i][:, 0, :], start=True, stop=True), reads=[bqkT[i]], writes=[bpss[i]])
                P.op("vector", lambda e, i=i: e.tensor_tensor(out=smk[i][:], in0=pss[i][:], in1=maskT[:], op=ALU.mult), reads=[bpss[i], bmask], writes=[bsmk[i]])
                P.op("tensor", lambda e, i=i, n=n: e.matmul(pso[i][:], lhsT=smk[i][:], rhs=v[:, n, :], start=True, stop=False), reads=[bsmk[i], bv], writes=[bpso[i]])
                P.op("tensor", lambda e, i=i: e.matmul(pso[i][:], lhsT=qtT[i][:], rhs=Sst[:], start=False, stop=True), reads=[bqtT[i], bS], writes=[bpso[i]])
                P.op("tensor", lambda e, i=i, n=n: e.matmul(pkv[i][:], lhsT=kt[:, n, :], rhs=v[:, n, :], start=True, stop=True), reads=[bkt, bv], writes=[bpkv[i]])
                P.op("vector", lambda e, i=i, sl=sl: e.scalar_tensor_tensor(out=Sst[:], in0=Sst[:], scalar=g128[:, sl:sl + 1], in1=pkv[i][:], op0=ALU.mult, op1=ALU.add),
                     reads=[bpkv[i], bg128, bS], writes=[bS])
                P.op("vector", lambda e, i=i: e.bn_stats(out=stats[i][:], in_=pso[i][:]), reads=[bpso[i]], writes=[bstats[i]])
                P.op("vector", lambda e, i=i: e.bn_aggr(out=mv[i][:], in_=stats[i][:]), reads=[bstats[i]], writes=[bmv[i]])
                P.op("scalar", lambda e, i=i: e.activation(out=rs[i][:], in_=mv[i][:, 1:2], func=AF.Sqrt, bias=epst[:, 0:1]), reads=[bmv[i], beps], writes=[brs[i]])
                P.op("vector", lambda e, i=i: e.reciprocal(out=rs[i][:], in_=rs[i][:]), reads=[brs[i]], writes=[brs[i]])
                P.op("vector", lambda e, i=i, n=n: e.tensor_scalar(out=o[:, n, :], in0=pso[i][:], scalar1=mv[i][:, 0:1], scalar2=rs[i][:, 0:1], op0=ALU.subtract, op1=ALU.mult),
                     reads=[bpso[i], bmv[i], brs[i]], writes=[bo])
                P.op("gpsimd", lambda e, n=n: e.tensor_tensor(out=o[:, n, :], in0=o[:, n, :], in1=gng[:], op=ALU.mult), reads=[bo, bgng], writes=[bo])
            P.op("scalar", lambda e: e.activation(out=g[:], in_=g[:], func=AF.Silu), reads=[bg], writes=[bg])
            P.op("vector", lambda e: e.tensor_tensor(out=o[:], in0=o[:], in1=g[:], op=ALU.mult), reads=[bo, bg], writes=[bo])
            for n in range(NT):
                i = n % 2
                P.op("tensor", lambda e, n=n, i=i: e.transpose(pss[i][:], o[:, n, :], ident[:]), reads=[bo, bident], writes=[bpss[i]])
                P.op("scalar", lambda e, n=n, i=i: e.copy(out=oT[:, n * 128:(n + 1) * 128], in_=pss[i][:]), reads=[bpss[i]], writes=[boT])
            for hh in range(4):
                cs_ = slice(hh * 1024, (hh + 1) * 1024)
                P.dma("sync", ysrc[320 + 128 * sl:448 + 128 * sl, cs_], oT[:, cs_], reads=[boT], writes=[bout])
        P.final_wait("sync", [bout]); P.final_wait("gpsimd", [bout])
        P.emit()


def p2c_consts(heads):
    maskT = np.zeros((2, 128, 128), np.float32); qdec = np.zeros((2, 64, 128), np.float32)
    kdec = np.zeros((128, 2), np.float32); g128 = np.zeros((64, 2), np.float32)
    idx = np.arange(128)
    for s, hd in enumerate(heads):
        lg = float(np.log1p(-np.exp2(np.float32(-5.0 - hd))).astype(np.float32))
        m = idx[:, None]; c = idx[None, :]
        same = (m // 64) == (c // 64)
        earlier = (m // 64) < (c // 64)
        dec = np.where(same, np.exp(lg * np.abs(c - m)), np.where(earlier, np.exp(lg * (c - m)), 0.0))
        maskT[s] = (dec * 0.125).astype(np.float32)
        qdec[s] = np.exp(lg * (idx + 1.0))[None, :].astype(np.float32)
        kdec[:, s] = (np.exp(lg * (127.0 - idx)) * 0.125).astype(np.float32)
        g128[:, s] = np.float32(np.exp(lg * 128.0))
    invf = (10000.0 ** (-np.arange(0, 64, 2, dtype=np.float32) / 64)).astype(np.float32)
    return dict(maskT=maskT, qdec=qdec, kdec=kdec, g128=g128, invf=np.ascontiguousarray(np.broadcast_to(invf, (128, 32))),
                ident=np.eye(128, dtype=np.float32))


C_SLOTS = [(0, 1), (2, 3), (4, 5), (4, 5)]


VARA = False

S = 4096
NT = 32
W3 = 192
NEG_EHALF = -math.exp(-0.5)
A_GN_EPS = 64e-5


def phase_p2a(nc, io, l, nt=NT, dbg=9):
    has_vres = l > 0
    tg = "a%d_" % l
    tmA = io["tm_all"]; fm = io["fm_all"]
    rkv_in = tmA[:, 0:576]; mu_in = io["mu_rkv%d" % l]
    wd_in = fm[0:96, :]; ad_in = fm[96:192, :]; gd_in = fm[192:448, :]
    lmu_in = io["lmu%d" % l]; w2_in = io["w2c%d" % l]; a2_in = io["a2c%d" % l]; g2_in = io["g2c%d" % l]
    par_in = io["par%d" % l]; cst_in = io["cst"]; m4_in = io["mask4"]
    if has_vres:
        vd_in = fm[448:512, :]; v2_in = io["v2c"]
    vf_in = io["vfirst"]
    v_out = io["vfirst"]
    ysrc = io["y_src"]
    P = Prog(nc, tg)
    with ExitStack() as es:
        sb = lambda n, shp, d=F32: es.enter_context(nc.sbuf_tensor(tg + n, shp, d))
        V = lambda fn, r, w: P.op("vector", fn, reads=r, writes=w)
        A = lambda fn, r, w: P.op("scalar", fn, reads=r, writes=w)
        G = lambda fn, r, w: P.op("gpsimd", fn, reads=r, writes=w)
        T = lambda fn, r, w: P.op("tensor", fn, reads=r, writes=w)
        cst = sb("cst", [128, 4, 128]); m4 = sb("m4", [128, 4, 128]); par = sb("par", [128, 8, W3]); mu = sb("mu", [128, 576])
        lmu = sb("lmu", [128, 5]); w2 = sb("w2", [96, W3]); a2 = sb("a2", [96, W3]); g2 = sb("g2", [128, 2, W3])
        ones = sb("ones", [128, 1]); epsg = sb("epsg", [128, 1])
        bcst, bm4, bpar, bmu, blmu, bw2, ba2, bg2, bones, bepsg = (Buf() for _ in range(10))
        P.dma("sync", cst[:], cst_in, writes=[bcst]); P.dma("sync", m4[:], m4_in, writes=[bm4])
        P.dma("sync", par[:], par_in, writes=[bpar]); P.dma("sync", mu[:], mu_in, writes=[bmu])
        P.dma("sync", lmu[:], lmu_in, writes=[blmu]); P.dma("sync", w2[:], w2_in, writes=[bw2])
        P.dma("sync", a2[:], a2_in, writes=[ba2])
        P.dma("sync", g2[:], g2_in.rearrange("(c p) n -> p c n", p=128), writes=[bg2])
        G(lambda e: e.memset(ones[:], 1.0), [], [bones]); G(lambda e: e.memset(epsg[:], A_GN_EPS), [], [bepsg])
        ident, triu, trirev, maskn = cst[:, 0, :], cst[:, 1, :], cst[:, 2, :], cst[:, 3, :]
        if has_vres:
            v2 = sb("v2", [64, W3]); bv2 = Buf()
            P.dma("sync", v2[:], v2_in, writes=[bv2])
        ltmp = sb("ltmp", [128, S]); bltmp = Buf()
        lo = {}
        specs = [("wd", wd_in, 96, 0, AF.Tanh), ("ad", ad_in, 96, 1, None), ("gd0", gd_in[0:128], 128, 2, AF.Sigmoid),
                 ("gd1", gd_in[128:256], 128, 3, AF.Sigmoid)]
        if has_vres:
            specs.append(("vd", vd_in, 64, 4, None))
        for name, src, rows, mcol, fn in specs:
            t = sb("lo_" + name, [rows, S + 1]); b = Buf()
            G(lambda e, t=t: e.memset(t[:, 0:1], 0.0), [], [b])
            for i in range(4):
                P.dma("sync" if i % 2 == 0 else "gpsimd", t[:, 1 + i * 1024:1 + (i + 1) * 1024], src[:, i * 1024:(i + 1) * 1024], writes=[b])
            V(lambda e, t=t, rows=rows: e.tensor_tensor(out=ltmp[:rows, :], in0=t[:, 0:S], in1=t[:, 1:S + 1], op=ALU.subtract), [b, bltmp], [bltmp])
            V(lambda e, t=t, rows=rows, mcol=mcol: e.scalar_tensor_tensor(out=t[:, 1:S + 1], in0=ltmp[:rows, :], scalar=lmu[:rows, mcol:mcol + 1],
                                                                      in1=t[:, 1:S + 1], op0=ALU.mult, op1=ALU.add), [bltmp, blmu, b], [b])
            if fn is not None:
                A(lambda e, t=t, fn=fn: e.activation(out=t[:, 1:S + 1], in_=t[:, 1:S + 1], func=fn), [b], [b])
            lo[name] = (t, b)
        NS = 2
        cur = [sb("cur%d" % i, [128, 576]) for i in range(NS)]; prv = [sb("prv%d" % i, [128, 576]) for i in range(NS)]
        bcur = [Buf() for _ in range(NS)]; bprv = [Buf() for _ in range(NS)]
        tt = [sb("tt%d" % i, [128, 8, W3]) for i in range(NS)]; btt = [Buf() for _ in range(NS)]
        rk = [sb("rk%d" % i, [128, 3]) for i in range(NS)]; brk = [Buf() for _ in range(NS)]
        tm = sb("tm", [128, 4, W3]); btm = Buf()
        ss = sb("ss", [128, 3]); bss = Buf()
        if has_vres:
            vf = [sb("vf%d" % i, [128, W3]) for i in range(NS)]; bvf = [Buf() for _ in range(NS)]
        ee = sb("ee", [128, 4, W3]); bee = Buf()
        dd = sb("dd", [128, 6, W3]); bdd = Buf()
        wc = sb("wc", [64, 4]); bwc = Buf()
        T4 = sb("T4", [64, 4, 128]); bT4 = Buf()
        M4 = sb("M4", [128, 4, 128]); bM4 = Buf()
        Pk = [sb("Pk%d" % i, [128, 128]) for i in range(2)]; bPk = [Buf(), Buf()]
        Qk = [sb("Qk%d" % i, [128, 128]) for i in range(2)]; bQk = [Buf(), Buf()]
        Z = sb("Z", [128, 128]); bZ = Buf()
        nX = sb("nX", [128, 64]); bnX = Buf()
        U = sb("U", [128, 64]); bU = Buf()
        ST = [sb("ST%d" % h, [64, 64]) for h in range(3)]; bST = [Buf() for _ in range(3)]
        yt = [sb("yt%d" % i, [128, W3]) for i in range(NS)]; byt = [Buf() for _ in range(NS)]
        st6 = sb("st6", [128, 6]); bst6 = Buf(); mv = sb("mv", [128, 2]); bmv = Buf(); rsd = sb("rsd", [128, 1]); brsd = Buf()
        for h in range(3):
            G(lambda e, h=h: e.memset(ST[h][:], 0.0), [], [bST[h]])
        ytT = [sb("ytT%d" % i, [128, 2, 128], BF16) for i in range(NS)]; bytT = [Buf() for _ in range(NS)]
        pb = [es.enter_context(nc.psum_tensor(tg + "pb%d" % i, [128, 512], F32)) for i in range(8)]
        bpb = [Buf() for _ in range(8)]
        pL1, pL2, pGG, pTT, pPP, pIV, pCH = pb[0], pb[1], pb[2], pb[3], pb[4], pb[5], pb[6]
        bL1, bL2, bGG, bTT, bPP, bIV, bCH = bpb[0], bpb[1], bpb[2], bpb[3], bpb[4], bpb[5], bpb[6]
        rkv_v = rkv_in
        bout = Buf()
        wdT, bwd = lo["wd"]; adT, bad = lo["ad"]; gd0, bgd0 = lo["gd0"]; gd1, bgd1 = lo["gd1"]
        for n in range(nt):
            s = n % NS
            t0 = n * 128
            c_, p_, t_ = cur[s], prv[s], tt[s]
            P.dma("sync", c_[:], rkv_v[t0:t0 + 128, :], writes=[bcur[s]])
            if n == 0:
                G(lambda e, p_=p_: e.memset(p_[:], 0.0), [], [bprv[s]])
                P.dma("gpsimd", p_[1:128, :], rkv_v[0:127, :], writes=[bprv[s]])
            else:
                P.dma("gpsimd", p_[:], rkv_v[t0 - 1:t0 + 127, :], writes=[bprv[s]])
            if has_vres:
                P.dma("sync", vf[s][:], vf_in[t0:t0 + 128, :], writes=[bvf[s]])
            V(lambda e, c_=c_, p_=p_: e.tensor_tensor(out=p_[:], in0=p_[:], in1=c_[:], op=ALU.subtract), [bcur[s], bprv[s]], [bprv[s]])
            G(lambda e, p_=p_: e.tensor_tensor(out=p_[:], in0=p_[:], in1=mu[:], op=ALU.mult), [bprv[s], bmu], [bprv[s]])
            V(lambda e, c_=c_, p_=p_: e.tensor_tensor(out=c_[:], in0=c_[:], in1=p_[:], op=ALU.add), [bcur[s], bprv[s]], [bcur[s]])
            r_, k_, v_ = c_[:, 0:W3], c_[:, W3:2 * W3], c_[:, 2 * W3:3 * W3]
            tsl = slice(1 + t0, 1 + t0 + 128)
            T(lambda e, tsl=tsl: e.matmul(pL1[:, 0:W3], lhsT=wdT[:, tsl], rhs=w2[:], start=True, stop=True), [bwd, bw2], [bL1])
            T(lambda e, tsl=tsl: e.matmul(pL1[:, W3:2 * W3], lhsT=adT[:, tsl], rhs=a2[:], start=True, stop=True), [bad, ba2], [bL1])
            T(lambda e, tsl=tsl: e.matmul(pL2[:, 0:W3], lhsT=gd0[:, tsl], rhs=g2[:, 0, :], start=True, stop=False), [bgd0, bg2], [bL2])
            T(lambda e, tsl=tsl: e.matmul(pL2[:, 0:W3], lhsT=gd1[:, tsl], rhs=g2[:, 1, :], start=False, stop=True), [bgd1, bg2], [bL2])
            if has_vres:
                vdT, bvd = lo["vd"]
                T(lambda e, tsl=tsl: e.matmul(pL2[:, W3:2 * W3], lhsT=vdT[:, tsl], rhs=v2[:], start=True, stop=True), [bvd, bv2], [bL2])
            V(lambda e, t_=t_: e.tensor_tensor(out=t_[:, 5, :], in0=pL1[:, 0:W3], in1=par[:, 0, :], op=ALU.add), [bL1, bpar, btt[s]], [btt[s]])
            A(lambda e, t_=t_: e.activation(out=t_[:, 5, :], in_=t_[:, 5, :], func=AF.Sigmoid), [btt[s]], [btt[s]])
            V(lambda e, t_=t_: e.tensor_scalar(out=t_[:, 5, :], in0=t_[:, 5, :], scalar1=NEG_EHALF, scalar2=None, op0=ALU.mult), [btt[s]], [btt[s]])
            V(lambda e, t_=t_: e.tensor_tensor(out=t_[:, 7, :], in0=pL1[:, W3:2 * W3], in1=par[:, 1, :], op=ALU.add), [bL1, bpar, btt[s]], [btt[s]])
            A(lambda e, t_=t_: e.activation(out=t_[:, 7, :], in_=t_[:, 7, :], func=AF.Sigmoid), [btt[s]], [btt[s]])
            A(lambda e, t_=t_: e.copy(out=t_[:, 6, :], in_=pL2[:, 0:W3]), [bL2, btt[s]], [btt[s]])
            G(lambda e, t_=t_, r_=r_: e.tensor_copy(out=t_[:, 0, :], in_=r_), [bcur[s], btt[s]], [btt[s]])
            if has_vres:
                V(lambda e: e.tensor_tensor(out=tm[:, 0, :], in0=pL2[:, W3:2 * W3], in1=par[:, 2, :], op=ALU.add), [bL2, bpar, btm], [btm])
                A(lambda e: e.activation(out=tm[:, 0, :], in_=tm[:, 0, :], func=AF.Sigmoid), [btm], [btm])
                V(lambda e, v_=v_, s=s: e.tensor_tensor(out=tm[:, 1, :], in0=vf[s][:], in1=v_, op=ALU.subtract), [bvf[s], bcur[s], btm], [btm])
                V(lambda e: e.tensor_tensor(out=tm[:, 1, :], in0=tm[:, 1, :], in1=tm[:, 0, :], op=ALU.mult), [btm], [btm])
                V(lambda e, t_=t_, v_=v_: e.tensor_tensor(out=t_[:, 2, :], in0=tm[:, 1, :], in1=v_, op=ALU.add), [btm, bcur[s], btt[s]], [btt[s]])
            else:
                G(lambda e, t_=t_, v_=v_: e.tensor_copy(out=t_[:, 2, :], in_=v_), [bcur[s], btt[s]], [btt[s]])
            if not has_vres:
                P.dma("sync", v_out[t0:t0 + 128, :], t_[:, 2, :], reads=[btt[s]], writes=[bout])
            V(lambda e, t_=t_, k_=k_: e.tensor_tensor(out=t_[:, 3, :], in0=k_, in1=par[:, 3, :], op=ALU.mult), [bcur[s], bpar, btt[s]], [btt[s]])
            V(lambda e, t_=t_: e.tensor_tensor(out=tm[:, 2, :], in0=t_[:, 3, :], in1=t_[:, 3, :], op=ALU.mult), [btt[s], btm], [btm])
            V(lambda e: e.tensor_reduce(out=ss[:], in_=tm[:, 2, :].rearrange("p (h j) -> p h j", h=3), axis=AX.X, op=ALU.add), [btm, bss], [bss])
            A(lambda e: e.activation(out=ss[:], in_=ss[:], func=AF.Sqrt), [bss], [bss])
            V(lambda e: e.tensor_scalar(out=ss[:], in0=ss[:], scalar1=1e-12, scalar2=None, op0=ALU.max), [bss], [bss])
            V(lambda e: e.reciprocal(out=ss[:], in_=ss[:]), [bss], [bss])
            for h in range(3):
                hs = slice(64 * h, 64 * h + 64)
                V(lambda e, t_=t_, hs=hs, h=h: e.tensor_scalar(out=t_[:, 3, hs], in0=t_[:, 3, hs], scalar1=ss[:, h:h + 1], scalar2=None, op0=ALU.mult),
                  [bss, btt[s]], [btt[s]])
            V(lambda e, t_=t_: e.scalar_tensor_tensor(out=tm[:, 3, :], in0=t_[:, 7, :], scalar=-1.0, in1=par[:, 4, :], op0=ALU.add, op1=ALU.mult),
              [btt[s], bpar, btm], [btm])
            V(lambda e, t_=t_, k_=k_: e.scalar_tensor_tensor(out=t_[:, 1, :], in0=tm[:, 3, :], scalar=1.0, in1=k_, op0=ALU.add, op1=ALU.mult),
              [btm, bcur[s], btt[s]], [btt[s]])
            G(lambda e, t_=t_: e.tensor_tensor(out=t_[:, 4, :], in0=t_[:, 3, :], in1=t_[:, 7, :], op=ALU.mult), [btt[s]], [btt[s]])
            V(lambda e, t_=t_: e.tensor_tensor(out=tm[:, 3, :], in0=t_[:, 0, :], in1=t_[:, 1, :], op=ALU.mult), [btt[s], btm], [btm])
            V(lambda e: e.tensor_tensor(out=tm[:, 3, :], in0=tm[:, 3, :], in1=par[:, 5, :], op=ALU.mult), [btm, bpar], [btm])
            V(lambda e, s=s: e.tensor_reduce(out=rk[s][:], in_=tm[:, 3, :].rearrange("p (h j) -> p h j", h=3), axis=AX.X, op=ALU.add), [btm, brk[s]], [brk[s]])
            if dbg <= 1:
                P.dma("sync", y_out[t0:t0 + 128, :], t_[:, 6, :], reads=[btt[s]], writes=[bout])
                continue
            T(lambda e, t_=t_: e.matmul(pGG[:, 0:W3], lhsT=triu, rhs=t_[:, 5, :], start=True, stop=True), [bcst, btt[s]], [bGG])
            T(lambda e, t_=t_: e.matmul(pGG[:, W3:2 * W3], lhsT=trirev, rhs=t_[:, 5, :], start=True, stop=True), [bcst, btt[s]], [bGG])
            for h in range(3):
                T(lambda e, t_=t_, h=h: e.matmul(pGG[0:64, 400 + h:401 + h], lhsT=t_[:, 5, 64 * h:64 * h + 64], rhs=ones[:], start=True, stop=True),
                  [btt[s], bones], [bGG])
            A(lambda e: e.activation(out=ee[:, 0, :], in_=pGG[:, 0:W3], func=AF.Exp), [bGG, bee], [bee])
            A(lambda e: e.activation(out=ee[:, 1, :], in_=pGG[:, 0:W3], func=AF.Exp, scale=-1.0), [bGG, bee], [bee])
            V(lambda e, t_=t_: e.tensor_tensor(out=ee[:, 2, :], in0=pGG[:, 0:W3], in1=t_[:, 5, :], op=ALU.subtract), [bGG, btt[s], bee], [bee])
            A(lambda e: e.activation(out=ee[:, 2, :], in_=ee[:, 2, :], func=AF.Exp), [bee], [bee])
            A(lambda e: e.activation(out=ee[:, 3, :], in_=pGG[:, W3:2 * W3], func=AF.Exp), [bGG, bee], [bee])
            A(lambda e: e.activation(out=wc[:, 0:3], in_=pGG[0:64, 400:403], func=AF.Exp), [bGG, bwc], [bwc])
            for (di, ti, ei, eng) in ((0, 0, 0, V), (1, 3, 2, G), (2, 1, 1, V), (3, 4, 1, G), (4, 1, 3, V), (5, 4, 3, G)):
                eng(lambda e, t_=t_, di=di, ti=ti, ei=ei: e.tensor_tensor(out=dd[:, di, :], in0=t_[:, ti, :], in1=ee[:, ei, :], op=ALU.mult),
                    [btt[s], bee, bdd], [bdd])
            if dbg <= 2:
                P.dma("sync", y_out[t0:t0 + 128, :], dd[:, 0, :], reads=[bdd], writes=[bout])
                continue
            for h in range(3):
                hs = slice(64 * h, 64 * h + 64)
                for j in range(4):
                    T(lambda e, j=j, hs=hs: e.transpose(pTT[0:64, j * 128:(j + 1) * 128], dd[:, j, hs], ident), [bdd, bcst], [bTT])
                A(lambda e: e.copy(out=T4[:], in_=pTT[0:64, :].rearrange("p (a b) -> p a b", a=4)), [bTT, bT4], [bT4])
                if dbg <= 3:
                    continue
                T(lambda e: e.matmul(pPP[:, 0:256], lhsT=T4[:, 2, :], rhs=T4[:, 0:2, :].rearrange("p a b -> p (a b)"), start=True, stop=True), [bT4], [bPP])
                T(lambda e: e.matmul(pPP[:, 256:512], lhsT=T4[:, 3, :], rhs=T4[:, 0:2, :].rearrange("p a b -> p (a b)"), start=True, stop=True), [bT4], [bPP])
                T(lambda e: e.matmul(pIV[:, 0:128], lhsT=T4[:, 1, :], rhs=T4[:, 3, :], start=True, stop=True), [bT4], [bIV])
                V(lambda e: e.tensor_tensor(out=M4[:].rearrange("p a b -> p (a b)"), in0=pPP[:], in1=m4[:].rearrange("p a b -> p (a b)"), op=ALU.mult),
                  [bPP, bm4, bM4], [bM4])
                V(lambda e: e.tensor_tensor(out=Pk[0][:], in0=pIV[:, 0:128], in1=maskn, op=ALU.mult), [bIV, bcst, bPk[0]], [bPk[0]])
                (V if VARA else G)(lambda e: e.tensor_copy(out=Qk[0][:], in_=M4[:, 3, :]), [bM4, bQk[0]], [bQk[0]])
                (V if VARA else G)(lambda e: e.tensor_tensor(out=Z[:], in0=ident, in1=M4[:, 3, :], op=ALU.subtract), [bM4, bcst, bZ], [bZ])
                if dbg <= 4:
                    continue
                for lv in range(1, 7):
                    pi, ci = (lv - 1) % 2, lv % 2
                    STEP = 9
                    if lv < 6 and STEP >= 1:
                        T(lambda e, pi=pi: e.matmul(pIV[:, 128:256], lhsT=Pk[pi][:], rhs=Qk[pi][:], start=True, stop=True), [bPk[pi], bQk[pi]], [bIV])
                    if STEP >= 2:
                        T(lambda e, pi=pi: e.matmul(pIV[:, 256:384], lhsT=Qk[pi][:], rhs=Pk[pi][:], start=True, stop=True), [bPk[pi], bQk[pi]], [bIV])
                    if lv < 6 and STEP >= 3:
                        A(lambda e, ci=ci: e.copy(out=Qk[ci][:], in_=pIV[:, 128:256]), [bIV, bQk[ci]], [bQk[ci]])
                    if STEP >= 4:
                        A(lambda e, ci=ci: e.copy(out=Pk[ci][:], in_=pIV[:, 256:384]), [bIV, bPk[ci]], [bPk[ci]])
                    if STEP >= 5:
                        T(lambda e, ci=ci: e.matmul(pIV[:, 384:512], lhsT=Pk[ci][:], rhs=Z[:], start=True, stop=True), [bPk[ci], bZ], [bIV])
                    if STEP >= 6:
                        V(lambda e: e.tensor_tensor(out=Z[:], in0=Z[:], in1=pIV[:, 384:512], op=ALU.add), [bIV, bZ], [bZ])
                if dbg <= 5:
                    continue
                vh = t_[:, 2, hs]
                T(lambda e, vh=vh: e.matmul(pCH[:, 0:64], lhsT=M4[:, 1, :], rhs=vh, start=True, stop=False), [bM4, btt[s]], [bCH])
                T(lambda e, h=h: e.matmul(pCH[:, 0:64], lhsT=T4[:, 1, :], rhs=ST[h][:], start=False, stop=True), [bT4, bST[h]], [bCH])
                A(lambda e: e.activation(out=nX[:], in_=pCH[:, 0:64], func=AF.Copy, scale=-1.0), [bCH, bnX], [bnX])
                T(lambda e: e.matmul(pCH[:, 64:128], lhsT=Z[:], rhs=nX[:], start=True, stop=True), [bZ, bnX], [bCH])
                V(lambda e: e.tensor_scalar(out=U[:], in0=pCH[:, 64:128], scalar1=1.0, scalar2=None, op0=ALU.mult), [bCH, bU], [bU])
                T(lambda e, h=h: e.matmul(pCH[:, 128:192], lhsT=T4[:, 0, :], rhs=ST[h][:], start=True, stop=False), [bT4, bST[h]], [bCH])
                T(lambda e: e.matmul(pCH[:, 128:192], lhsT=M4[:, 2, :], rhs=U[:], start=False, stop=False), [bM4, bU], [bCH])
                T(lambda e, vh=vh: e.matmul(pCH[:, 128:192], lhsT=M4[:, 0, :], rhs=vh, start=False, stop=True), [bM4, btt[s]], [bCH])
                T(lambda e, hs=hs: e.matmul(pCH[0:64, 192:256], lhsT=dd[:, 5, hs], rhs=U[:], start=True, stop=False), [bdd, bU], [bCH])
                T(lambda e, hs=hs, vh=vh: e.matmul(pCH[0:64, 192:256], lhsT=dd[:, 4, hs], rhs=vh, start=False, stop=True), [bdd, btt[s]], [bCH])
                V(lambda e, h=h: e.scalar_tensor_tensor(out=ST[h][:], in0=ST[h][:], scalar=wc[:, h:h + 1], in1=pCH[0:64, 192:256], op0=ALU.mult, op1=ALU.add),
                  [bCH, bwc, bST[h]], [bST[h]])
                V(lambda e: e.bn_stats(out=st6[:], in_=pCH[:, 128:192]), [bCH, bst6], [bst6])
                V(lambda e: e.bn_aggr(out=mv[:], in_=st6[:]), [bst6, bmv], [bmv])
                A(lambda e: e.activation(out=rsd[:], in_=mv[:, 1:2], func=AF.Sqrt, bias=epsg[:, 0:1]), [bmv, bepsg, brsd], [brsd])
                V(lambda e: e.reciprocal(out=rsd[:], in_=rsd[:]), [brsd], [brsd])
                V(lambda e, s=s, hs=hs: e.tensor_scalar(out=yt[s][:, hs], in0=pCH[:, 128:192], scalar1=mv[:, 0:1], scalar2=rsd[:, 0:1], op0=ALU.subtract, op1=ALU.mult),
                  [bCH, bmv, brsd, byt[s]], [byt[s]])
            if dbg <= 5:
                P.dma("sync", y_out[t0:t0 + 128, :], dd[:, 0, :], reads=[bdd, bT4, bM4, bZ, bPk[0], bPk[1], bQk[0], bQk[1]], writes=[bout])
                continue
            G(lambda e, s=s: e.tensor_tensor(out=yt[s][:], in0=yt[s][:], in1=par[:, 6, :], op=ALU.mult), [byt[s], bpar], [byt[s]])
            G(lambda e, s=s: e.tensor_tensor(out=yt[s][:], in0=yt[s][:], in1=par[:, 7, :], op=ALU.add), [byt[s], bpar], [byt[s]])
            for h in range(3):
                hs = slice(64 * h, 64 * h + 64)
                V(lambda e, s=s, hs=hs, h=h, t_=t_: e.scalar_tensor_tensor(out=yt[s][:, hs], in0=t_[:, 2, hs], scalar=rk[s][:, h:h + 1], in1=yt[s][:, hs],
                                                                          op0=ALU.mult, op1=ALU.add), [btt[s], brk[s], byt[s]], [byt[s]])
            V(lambda e, s=s, t_=t_: e.tensor_tensor(out=yt[s][:], in0=yt[s][:], in1=t_[:, 6, :], op=ALU.mult), [byt[s], btt[s]], [byt[s]])
            T(lambda e, s=s: e.transpose(pb[7][:, 0:128], yt[s][:, 0:128], ident), [byt[s], bcst], [bpb[7]])
            T(lambda e, s=s: e.transpose(pb[7][0:64, 128:256], yt[s][:, 128:192], ident), [byt[s], bcst], [bpb[7]])
            A(lambda e, s=s: e.copy(out=ytT[s][:, 0, :], in_=pb[7][:, 0:128]), [bpb[7], bytT[s]], [bytT[s]])
            A(lambda e, s=s: e.copy(out=ytT[s][0:64, 1, :], in_=pb[7][0:64, 128:256]), [bpb[7], bytT[s]], [bytT[s]])
            P.dma("sync", ysrc[0:128, t0:t0 + 128], ytT[s][:, 0, :], reads=[bytT[s]], writes=[bout])
            P.dma("sync", ysrc[128:192, t0:t0 + 128], ytT[s][0:64, 1, :], reads=[bytT[s]], writes=[bout])
        P.final_wait("sync", [bout]); P.final_wait("gpsimd", [bout])
        P.emit()


def phase_p2a_il(nc, io, l, nt=NT, dbg=9):
    has_vres = l > 0
    tg = "a%d_" % l
    tmA = io["tm_all"]; fm = io["fm_all"]
    rkv_in = tmA[:, 0:576]; mu_in = io["mu_rkv%d" % l]
    wd_in = fm[0:96, :]; ad_in = fm[96:192, :]; gd_in = fm[192:448, :]
    lmu_in = io["lmu%d" % l]; w2_in = io["w2c%d" % l]; a2_in = io["a2c%d" % l]; g2_in = io["g2c%d" % l]
    par_in = io["par%d" % l]; cst_in = io["cst"]; m4_in = io["mask4"]
    if has_vres:
        vd_in = fm[448:512, :]; v2_in = io["v2c"]
    vf_in = io["vfirst"]
    v_out = io["vfirst"]
    ysrc = io["y_src"]
    P = Prog(nc, tg)
    with ExitStack() as es:
        sb = lambda n, shp, d=F32: es.enter_context(nc.sbuf_tensor(tg + n, shp, d))
        PSB = []

        def _x(r, w):
            ids = set(id(b) for b in PSB)
            return [b for b in r if id(b) not in ids], list(w) + [b for b in r if id(b) in ids]
        V = lambda fn, r, w: P.op("vector", fn, *_x(r, w))
        A = lambda fn, r, w: P.op("scalar", fn, *_x(r, w))
        G = lambda fn, r, w: P.op("gpsimd", fn, reads=r, writes=w)
        T = lambda fn, r, w: P.op("tensor", fn, reads=r, writes=w)
        cst = sb("cst", [128, 4, 128]); m4 = sb("m4", [128, 4, 128]); par = sb("par", [128, 8, W3]); mu = sb("mu", [128, 576])
        lmu = sb("lmu", [128, 5]); w2 = sb("w2", [96, W3]); a2 = sb("a2", [96, W3]); g2 = sb("g2", [128, 2, W3])
        ones = sb("ones", [128, 1]); epsg = sb("epsg", [128, 1])
        bcst, bm4, bpar, bmu, blmu, bw2, ba2, bg2, bones, bepsg = (Buf() for _ in range(10))
        P.dma("sync", cst[:], cst_in, writes=[bcst]); P.dma("sync", m4[:], m4_in, writes=[bm4])
        P.dma("sync", par[:], par_in, writes=[bpar]); P.dma("sync", mu[:], mu_in, writes=[bmu])
        P.dma("sync", lmu[:], lmu_in, writes=[blmu]); P.dma("sync", w2[:], w2_in, writes=[bw2])
        P.dma("sync", a2[:], a2_in, writes=[ba2])
        P.dma("sync", g2[:], g2_in.rearrange("(c p) n -> p c n", p=128), writes=[bg2])
        G(lambda e: e.memset(ones[:], 1.0), [], [bones]); G(lambda e: e.memset(epsg[:], A_GN_EPS), [], [bepsg])
        ident, triu, trirev, maskn = cst[:, 0, :], cst[:, 1, :], cst[:, 2, :], cst[:, 3, :]
        if has_vres:
            v2 = sb("v2", [64, W3]); bv2 = Buf()
            P.dma("sync", v2[:], v2_in, writes=[bv2])
        ltmp = sb("ltmp", [128, S]); bltmp = Buf()
        lo = {}
        specs = [("wd", wd_in, 96, 0, AF.Tanh), ("ad", ad_in, 96, 1, None), ("gd0", gd_in[0:128], 128, 2, AF.Sigmoid),
                 ("gd1", gd_in[128:256], 128, 3, AF.Sigmoid)]
        if has_vres:
            specs.append(("vd", vd_in, 64, 4, None))
        for name, src, rows, mcol, fn in specs:
            t = sb("lo_" + name, [rows, S + 1]); b = Buf()
            G(lambda e, t=t: e.memset(t[:, 0:1], 0.0), [], [b])
            for i in range(4):
                P.dma("sync" if i % 2 == 0 else "gpsimd", t[:, 1 + i * 1024:1 + (i + 1) * 1024], src[:, i * 1024:(i + 1) * 1024], writes=[b])
            V(lambda e, t=t, rows=rows: e.tensor_tensor(out=ltmp[:rows, :], in0=t[:, 0:S], in1=t[:, 1:S + 1], op=ALU.subtract), [b, bltmp], [bltmp])
            V(lambda e, t=t, rows=rows, mcol=mcol: e.scalar_tensor_tensor(out=t[:, 1:S + 1], in0=ltmp[:rows, :], scalar=lmu[:rows, mcol:mcol + 1],
                                                                      in1=t[:, 1:S + 1], op0=ALU.mult, op1=ALU.add), [bltmp, blmu, b], [b])
            if fn is not None:
                A(lambda e, t=t, fn=fn: e.activation(out=t[:, 1:S + 1], in_=t[:, 1:S + 1], func=fn), [b], [b])
            lo[name] = (t, b)
        NS = 2
        cur = [sb("cur%d" % i, [128, 576]) for i in range(NS)]; prv = [sb("prv%d" % i, [128, 576]) for i in range(NS)]
        bcur = [Buf() for _ in range(NS)]; bprv = [Buf() for _ in range(NS)]
        tt = [sb("tt%d" % i, [128, 8, W3]) for i in range(NS)]; btt = [Buf() for _ in range(NS)]
        rk = [sb("rk%d" % i, [128, 3]) for i in range(NS)]; brk = [Buf() for _ in range(NS)]
        tm = sb("tm", [128, 4, W3]); btm = Buf()
        ss = sb("ss", [128, 3]); bss = Buf()
        if has_vres:
            vf = [sb("vf%d" % i, [128, W3]) for i in range(NS)]; bvf = [Buf() for _ in range(NS)]
        ee = [sb("ee%d" % i, [128, 4, W3]) for i in range(NS)]; bee = [Buf() for _ in range(NS)]
        dd = [sb("dd%d" % i, [128, 6, W3]) for i in range(NS)]; bdd = [Buf() for _ in range(NS)]
        wc = [sb("wc%d" % i, [64, 4]) for i in range(NS)]; bwc = [Buf() for _ in range(NS)]
        yt = [sb("yt%d" % i, [128, W3]) for i in range(NS)]; byt = [Buf() for _ in range(NS)]
        ytT = [sb("ytT%d" % i, [128, 2, 128], BF16) for i in range(NS)]; bytT = [Buf() for _ in range(NS)]
        NZ = 2
        T4 = [sb("T4_%d" % z, [64, 4, 128]) for z in range(NZ)]; bT4 = [Buf() for _ in range(NZ)]
        M4 = [sb("M4_%d" % z, [128, 4, 128]) for z in range(NZ)]; bM4 = [Buf() for _ in range(NZ)]
        Pk = [[sb("Pk%d_%d" % (z, i), [128, 128]) for i in range(2)] for z in range(NZ)]; bPk = [[Buf(), Buf()] for _ in range(NZ)]
        Qk = [[sb("Qk%d_%d" % (z, i), [128, 128]) for i in range(2)] for z in range(NZ)]; bQk = [[Buf(), Buf()] for _ in range(NZ)]
        Zt = [sb("Z%d" % z, [128, 128]) for z in range(NZ)]; bZ = [Buf() for _ in range(NZ)]
        nX = [sb("nX%d" % z, [128, 64]) for z in range(NZ)]; bnX = [Buf() for _ in range(NZ)]
        Ut = [sb("U%d" % z, [128, 64]) for z in range(NZ)]; bU = [Buf() for _ in range(NZ)]
        st6 = [sb("st6_%d" % z, [128, 6]) for z in range(NZ)]; bst6 = [Buf() for _ in range(NZ)]
        mv = [sb("mv%d" % z, [128, 2]) for z in range(NZ)]; bmv = [Buf() for _ in range(NZ)]
        rsd = [sb("rsd%d" % z, [128, 1]) for z in range(NZ)]; brsd = [Buf() for _ in range(NZ)]
        ST = [sb("ST%d" % h, [64, 64]) for h in range(3)]; bST = [Buf() for _ in range(3)]
        for h in range(3):
            G(lambda e, h=h: e.memset(ST[h][:], 0.0), [], [bST[h]])
        pb = [es.enter_context(nc.psum_tensor(tg + "pb%d" % i, [128, 512], F32)) for i in range(8)]
        pL1, pL2, pGG = pb[0], pb[0], pb[1]
        bL1 = Buf(); bL2 = bL1; byT1 = bL1; byT2 = bL1; bGG = Buf()
        pX = [pb[2], pb[3]]; bX = [Buf(), Buf()]
        pIVs = [pb[4], pb[5]]; bIVs = [Buf(), Buf()]
        pCHs = [pb[6], pb[7]]; bCHs = [Buf(), Buf()]
        PSB.extend([bL1, bGG] + bX + bIVs + bCHs)
        rkv_v = rkv_in
        bout = Buf()
        wdT, bwd = lo["wd"]; adT, bad = lo["ad"]; gd0, bgd0 = lo["gd0"]; gd1, bgd1 = lo["gd1"]

        def prep(n):
            s = n % NS
            t0 = n * 128
            c_, p_, t_ = cur[s], prv[s], tt[s]
            P.dma("sync", c_[:], rkv_v[t0:t0 + 128, :], writes=[bcur[s]])
            if n == 0:
                G(lambda e: e.memset(p_[:], 0.0), [], [bprv[s]])
                P.dma("gpsimd", p_[1:128, :], rkv_v[0:127, :], writes=[bprv[s]])
            else:
                P.dma("gpsimd", p_[:], rkv_v[t0 - 1:t0 + 127, :], writes=[bprv[s]])
            if has_vres:
                P.dma("sync", vf[s][:], vf_in[t0:t0 + 128, :], writes=[bvf[s]])
            V(lambda e: e.tensor_tensor(out=p_[:], in0=p_[:], in1=c_[:], op=ALU.subtract), [bcur[s], bprv[s]], [bprv[s]])
            G(lambda e: e.tensor_tensor(out=p_[:], in0=p_[:], in1=mu[:], op=ALU.mult), [bprv[s], bmu], [bprv[s]])
            V(lambda e: e.tensor_tensor(out=c_[:], in0=c_[:], in1=p_[:], op=ALU.add), [bcur[s], bprv[s]], [bcur[s]])
            r_, k_, v_ = c_[:, 0:W3], c_[:, W3:2 * W3], c_[:, 2 * W3:3 * W3]
            tsl = slice(1 + t0, 1 + t0 + 128)
            T(lambda e: e.matmul(pL1[:, 0:W3], lhsT=wdT[:, tsl], rhs=w2[:], start=True, stop=True), [bwd, bw2], [bL1])
            T(lambda e: e.matmul(pL1[:, W3:2 * W3], lhsT=adT[:, tsl], rhs=a2[:], start=True, stop=True), [bad, ba2], [bL1])
            V(lambda e: e.tensor_tensor(out=t_[:, 5, :], in0=pL1[:, 0:W3], in1=par[:, 0, :], op=ALU.add), [bL1, bpar, btt[s]], [btt[s]])
            A(lambda e: e.activation(out=t_[:, 5, :], in_=t_[:, 5, :], func=AF.Sigmoid), [btt[s]], [btt[s]])
            V(lambda e: e.tensor_scalar(out=t_[:, 5, :], in0=t_[:, 5, :], scalar1=NEG_EHALF, scalar2=None, op0=ALU.mult), [btt[s]], [btt[s]])
            V(lambda e: e.tensor_tensor(out=t_[:, 7, :], in0=pL1[:, W3:2 * W3], in1=par[:, 1, :], op=ALU.add), [bL1, bpar, btt[s]], [btt[s]])
            A(lambda e: e.activation(out=t_[:, 7, :], in_=t_[:, 7, :], func=AF.Sigmoid), [btt[s]], [btt[s]])
            T(lambda e: e.matmul(pL2[:, 0:W3], lhsT=gd0[:, tsl], rhs=g2[:, 0, :], start=True, stop=False), [bgd0, bg2], [bL2])
            T(lambda e: e.matmul(pL2[:, 0:W3], lhsT=gd1[:, tsl], rhs=g2[:, 1, :], start=False, stop=True), [bgd1, bg2], [bL2])
            if has_vres:
                vdT, bvd = lo["vd"]
                T(lambda e: e.matmul(pL2[:, W3:2 * W3], lhsT=vdT[:, tsl], rhs=v2[:], start=True, stop=True), [bvd, bv2], [bL2])
            A(lambda e: e.copy(out=t_[:, 6, :], in_=pL2[:, 0:W3]), [bL2, btt[s]], [btt[s]])
            G(lambda e: e.tensor_copy(out=t_[:, 0, :], in_=r_), [bcur[s], btt[s]], [btt[s]])
            if has_vres:
                V(lambda e: e.tensor_tensor(out=tm[:, 0, :], in0=pL2[:, W3:2 * W3], in1=par[:, 2, :], op=ALU.add), [bL2, bpar, btm], [btm])
                A(lambda e: e.activation(out=tm[:, 0, :], in_=tm[:, 0, :], func=AF.Sigmoid), [btm], [btm])
                V(lambda e: e.tensor_tensor(out=tm[:, 1, :], in0=vf[s][:], in1=v_, op=ALU.subtract), [bvf[s], bcur[s], btm], [btm])
                V(lambda e: e.tensor_tensor(out=tm[:, 1, :], in0=tm[:, 1, :], in1=tm[:, 0, :], op=ALU.mult), [btm], [btm])
                V(lambda e: e.tensor_tensor(out=t_[:, 2, :], in0=tm[:, 1, :], in1=v_, op=ALU.add), [btm, bcur[s], btt[s]], [btt[s]])
            else:
                G(lambda e: e.tensor_copy(out=t_[:, 2, :], in_=v_), [bcur[s], btt[s]], [btt[s]])
                P.dma("sync", v_out[t0:t0 + 128, :], t_[:, 2, :], reads=[btt[s]], writes=[bout])
            V(lambda e: e.tensor_tensor(out=t_[:, 3, :], in0=k_, in1=par[:, 3, :], op=ALU.mult), [bcur[s], bpar, btt[s]], [btt[s]])
            V(lambda e: e.tensor_tensor(out=tm[:, 2, :], in0=t_[:, 3, :], in1=t_[:, 3, :], op=ALU.mult), [btt[s], btm], [btm])
            V(lambda e: e.tensor_reduce(out=ss[:], in_=tm[:, 2, :].rearrange("p (h j) -> p h j", h=3), axis=AX.X, op=ALU.add), [btm, bss], [bss])
            A(lambda e: e.activation(out=ss[:], in_=ss[:], func=AF.Sqrt), [bss], [bss])
            V(lambda e: e.tensor_scalar(out=ss[:], in0=ss[:], scalar1=1e-12, scalar2=None, op0=ALU.max), [bss], [bss])
            V(lambda e: e.reciprocal(out=ss[:], in_=ss[:]), [bss], [bss])
            for h in range(3):
                hs = slice(64 * h, 64 * h + 64)
                V(lambda e, hs=hs, h=h: e.tensor_scalar(out=t_[:, 3, hs], in0=t_[:, 3, hs], scalar1=ss[:, h:h + 1], scalar2=None, op0=ALU.mult),
                  [bss, btt[s]], [btt[s]])
            V(lambda e: e.scalar_tensor_tensor(out=tm[:, 3, :], in0=t_[:, 7, :], scalar=-1.0, in1=par[:, 4, :], op0=ALU.add, op1=ALU.mult),
              [btt[s], bpar, btm], [btm])
            V(lambda e: e.scalar_tensor_tensor(out=t_[:, 1, :], in0=tm[:, 3, :], scalar=1.0, in1=k_, op0=ALU.add, op1=ALU.mult),
              [btm, bcur[s], btt[s]], [btt[s]])
            G(lambda e: e.tensor_tensor(out=t_[:, 4, :], in0=t_[:, 3, :], in1=t_[:, 7, :], op=ALU.mult), [btt[s]], [btt[s]])
            V(lambda e: e.tensor_tensor(out=tm[:, 3, :], in0=t_[:, 0, :], in1=t_[:, 1, :], op=ALU.mult), [btt[s], btm], [btm])
            V(lambda e: e.tensor_tensor(out=tm[:, 3, :], in0=tm[:, 3, :], in1=par[:, 5, :], op=ALU.mult), [btm, bpar], [btm])
            V(lambda e: e.tensor_reduce(out=rk[s][:], in_=tm[:, 3, :].rearrange("p (h j) -> p h j", h=3), axis=AX.X, op=ALU.add), [btm, brk[s]], [brk[s]])
            T(lambda e: e.matmul(pGG[:, 0:W3], lhsT=triu, rhs=t_[:, 5, :], start=True, stop=True), [bcst, btt[s]], [bGG])
            T(lambda e: e.matmul(pGG[:, W3:2 * W3], lhsT=trirev, rhs=t_[:, 5, :], start=True, stop=True), [bcst, btt[s]], [bGG])
            for h in range(3):
                T(lambda e, h=h: e.matmul(pGG[0:64, 400 + h:401 + h], lhsT=t_[:, 5, 64 * h:64 * h + 64], rhs=ones[:], start=True, stop=True),
                  [btt[s], bones], [bGG])
            e_, d_ = ee[s], dd[s]
            A(lambda e: e.activation(out=e_[:, 0, :], in_=pGG[:, 0:W3], func=AF.Exp), [bGG, bee[s]], [bee[s]])
            A(lambda e: e.activation(out=e_[:, 1, :], in_=pGG[:, 0:W3], func=AF.Exp, scale=-1.0), [bGG, bee[s]], [bee[s]])
            V(lambda e: e.tensor_tensor(out=e_[:, 2, :], in0=pGG[:, 0:W3], in1=t_[:, 5, :], op=ALU.subtract), [bGG, btt[s], bee[s]], [bee[s]])
            A(lambda e: e.activation(out=e_[:, 2, :], in_=e_[:, 2, :], func=AF.Exp), [bee[s]], [bee[s]])
            A(lambda e: e.activation(out=e_[:, 3, :], in_=pGG[:, W3:2 * W3], func=AF.Exp), [bGG, bee[s]], [bee[s]])
            A(lambda e: e.activation(out=wc[s][:, 0:3], in_=pGG[0:64, 400:403], func=AF.Exp), [bGG, bwc[s]], [bwc[s]])
            for (di, ti, ei, eng) in ((0, 0, 0, V), (1, 3, 2, G), (2, 1, 1, V), (3, 4, 1, G), (4, 1, 3, V), (5, 4, 3, G)):
                eng(lambda e, di=di, ti=ti, ei=ei: e.tensor_tensor(out=d_[:, di, :], in0=t_[:, ti, :], in1=e_[:, ei, :], op=ALU.mult),
                    [btt[s], bee[s], bdd[s]], [bdd[s]])

        def head_gen(n, h, z):
            s = n % NS
            t_, d_ = tt[s], dd[s]
            hs = slice(64 * h, 64 * h + 64)
            X, IV = pX[z], pIVs[z]
            CH = pCHs[z][:, 0:256]
            CH64 = pCHs[z][0:64, 0:256]
            bXz, bIV, bCH = bX[z], bIVs[z], bCHs[z]
            t4, m4_, Z, nx, U = T4[z], M4[z], Zt[z], nX[z], Ut[z]
            for j in range(4):
                T(lambda e, j=j: e.transpose(X[0:64, j * 128:(j + 1) * 128], d_[:, j, hs], ident), [bdd[s], bcst], [bXz])
                yield
            A(lambda e: e.copy(out=t4[:], in_=X[0:64, :].rearrange("p (a b) -> p a b", a=4)), [bXz, bT4[z]], [bT4[z]])
            yield
            T(lambda e: e.matmul(X[:, 0:256], lhsT=t4[:, 2, :], rhs=t4[:, 0:2, :].rearrange("p a b -> p (a b)"), start=True, stop=True), [bT4[z]], [bXz])
            T(lambda e: e.matmul(X[:, 256:512], lhsT=t4[:, 3, :], rhs=t4[:, 0:2, :].rearrange("p a b -> p (a b)"), start=True, stop=True), [bT4[z]], [bXz])
            T(lambda e: e.matmul(IV[:, 0:128], lhsT=t4[:, 1, :], rhs=t4[:, 3, :], start=True, stop=True), [bT4[z]], [bIV])
            yield
            V(lambda e: e.tensor_tensor(out=m4_[:].rearrange("p a b -> p (a b)"), in0=X[:], in1=m4[:].rearrange("p a b -> p (a b)"), op=ALU.mult),
              [bXz, bm4, bM4[z]], [bM4[z]])
            V(lambda e: e.tensor_tensor(out=Pk[z][0][:], in0=IV[:, 0:128], in1=maskn, op=ALU.mult), [bIV, bcst, bPk[z][0]], [bPk[z][0]])
            yield
            A(lambda e: e.copy(out=Qk[z][0][:], in_=m4_[:, 3, :]), [bM4[z], bQk[z][0]], [bQk[z][0]])
            V(lambda e: e.tensor_tensor(out=Z[:], in0=ident, in1=m4_[:, 3, :], op=ALU.subtract), [bM4[z], bcst, bZ[z]], [bZ[z]])
            yield
            for lv in range(1, 7):
                pi, ci = (lv - 1) % 2, lv % 2
                if lv < 6:
                    T(lambda e, pi=pi: e.matmul(IV[:, 128:256], lhsT=Pk[z][pi][:], rhs=Qk[z][pi][:], start=True, stop=True), [bPk[z][pi], bQk[z][pi]], [bIV])
                T(lambda e, pi=pi: e.matmul(IV[:, 256:384], lhsT=Qk[z][pi][:], rhs=Pk[z][pi][:], start=True, stop=True), [bPk[z][pi], bQk[z][pi]], [bIV])
                yield
                if lv < 6:
                    A(lambda e, ci=ci: e.copy(out=Qk[z][ci][:], in_=IV[:, 128:256]), [bIV, bQk[z][ci]], [bQk[z][ci]])
                A(lambda e, ci=ci: e.copy(out=Pk[z][ci][:], in_=IV[:, 256:384]), [bIV, bPk[z][ci]], [bPk[z][ci]])
                yield
                T(lambda e, ci=ci: e.matmul(IV[:, 384:512], lhsT=Pk[z][ci][:], rhs=Z[:], start=True, stop=True), [bPk[z][ci], bZ[z]], [bIV])
                yield
                V(lambda e: e.tensor_tensor(out=Z[:], in0=Z[:], in1=IV[:, 384:512], op=ALU.add), [bIV, bZ[z]], [bZ[z]])
                yield
            vh = t_[:, 2, hs]
            T(lambda e: e.matmul(CH[:, 0:64], lhsT=m4_[:, 1, :], rhs=vh, start=True, stop=False), [bM4[z], btt[s]], [bCH])
            T(lambda e: e.matmul(CH[:, 0:64], lhsT=t4[:, 1, :], rhs=ST[h][:], start=False, stop=True), [bT4[z], bST[h]], [bCH])
            yield
            A(lambda e: e.activation(out=nx[:], in_=CH[:, 0:64], func=AF.Copy, scale=-1.0), [bCH, bnX[z]], [bnX[z]])
            yield
            T(lambda e: e.matmul(CH[:, 64:128], lhsT=Z[:], rhs=nx[:], start=True, stop=True), [bZ[z], bnX[z]], [bCH])
            yield
            V(lambda e: e.tensor_scalar(out=U[:], in0=CH[:, 64:128], scalar1=1.0, scalar2=None, op0=ALU.mult), [bCH, bU[z]], [bU[z]])
            yield
            T(lambda e: e.matmul(CH[:, 128:192], lhsT=t4[:, 0, :], rhs=ST[h][:], start=True, stop=False), [bT4[z], bST[h]], [bCH])
            T(lambda e: e.matmul(CH[:, 128:192], lhsT=m4_[:, 2, :], rhs=U[:], start=False, stop=False), [bM4[z], bU[z]], [bCH])
            T(lambda e: e.matmul(CH[:, 128:192], lhsT=m4_[:, 0, :], rhs=vh, start=False, stop=True), [bM4[z], btt[s]], [bCH])
            T(lambda e: e.matmul(CH64[:, 192:256], lhsT=d_[:, 5, hs], rhs=U[:], start=True, stop=False), [bdd[s], bU[z]], [bCH])
            T(lambda e: e.matmul(CH64[:, 192:256], lhsT=d_[:, 4, hs], rhs=vh, start=False, stop=True), [bdd[s], btt[s]], [bCH])
            yield
            V(lambda e: e.scalar_tensor_tensor(out=ST[h][:], in0=ST[h][:], scalar=wc[s][:, h:h + 1], in1=CH64[:, 192:256], op0=ALU.mult, op1=ALU.add),
              [bCH, bwc[s], bST[h]], [bST[h]])
            V(lambda e: e.bn_stats(out=st6[z][:], in_=CH[:, 128:192]), [bCH, bst6[z]], [bst6[z]])
            yield
            V(lambda e: e.bn_aggr(out=mv[z][:], in_=st6[z][:]), [bst6[z], bmv[z]], [bmv[z]])
            yield
            A(lambda e: e.activation(out=rsd[z][:], in_=mv[z][:, 1:2], func=AF.Sqrt, bias=epsg[:, 0:1]), [bmv[z], bepsg, brsd[z]], [brsd[z]])
            yield
            V(lambda e: e.reciprocal(out=rsd[z][:], in_=rsd[z][:]), [brsd[z]], [brsd[z]])
            yield
            V(lambda e: e.tensor_scalar(out=yt[s][:, hs], in0=CH[:, 128:192], scalar1=mv[z][:, 0:1], scalar2=rsd[z][:, 0:1], op0=ALU.subtract, op1=ALU.mult),
              [bCH, bmv[z], brsd[z], byt[s]], [byt[s]])
            yield

        def post(n):
            s = n % NS
            t0 = n * 128
            t_ = tt[s]
            G(lambda e: e.tensor_tensor(out=yt[s][:], in0=yt[s][:], in1=par[:, 6, :], op=ALU.mult), [byt[s], bpar], [byt[s]])
            G(lambda e: e.tensor_tensor(out=yt[s][:], in0=yt[s][:], in1=par[:, 7, :], op=ALU.add), [byt[s], bpar], [byt[s]])
            for h in range(3):
                hs = slice(64 * h, 64 * h + 64)
                V(lambda e, hs=hs, h=h: e.scalar_tensor_tensor(out=yt[s][:, hs], in0=t_[:, 2, hs], scalar=rk[s][:, h:h + 1], in1=yt[s][:, hs],
                                                                op0=ALU.mult, op1=ALU.add), [btt[s], brk[s], byt[s]], [byt[s]])
            V(lambda e: e.tensor_tensor(out=yt[s][:], in0=yt[s][:], in1=t_[:, 6, :], op=ALU.mult), [byt[s], btt[s]], [byt[s]])
            T(lambda e: e.transpose(pL1[:, 384:512], yt[s][:, 0:128], ident), [byt[s], bcst], [byT1])
            A(lambda e: e.copy(out=ytT[s][:, 0, :], in_=pL1[:, 384:512]), [byT1, bytT[s]], [bytT[s]])
            T(lambda e: e.transpose(pL1[0:64, 384:512], yt[s][:, 128:192], ident), [byt[s], bcst], [byT1])
            A(lambda e: e.copy(out=ytT[s][0:64, 1, :], in_=pL1[0:64, 384:512]), [byT1, bytT[s]], [bytT[s]])
            P.dma("sync", ysrc[0:128, t0:t0 + 128], ytT[s][:, 0, :], reads=[bytT[s]], writes=[bout])
            P.dma("sync", ysrc[128:192, t0:t0 + 128], ytT[s][0:64, 1, :], reads=[bytT[s]], writes=[bout])

        jobs = [(n, h) for n in range(nt) for h in range(3)]
        prepped = [-1]
        done = {}
        for p0 in range(0, len(jobs), NZ):
            pair = jobs[p0:p0 + NZ]
            for (n, h) in pair:
                while prepped[0] < n:
                    prepped[0] += 1
                    prep(prepped[0])
            gens = [head_gen(n, h, z) for z, (n, h) in enumerate(pair)]
            while gens:
                for g_ in list(gens):
                    try:
                        next(g_)
                    except StopIteration:
                        gens.remove(g_)
            for (n, h) in pair:
                done[n] = done.get(n, 0) + 1
                if done[n] == 3:
                    post(n)
        P.final_wait("sync", [bout]); P.final_wait("gpsimd", [bout])
        P.emit()


def p2a_consts():
    i = np.arange(128)
    row, col = i[:, None], i[None, :]
    cst = np.stack([np.eye(128), (row <= col), (row > col), (row > col)], axis=1).astype(np.float32)
    incl = (row <= col).astype(np.float32); strict = (row < col).astype(np.float32)
    mask4 = np.stack([incl, strict, incl, strict], axis=1).astype(np.float32)
    return dict(cst=np.ascontiguousarray(cst), mask4=np.ascontiguousarray(mask4))


NTM = 1344
NFM = 768
NY = 576
GROUPS = [[0, 1, 2, 3], [4, 5, 6, 7]]


def phase_norm(nc, io, l, x_ap):
    tg = "n%d_" % l
    P = Prog(nc, tg)
    with ExitStack() as es:
        xs = es.enter_context(nc.sbuf_tensor(tg + "xs", [128, KC, TOK], F32))
        bxs, bhs = Buf(), Buf()
        xv = x_ap.rearrange("(c p) t -> p c t", p=128)
        for i in range(4):
            P.dma("sync", xs[:, i * 4:(i + 1) * 4, :], xv[:, i * 4:(i + 1) * 4, :], writes=[bxs])
        hT, bh = rms_to_bf16(P, nc, es, xs, bxs, io["g1_%d" % l], tg)
        hv = io["h_src"].rearrange("(c p) t -> p c t", p=128)
        for i in range(4):
            P.dma("sync", hv[:, i * 4:(i + 1) * 4, :], hT[:, i * 4:(i + 1) * 4, :], reads=[bh], writes=[bhs])
        P.final_wait("sync", [bhs]); P.final_wait("gpsimd", [bhs])
        P.emit()


def phase_inproj(nc, io, l):
    tg = "p%d_" % l
    has_vres = l > 0
    P = Prog(nc, tg)
    h_src, h_all, tm_all, fm_all = io["h_src_t"], io["h_all_t"], io["tm_all"], io["fm_all"]
    wtm_ap, wfm_ap = io["wtm%d" % l], io["wfm%d" % l]
    with ExitStack() as es:
        sb = lambda n, shp, d=F32: es.enter_context(nc.sbuf_tensor(tg + n, shp, d))
        bhall = Buf()
        for kk in range(4):
            P.cc(lambda e, kk=kk: e.collective_compute("AllGather", mybir.AluOpType.bypass, replica_groups=GROUPS,
                                                       ins=[h_src.ap()[kk * 512:(kk + 1) * 512, :]], outs=[h_all.ap()[kk * 2048:(kk + 1) * 2048, :]]), writes=[bhall])
        wtm = sb("wtm", [128, KC, NTM], BF16); wfm = sb("wfm", [128, KC, NFM], BF16)
        bwtm, bwfm = Buf(), Buf()
        stg = [sb("stg%d" % i, [128, NTM], F32) for i in range(2)]
        bstg = [Buf(), Buf()]
        k = 0
        for (w_ap, wt, bw, n) in ((wtm_ap, wtm, bwtm, NTM), (wfm_ap, wfm, bwfm, NFM)):
            for c in range(KC):
                s = k % 2
                k += 1
                P.dma("sync" if s == 0 else "gpsimd", stg[s][:, :n], w_ap[c * 128:(c + 1) * 128, :], writes=[bstg[s]])
                if s == 0:
                    P.op("gpsimd", lambda e, s=s, wt=wt, c=c, n=n: e.tensor_copy(out=wt[:, c, :], in_=stg[s][:, :n]), reads=[bstg[s]], writes=[bw])
                else:
                    P.op("scalar", lambda e, s=s, wt=wt, c=c, n=n: e.copy(out=wt[:, c, :], in_=stg[s][:, :n]), reads=[bstg[s]], writes=[bw])
        hT = [sb("hT%d" % i, [128, KC, TOK], BF16) for i in range(2)]
        bhT = [Buf(), Buf()]
        otm = [sb("otm%d" % i, [128, NTM], F32) for i in range(2)]
        botm = [Buf(), Buf()]
        ofm = [sb("ofm%d" % i, [128, 512], F32) for i in range(4)]
        bofm = [Buf() for _ in range(4)]
        pt = [es.enter_context(nc.psum_tensor(tg + "pt%d" % i, [128, 512], F32)) for i in range(6)]
        bpt = [Buf() for _ in range(6)]
        hav = h_all.ap()
        btm, bfm = Buf(), Buf()
        fm_blocks = [(0, 96), (96, 192), (192, 320), (320, 448)] + ([(448, 512)] if has_vres else []) + [(512, 640), (640, 768)]
        tm_blocks = [(0, 512), (512, 1024), (1024, NTM)]
        pk = 0
        fk = 0
        for r in range(4):
            hs = r % 2
            for i in range(4):
                hv = hav[i * 2048 + r * 512:i * 2048 + (r + 1) * 512, :].rearrange("(c p) t -> p c t", p=128)
                P.dma("sync" if i % 2 == 0 else "gpsimd", hT[hs][:, i * 4:(i + 1) * 4, :], hv, reads=[bhall], writes=[bhT[hs]])
            for tl in range(8):
                os_ = (r * 8 + tl) % 2
                tsl = slice(tl * 128, (tl + 1) * 128)
                for bi, (c0, c1) in enumerate(tm_blocks):
                    q = pk % 6
                    pk += 1
                    for c in range(KC):
                        P.op("tensor", lambda e, q=q, c=c, hs=hs, tsl=tsl, c0=c0, c1=c1: e.matmul(pt[q][:, 0:c1 - c0], lhsT=hT[hs][:, c, tsl], rhs=wtm[:, c, c0:c1],
                                                                                                  start=(c == 0), stop=(c == KC - 1)),
                             reads=[bhT[hs], bwtm], writes=[bpt[q]])
                    if bi % 2 == 0:
                        P.op("scalar", lambda e, q=q, os_=os_, c0=c0, c1=c1: e.copy(out=otm[os_][:, c0:c1], in_=pt[q][:, 0:c1 - c0]), reads=[bpt[q]], writes=[botm[os_]])
                    else:
                        P.op("vector", lambda e, q=q, os_=os_, c0=c0, c1=c1: e.tensor_scalar(out=otm[os_][:, c0:c1], in0=pt[q][:, 0:c1 - c0], scalar1=1.0, scalar2=None, op0=ALU.mult),
                             reads=[bpt[q]], writes=[botm[os_]])
                t0 = r * 1024 + tl * 128
                P.dma("sync", tm_all[t0:t0 + 128, :], otm[os_][:], reads=[botm[os_]], writes=[btm])
            for (b0, b1) in fm_blocks:
                mw = b1 - b0
                for hf in range(2):
                    q = pk % 6
                    pk += 1
                    fs = fk % 4
                    fk += 1
                    sl = slice(hf * 512, (hf + 1) * 512)
                    for c in range(KC):
                        P.op("tensor", lambda e, q=q, c=c, hs=hs, sl=sl, b0=b0, b1=b1, mw=mw: e.matmul(pt[q][:mw, :], lhsT=wfm[:, c, b0:b1], rhs=hT[hs][:, c, sl],
                                                                                                       start=(c == 0), stop=(c == KC - 1)),
                             reads=[bhT[hs], bwfm], writes=[bpt[q]])
                    if fk % 2 == 0:
                        P.op("scalar", lambda e, q=q, fs=fs, mw=mw: e.copy(out=ofm[fs][:mw, :], in_=pt[q][:mw, :]), reads=[bpt[q]], writes=[bofm[fs]])
                    else:
                        P.op("vector", lambda e, q=q, fs=fs, mw=mw: e.tensor_scalar(out=ofm[fs][:mw, :], in0=pt[q][:mw, :], scalar1=1.0, scalar2=None, op0=ALU.mult),
                             reads=[bpt[q]], writes=[bofm[fs]])
                    P.dma("gpsimd", fm_all[b0:b1, r * 1024 + hf * 512:r * 1024 + (hf + 1) * 512], ofm[fs][:mw, :], reads=[bofm[fs]], writes=[bfm])
        P.final_wait("sync", [btm, bfm]); P.final_wait("gpsimd", [btm, bfm])
        P.emit()


DFF = 5632
NFB = DFF // 128
GRP = 4
KY = 18


def phase_p3(nc, io, l, x_ap, out_ap, final):
    tg = "f%d_" % l
    y_src, y_all = io["y_src_t"], io["y_all_t"]
    wout, g2, wg, wu, wd = io["wout%d" % l], io["g2_%d" % l], io["wg%d" % l], io["wu%d" % l], io["wd%d" % l]
    P = Prog(nc, tg)
    with ExitStack() as es:
        sb = lambda n, shp, dt: es.enter_context(nc.sbuf_tensor(tg + n, shp, dt))
        pst = lambda n: es.enter_context(nc.psum_tensor(tg + n, [128, 512], F32))
        byall = Buf()
        for kk in range(6):
            P.cc(lambda e, kk=kk: e.collective_compute("AllGather", mybir.AluOpType.bypass, replica_groups=GROUPS,
                                                       ins=[y_src.ap()[kk * 96:(kk + 1) * 96, :]], outs=[y_all.ap()[kk * 384:(kk + 1) * 384, :]]), writes=[byall])
        xs = sb("xs", [128, KC, TOK], F32)
        hT = sb("hT", [128, KY, TOK], BF16)
        bxs, bh = Buf(), Buf()
        NST = 4
        stg = [sb("stg%d" % i, [128, 2048], F32) for i in range(NST)]
        wbf = [sb("wbf%d" % i, [128, 2048], BF16) for i in range(NST)]
        bstg = [Buf() for _ in range(NST)]
        bwbf = [Buf() for _ in range(NST)]
        act = [sb("act%d" % i, [128, GRP, TOK], BF16) for i in range(2)]
        bact = [Buf(), Buf()]
        sg = [sb("sg%d" % i, [128, 512], F32) for i in range(2)]
        bsg = [Buf(), Buf()]
        sel = sb("sel", [128, 4], F32); bsel = Buf()
        pg = [pst("pg%d" % i) for i in range(2)]
        pu = [pst("pu%d" % i) for i in range(2)]
        pd = [pst("pd%d" % i) for i in range(2)]
        bpg, bpu, bpd = [Buf(), Buf()], [Buf(), Buf()], [Buf(), Buf()]
        xv = x_ap.rearrange("(c p) t -> p c t", p=128)
        for i in range(4):
            P.dma("sync", xs[:, i * 4:(i + 1) * 4, :], xv[:, i * 4:(i + 1) * 4, :], writes=[bxs])
        P.dma("sync", sel[:], io["sel"], writes=[bsel])
        st = [0]

        def load_cast(src_ap, view3=None):
            s = st[0] % NST
            st[0] += 1
            dst = stg[s][:] if view3 is None else stg[s][:].rearrange(view3[0], **view3[1])
            P.dma("sync", dst, src_ap, writes=[bstg[s]])
            P.op("gpsimd", lambda e, s=s: e.tensor_copy(out=wbf[s][:], in_=stg[s][:]), reads=[bstg[s]], writes=[bwbf[s]])
            return wbf[s], bwbf[s]

        yst = [stg[i][:].bitcast(BF16) for i in range(2)]
        yav = y_all.ap()
        for c in range(KY):
            s = c % 2
            P.dma("sync" if s == 0 else "gpsimd", yst[s], yav[c * 128:(c + 1) * 128, :], reads=[byall], writes=[bstg[s]])
            P.op("vector", lambda e, c=c, s=s: e.tensor_scalar(out=hT[:, c, :], in0=yst[s][:, 0:TOK], scalar1=sel[:, 0:1], scalar2=None, op0=ALU.mult),
                 reads=[bstg[s], bsel], writes=[bh])
            for gi in range(1, 4):
                P.op("vector", lambda e, c=c, s=s, gi=gi: e.scalar_tensor_tensor(out=hT[:, c, :], in0=yst[s][:, gi * TOK:(gi + 1) * TOK], scalar=sel[:, gi:gi + 1],
                                                                                 in1=hT[:, c, :], op0=ALU.mult, op1=ALU.add),
                     reads=[bstg[s], bsel, bh], writes=[bh])
        st[0] = 2
        wov = wout.rearrange("(c p) n -> p c n", p=128)
        k = 0
        for m in range(KC):
            t, b = load_cast(wov[:, 0:KC, m * 128:(m + 1) * 128], ("p (c n) -> p c n", dict(c=KC)))
            tv = t[:].rearrange("p (c n) -> p c n", c=KC)
            s2 = st[0] % NST
            st[0] += 1
            P.dma("sync", stg[s2][:, 0:256].rearrange("p (c n) -> p c n", c=2), wov[:, KC:KY, m * 128:(m + 1) * 128], writes=[bstg[s2]])
            P.op("gpsimd", lambda e, s2=s2: e.tensor_copy(out=wbf[s2][:, 0:256], in_=stg[s2][:, 0:256]), reads=[bstg[s2]], writes=[bwbf[s2]])
            t2v = wbf[s2][:, 0:256].rearrange("p (c n) -> p c n", c=2)
            b2 = bwbf[s2]
            for hf in range(2):
                q = k % 2
                k += 1
                sl = slice(hf * 512, (hf + 1) * 512)
                for c in range(KY):
                    lt = tv[:, c, :] if c < KC else t2v[:, c - KC, :]
                    P.op("tensor", lambda e, lt=lt, c=c, q=q, sl=sl: e.matmul(pd[q][:], lhsT=lt, rhs=hT[:, c, sl], start=(c == 0), stop=(c == KY - 1)),
                         reads=[b, b2, bh], writes=[bpd[q]])
                P.op("vector", lambda e, m=m, q=q, sl=sl: e.tensor_tensor(out=xs[:, m, sl], in0=xs[:, m, sl], in1=pd[q][:], op=ALU.add),
                     reads=[bpd[q], bxs], writes=[bxs])
        scr = {}
        rms_to_bf16(P, nc, es, xs, bxs, g2, tg + "n2", out_tile=hT, out_buf=bh, scr=scr)
        wgv = wg.rearrange("(c p) n -> p c n", p=128)
        wuv = wu.rearrange("(c p) n -> p c n", p=128)
        for grp in range(NFB // GRP):
            a = act[grp % 2]
            ba = bact[grp % 2]
            wds = []
            for j in range(GRP):
                fb = grp * GRP + j
                tg_, bg_ = load_cast(wgv[:, :, fb * 128:(fb + 1) * 128], ("p (c n) -> p c n", dict(c=KC)))
                tu, bu_ = load_cast(wuv[:, :, fb * 128:(fb + 1) * 128], ("p (c n) -> p c n", dict(c=KC)))
                tgv = tg_[:].rearrange("p (c n) -> p c n", c=KC)
                tuv = tu[:].rearrange("p (c n) -> p c n", c=KC)
                for hf in range(2):
                    sl = slice(hf * 512, (hf + 1) * 512)
                    for c in range(KC):
                        P.op("tensor", lambda e, tgv=tgv, c=c, hf=hf, sl=sl: e.matmul(pg[hf][:], lhsT=tgv[:, c, :], rhs=hT[:, c, sl],
                                                                                   start=(c == 0), stop=(c == KC - 1)),
                             reads=[bg_, bh], writes=[bpg[hf]])
                    for c in range(KC):
                        P.op("tensor", lambda e, tuv=tuv, c=c, hf=hf, sl=sl: e.matmul(pu[hf][:], lhsT=tuv[:, c, :], rhs=hT[:, c, sl],
                                                                                   start=(c == 0), stop=(c == KC - 1)),
                             reads=[bu_, bh], writes=[bpu[hf]])
                    P.op("scalar", lambda e, hf=hf: e.activation(out=sg[hf][:], in_=pg[hf][:], func=AF.Silu),
                         reads=[bpg[hf]], writes=[bsg[hf]])
                    P.op("vector", lambda e, hf=hf, a=a, j=j, sl=sl: e.tensor_tensor(out=a[:, j, sl], in0=sg[hf][:], in1=pu[hf][:], op=ALU.mult),
                         reads=[bsg[hf], bpu[hf]], writes=[ba])
            for j in range(GRP):
                fb = grp * GRP + j
                wds.append(load_cast(wd[fb * 128:(fb + 1) * 128, :]))
            for m in range(KC):
                for hf in range(2):
                    q = k % 2
                    k += 1
                    sl = slice(hf * 512, (hf + 1) * 512)
                    for j in range(GRP):
                        td, bd_ = wds[j]
                        P.op("tensor", lambda e, td=td, j=j, m=m, q=q, sl=sl, a=a: e.matmul(pd[q][:], lhsT=td[:, m * 128:(m + 1) * 128], rhs=a[:, j, sl],
                                                                                         start=(j == 0), stop=(j == GRP - 1)),
                             reads=[bd_, ba], writes=[bpd[q]])
                    P.op("vector", lambda e, m=m, q=q, sl=sl: e.tensor_tensor(out=xs[:, m, sl], in0=xs[:, m, sl], in1=pd[q][:], op=ALU.add),
                         reads=[bpd[q], bxs], writes=[bxs])
        bout = Buf()
        ov = out_ap.rearrange("(c p) t -> p c t", p=128)
        if final:
            rms_to_bf16(P, nc, es, xs, bxs, io["gf"], tg + "n3", out_tile=xs, out_buf=bxs, scr=scr)
        for i in range(4):
            P.dma("sync", ov[:, i * 4:(i + 1) * 4, :], xs[:, i * 4:(i + 1) * 4, :], reads=[bxs], writes=[bout])
        P.final_wait("sync", [bout]); P.final_wait("gpsimd", [bout])
        P.emit()


def build_fused(upto=99, dbg=None, only=None):
    nc = bass.Bass("TRN2", target_bir_lowering=False)
    _POOL[0] = SemPool(nc)
    io = {}
    ext = lambda n, shp, d=F32: io.__setitem__(n, nc.dram_tensor(n, shp, d, kind="ExternalInput").ap())
    ext("xT", [D, TOK]); ext("sel", [128, 4]); ext("gf", [128, KC])
    ext("cst", [128, 4, 128]); ext("mask4", [128, 4, 128])
    ext("pos", [128, NT], I32); ext("invf", [128, 32]); ext("ident", [128, 128]); ext("maskT", [2, 128, 128]); ext("qdec", [2, 64, 128])
    ext("kdec", [128, 2]); ext("g128", [64, 2]); ext("v2c", [64, W3])
    for l in range(2):
        ext("g1_%d" % l, [128, KC]); ext("wtm%d" % l, [D, NTM]); ext("wfm%d" % l, [D, NFM])
        ext("mu_rkv%d" % l, [128, 576]); ext("lmu%d" % l, [128, 5]); ext("w2c%d" % l, [96, W3]); ext("a2c%d" % l, [96, W3])
        ext("g2c%d" % l, [256, W3]); ext("par%d" % l, [128, 8, W3])
        ext("sm%d" % l, [128, 8]); ext("wa%d" % l, [2, 64, 64]); ext("wx%d" % l, [2, 64, 64])
        ext("gng%d" % l, [2, 128, 128])
        if upto > 6 * l + 5:
            ext("wout%d" % l, [KY * 128, D]); ext("g2_%d" % l, [128, KC]); ext("wg%d" % l, [D, DFF]); ext("wu%d" % l, [D, DFF]); ext("wd%d" % l, [DFF, D])
    out = nc.dram_tensor("outT", [D, TOK], F32, kind="ExternalOutput").ap()
    io["h_src_t"] = nc.dram_tensor("h_src", [D, TOK], BF16); io["h_src"] = io["h_src_t"].ap()
    io["h_all_t"] = nc.dram_tensor("h_all", [4 * D, TOK], BF16)
    io["tm_all"] = nc.dram_tensor("tm_all", [S, NTM], F32).ap()
    io["fm_all"] = nc.dram_tensor("fm_all", [NFM, S], F32).ap()
    io["y_src_t"] = nc.dram_tensor("y_src", [NY, S], BF16); io["y_src"] = io["y_src_t"].ap()
    io["y_all_t"] = nc.dram_tensor("y_all", [4 * NY, S], BF16)
    io["vfirst"] = nc.dram_tensor("vfirst", [S, W3], F32).ap()
    x_cur = nc.dram_tensor("x_cur", [D, TOK], F32).ap()
    io["x_cur"] = x_cur
    ph = 0
    for l in range(2):
        steps = [lambda l=l: phase_norm(nc, io, l, io["xT"] if l == 0 else x_cur),
                 lambda l=l: phase_inproj(nc, io, l),
                 lambda l=l: phase_p2a_il(nc, io, l),
                 lambda l=l: phase_p2b(nc, io, l),
                 lambda l=l: phase_p2c(nc, io, l),
                 lambda l=l: phase_p3(nc, io, l, io["xT"] if l == 0 else x_cur, out if l == 1 else x_cur, final=(l == 1))]
        for st_ in steps:
            if ph < upto and (only is None or ph in only):
                st_()
            ph += 1
    if dbg is not None:
        src_ap = io[dbg]
        dout = nc.dram_tensor("dbg", list(src_ap.shape), src_ap.dtype, kind="ExternalOutput").ap()
        P = Prog(nc, "dbg_")
        b = Buf()
        rows = src_ap.shape[0]
        stp = max(1, rows // 8)
        for r0 in range(0, rows, stp):
            P.dma("sync", dout[r0:r0 + stp], src_ap[r0:r0 + stp], writes=[b])
        P.final_wait("sync", [b])
        P.emit()
    _POOL[0].close()
    _POOL[0] = None
    return nc


def core_inputs(d, c):
    b, q = c // 4, c % 4
    f32 = lambda a: np.ascontiguousarray(np.asarray(a, dtype=np.float32))
    lay = lambda g: f32(g.reshape(16, 128).T)
    rep = lambda v: f32(np.broadcast_to(np.asarray(v, np.float32), (128, v.shape[-1])))
    xf = d["x"].reshape(-1, D)
    im = {"xT": f32(xf[c * 1024:(c + 1) * 1024].T), "gf": lay(d["final_norm_g"])}
    sel = np.zeros((128, 4), np.float32); sel[:, q] = 1.0
    im["sel"] = sel
    im.update(p2a_consts())
    heads = C_SLOTS[q]
    im.update(p2c_consts(heads))
    im["pos"] = np.ascontiguousarray(d["positions"][b].reshape(32, 128).T.astype(np.int32))
    cA = slice(192 * q, 192 * q + 192)
    cols_rkv = np.r_[192 * q:192 * q + 192, 768 + 192 * q:768 + 192 * q + 192, 1536 + 192 * q:1536 + 192 * q + 192]
    c0 = 2752 + 1024
    cols_c = np.concatenate([np.r_[c0 + 64 * h:c0 + 64 * h + 64] for h in heads] + [np.r_[c0 + 384 + 64 * h:c0 + 384 + 64 * h + 64] for h in heads]
                            + [np.r_[c0 + 768 + 128 * h:c0 + 768 + 128 * h + 128] for h in heads]
                            + [np.r_[c0 + 1536 + 128 * h:c0 + 1536 + 128 * h + 128] for h in heads])
    im["v2c"] = f32(d["rwkv_v2"][0][:, cA])
    perm = np.full(KY * 128, -1, np.int64)
    for r in range(4):
        loc = np.full(NY, -1, np.int64)
        loc[0:192] = np.r_[192 * r:192 * r + 192]
        loc[192:320] = 768 + np.r_[128 * r:128 * r + 128]
        if r < 3:
            for s_, h in enumerate(C_SLOTS[r]):
                loc[320 + 128 * s_:448 + 128 * s_] = 1280 + np.r_[128 * h:128 * h + 128]
        for kk in range(6):
            perm[kk * 384 + r * 96:kk * 384 + (r + 1) * 96] = loc[96 * kk:96 * (kk + 1)]
    for l in range(2):
        w = d["w_in"][l]
        mu = d["tshift_mu"][l]
        im["g1_%d" % l] = lay(d["norm1_g"][l])
        im["wtm%d" % l] = f32(w[:, np.concatenate([cols_rkv, cols_c])])
        wfm = np.zeros((D, NFM), np.float32)
        wfm[:, 0:448] = w[:, 2304:2752]
        if l > 0:
            wfm[:, 448:512] = d["w_in_vres"][l - 1]
        A0 = 2752
        wfm[:, 512:640] = w[:, A0 + 128 * q:A0 + 128 * q + 128]
        wfm[:, 640:768] = w[:, A0 + 512 + 128 * q:A0 + 512 + 128 * q + 128]
        im["wfm%d" % l] = wfm
        im["mu_rkv%d" % l] = rep(mu[cols_rkv])
        lmu = np.zeros((128, 5), np.float32)
        lmu[:96, 0] = mu[2304:2400]; lmu[:96, 1] = mu[2400:2496]; lmu[:, 2] = mu[2496:2624]; lmu[:, 3] = mu[2624:2752]
        if l > 0:
            lmu[:64, 4] = d["tshift_mu_vres"][l - 1]
        im["lmu%d" % l] = lmu
        im["w2c%d" % l] = f32(d["rwkv_w2"][l][:, cA]); im["a2c%d" % l] = f32(d["rwkv_a2"][l][:, cA]); im["g2c%d" % l] = f32(d["rwkv_g2"][l][:, cA])
        v0 = d["rwkv_v0"][l - 1][cA] if l > 0 else np.zeros(192, np.float32)
        par = np.stack([d["rwkv_w0"][l][cA], d["rwkv_a0"][l][cA], v0, d["rwkv_k_k"][l][cA], d["rwkv_k_a"][l][cA],
                        d["rwkv_r_k"][l].reshape(-1)[cA], d["rwkv_ln_g"][l][cA], d["rwkv_ln_b"][l][cA]])
        im["par%d" % l] = f32(np.broadcast_to(par[None].astype(np.float32), (128, 8, 192)))
        ch = slice(128 * q, 128 * q + 128)
        cw = d["lru_conv_w"][l]
        im["sm%d" % l] = f32(np.stack([cw[0, ch], cw[1, ch], cw[2, ch], cw[3, ch], d["lru_conv_b"][l][ch], d["lru_ba"][l][ch],
                                       d["lru_bx"][l][ch], d["lru_lambda"][l][ch]], axis=1))
        im["wa%d" % l] = f32(d["lru_wa"][l][2 * q:2 * q + 2]); im["wx%d" % l] = f32(d["lru_wx"][l][2 * q:2 * q + 2])
        im["gng%d" % l] = f32(np.stack([np.broadcast_to(d["ret_gn_g"][l][128 * h:128 * h + 128], (128, 128)) for h in heads]))
        wo = np.zeros((KY * 128, D), np.float32)
        ok = perm >= 0
        wo[ok] = d["w_out"][l][perm[ok]]
        im["wout%d" % l] = wo
        im["g2_%d" % l] = lay(d["norm2_g"][l])
        im["wg%d" % l] = f32(d["ffn_w_gate"][l]); im["wu%d" % l] = f32(d["ffn_w_up"][l]); im["wd%d" % l] = f32(d["ffn_w_down"][l])
    return im


def kernel(**inputs):
    d = {k: np.asarray(v) for k, v in inputs.items()}
    Bn, Sn, Dn = d["x"].shape
    nc = build_fused()
    in_maps = [core_inputs(d, c) for c in range(8)]
    res = run_bass_kernel_spmd(nc, in_maps, core_ids=list(range(8)))
    out = np.concatenate([r["outT"].T for r in res.results], axis=0).reshape(Bn, Sn, Dn)
    return np.ascontiguousarray(out.astype(np.float32))
```

```python
import math
from contextlib import ExitStack
import numpy as np
import concourse.bass as bass
import concourse.mybir as mybir
from concourse.bass_utils import run_bass_kernel_spmd

F32 = mybir.dt.float32
BF16 = mybir.dt.bfloat16
I32 = mybir.dt.int32
AF = mybir.ActivationFunctionType
ALU = mybir.AluOpType
AX = mybir.AxisListType


class Buf:
    __slots__ = ("name", "last_w", "readers")

    def __init__(self, name=""):
        self.name = name
        self.last_w = None
        self.readers = []


class SemPool:
    def __init__(self, nc):
        self.nc = nc
        self.sems = {}
        self._ctx = []
        self.count = {e: 0 for e in Prog.ENGS}
        self.dma_i = {"sync": 0, "gpsimd": 0, "scalar": 0}
        self.n_cc = 0

    def sem(self, key):
        if key not in self.sems:
            cm = self.nc.semaphore("s_" + key)
            self._ctx.append(cm)
            self.sems[key] = cm.__enter__()
        return self.sems[key]

    def close(self):
        for cm in reversed(self._ctx):
            cm.__exit__(None, None, None)


_POOL = [None]


class Prog:
    ENGS = ("sync", "scalar", "vector", "gpsimd", "tensor")

    def __init__(self, nc, tag="", n_dma_sems=12, self_sync=True):
        self.nc = nc
        self.tag = tag
        self.self_sync = self_sync
        self.own_pool = _POOL[0] is None
        self.pool = SemPool(nc) if self.own_pool else _POOL[0]
        self.ops = {e: [] for e in self.ENGS}
        self.waited = {e: {} for e in self.ENGS}
        self.n_dma_sems = n_dma_sems

    def _collect(self, eng, reads, writes):
        need = {}
        def req(tok):
            if tok is None:
                return
            k, v = tok
            if need.get(k, 0) < v:
                need[k] = v
        for b in reads:
            req(b.last_w)
        for b in writes:
            req(b.last_w)
            for r in b.readers:
                req(r)
        waits = []
        wd = self.waited[eng]
        for k, v in need.items():
            if k == "e_" + eng and (eng == "tensor" or not self.self_sync):
                continue
            if wd.get(k, 0) >= v:
                continue
            wd[k] = v
            waits.append((k, v))
        return waits

    def _mark(self, tok, reads, writes):
        for b in reads:
            b.readers.append(tok)
        for b in writes:
            b.last_w = tok
            b.readers = []
        return tok

    def op(self, eng, fn, reads=(), writes=()):
        waits = self._collect(eng, reads, writes)
        self.pool.count[eng] += 1
        tok = ("e_" + eng, self.pool.count[eng])
        self.ops[eng].append((waits, fn, tok[0], 1))
        return self._mark(tok, reads, writes)

    def dma(self, eng, out, in_, reads=(), writes=(), **kw):
        i = self.pool.dma_i[eng]
        self.pool.dma_i[eng] = i + 1
        key = "d_%s_%d" % (eng, i % self.n_dma_sems)
        val = 16 * (i // self.n_dma_sems + 1)
        waits = self._collect(eng, reads, writes)
        if i >= self.n_dma_sems:
            pv = val - 16
            if self.waited[eng].get(key, 0) < pv:
                self.waited[eng][key] = pv
                waits.append((key, pv))
        fn = lambda e, out=out, in_=in_, kw=kw: e.dma_start(out=out, in_=in_, **kw)
        self.ops[eng].append((waits, fn, key, 16))
        return self._mark((key, val), reads, writes)

    def cc(self, fn, reads=(), writes=()):
        eng = "gpsimd"
        self.pool.n_cc += 1
        key = "cc_%d" % self.pool.n_cc
        waits = self._collect(eng, reads, writes)
        self.ops[eng].append((waits, fn, key, None))
        return self._mark((key, 1), reads, writes)

    def final_wait(self, eng, bufs):
        waits = self._collect(eng, bufs, ())
        self.ops[eng].append((waits, None, None, 0))

    def emit(self):
        nc = self.nc
        sem = self.pool.sem
        with nc.Block() as block:
            for e in self.ENGS:
                ops = self.ops[e]
                if not ops:
                    continue

                def body(engine, ops=ops):
                    for waits, fn, key, inc in ops:
                        for k, v in waits:
                            engine.wait_ge(sem(k), v)
                        if fn is not None:
                            if inc is None:
                                fn(engine).then_inc(sem(key))
                            else:
                                fn(engine).then_inc(sem(key), inc)
                getattr(block, e)(body)
        if self.own_pool:
            self.pool.close()


D = 2048
KC = 16
TOK = 1024
EPS = 1e-6


def rms_to_bf16(P, nc, es, xs, bxs, g_ap, name, out_tile=None, out_buf=None, scr=None):
    sb = lambda n, shp, dt: es.enter_context(nc.sbuf_tensor(name + n, shp, dt))
    ps_ = lambda n, shp, dt: es.enter_context(nc.psum_tensor(name + n, shp, dt))
    if scr is None:
        scr = {}
    if "ones" not in scr:
        scr["ones"] = sb("ones", [128, 128], F32)
        scr["sq"] = [sb("sq%d" % i, [128, TOK], F32) for i in range(2)]
        scr["rstd"] = sb("rstd", [128, TOK], F32)
        scr["pss"] = [ps_("ss%d" % i, [128, 512], F32) for i in range(2)]
        scr["eps"] = sb("eps", [128, 1], F32)
        scr["b"] = dict(ones=Buf(), rstd=Buf(), sq=[Buf(), Buf()], ps=[Buf(), Buf()], eps=Buf())
        P.op("gpsimd", lambda e: e.memset(scr["ones"][:], 1.0), writes=[scr["b"]["ones"]])
        P.op("gpsimd", lambda e: e.memset(scr["eps"][:], EPS), writes=[scr["b"]["eps"]])
    ones, sq, rstd, pss, epst = scr["ones"], scr["sq"], scr["rstd"], scr["pss"], scr["eps"]
    B = scr["b"]
    bones, brstd, bsq, bps, beps = B["ones"], B["rstd"], B["sq"], B["ps"], B["eps"]
    gs = sb("g", [128, KC], F32)
    hT = out_tile if out_tile is not None else sb("hT", [128, KC, TOK], BF16)
    bg = Buf()
    bh = out_buf if out_buf is not None else Buf()
    P.dma("sync", gs[:], g_ap, writes=[bg])
    for c in range(KC):
        s = sq[c % 2]
        P.op("scalar", lambda e, c=c, s=s: e.activation(out=s[:], in_=xs[:, c, :], func=AF.Square),
             reads=[bxs], writes=[bsq[c % 2]])
        for hf in range(2):
            P.op("tensor", lambda e, c=c, s=s, hf=hf: e.matmul(pss[hf][:], lhsT=ones[:], rhs=s[:, hf * 512:(hf + 1) * 512],
                                                               start=(c == 0), stop=(c == KC - 1)),
                 reads=[bones, bsq[c % 2]], writes=[bps[hf]])
    for hf in range(2):
        sl = slice(hf * 512, (hf + 1) * 512)
        P.op("scalar", lambda e, hf=hf, sl=sl: e.activation(out=rstd[:, sl], in_=pss[hf][:], func=AF.Sqrt,
                                                            scale=1.0 / D, bias=epst[:, 0:1]),
             reads=[bps[hf], beps], writes=[brstd])
        P.op("vector", lambda e, sl=sl: e.reciprocal(out=rstd[:, sl], in_=rstd[:, sl]), reads=[brstd], writes=[brstd])
    for c in range(KC):
        P.op("vector", lambda e, c=c: e.scalar_tensor_tensor(out=hT[:, c, :], in0=xs[:, c, :], scalar=gs[:, c:c + 1],
                                                         in1=rstd[:], op0=ALU.mult, op1=ALU.mult),
             reads=[bxs, bg, brstd], writes=[bh])
    return hT, bh


DFF = 5632
NFB = DFF // 128
GRP = 4


S = 4096


def phase_p2b(nc, io, l):
    tg = "b%d_" % l
    gT = io["fm_all"][512:640, :]; xT = io["fm_all"][640:768, :]
    sm = io["sm%d" % l]; wa = io["wa%d" % l]; wx = io["wx%d" % l]
    out = io["y_src"][192:320, :]
    P = Prog(nc, tg)
    with ExitStack() as es:
        sb = lambda n, shp, dt=F32: es.enter_context(nc.sbuf_tensor(tg + n, shp, dt))
        g = sb("g", [128, S]); x = sb("x", [128, S]); xc = sb("xc", [128, S])
        r = sb("r", [128, S]); ii = sb("ii", [128, S]); a = sb("a", [128, S]); u = sb("u", [128, S])
        h = sb("h", [128, S])
        sms = sb("sms", [128, 8]); wab = sb("wab", [128, 128]); wxb = sb("wxb", [128, 128])
        t1 = sb("t1", [128, 8])
        ps = [es.enter_context(nc.psum_tensor(tg + "ps%d" % i, [128, 512], F32)) for i in range(4)]
        bps = [Buf() for _ in range(4)]
        bg, bx, bxc, br, bi, ba_, bu, bh, bsm, bwa, bwx, bt1 = (Buf() for _ in range(12))
        for i in range(4):
            sl = slice(i * 1024, (i + 1) * 1024)
            P.dma("sync", x[:, sl], xT[:, sl], writes=[bx])
        for i in range(4):
            sl = slice(i * 1024, (i + 1) * 1024)
            P.dma("gpsimd", g[:, sl], gT[:, sl], writes=[bg])
        P.dma("sync", sms[:], sm, writes=[bsm])
        P.op("gpsimd", lambda e: e.memset(wab[:], 0.0), writes=[bwa])
        P.op("gpsimd", lambda e: e.memset(wxb[:], 0.0), writes=[bwx])
        for bl in range(2):
            P.dma("sync", wab[bl * 64:(bl + 1) * 64, bl * 64:(bl + 1) * 64], wa[bl], writes=[bwa])
            P.dma("sync", wxb[bl * 64:(bl + 1) * 64, bl * 64:(bl + 1) * 64], wx[bl], writes=[bwx])
        P.op("vector", lambda e: e.tensor_scalar(out=xc[:], in0=x[:], scalar1=sms[:, 3:4], scalar2=sms[:, 4:5], op0=ALU.mult, op1=ALU.add),
             reads=[bx, bsm], writes=[bxc])
        for sh in (1, 2, 3):
            P.op("vector", lambda e, sh=sh: e.scalar_tensor_tensor(out=xc[:, sh:], in0=x[:, :S - sh], scalar=sms[:, 3 - sh:4 - sh],
                                                                   in1=xc[:, sh:], op0=ALU.mult, op1=ALU.add),
                 reads=[bx, bsm, bxc], writes=[bxc])
        P.op("scalar", lambda e: e.activation(out=t1[:, 0:1], in_=sms[:, 7:8], func=AF.Exp, scale=-1.0), reads=[bsm], writes=[bt1])
        P.op("vector", lambda e: e.tensor_scalar(out=t1[:, 1:2], in0=t1[:, 0:1], scalar1=2.0, scalar2=None, op0=ALU.add), reads=[bt1], writes=[bt1])
        P.op("vector", lambda e: e.reciprocal(out=t1[:, 1:2], in_=t1[:, 1:2]), reads=[bt1], writes=[bt1])
        P.op("vector", lambda e: e.tensor_tensor(out=t1[:, 2:3], in0=t1[:, 0:1], in1=t1[:, 1:2], op=ALU.mult), reads=[bt1], writes=[bt1])
        P.op("vector", lambda e: e.tensor_tensor(out=t1[:, 3:4], in0=t1[:, 2:3], in1=t1[:, 2:3], op=ALU.mult), reads=[bt1], writes=[bt1])
        P.op("vector", lambda e: e.memset(t1[:, 4:5], 1.0 / 13.0), reads=[bt1], writes=[bt1])
        for cf in (1.0 / 11, 1.0 / 9, 1.0 / 7, 1.0 / 5, 1.0 / 3, 1.0):
            P.op("vector", lambda e, cf=cf: e.tensor_scalar(out=t1[:, 4:5], in0=t1[:, 4:5], scalar1=t1[:, 3:4], scalar2=float(cf), op0=ALU.mult, op1=ALU.add),
                 reads=[bt1], writes=[bt1])
        P.op("vector", lambda e: e.scalar_tensor_tensor(out=t1[:, 5:6], in0=t1[:, 4:5], scalar=-16.0, in1=t1[:, 2:3], op0=ALU.mult, op1=ALU.mult),
             reads=[bt1], writes=[bt1])
        for pc in range(8):
            sl = slice(pc * 512, (pc + 1) * 512)
            q0, q1 = (2 * pc) % 4, (2 * pc + 1) % 4
            P.op("tensor", lambda e, q0=q0, sl=sl: e.matmul(ps[q0][:], lhsT=wab[:], rhs=xc[:, sl], start=True, stop=True),
                 reads=[bwa, bxc], writes=[bps[q0]])
            P.op("tensor", lambda e, q1=q1, sl=sl: e.matmul(ps[q1][:], lhsT=wxb[:], rhs=xc[:, sl], start=True, stop=True),
                 reads=[bwx, bxc], writes=[bps[q1]])
            P.op("scalar", lambda e, q0=q0, sl=sl: e.activation(out=r[:, sl], in_=ps[q0][:], func=AF.Sigmoid, bias=sms[:, 5:6]),
                 reads=[bps[q0], bsm], writes=[br])
            P.op("scalar", lambda e, q1=q1, sl=sl: e.activation(out=ii[:, sl], in_=ps[q1][:], func=AF.Sigmoid, bias=sms[:, 6:7]),
                 reads=[bps[q1], bsm], writes=[bi])
        P.op("scalar", lambda e: e.activation(out=a[:], in_=r[:], func=AF.Exp, scale=t1[:, 5:6]), reads=[br, bt1], writes=[ba_])
        P.op("vector", lambda e: e.tensor_tensor(out=u[:], in0=a[:], in1=a[:], op=ALU.mult), reads=[ba_], writes=[bu])
        P.op("vector", lambda e: e.tensor_scalar(out=u[:], in0=u[:], scalar1=-1.0, scalar2=1.0, op0=ALU.mult, op1=ALU.add), reads=[bu], writes=[bu])
        P.op("scalar", lambda e: e.activation(out=u[:], in_=u[:], func=AF.Sqrt), reads=[bu], writes=[bu])
        P.op("vector", lambda e: e.tensor_tensor(out=ii[:], in0=ii[:], in1=xc[:], op=ALU.mult), reads=[bi, bxc], writes=[bi])
        P.op("vector", lambda e: e.tensor_tensor(out=u[:], in0=u[:], in1=ii[:], op=ALU.mult), reads=[bu, bi], writes=[bu])
        P.op("vector", lambda e: e.tensor_tensor_scan(out=h[:], data0=a[:], data1=u[:], initial=0.0, op0=ALU.mult, op1=ALU.add),
             reads=[ba_, bu], writes=[bh])
        P.op("scalar", lambda e: e.activation(out=r[:], in_=g[:], func=AF.Square), reads=[bg, br], writes=[br])
        P.op("vector", lambda e: e.tensor_scalar(out=r[:], in0=r[:], scalar1=0.044715, scalar2=1.0, op0=ALU.mult, op1=ALU.add), reads=[br], writes=[br])
        P.op("vector", lambda e: e.tensor_tensor(out=r[:], in0=r[:], in1=g[:], op=ALU.mult), reads=[br, bg], writes=[br])
        P.op("scalar", lambda e: e.activation(out=r[:], in_=r[:], func=AF.Sigmoid, scale=1.5957691216057308), reads=[br], writes=[br])
        P.op("vector", lambda e: e.tensor_tensor(out=r[:], in0=r[:], in1=g[:], op=ALU.mult), reads=[br, bg], writes=[br])
        P.op("vector", lambda e: e.tensor_tensor(out=h[:], in0=h[:], in1=r[:], op=ALU.mult), reads=[br, bh], writes=[bh])
        bout = Buf()
        hb = sb("hb", [128, S], BF16); bhb = Buf()
        P.op("scalar", lambda e: e.copy(out=hb[:], in_=h[:]), reads=[bh], writes=[bhb])
        for i in range(4):
            sl = slice(i * 1024, (i + 1) * 1024)
            P.dma("sync", out[:, sl], hb[:, sl], reads=[bhb], writes=[bout])
        P.final_wait("sync", [bout]); P.final_wait("gpsimd", [bout])
        P.emit()


S = 4096
NT = 32
PI = math.pi
C1 = 6.28125
C2 = 2 * math.pi - 6.28125


def phase_p2c(nc, io, l):
    tg = "c%d_" % l
    tm = io["tm_all"]
    q_in = [tm[:, 576 + 64 * s:576 + 64 * s + 64] for s in range(2)]
    k_in = [tm[:, 704 + 64 * s:704 + 64 * s + 64] for s in range(2)]
    v_in = [tm[:, 832 + 128 * s:832 + 128 * s + 128] for s in range(2)]
    g_in = [tm[:, 1088 + 128 * s:1088 + 128 * s + 128] for s in range(2)]
    pos_in = io["pos"]; invf_in = io["invf"]; ident_in = io["ident"]
    mask_in = io["maskT"]; qdec_in = io["qdec"]; kdec_in = io["kdec"]; g128_in = io["g128"]; gng_in = io["gng%d" % l]
    ysrc = io["y_src"]
    P = Prog(nc, tg)
    with ExitStack() as es:
        sb = lambda n, shp, d=F32: es.enter_context(nc.sbuf_tensor(tg + n, shp, d))
        pst = lambda n, shp: es.enter_context(nc.psum_tensor(tg + n, shp, F32))
        posi = sb("posi", [128, NT], I32); posf = sb("posf", [128, NT]); invf = sb("invf", [128, 32])
        ident = sb("ident", [128, 128])
        ang = sb("ang", [128, NT, 32]); nf = sb("nf", [128, NT, 32]); ni = sb("ni", [128, NT, 32], I32)
        sn = sb("sn", [128, NT, 32]); cs = sb("cs", [128, NT, 32]); tmp = sb("tmp", [128, NT, 32]); tmp2 = sb("tmp2", [128, NT, 32])
        epst = sb("epst", [128, 1])
        bpos, binvf, bident, bang, bnf, bsn, bcs, btmp, btmp2, beps = (Buf() for _ in range(10))
        P.dma("sync", posi[:], pos_in, writes=[bpos])
        P.dma("sync", invf[:], invf_in, writes=[binvf])
        P.dma("sync", ident[:], ident_in, writes=[bident])
        P.op("gpsimd", lambda e: e.memset(epst[:], 1e-5), writes=[beps])
        P.op("vector", lambda e: e.tensor_copy(out=posf[:], in_=posi[:]), reads=[bpos], writes=[bpos])
        for n in range(NT):
            P.op("vector", lambda e, n=n: e.tensor_scalar(out=ang[:, n, :], in0=invf[:], scalar1=posf[:, n:n + 1], scalar2=None, op0=ALU.mult),
                 reads=[bpos, binvf], writes=[bang])
        P.op("vector", lambda e: e.tensor_scalar(out=nf[:], in0=ang[:], scalar1=1.0 / (2 * PI), scalar2=None, op0=ALU.mult), reads=[bang], writes=[bnf])
        P.op("vector", lambda e: e.tensor_copy(out=ni[:], in_=nf[:]), reads=[bnf], writes=[bnf])
        P.op("vector", lambda e: e.tensor_copy(out=nf[:], in_=ni[:]), reads=[bnf], writes=[bnf])
        P.op("vector", lambda e: e.scalar_tensor_tensor(out=ang[:], in0=nf[:], scalar=-C1, in1=ang[:], op0=ALU.mult, op1=ALU.add), reads=[bnf, bang], writes=[bang])
        P.op("vector", lambda e: e.scalar_tensor_tensor(out=ang[:], in0=nf[:], scalar=-C2, in1=ang[:], op0=ALU.mult, op1=ALU.add), reads=[bnf, bang], writes=[bang])

        def wrap(dst, bdst, src, bsrc, shift):
            if shift != 0.0:
                P.op("vector", lambda e: e.tensor_scalar(out=dst[:], in0=src[:], scalar1=float(shift), scalar2=None, op0=ALU.add), reads=[bsrc], writes=[bdst])
                src_, bsrc_ = dst, bdst
            else:
                src_, bsrc_ = src, bsrc
            P.op("vector", lambda e: e.tensor_scalar(out=tmp[:], in0=src_[:], scalar1=PI, scalar2=-2 * PI, op0=ALU.is_gt, op1=ALU.mult), reads=[bsrc_], writes=[btmp])
            P.op("vector", lambda e: e.tensor_scalar(out=tmp2[:], in0=src_[:], scalar1=-PI, scalar2=2 * PI, op0=ALU.is_lt, op1=ALU.mult), reads=[bsrc_], writes=[btmp2])
            P.op("vector", lambda e: e.tensor_tensor(out=dst[:], in0=src_[:], in1=tmp[:], op=ALU.add), reads=[bsrc_, btmp], writes=[bdst])
            P.op("vector", lambda e: e.tensor_tensor(out=dst[:], in0=dst[:], in1=tmp2[:], op=ALU.add), reads=[bdst, btmp2], writes=[bdst])
        wrap(sn, bsn, ang, bang, 0.0)
        wrap(cs, bcs, sn, bsn, PI / 2)
        P.op("scalar", lambda e: e.activation(out=sn[:], in_=sn[:], func=AF.Sin), reads=[bsn], writes=[bsn])
        P.op("scalar", lambda e: e.activation(out=cs[:], in_=cs[:], func=AF.Sin), reads=[bcs], writes=[bcs])

        q = sb("q", [128, NT, 64]); k = sb("k", [128, NT, 64]); qr = sb("qr", [128, NT, 64]); kr = sb("kr", [128, NT, 64])
        kt = sb("kt", [128, NT, 64])
        v = sb("v", [128, NT, 128]); g = sb("g", [128, NT, 128]); o = sb("o", [128, NT, 128])
        maskT = sb("maskT", [128, 128]); qdec = sb("qdec", [64, 128]); kdec = sb("kdec", [128, 2]); g128 = sb("g128", [64, 2])
        gng = sb("gng", [128, 128])
        Sst = sb("Sst", [64, 128])
        qkT = [sb("qkT%d" % i, [64, 2, 128]) for i in range(2)]
        qtT = [sb("qtT%d" % i, [64, 128]) for i in range(2)]
        smk = [sb("smk%d" % i, [128, 128]) for i in range(2)]
        stats = [sb("stats%d" % i, [128, 6]) for i in range(2)]
        mv = [sb("mv%d" % i, [128, 2]) for i in range(2)]
        rs = [sb("rs%d" % i, [128, 1]) for i in range(2)]
        psT = [pst("psT%d" % i, [64, 2, 128]) for i in range(2)]
        pss = [pst("pss%d" % i, [128, 128]) for i in range(2)]
        pso = [pst("pso%d" % i, [128, 128]) for i in range(2)]
        pkv = [pst("pkv%d" % i, [64, 128]) for i in range(2)]
        bq, bk, bqr, bkr, bkt, bv, bg, bo, bmask, bqdec, bkdec, bg128, bgng, bS = (Buf() for _ in range(14))
        bqkT, bqtT, bsmk, bstats, bmv, brs, bpsT, bpss, bpso, bpkv = ([Buf(), Buf()] for _ in range(10))
        P.dma("sync", kdec[:], kdec_in, writes=[bkdec])
        P.dma("sync", g128[:], g128_in, writes=[bg128])
        bout = Buf()
        oT = sb("oT", [128, S], BF16); boT = Buf()
        for sl in range(2):
            vw = lambda ap: ap.rearrange("(n p) d -> p n d", p=128)
            P.dma("sync", q[:], vw(q_in[sl]), writes=[bq])
            P.dma("gpsimd", k[:], vw(k_in[sl]), writes=[bk])
            for hh in range(2):
                P.dma("sync", v[:, hh * 16:(hh + 1) * 16, :], vw(v_in[sl])[:, hh * 16:(hh + 1) * 16, :], writes=[bv])
                P.dma("gpsimd", g[:, hh * 16:(hh + 1) * 16, :], vw(g_in[sl])[:, hh * 16:(hh + 1) * 16, :], writes=[bg])
            P.dma("sync", maskT[:], mask_in[sl], writes=[bmask])
            P.dma("sync", qdec[:], qdec_in[sl], writes=[bqdec])
            P.dma("sync", gng[:], gng_in[sl], writes=[bgng])
            P.op("gpsimd", lambda e: e.memset(Sst[:], 0.0), writes=[bS])
            for (src, bsrc, dst, bdst) in ((q, bq, qr, bqr), (k, bk, kr, bkr)):
                s1, s2 = src[:, :, 0:32], src[:, :, 32:64]
                d1, d2 = dst[:, :, 0:32], dst[:, :, 32:64]
                P.op("vector", lambda e, s1=s1, d1=d1: e.tensor_tensor(out=d1, in0=s1, in1=cs[:], op=ALU.mult), reads=[bsrc, bcs], writes=[bdst])
                P.op("vector", lambda e, s2=s2: e.tensor_tensor(out=tmp[:], in0=s2, in1=sn[:], op=ALU.mult), reads=[bsrc, bsn], writes=[btmp])
                P.op("vector", lambda e, d1=d1: e.tensor_tensor(out=d1, in0=d1, in1=tmp[:], op=ALU.subtract), reads=[btmp, bdst], writes=[bdst])
                P.op("vector", lambda e, s2=s2, d2=d2: e.tensor_tensor(out=d2, in0=s2, in1=cs[:], op=ALU.mult), reads=[bsrc, bcs], writes=[bdst])
                P.op("vector", lambda e, s1=s1: e.tensor_tensor(out=tmp2[:], in0=s1, in1=sn[:], op=ALU.mult), reads=[bsrc, bsn], writes=[btmp2])
                P.op("vector", lambda e, d2=d2: e.tensor_tensor(out=d2, in0=d2, in1=tmp2[:], op=ALU.add), reads=[btmp2, bdst], writes=[bdst])
            P.op("vector", lambda e, sl=sl: e.tensor_scalar(out=kt[:], in0=kr[:], scalar1=kdec[:, sl:sl + 1], scalar2=None, op0=ALU.mult),
                 reads=[bkr, bkdec], writes=[bkt])
            for n in range(NT):
                i = n % 2
                P.op("tensor", lambda e, n=n, i=i: e.transpose(psT[i][:, 0, :], qr[:, n, :], ident[:]), reads=[bqr, bident], writes=[bpsT[i]])
                P.op("tensor", lambda e, n=n, i=i: e.transpose(psT[i][:, 1, :], kr[:, n, :], ident[:]), reads=[bkr, bident], writes=[bpsT[i]])
                P.op("scalar", lambda e, i=i: e.copy(out=qkT[i][:], in_=psT[i][:]), reads=[bpsT[i]], writes=[bqkT[i]])
                P.op("vector", lambda e, i=i: e.tensor_tensor(out=qtT[i][:], in0=psT[i][:, 0, :], in1=qdec[:], op=ALU.mult), reads=[bqdec], writes=[bqtT[i], bpsT[i]])
                P.op("tensor", lambda e, i=i: e.matmul(pss[i][:], lhsT=qkT[i][:, 1, :], rhs=qkT[i][:, 0, :], start=True, stop=True), reads=[bqkT[i]], writes=[bpss[i]])
                P.op("vector", lambda e, i=i: e.tensor_tensor(out=smk[i][:], in0=pss[i][:], in1=maskT[:], op=ALU.mult), reads=[bpss[i], bmask], writes=[bsmk[i]])
                P.op("tensor", lambda e, i=i, n=n: e.matmul(pso[i][:], lhsT=smk[i][:], rhs=v[:, n, :], start=True, stop=False), reads=[bsmk[i], bv], writes=[bpso[i]])
                P.op("tensor", lambda e, i=i: e.matmul(pso[i][:], lhsT=qtT[i][:], rhs=Sst[:], start=False, stop=True), reads=[bqtT[i], bS], writes=[bpso[i]])
                P.op("tensor", lambda e, i=i, n=n: e.matmul(pkv[i][:], lhsT=kt[:, n, :], rhs=v[:, n, :], start=True, stop=True), reads=[bkt, bv], writes=[bpkv[i]])
                P.op("vector", lambda e, i=i, sl=sl: e.scalar_tensor_tensor(out=Sst[:], in0=Sst[:], scalar=g128[:, sl:sl + 1], in1=pkv[i][:], op0=ALU.mult, op1=ALU.add),
                     reads=[bpkv[i], bg128, bS], writes=[bS])
                P.op("vector", lambda e, i=i: e.bn_stats(out=stats[i][:], in_=pso[i][:]), reads=[bpso[i]], writes=[bstats[i]])
                P.op("vector", lambda e, i=i: e.bn_aggr(out=mv[i][:], in_=stats[i][:]), reads=[bstats[i]], writes=[bmv[i]])
                P.op("scalar", lambda e, i=i: e.activation(out=rs[i][:], in_=mv[i][:, 1:2], func=AF.Sqrt, bias=epst[:, 0:1]), reads=[bmv[i], beps], writes=[brs[i]])
                P.op("vector", lambda e, i=i: e.reciprocal(out=rs[i][:], in_=rs[i][:]), reads=[brs[i]], writes=[brs[i]])
                P.op("vector", lambda e, i=i, n=n: e.tensor_scalar(out=o[:, n, :], in0=pso[i][:], scalar1=mv[i][:, 0:1], scalar2=rs[i][:, 0:1], op0=ALU.subtract, op1=ALU.mult),
                     reads=[bpso[i], bmv[i], brs[i]], writes=[bo])
                P.op("gpsimd", lambda e, n=n: e.tensor_tensor(out=o[:, n, :], in0=o[:, n, :], in1=gng[:], op=ALU.mult), reads=[bo, bgng], writes=[bo])
            P.op("scalar", lambda e: e.activation(out=g[:], in_=g[:], func=AF.Silu), reads=[bg], writes=[bg])
            P.op("vector", lambda e: e.tensor_tensor(out=o[:], in0=o[:], in1=g[:], op=ALU.mult), reads=[bo, bg], writes=[bo])
            for n in range(NT):
                i = n % 2
                P.op("tensor", lambda e, n=n, i=i: e.transpose(pss[i][:], o[:, n, :], ident[:]), reads=[bo, bident], writes=[bpss[i]])
                P.op("scalar", lambda e, n=n, i=i: e.copy(out=oT[:, n * 128:(n + 1) * 128], in_=pss[i][:]), reads=[bpss[i]], writes=[boT])
            for hh in range(4):
                cs_ = slice(hh * 1024, (hh + 1) * 1024)
                P.dma("sync", ysrc[320 + 128 * sl:448 + 128 * sl, cs_], oT[:, cs_], reads=[boT], writes=[bout])
        P.final_wait("sync", [bout]); P.final_wait("gpsimd", [bout])
        P.emit()


def p2c_consts(heads):
    maskT = np.zeros((2, 128, 128), np.float32); qdec = np.zeros((2, 64, 128), np.float32)
    kdec = np.zeros((128, 2), np.float32); g128 = np.zeros((64, 2), np.float32)
    idx = np.arange(128)
    for s, hd in enumerate(heads):
        lg = float(np.log1p(-np.exp2(np.float32(-5.0 - hd))).astype(np.float32))
        m = idx[:, None]; c = idx[None, :]
        same = (m // 64) == (c // 64)
        earlier = (m // 64) < (c // 64)
        dec = np.where(same, np.exp(lg * np.abs(c - m)), np.where(earlier, np.exp(lg * (c - m)), 0.0))
        maskT[s] = (dec * 0.125).astype(np.float32)
        qdec[s] = np.exp(lg * (idx + 1.0))[None, :].astype(np.float32)
        kdec[:, s] = (np.exp(lg * (127.0 - idx)) * 0.125).astype(np.float32)
        g128[:, s] = np.float32(np.exp(lg * 128.0))
    invf = (10000.0 ** (-np.arange(0, 64, 2, dtype=np.float32) / 64)).astype(np.float32)
    return dict(maskT=maskT, qdec=qdec, kdec=kdec, g128=g128, invf=np.ascontiguousarray(np.broadcast_to(invf, (128, 32))),
                ident=np.eye(128, dtype=np.float32))


C_SLOTS = [(0, 1), (2, 3), (4, 5), (4, 5)]


VARA = False

S = 4096
NT = 32
W3 = 192
NEG_EHALF = -math.exp(-0.5)
A_GN_EPS = 64e-5


def phase_p2a(nc, io, l, nt=NT, dbg=9):
    has_vres = l > 0
    tg = "a%d_" % l
    tmA = io["tm_all"]; fm = io["fm_all"]
    rkv_in = tmA[:, 0:576]; mu_in = io["mu_rkv%d" % l]
    wd_in = fm[0:96, :]; ad_in = fm[96:192, :]; gd_in = fm[192:448, :]
    lmu_in = io["lmu%d" % l]; w2_in = io["w2c%d" % l]; a2_in = io["a2c%d" % l]; g2_in = io["g2c%d" % l]
    par_in = io["par%d" % l]; cst_in = io["cst"]; m4_in = io["mask4"]
    if has_vres:
        vd_in = fm[448:512, :]; v2_in = io["v2c"]
    vf_in = io["vfirst"]
    v_out = io["vfirst"]
    ysrc = io["y_src"]
    P = Prog(nc, tg)
    with ExitStack() as es:
        sb = lambda n, shp, d=F32: es.enter_context(nc.sbuf_tensor(tg + n, shp, d))
        V = lambda fn, r, w: P.op("vector", fn, reads=r, writes=w)
        A = lambda fn, r, w: P.op("scalar", fn, reads=r, writes=w)
        G = lambda fn, r, w: P.op("gpsimd", fn, reads=r, writes=w)
        T = lambda fn, r, w: P.op("tensor", fn, reads=r, writes=w)
        cst = sb("cst", [128, 4, 128]); m4 = sb("m4", [128, 4, 128]); par = sb("par", [128, 8, W3]); mu = sb("mu", [128, 576])
        lmu = sb("lmu", [128, 5]); w2 = sb("w2", [96, W3]); a2 = sb("a2", [96, W3]); g2 = sb("g2", [128, 2, W3])
        ones = sb("ones", [128, 1]); epsg = sb("epsg", [128, 1])
        bcst, bm4, bpar, bmu, blmu, bw2, ba2, bg2, bones, bepsg = (Buf() for _ in range(10))
        P.dma("sync", cst[:], cst_in, writes=[bcst]); P.dma("sync", m4[:], m4_in, writes=[bm4])
        P.dma("sync", par[:], par_in, writes=[bpar]); P.dma("sync", mu[:], mu_in, writes=[bmu])
        P.dma("sync", lmu[:], lmu_in, writes=[blmu]); P.dma("sync", w2[:], w2_in, writes=[bw2])
        P.dma("sync", a2[:], a2_in, writes=[ba2])
        P.dma("sync", g2[:], g2_in.rearrange("(c p) n -> p c n", p=128), writes=[bg2])
        G(lambda e: e.memset(ones[:], 1.0), [], [bones]); G(lambda e: e.memset(epsg[:], A_GN_EPS), [], [bepsg])
        ident, triu, trirev, maskn = cst[:, 0, :], cst[:, 1, :], cst[:, 2, :], cst[:, 3, :]
        if has_vres:
            v2 = sb("v2", [64, W3]); bv2 = Buf()
            P.dma("sync", v2[:], v2_in, writes=[bv2])
        ltmp = sb("ltmp", [128, S]); bltmp = Buf()
        lo = {}
        specs = [("wd", wd_in, 96, 0, AF.Tanh), ("ad", ad_in, 96, 1, None), ("gd0", gd_in[0:128], 128, 2, AF.Sigmoid),
                 ("gd1", gd_in[128:256], 128, 3, AF.Sigmoid)]
        if has_vres:
            specs.append(("vd", vd_in, 64, 4, None))
        for name, src, rows, mcol, fn in specs:
            t = sb("lo_" + name, [rows, S + 1]); b = Buf()
            G(lambda e, t=t: e.memset(t[:, 0:1], 0.0), [], [b])
            for i in range(4):
                P.dma("sync" if i % 2 == 0 else "gpsimd", t[:, 1 + i * 1024:1 + (i + 1) * 1024], src[:, i * 1024:(i + 1) * 1024], writes=[b])
            V(lambda e, t=t, rows=rows: e.tensor_tensor(out=ltmp[:rows, :], in0=t[:, 0:S], in1=t[:, 1:S + 1], op=ALU.subtract), [b, bltmp], [bltmp])
            V(lambda e, t=t, rows=rows, mcol=mcol: e.scalar_tensor_tensor(out=t[:, 1:S + 1], in0=ltmp[:rows, :], scalar=lmu[:rows, mcol:mcol + 1],
                                                                      in1=t[:, 1:S + 1], op0=ALU.mult, op1=ALU.add), [bltmp, blmu, b], [b])
            if fn is not None:
                A(lambda e, t=t, fn=fn: e.activation(out=t[:, 1:S + 1], in_=t[:, 1:S + 1], func=fn), [b], [b])
            lo[name] = (t, b)
        NS = 2
        cur = [sb("cur%d" % i, [128, 576]) for i in range(NS)]; prv = [sb("prv%d" % i, [128, 576]) for i in range(NS)]
        bcur = [Buf() for _ in range(NS)]; bprv = [Buf() for _ in range(NS)]
        tt = [sb("tt%d" % i, [128, 8, W3]) for i in range(NS)]; btt = [Buf() for _ in range(NS)]
        rk = [sb("rk%d" % i, [128, 3]) for i in range(NS)]; brk = [Buf() for _ in range(NS)]
        tm = sb("tm", [128, 4, W3]); btm = Buf()
        ss = sb("ss", [128, 3]); bss = Buf()
        if has_vres:
            vf = [sb("vf%d" % i, [128, W3]) for i in range(NS)]; bvf = [Buf() for _ in range(NS)]
        ee = sb("ee", [128, 4, W3]); bee = Buf()
        dd = sb("dd", [128, 6, W3]); bdd = Buf()
        wc = sb("wc", [64, 4]); bwc = Buf()
        T4 = sb("T4", [64, 4, 128]); bT4 = Buf()
        M4 = sb("M4", [128, 4, 128]); bM4 = Buf()
        Pk = [sb("Pk%d" % i, [128, 128]) for i in range(2)]; bPk = [Buf(), Buf()]
        Qk = [sb("Qk%d" % i, [128, 128]) for i in range(2)]; bQk = [Buf(), Buf()]
        Z = sb("Z", [128, 128]); bZ = Buf()
        nX = sb("nX", [128, 64]); bnX = Buf()
        U = sb("U", [128, 64]); bU = Buf()
        ST = [sb("ST%d" % h, [64, 64]) for h in range(3)]; bST = [Buf() for _ in range(3)]
        yt = [sb("yt%d" % i, [128, W3]) for i in range(NS)]; byt = [Buf() for _ in range(NS)]
        st6 = sb("st6", [128, 6]); bst6 = Buf(); mv = sb("mv", [128, 2]); bmv = Buf(); rsd = sb("rsd", [128, 1]); brsd = Buf()
        for h in range(3):
            G(lambda e, h=h: e.memset(ST[h][:], 0.0), [], [bST[h]])
        ytT = [sb("ytT%d" % i, [128, 2, 128], BF16) for i in range(NS)]; bytT = [Buf() for _ in range(NS)]
        pb = [es.enter_context(nc.psum_tensor(tg + "pb%d" % i, [128, 512], F32)) for i in range(8)]
        bpb = [Buf() for _ in range(8)]
        pL1, pL2, pGG, pTT, pPP, pIV, pCH = pb[0], pb[1], pb[2], pb[3], pb[4], pb[5], pb[6]
        bL1, bL2, bGG, bTT, bPP, bIV, bCH = bpb[0], bpb[1], bpb[2], bpb[3], bpb[4], bpb[5], bpb[6]
        rkv_v = rkv_in
        bout = Buf()
        wdT, bwd = lo["wd"]; adT, bad = lo["ad"]; gd0, bgd0 = lo["gd0"]; gd1, bgd1 = lo["gd1"]
        for n in range(nt):
            s = n % NS
            t0 = n * 128
            c_, p_, t_ = cur[s], prv[s], tt[s]
            P.dma("sync", c_[:], rkv_v[t0:t0 + 128, :], writes=[bcur[s]])
            if n == 0:
                G(lambda e, p_=p_: e.memset(p_[:], 0.0), [], [bprv[s]])
                P.dma("gpsimd", p_[1:128, :], rkv_v[0:127, :], writes=[bprv[s]])
            else:
                P.dma("gpsimd", p_[:], rkv_v[t0 - 1:t0 + 127, :], writes=[bprv[s]])
            if has_vres:
                P.dma("sync", vf[s][:], vf_in[t0:t0 + 128, :], writes=[bvf[s]])
            V(lambda e, c_=c_, p_=p_: e.tensor_tensor(out=p_[:], in0=p_[:], in1=c_[:], op=ALU.subtract), [bcur[s], bprv[s]], [bprv[s]])
            G(lambda e, p_=p_: e.tensor_tensor(out=p_[:], in0=p_[:], in1=mu[:], op=ALU.mult), [bprv[s], bmu], [bprv[s]])
            V(lambda e, c_=c_, p_=p_: e.tensor_tensor(out=c_[:], in0=c_[:], in1=p_[:], op=ALU.add), [bcur[s], bprv[s]], [bcur[s]])
            r_, k_, v_ = c_[:, 0:W3], c_[:, W3:2 * W3], c_[:, 2 * W3:3 * W3]
            tsl = slice(1 + t0, 1 + t0 + 128)
            T(lambda e, tsl=tsl: e.matmul(pL1[:, 0:W3], lhsT=wdT[:, tsl], rhs=w2[:], start=True, stop=True), [bwd, bw2], [bL1])
            T(lambda e, tsl=tsl: e.matmul(pL1[:, W3:2 * W3], lhsT=adT[:, tsl], rhs=a2[:], start=True, stop=True), [bad, ba2], [bL1])
            T(lambda e, tsl=tsl: e.matmul(pL2[:, 0:W3], lhsT=gd0[:, tsl], rhs=g2[:, 0, :], start=True, stop=False), [bgd0, bg2], [bL2])
            T(lambda e, tsl=tsl: e.matmul(pL2[:, 0:W3], lhsT=gd1[:, tsl], rhs=g2[:, 1, :], start=False, stop=True), [bgd1, bg2], [bL2])
            if has_vres:
                vdT, bvd = lo["vd"]
                T(lambda e, tsl=tsl: e.matmul(pL2[:, W3:2 * W3], lhsT=vdT[:, tsl], rhs=v2[:], start=True, stop=True), [bvd, bv2], [bL2])
            V(lambda e, t_=t_: e.tensor_tensor(out=t_[:, 5, :], in0=pL1[:, 0:W3], in1=par[:, 0, :], op=ALU.add), [bL1, bpar, btt[s]], [btt[s]])
            A(lambda e, t_=t_: e.activation(out=t_[:, 5, :], in_=t_[:, 5, :], func=AF.Sigmoid), [btt[s]], [btt[s]])
            V(lambda e, t_=t_: e.tensor_scalar(out=t_[:, 5, :], in0=t_[:, 5, :], scalar1=NEG_EHALF, scalar2=None, op0=ALU.mult), [btt[s]], [btt[s]])
            V(lambda e, t_=t_: e.tensor_tensor(out=t_[:, 7, :], in0=pL1[:, W3:2 * W3], in1=par[:, 1, :], op=ALU.add), [bL1, bpar, btt[s]], [btt[s]])
            A(lambda e, t_=t_: e.activation(out=t_[:, 7, :], in_=t_[:, 7, :], func=AF.Sigmoid), [btt[s]], [btt[s]])
            A(lambda e, t_=t_: e.copy(out=t_[:, 6, :], in_=pL2[:, 0:W3]), [bL2, btt[s]], [btt[s]])
            G(lambda e, t_=t_, r_=r_: e.tensor_copy(out=t_[:, 0, :], in_=r_), [bcur[s], btt[s]], [btt[s]])
            if has_vres:
                V(lambda e: e.tensor_tensor(out=tm[:, 0, :], in0=pL2[:, W3:2 * W3], in1=par[:, 2, :], op=ALU.add), [bL2, bpar, btm], [btm])
                A(lambda e: e.activation(out=tm[:, 0, :], in_=tm[:, 0, :], func=AF.Sigmoid), [btm], [btm])
                V(lambda e, v_=v_, s=s: e.tensor_tensor(out=tm[:, 1, :], in0=vf[s][:], in1=v_, op=ALU.subtract), [bvf[s], bcur[s], btm], [btm])
                V(lambda e: e.tensor_tensor(out=tm[:, 1, :], in0=tm[:, 1, :], in1=tm[:, 0, :], op=ALU.mult), [btm], [btm])
                V(lambda e, t_=t_, v_=v_: e.tensor_tensor(out=t_[:, 2, :], in0=tm[:, 1, :], in1=v_, op=ALU.add), [btm, bcur[s], btt[s]], [btt[s]])
            else:
                G(lambda e, t_=t_, v_=v_: e.tensor_copy(out=t_[:, 2, :], in_=v_), [bcur[s], btt[s]], [btt[s]])
            if not has_vres:
                P.dma("sync", v_out[t0:t0 + 128, :], t_[:, 2, :], reads=[btt[s]], writes=[bout])
            V(lambda e, t_=t_, k_=k_: e.tensor_tensor(out=t_[:, 3, :], in0=k_, in1=par[:, 3, :], op=ALU.mult), [bcur[s], bpar, btt[s]], [btt[s]])
            V(lambda e, t_=t_: e.tensor_tensor(out=tm[:, 2, :], in0=t_[:, 3, :], in1=t_[:, 3, :], op=ALU.mult), [btt[s], btm], [btm])
            V(lambda e: e.tensor_reduce(out=ss[:], in_=tm[:, 2, :].rearrange("p (h j) -> p h j", h=3), axis=AX.X, op=ALU.add), [btm, bss], [bss])
            A(lambda e: e.activation(out=ss[:], in_=ss[:], func=AF.Sqrt), [bss], [bss])
            V(lambda e: e.tensor_scalar(out=ss[:], in0=ss[:], scalar1=1e-12, scalar2=None, op0=ALU.max), [bss], [bss])
            V(lambda e: e.reciprocal(out=ss[:], in_=ss[:]), [bss], [bss])
            for h in range(3):
                hs = slice(64 * h, 64 * h + 64)
                V(lambda e, t_=t_, hs=hs, h=h: e.tensor_scalar(out=t_[:, 3, hs], in0=t_[:, 3, hs], scalar1=ss[:, h:h + 1], scalar2=None, op0=ALU.mult),
                  [bss, btt[s]], [btt[s]])
            V(lambda e, t_=t_: e.scalar_tensor_tensor(out=tm[:, 3, :], in0=t_[:, 7, :], scalar=-1.0, in1=par[:, 4, :], op0=ALU.add, op1=ALU.mult),
              [btt[s], bpar, btm], [btm])
            V(lambda e, t_=t_, k_=k_: e.scalar_tensor_tensor(out=t_[:, 1, :], in0=tm[:, 3, :], scalar=1.0, in1=k_, op0=ALU.add, op1=ALU.mult),
              [btm, bcur[s], btt[s]], [btt[s]])
            G(lambda e, t_=t_: e.tensor_tensor(out=t_[:, 4, :], in0=t_[:, 3, :], in1=t_[:, 7, :], op=ALU.mult), [btt[s]], [btt[s]])
            V(lambda e, t_=t_: e.tensor_tensor(out=tm[:, 3, :], in0=t_[:, 0, :], in1=t_[:, 1, :], op=ALU.mult), [btt[s], btm], [btm])
            V(lambda e: e.tensor_tensor(out=tm[:, 3, :], in0=tm[:, 3, :], in1=par[:, 5, :], op=ALU.mult), [btm, bpar], [btm])
            V(lambda e, s=s: e.tensor_reduce(out=rk[s][:], in_=tm[:, 3, :].rearrange("p (h j) -> p h j", h=3), axis=AX.X, op=ALU.add), [btm, brk[s]], [brk[s]])
            if dbg <= 1:
                P.dma("sync", y_out[t0:t0 + 128, :], t_[:, 6, :], reads=[btt[s]], writes=[bout])
                continue
            T(lambda e, t_=t_: e.matmul(pGG[:, 0:W3], lhsT=triu, rhs=t_[:, 5, :], start=True, stop=True), [bcst, btt[s]], [bGG])
            T(lambda e, t_=t_: e.matmul(pGG[:, W3:2 * W3], lhsT=trirev, rhs=t_[:, 5, :], start=True, stop=True), [bcst, btt[s]], [bGG])
            for h in range(3):
                T(lambda e, t_=t_, h=h: e.matmul(pGG[0:64, 400 + h:401 + h], lhsT=t_[:, 5, 64 * h:64 * h + 64], rhs=ones[:], start=True, stop=True),
                  [btt[s], bones], [bGG])
            A(lambda e: e.activation(out=ee[:, 0, :], in_=pGG[:, 0:W3], func=AF.Exp), [bGG, bee], [bee])
            A(lambda e: e.activation(out=ee[:, 1, :], in_=pGG[:, 0:W3], func=AF.Exp, scale=-1.0), [bGG, bee], [bee])
            V(lambda e, t_=t_: e.tensor_tensor(out=ee[:, 2, :], in0=pGG[:, 0:W3], in1=t_[:, 5, :], op=ALU.subtract), [bGG, btt[s], bee], [bee])
            A(lambda e: e.activation(out=ee[:, 2, :], in_=ee[:, 2, :], func=AF.Exp), [bee], [bee])
            A(lambda e: e.activation(out=ee[:, 3, :], in_=pGG[:, W3:2 * W3], func=AF.Exp), [bGG, bee], [bee])
            A(lambda e: e.activation(out=wc[:, 0:3], in_=pGG[0:64, 400:403], func=AF.Exp), [bGG, bwc], [bwc])
            for (di, ti, ei, eng) in ((0, 0, 0, V), (1, 3, 2, G), (2, 1, 1, V), (3, 4, 1, G), (4, 1, 3, V), (5, 4, 3, G)):
                eng(lambda e, t_=t_, di=di, ti=ti, ei=ei: e.tensor_tensor(out=dd[:, di, :], in0=t_[:, ti, :], in1=ee[:, ei, :], op=ALU.mult),
                    [btt[s], bee, bdd], [bdd])
            if dbg <= 2:
                P.dma("sync", y_out[t0:t0 + 128, :], dd[:, 0, :], reads=[bdd], writes=[bout])
                continue
            for h in range(3):
                hs = slice(64 * h, 64 * h + 64)
                for j in range(4):
                    T(lambda e, j=j, hs=hs: e.transpose(pTT[0:64, j * 128:(j + 1) * 128], dd[:, j, hs], ident), [bdd, bcst], [bTT])
                A(lambda e: e.copy(out=T4[:], in_=pTT[0:64, :].rearrange("p (a b) -> p a b", a=4)), [bTT, bT4], [bT4])
                if dbg <= 3:
                    continue
                T(lambda e: e.matmul(pPP[:, 0:256], lhsT=T4[:, 2, :], rhs=T4[:, 0:2, :].rearrange("p a b -> p (a b)"), start=True, stop=True), [bT4], [bPP])
                T(lambda e: e.matmul(pPP[:, 256:512], lhsT=T4[:, 3, :], rhs=T4[:, 0:2, :].rearrange("p a b -> p (a b)"), start=True, stop=True), [bT4], [bPP])
                T(lambda e: e.matmul(pIV[:, 0:128], lhsT=T4[:, 1, :], rhs=T4[:, 3, :], start=True, stop=True), [bT4], [bIV])
                V(lambda e: e.tensor_tensor(out=M4[:].rearrange("p a b -> p (a b)"), in0=pPP[:], in1=m4[:].rearrange("p a b -> p (a b)"), op=ALU.mult),
                  [bPP, bm4, bM4], [bM4])
                V(lambda e: e.tensor_tensor(out=Pk[0][:], in0=pIV[:, 0:128], in1=maskn, op=ALU.mult), [bIV, bcst, bPk[0]], [bPk[0]])
                (V if VARA else G)(lambda e: e.tensor_copy(out=Qk[0][:], in_=M4[:, 3, :]), [bM4, bQk[0]], [bQk[0]])
                (V if VARA else G)(lambda e: e.tensor_tensor(out=Z[:], in0=ident, in1=M4[:, 3, :], op=ALU.subtract), [bM4, bcst, bZ], [bZ])
                if dbg <= 4:
                    continue
                for lv in range(1, 7):
                    pi, ci = (lv - 1) % 2, lv % 2
                    STEP = 9
                    if lv < 6 and STEP >= 1:
                        T(lambda e, pi=pi: e.matmul(pIV[:, 128:256], lhsT=Pk[pi][:], rhs=Qk[pi][:], start=True, stop=True), [bPk[pi], bQk[pi]], [bIV])
                    if STEP >= 2:
                        T(lambda e, pi=pi: e.matmul(pIV[:, 256:384], lhsT=Qk[pi][:], rhs=Pk[pi][:], start=True, stop=True), [bPk[pi], bQk[pi]], [bIV])
                    if lv < 6 and STEP >= 3:
                        A(lambda e, ci=ci: e.copy(out=Qk[ci][:], in_=pIV[:, 128:256]), [bIV, bQk[ci]], [bQk[ci]])
                    if STEP >= 4:
                        A(lambda e, ci=ci: e.copy(out=Pk[ci][:], in_=pIV[:, 256:384]), [bIV, bPk[ci]], [bPk[ci]])
                    if STEP >= 5:
                        T(lambda e, ci=ci: e.matmul(pIV[:, 384:512], lhsT=Pk[ci][:], rhs=Z[:], start=True, stop=True), [bPk[ci], bZ], [bIV])
                    if STEP >= 6:
                        V(lambda e: e.tensor_tensor(out=Z[:], in0=Z[:], in1=pIV[:, 384:512], op=ALU.add), [bIV, bZ], [bZ])
                if dbg <= 5:
                    continue
                vh = t_[:, 2, hs]
                T(lambda e, vh=vh: e.matmul(pCH[:, 0:64], lhsT=M4[:, 1, :], rhs=vh, start=True, stop=False), [bM4, btt[s]], [bCH])
                T(lambda e, h=h: e.matmul(pCH[:, 0:64], lhsT=T4[:, 1, :], rhs=ST[h][:], start=False, stop=True), [bT4, bST[h]], [bCH])
                A(lambda e: e.activation(out=nX[:], in_=pCH[:, 0:64], func=AF.Copy, scale=-1.0), [bCH, bnX], [bnX])
                T(lambda e: e.matmul(pCH[:, 64:128], lhsT=Z[:], rhs=nX[:], start=True, stop=True), [bZ, bnX], [bCH])
                V(lambda e: e.tensor_scalar(out=U[:], in0=pCH[:, 64:128], scalar1=1.0, scalar2=None, op0=ALU.mult), [bCH, bU], [bU])
                T(lambda e, h=h: e.matmul(pCH[:, 128:192], lhsT=T4[:, 0, :], rhs=ST[h][:], start=True, stop=False), [bT4, bST[h]], [bCH])
                T(lambda e: e.matmul(pCH[:, 128:192], lhsT=M4[:, 2, :], rhs=U[:], start=False, stop=False), [bM4, bU], [bCH])
                T(lambda e, vh=vh: e.matmul(pCH[:, 128:192], lhsT=M4[:, 0, :], rhs=vh, start=False, stop=True), [bM4, btt[s]], [bCH])
                T(lambda e, hs=hs: e.matmul(pCH[0:64, 192:256], lhsT=dd[:, 5, hs], rhs=U[:], start=True, stop=False), [bdd, bU], [bCH])
                T(lambda e, hs=hs, vh=vh: e.matmul(pCH[0:64, 192:256], lhsT=dd[:, 4, hs], rhs=vh, start=False, stop=True), [bdd, btt[s]], [bCH])
                V(lambda e, h=h: e.scalar_tensor_tensor(out=ST[h][:], in0=ST[h][:], scalar=wc[:, h:h + 1], in1=pCH[0:64, 192:256], op0=ALU.mult, op1=ALU.add),
                  [bCH, bwc, bST[h]], [bST[h]])
                V(lambda e: e.bn_stats(out=st6[:], in_=pCH[:, 128:192]), [bCH, bst6], [bst6])
                V(lambda e: e.bn_aggr(out=mv[:], in_=st6[:]), [bst6, bmv], [bmv])
                A(lambda e: e.activation(out=rsd[:], in_=mv[:, 1:2], func=AF.Sqrt, bias=epsg[:, 0:1]), [bmv, bepsg, brsd], [brsd])
                V(lambda e: e.reciprocal(out=rsd[:], in_=rsd[:]), [brsd], [brsd])
                V(lambda e, s=s, hs=hs: e.tensor_scalar(out=yt[s][:, hs], in0=pCH[:, 128:192], scalar1=mv[:, 0:1], scalar2=rsd[:, 0:1], op0=ALU.subtract, op1=ALU.mult),
                  [bCH, bmv, brsd, byt[s]], [byt[s]])
            if dbg <= 5:
                P.dma("sync", y_out[t0:t0 + 128, :], dd[:, 0, :], reads=[bdd, bT4, bM4, bZ, bPk[0], bPk[1], bQk[0], bQk[1]], writes=[bout])
                continue
            G(lambda e, s=s: e.tensor_tensor(out=yt[s][:], in0=yt[s][:], in1=par[:, 6, :], op=ALU.mult), [byt[s], bpar], [byt[s]])
            G(lambda e, s=s: e.tensor_tensor(out=yt[s][:], in0=yt[s][:], in1=par[:, 7, :], op=ALU.add), [byt[s], bpar], [byt[s]])
            for h in range(3):
                hs = slice(64 * h, 64 * h + 64)
                V(lambda e, s=s, hs=hs, h=h, t_=t_: e.scalar_tensor_tensor(out=yt[s][:, hs], in0=t_[:, 2, hs], scalar=rk[s][:, h:h + 1], in1=yt[s][:, hs],
                                                                          op0=ALU.mult, op1=ALU.add), [btt[s], brk[s], byt[s]], [byt[s]])
            V(lambda e, s=s, t_=t_: e.tensor_tensor(out=yt[s][:], in0=yt[s][:], in1=t_[:, 6, :], op=ALU.mult), [byt[s], btt[s]], [byt[s]])
            T(lambda e, s=s: e.transpose(pb[7][:, 0:128], yt[s][:, 0:128], ident), [byt[s], bcst], [bpb[7]])
            T(lambda e, s=s: e.transpose(pb[7][0:64, 128:256], yt[s][:, 128:192], ident), [byt[s], bcst], [bpb[7]])
            A(lambda e, s=s: e.copy(out=ytT[s][:, 0, :], in_=pb[7][:, 0:128]), [bpb[7], bytT[s]], [bytT[s]])
            A(lambda e, s=s: e.copy(out=ytT[s][0:64, 1, :], in_=pb[7][0:64, 128:256]), [bpb[7], bytT[s]], [bytT[s]])
            P.dma("sync", ysrc[0:128, t0:t0 + 128], ytT[s][:, 0, :], reads=[bytT[s]], writes=[bout])
            P.dma("sync", ysrc[128:192, t0:t0 + 128], ytT[s][0:64, 1, :], reads=[bytT[s]], writes=[bout])
        P.final_wait("sync", [bout]); P.final_wait("gpsimd", [bout])
        P.emit()


def phase_p2a_il(nc, io, l, nt=NT, dbg=9):
    has_vres = l > 0
    tg = "a%d_" % l
    tmA = io["tm_all"]; fm = io["fm_all"]
    rkv_in = tmA[:, 0:576]; mu_in = io["mu_rkv%d" % l]
    wd_in = fm[0:96, :]; ad_in = fm[96:192, :]; gd_in = fm[192:448, :]
    lmu_in = io["lmu%d" % l]; w2_in = io["w2c%d" % l]; a2_in = io["a2c%d" % l]; g2_in = io["g2c%d" % l]
    par_in = io["par%d" % l]; cst_in = io["cst"]; m4_in = io["mask4"]
    if has_vres:
        vd_in = fm[448:512, :]; v2_in = io["v2c"]
    vf_in = io["vfirst"]
    v_out = io["vfirst"]
    ysrc = io["y_src"]
    P = Prog(nc, tg)
    with ExitStack() as es:
        sb = lambda n, shp, d=F32: es.enter_context(nc.sbuf_tensor(tg + n, shp, d))
        PSB = []

        def _x(r, w):
            ids = set(id(b) for b in PSB)
            return [b for b in r if id(b) not in ids], list(w) + [b for b in r if id(b) in ids]
        V = lambda fn, r, w: P.op("vector", fn, *_x(r, w))
        A = lambda fn, r, w: P.op("scalar", fn, *_x(r, w))
        G = lambda fn, r, w: P.op("gpsimd", fn, reads=r, writes=w)
        T = lambda fn, r, w: P.op("tensor", fn, reads=r, writes=w)
        cst = sb("cst", [128, 4, 128]); m4 = sb("m4", [128, 4, 128]); par = sb("par", [128, 8, W3]); mu = sb("mu", [128, 576])
        lmu = sb("lmu", [128, 5]); w2 = sb("w2", [96, W3]); a2 = sb("a2", [96, W3]); g2 = sb("g2", [128, 2, W3])
        ones = sb("ones", [128, 1]); epsg = sb("epsg", [128, 1])
        bcst, bm4, bpar, bmu, blmu, bw2, ba2, bg2, bones, bepsg = (Buf() for _ in range(10))
        P.dma("sync", cst[:], cst_in, writes=[bcst]); P.dma("sync", m4[:], m4_in, writes=[bm4])
        P.dma("sync", par[:], par_in, writes=[bpar]); P.dma("sync", mu[:], mu_in, writes=[bmu])
        P.dma("sync", lmu[:], lmu_in, writes=[blmu]); P.dma("sync", w2[:], w2_in, writes=[bw2])
        P.dma("sync", a2[:], a2_in, writes=[ba2])
        P.dma("sync", g2[:], g2_in.rearrange("(c p) n -> p c n", p=128), writes=[bg2])
        G(lambda e: e.memset(ones[:], 1.0), [], [bones]); G(lambda e: e.memset(epsg[:], A_GN_EPS), [], [bepsg])
        ident, triu, trirev, maskn = cst[:, 0, :], cst[:, 1, :], cst[:, 2, :], cst[:, 3, :]
        if has_vres:
            v2 = sb("v2", [64, W3]); bv2 = Buf()
            P.dma("sync", v2[:], v2_in, writes=[bv2])
        ltmp = sb("ltmp", [128, S]); bltmp = Buf()
        lo = {}
        specs = [("wd", wd_in, 96, 0, AF.Tanh), ("ad", ad_in, 96, 1, None), ("gd0", gd_in[0:128], 128, 2, AF.Sigmoid),
                 ("gd1", gd_in[128:256], 128, 3, AF.Sigmoid)]
        if has_vres:
            specs.append(("vd", vd_in, 64, 4, None))
        for name, src, rows, mcol, fn in specs:
            t = sb("lo_" + name, [rows, S + 1]); b = Buf()
            G(lambda e, t=t: e.memset(t[:, 0:1], 0.0), [], [b])
            for i in range(4):
                P.dma("sync" if i % 2 == 0 else "gpsimd", t[:, 1 + i * 1024:1 + (i + 1) * 1024], src[:, i * 1024:(i + 1) * 1024], writes=[b])
            V(lambda e, t=t, rows=rows: e.tensor_tensor(out=ltmp[:rows, :], in0=t[:, 0:S], in1=t[:, 1:S + 1], op=ALU.subtract), [b, bltmp], [bltmp])
            V(lambda e, t=t, rows=rows, mcol=mcol: e.scalar_tensor_tensor(out=t[:, 1:S + 1], in0=ltmp[:rows, :], scalar=lmu[:rows, mcol:mcol + 1],
                                                                      in1=t[:, 1:S + 1], op0=ALU.mult, op1=ALU.add), [bltmp, blmu, b], [b])
            if fn is not None:
                A(lambda e, t=t, fn=fn: e.activation(out=t[:, 1:S + 1], in_=t[:, 1:S + 1], func=fn), [b], [b])
            lo[name] = (t, b)
        NS = 2
        cur = [sb("cur%d" % i, [128, 576]) for i in range(NS)]; prv = [sb("prv%d" % i, [128, 576]) for i in range(NS)]
        bcur = [Buf() for _ in range(NS)]; bprv = [Buf() for _ in range(NS)]
        tt = [sb("tt%d" % i, [128, 8, W3]) for i in range(NS)]; btt = [Buf() for _ in range(NS)]
        rk = [sb("rk%d" % i, [128, 3]) for i in range(NS)]; brk = [Buf() for _ in range(NS)]
        tm = sb("tm", [128, 4, W3]); btm = Buf()
        ss = sb("ss", [128, 3]); bss = Buf()
        if has_vres:
            vf = [sb("vf%d" % i, [128, W3]) for i in range(NS)]; bvf = [Buf() for _ in range(NS)]
        ee = [sb("ee%d" % i, [128, 4, W3]) for i in range(NS)]; bee = [Buf() for _ in range(NS)]
        dd = [sb("dd%d" % i, [128, 6, W3]) for i in range(NS)]; bdd = [Buf() for _ in range(NS)]
        wc = [sb("wc%d" % i, [64, 4]) for i in range(NS)]; bwc = [Buf() for _ in range(NS)]
        yt = [sb("yt%d" % i, [128, W3]) for i in range(NS)]; byt = [Buf() for _ in range(NS)]
        ytT = [sb("ytT%d" % i, [128, 2, 128], BF16) for i in range(NS)]; bytT = [Buf() for _ in range(NS)]
        NZ = 2
        T4 = [sb("T4_%d" % z, [64, 4, 128]) for z in range(NZ)]; bT4 = [Buf() for _ in range(NZ)]
        M4 = [sb("M4_%d" % z, [128, 4, 128]) for z in range(NZ)]; bM4 = [Buf() for _ in range(NZ)]
        Pk = [[sb("Pk%d_%d" % (z, i), [128, 128]) for i in range(2)] for z in range(NZ)]; bPk = [[Buf(), Buf()] for _ in range(NZ)]
        Qk = [[sb("Qk%d_%d" % (z, i), [128, 128]) for i in range(2)] for z in range(NZ)]; bQk = [[Buf(), Buf()] for _ in range(NZ)]
        Zt = [sb("Z%d" % z, [128, 128]) for z in range(NZ)]; bZ = [Buf() for _ in range(NZ)]
        nX = [sb("nX%d" % z, [128, 64]) for z in range(NZ)]; bnX = [Buf() for _ in range(NZ)]
        Ut = [sb("U%d" % z, [128, 64]) for z in range(NZ)]; bU = [Buf() for _ in range(NZ)]
        st6 = [sb("st6_%d" % z, [128, 6]) for z in range(NZ)]; bst6 = [Buf() for _ in range(NZ)]
        mv = [sb("mv%d" % z, [128, 2]) for z in range(NZ)]; bmv = [Buf() for _ in range(NZ)]
        rsd = [sb("rsd%d" % z, [128, 1]) for z in range(NZ)]; brsd = [Buf() for _ in range(NZ)]
        ST = [sb("ST%d" % h, [64, 64]) for h in range(3)]; bST = [Buf() for _ in range(3)]
        for h in range(3):
            G(lambda e, h=h: e.memset(ST[h][:], 0.0), [], [bST[h]])
        pb = [es.enter_context(nc.psum_tensor(tg + "pb%d" % i, [128, 512], F32)) for i in range(8)]
        pL1, pL2, pGG = pb[0], pb[0], pb[1]
        bL1 = Buf(); bL2 = bL1; byT1 = bL1; byT2 = bL1; bGG = Buf()
        pX = [pb[2], pb[3]]; bX = [Buf(), Buf()]
        pIVs = [pb[4], pb[5]]; bIVs = [Buf(), Buf()]
        pCHs = [pb[6], pb[7]]; bCHs = [Buf(), Buf()]
        PSB.extend([bL1, bGG] + bX + bIVs + bCHs)
        rkv_v = rkv_in
        bout = Buf()
        wdT, bwd = lo["wd"]; adT, bad = lo["ad"]; gd0, bgd0 = lo["gd0"]; gd1, bgd1 = lo["gd1"]

        def prep(n):
            s = n % NS
            t0 = n * 128
            c_, p_, t_ = cur[s], prv[s], tt[s]
            P.dma("sync", c_[:], rkv_v[t0:t0 + 128, :], writes=[bcur[s]])
            if n == 0:
                G(lambda e: e.memset(p_[:], 0.0), [], [bprv[s]])
                P.dma("gpsimd", p_[1:128, :], rkv_v[0:127, :], writes=[bprv[s]])
            else:
                P.dma("gpsimd", p_[:], rkv_v[t0 - 1:t0 + 127, :], writes=[bprv[s]])
            if has_vres:
                P.dma("sync", vf[s][:], vf_in[t0:t0 + 128, :], writes=[bvf[s]])
            V(lambda e: e.tensor_tensor(out=p_[:], in0=p_[:], in1=c_[:], op=ALU.subtract), [bcur[s], bprv[s]], [bprv[s]])
            G(lambda e: e.tensor_tensor(out=p_[:], in0=p_[:], in1=mu[:], op=ALU.mult), [bprv[s], bmu], [bprv[s]])
            V(lambda e: e.tensor_tensor(out=c_[:], in0=c_[:], in1=p_[:], op=ALU.add), [bcur[s], bprv[s]], [bcur[s]])
            r_, k_, v_ = c_[:, 0:W3], c_[:, W3:2 * W3], c_[:, 2 * W3:3 * W3]
            tsl = slice(1 + t0, 1 + t0 + 128)
            T(lambda e: e.matmul(pL1[:, 0:W3], lhsT=wdT[:, tsl], rhs=w2[:], start=True, stop=True), [bwd, bw2], [bL1])
            T(lambda e: e.matmul(pL1[:, W3:2 * W3], lhsT=adT[:, tsl], rhs=a2[:], start=True, stop=True), [bad, ba2], [bL1])
            V(lambda e: e.tensor_tensor(out=t_[:, 5, :], in0=pL1[:, 0:W3], in1=par[:, 0, :], op=ALU.add), [bL1, bpar, btt[s]], [btt[s]])
            A(lambda e: e.activation(out=t_[:, 5, :], in_=t_[:, 5, :], func=AF.Sigmoid), [btt[s]], [btt[s]])
            V(lambda e: e.tensor_scalar(out=t_[:, 5, :], in0=t_[:, 5, :], scalar1=NEG_EHALF, scalar2=None, op0=ALU.mult), [btt[s]], [btt[s]])
            V(lambda e: e.tensor_tensor(out=t_[:, 7, :], in0=pL1[:, W3:2 * W3], in1=par[:, 1, :], op=ALU.add), [bL1, bpar, btt[s]], [btt[s]])
            A(lambda e: e.activation(out=t_[:, 7, :], in_=t_[:, 7, :], func=AF.Sigmoid), [btt[s]], [btt[s]])
            T(lambda e: e.matmul(pL2[:, 0:W3], lhsT=gd0[:, tsl], rhs=g2[:, 0, :], start=True, stop=False), [bgd0, bg2], [bL2])
            T(lambda e: e.matmul(pL2[:, 0:W3], lhsT=gd1[:, tsl], rhs=g2[:, 1, :], start=False, stop=True), [bgd1, bg2], [bL2])
            if has_vres:
                vdT, bvd = lo["vd"]
                T(lambda e: e.matmul(pL2[:, W3:2 * W3], lhsT=vdT[:, tsl], rhs=v2[:], start=True, stop=True), [bvd, bv2], [bL2])
            A(lambda e: e.copy(out=t_[:, 6, :], in_=pL2[:, 0:W3]), [bL2, btt[s]], [btt[s]])
            G(lambda e: e.tensor_copy(out=t_[:, 0, :], in_=r_), [bcur[s], btt[s]], [btt[s]])
            if has_vres:
                V(lambda e: e.tensor_tensor(out=tm[:, 0, :], in0=pL2[:, W3:2 * W3], in1=par[:, 2, :], op=ALU.add), [bL2, bpar, btm], [btm])
                A(lambda e: e.activation(out=tm[:, 0, :], in_=tm[:, 0, :], func=AF.Sigmoid), [btm], [btm])
                V(lambda e: e.tensor_tensor(out=tm[:, 1, :], in0=vf[s][:], in1=v_, op=ALU.subtract), [bvf[s], bcur[s], btm], [btm])
                V(lambda e: e.tensor_tensor(out=tm[:, 1, :], in0=tm[:, 1, :], in1=tm[:, 0, :], op=ALU.mult), [btm], [btm])
                V(lambda e: e.tensor_tensor(out=t_[:, 2, :], in0=tm[:, 1, :], in1=v_, op=ALU.add), [btm, bcur[s], btt[s]], [btt[s]])
            else:
                G(lambda e: e.tensor_copy(out=t_[:, 2, :], in_=v_), [bcur[s], btt[s]], [btt[s]])
                P.dma("sync", v_out[t0:t0 + 128, :], t_[:, 2, :], reads=[btt[s]], writes=[bout])
            V(lambda e: e.tensor_tensor(out=t_[:, 3, :], in0=k_, in1=par[:, 3, :], op=ALU.mult), [bcur[s], bpar, btt[s]], [btt[s]])
            V(lambda e: e.tensor_tensor(out=tm[:, 2, :], in0=t_[:, 3, :], in1=t_[:, 3, :], op=ALU.mult), [btt[s], btm], [btm])
            V(lambda e: e.tensor_reduce(out=ss[:], in_=tm[:, 2, :].rearrange("p (h j) -> p h j", h=3), axis=AX.X, op=ALU.add), [btm, bss], [bss])
            A(lambda e: e.activation(out=ss[:], in_=ss[:], func=AF.Sqrt), [bss], [bss])
            V(lambda e: e.tensor_scalar(out=ss[:], in0=ss[:], scalar1=1e-12, scalar2=None, op0=ALU.max), [bss], [bss])
            V(lambda e: e.reciprocal(out=ss[:], in_=ss[:]), [bss], [bss])
            for h in range(3):
                hs = slice(64 * h, 64 * h + 64)
                V(lambda e, hs=hs, h=h: e.tensor_scalar(out=t_[:, 3, hs], in0=t_[:, 3, hs], scalar1=ss[:, h:h + 1], scalar2=None, op0=ALU.mult),
                  [bss, btt[s]], [btt[s]])
            V(lambda e: e.scalar_tensor_tensor(out=tm[:, 3, :], in0=t_[:, 7, :], scalar=-1.0, in1=par[:, 4, :], op0=ALU.add, op1=ALU.mult),
              [btt[s], bpar, btm], [btm])
            V(lambda e: e.scalar_tensor_tensor(out=t_[:, 1, :], in0=tm[:, 3, :], scalar=1.0, in1=k_, op0=ALU.add, op1=ALU.mult),
              [btm, bcur[s], btt[s]], [btt[s]])
            G(lambda e: e.tensor_tensor(out=t_[:, 4, :], in0=t_[:, 3, :], in1=t_[:, 7, :], op=ALU.mult), [btt[s]], [btt[s]])
            V(lambda e: e.tensor_tensor(out=tm[:, 3, :], in0=t_[:, 0, :], in1=t_[:, 1, :], op=ALU.mult), [btt[s], btm], [btm])
            V(lambda e: e.tensor_tensor(out=tm[:, 3, :], in0=tm[:, 3, :], in1=par[:, 5, :], op=ALU.mult), [btm, bpar], [btm])
            V(lambda e: e.tensor_reduce(out=rk[s][:], in_=tm[:, 3, :].rearrange("p (h j) -> p h j", h=3), axis=AX.X, op=ALU.add), [btm, brk[s]], [brk[s]])
            T(lambda e: e.matmul(pGG[:, 0:W3], lhsT=triu, rhs=t_[:, 5, :], start=True, stop=True), [bcst, btt[s]], [bGG])
            T(lambda e: e.matmul(pGG[:, W3:2 * W3], lhsT=trirev, rhs=t_[:, 5, :], start=True, stop=True), [bcst, btt[s]], [bGG])
            for h in range(3):
                T(lambda e, h=h: e.matmul(pGG[0:64, 400 + h:401 + h], lhsT=t_[:, 5, 64 * h:64 * h + 64], rhs=ones[:], start=True, stop=True),
                  [btt[s], bones], [bGG])
            e_, d_ = ee[s], dd[s]
            A(lambda e: e.activation(out=e_[:, 0, :], in_=pGG[:, 0:W3], func=AF.Exp), [bGG, bee[s]], [bee[s]])
            A(lambda e: e.activation(out=e_[:, 1, :], in_=pGG[:, 0:W3], func=AF.Exp, scale=-1.0), [bGG, bee[s]], [bee[s]])
            V(lambda e: e.tensor_tensor(out=e_[:, 2, :], in0=pGG[:, 0:W3], in1=t_[:, 5, :], op=ALU.subtract), [bGG, btt[s], bee[s]], [bee[s]])
            A(lambda e: e.activation(out=e_[:, 2, :], in_=e_[:, 2, :], func=AF.Exp), [bee[s]], [bee[s]])
            A(lambda e: e.activation(out=e_[:, 3, :], in_=pGG[:, W3:2 * W3], func=AF.Exp), [bGG, bee[s]], [bee[s]])
            A(lambda e: e.activation(out=wc[s][:, 0:3], in_=pGG[0:64, 400:403], func=AF.Exp), [bGG, bwc[s]], [bwc[s]])
            for (di, ti, ei, eng) in ((0, 0, 0, V), (1, 3, 2, G), (2, 1, 1, V), (3, 4, 1, G), (4, 1, 3, V), (5, 4, 3, G)):
                eng(lambda e, di=di, ti=ti, ei=ei: e.tensor_tensor(out=d_[:, di, :], in0=t_[:, ti, :], in1=e_[:, ei, :], op=ALU.mult),
                    [btt[s], bee[s], bdd[s]], [bdd[s]])

        def head_gen(n, h, z):
            s = n % NS
            t_, d_ = tt[s], dd[s]
            hs = slice(64 * h, 64 * h + 64)
            X, IV = pX[z], pIVs[z]
            CH = pCHs[z][:, 0:256]
            CH64 = pCHs[z][0:64, 0:256]
            bXz, bIV, bCH = bX[z], bIVs[z], bCHs[z]
            t4, m4_, Z, nx, U = T4[z], M4[z], Zt[z], nX[z], Ut[z]
            for j in range(4):
                T(lambda e, j=j: e.transpose(X[0:64, j * 128:(j + 1) * 128], d_[:, j, hs], ident), [bdd[s], bcst], [bXz])
                yield
            A(lambda e: e.copy(out=t4[:], in_=X[0:64, :].rearrange("p (a b) -> p a b", a=4)), [bXz, bT4[z]], [bT4[z]])
            yield
            T(lambda e: e.matmul(X[:, 0:256], lhsT=t4[:, 2, :], rhs=t4[:, 0:2, :].rearrange("p a b -> p (a b)"), start=True, stop=True), [bT4[z]], [bXz])
            T(lambda e: e.matmul(X[:, 256:512], lhsT=t4[:, 3, :], rhs=t4[:, 0:2, :].rearrange("p a b -> p (a b)"), start=True, stop=True), [bT4[z]], [bXz])
            T(lambda e: e.matmul(IV[:, 0:128], lhsT=t4[:, 1, :], rhs=t4[:, 3, :], start=True, stop=True), [bT4[z]], [bIV])
            yield
            V(lambda e: e.tensor_tensor(out=m4_[:].rearrange("p a b -> p (a b)"), in0=X[:], in1=m4[:].rearrange("p a b -> p (a b)"), op=ALU.mult),
              [bXz, bm4, bM4[z]], [bM4[z]])
            V(lambda e: e.tensor_tensor(out=Pk[z][0][:], in0=IV[:, 0:128], in1=maskn, op=ALU.mult), [bIV, bcst, bPk[z][0]], [bPk[z][0]])
            yield
            A(lambda e: e.copy(out=Qk[z][0][:], in_=m4_[:, 3, :]), [bM4[z], bQk[z][0]], [bQk[z][0]])
            V(lambda e: e.tensor_tensor(out=Z[:], in0=ident, in1=m4_[:, 3, :], op=ALU.subtract), [bM4[z], bcst, bZ[z]], [bZ[z]])
            yield
            for lv in range(1, 7):
                pi, ci = (lv - 1) % 2, lv % 2
                if lv < 6:
                    T(lambda e, pi=pi: e.matmul(IV[:, 128:256], lhsT=Pk[z][pi][:], rhs=Qk[z][pi][:], start=True, stop=True), [bPk[z][pi], bQk[z][pi]], [bIV])
                T(lambda e, pi=pi: e.matmul(IV[:, 256:384], lhsT=Qk[z][pi][:], rhs=Pk[z][pi][:], start=True, stop=True), [bPk[z][pi], bQk[z][pi]], [bIV])
                yield
                if lv < 6:
                    A(lambda e, ci=ci: e.copy(out=Qk[z][ci][:], in_=IV[:, 128:256]), [bIV, bQk[z][ci]], [bQk[z][ci]])
                A(lambda e, ci=ci: e.copy(out=Pk[z][ci][:], in_=IV[:, 256:384]), [bIV, bPk[z][ci]], [bPk[z][ci]])
                yield
                T(lambda e, ci=ci: e.matmul(IV[:, 384:512], lhsT=Pk[z][ci][:], rhs=Z[:], start=True, stop=True), [bPk[z][ci], bZ[z]], [bIV])
                yield
                V(lambda e: e.tensor_tensor(out=Z[:], in0=Z[:], in1=IV[:, 384:512], op=ALU.add), [bIV, bZ[z]], [bZ[z]])
                yield
            vh = t_[:, 2, hs]
            T(lambda e: e.matmul(CH[:, 0:64], lhsT=m4_[:, 1, :], rhs=vh, start=True, stop=False), [bM4[z], btt[s]], [bCH])
            T(lambda e: e.matmul(CH[:, 0:64], lhsT=t4[:, 1, :], rhs=ST[h][:], start=False, stop=True), [bT4[z], bST[h]], [bCH])
            yield
            A(lambda e: e.activation(out=nx[:], in_=CH[:, 0:64], func=AF.Copy, scale=-1.0), [bCH, bnX[z]], [bnX[z]])
            yield
            T(lambda e: e.matmul(CH[:, 64:128], lhsT=Z[:], rhs=nx[:], start=True, stop=True), [bZ[z], bnX[z]], [bCH])
            yield
            V(lambda e: e.tensor_scalar(out=U[:], in0=CH[:, 64:128], scalar1=1.0, scalar2=None, op0=ALU.mult), [bCH, bU[z]], [bU[z]])
            yield
            T(lambda e: e.matmul(CH[:, 128:192], lhsT=t4[:, 0, :], rhs=ST[h][:], start=True, stop=False), [bT4[z], bST[h]], [bCH])
            T(lambda e: e.matmul(CH[:, 128:192], lhsT=m4_[:, 2, :], rhs=U[:], start=False, stop=False), [bM4[z], bU[z]], [bCH])
            T(lambda e: e.matmul(CH[:, 128:192], lhsT=m4_[:, 0, :], rhs=vh, start=False, stop=True), [bM4[z], btt[s]], [bCH])
            T(lambda e: e.matmul(CH64[:, 192:256], lhsT=d_[:, 5, hs], rhs=U[:], start=True, stop=False), [bdd[s], bU[z]], [bCH])
            T(lambda e: e.matmul(CH64[:, 192:256], lhsT=d_[:, 4, hs], rhs=vh, start=False, stop=True), [bdd[s], btt[s]], [bCH])
            yield
            V(lambda e: e.scalar_tensor_tensor(out=ST[h][:], in0=ST[h][:], scalar=wc[s][:, h:h + 1], in1=CH64[:, 192:256], op0=ALU.mult, op1=ALU.add),
              [bCH, bwc[s], bST[h]], [bST[h]])
            V(lambda e: e.bn_stats(out=st6[z][:], in_=CH[:, 128:192]), [bCH, bst6[z]], [bst6[z]])
            yield
            V(lambda e: e.bn_aggr(out=mv[z][:], in_=st6[z][:]), [bst6[z], bmv[z]], [bmv[z]])
            yield
            A(lambda e: e.activation(out=rsd[z][:], in_=mv[z][:, 1:2], func=AF.Sqrt, bias=epsg[:, 0:1]), [bmv[z], bepsg, brsd[z]], [brsd[z]])
            yield
            V(lambda e: e.reciprocal(out=rsd[z][:], in_=rsd[z][:]), [brsd[z]], [brsd[z]])
            yield
            V(lambda e: e.tensor_scalar(out=yt[s][:, hs], in0=CH[:, 128:192], scalar1=mv[z][:, 0:1], scalar2=rsd[z][:, 0:1], op0=ALU.subtract, op1=ALU.mult),
              [bCH, bmv[z], brsd[z], byt[s]], [byt[s]])
            yield

        def post(n):
            s = n % NS
            t0 = n * 128
            t_ = tt[s]
            G(lambda e: e.tensor_tensor(out=yt[s][:], in0=yt[s][:], in1=par[:, 6, :], op=ALU.mult), [byt[s], bpar], [byt[s]])
            G(lambda e: e.tensor_tensor(out=yt[s][:], in0=yt[s][:], in1=par[:, 7, :], op=ALU.add), [byt[s], bpar], [byt[s]])
            for h in range(3):
                hs = slice(64 * h, 64 * h + 64)
                V(lambda e, hs=hs, h=h: e.scalar_tensor_tensor(out=yt[s][:, hs], in0=t_[:, 2, hs], scalar=rk[s][:, h:h + 1], in1=yt[s][:, hs],
                                                                op0=ALU.mult, op1=ALU.add), [btt[s], brk[s], byt[s]], [byt[s]])
            V(lambda e: e.tensor_tensor(out=yt[s][:], in0=yt[s][:], in1=t_[:, 6, :], op=ALU.mult), [byt[s], btt[s]], [byt[s]])
            T(lambda e: e.transpose(pL1[:, 384:512], yt[s][:, 0:128], ident), [byt[s], bcst], [byT1])
            A(lambda e: e.copy(out=ytT[s][:, 0, :], in_=pL1[:, 384:512]), [byT1, bytT[s]], [bytT[s]])
            T(lambda e: e.transpose(pL1[0:64, 384:512], yt[s][:, 128:192], ident), [byt[s], bcst], [byT1])
            A(lambda e: e.copy(out=ytT[s][0:64, 1, :], in_=pL1[0:64, 384:512]), [byT1, bytT[s]], [bytT[s]])
            P.dma("sync", ysrc[0:128, t0:t0 + 128], ytT[s][:, 0, :], reads=[bytT[s]], writes=[bout])
            P.dma("sync", ysrc[128:192, t0:t0 + 128], ytT[s][0:64, 1, :], reads=[bytT[s]], writes=[bout])

        jobs = [(n, h) for n in range(nt) for h in range(3)]
        prepped = [-1]
        done = {}
        for p0 in range(0, len(jobs), NZ):
            pair = jobs[p0:p0 + NZ]
            for (n, h) in pair:
                while prepped[0] < n:
                    prepped[0] += 1
                    prep(prepped[0])
            gens = [head_gen(n, h, z) for z, (n, h) in enumerate(pair)]
            while gens:
                for g_ in list(gens):
                    try:
                        next(g_)
                    except StopIteration:
                        gens.remove(g_)
            for (n, h) in pair:
                done[n] = done.get(n, 0) + 1
                if done[n] == 3:
                    post(n)
        P.final_wait("sync", [bout]); P.final_wait("gpsimd", [bout])
        P.emit()


def p2a_consts():
    i = np.arange(128)
    row, col = i[:, None], i[None, :]
    cst = np.stack([np.eye(128), (row <= col), (row > col), (row > col)], axis=1).astype(np.float32)
    incl = (row <= col).astype(np.float32); strict = (row < col).astype(np.float32)
    mask4 = np.stack([incl, strict, incl, strict], axis=1).astype(np.float32)
    return dict(cst=np.ascontiguousarray(cst), mask4=np.ascontiguousarray(mask4))


NTM = 1344
NFM = 768
NY = 576
GROUPS = [[0, 1, 2, 3], [4, 5, 6, 7]]


def phase_norm(nc, io, l, x_ap):
    tg = "n%d_" % l
    P = Prog(nc, tg)
    with ExitStack() as es:
        xs = es.enter_context(nc.sbuf_tensor(tg + "xs", [128, KC, TOK], F32))
        bxs, bhs = Buf(), Buf()
        xv = x_ap.rearrange("(c p) t -> p c t", p=128)
        for i in range(4):
            P.dma("sync", xs[:, i * 4:(i + 1) * 4, :], xv[:, i * 4:(i + 1) * 4, :], writes=[bxs])
        hT, bh = rms_to_bf16(P, nc, es, xs, bxs, io["g1_%d" % l], tg)
        hv = io["h_src"].rearrange("(c p) t -> p c t", p=128)
        for i in range(4):
            P.dma("sync", hv[:, i * 4:(i + 1) * 4, :], hT[:, i * 4:(i + 1) * 4, :], reads=[bh], writes=[bhs])
        P.final_wait("sync", [bhs]); P.final_wait("gpsimd", [bhs])
        P.emit()


def phase_inproj(nc, io, l):
    tg = "p%d_" % l
    has_vres = l > 0
    P = Prog(nc, tg)
    h_src, h_all, tm_all, fm_all = io["h_src_t"], io["h_all_t"], io["tm_all"], io["fm_all"]
    wtm_ap, wfm_ap = io["wtm%d" % l], io["wfm%d" % l]
    with ExitStack() as es:
        sb = lambda n, shp, d=F32: es.enter_context(nc.sbuf_tensor(tg + n, shp, d))
        bhall = Buf()
        for kk in range(4):
            P.cc(lambda e, kk=kk: e.collective_compute("AllGather", mybir.AluOpType.bypass, replica_groups=GROUPS,
                                                       ins=[h_src.ap()[kk * 512:(kk + 1) * 512, :]], outs=[h_all.ap()[kk * 2048:(kk + 1) * 2048, :]]), writes=[bhall])
        wtm = sb("wtm", [128, KC, NTM], BF16); wfm = sb("wfm", [128, KC, NFM], BF16)
        bwtm, bwfm = Buf(), Buf()
        stg = [sb("stg%d" % i, [128, NTM], F32) for i in range(2)]
        bstg = [Buf(), Buf()]
        k = 0
        for (w_ap, wt, bw, n) in ((wtm_ap, wtm, bwtm, NTM), (wfm_ap, wfm, bwfm, NFM)):
            for c in range(KC):
                s = k % 2
                k += 1
                P.dma("sync" if s == 0 else "gpsimd", stg[s][:, :n], w_ap[c * 128:(c + 1) * 128, :], writes=[bstg[s]])
                if s == 0:
                    P.op("gpsimd", lambda e, s=s, wt=wt, c=c, n=n: e.tensor_copy(out=wt[:, c, :], in_=stg[s][:, :n]), reads=[bstg[s]], writes=[bw])
                else:
                    P.op("scalar", lambda e, s=s, wt=wt, c=c, n=n: e.copy(out=wt[:, c, :], in_=stg[s][:, :n]), reads=[bstg[s]], writes=[bw])
        hT = [sb("hT%d" % i, [128, KC, TOK], BF16) for i in range(2)]
        bhT = [Buf(), Buf()]
        otm = [sb("otm%d" % i, [128, NTM], F32) for i in range(2)]
        botm = [Buf(), Buf()]
        ofm = [sb("ofm%d" % i, [128, 512], F32) for i in range(4)]
        bofm = [Buf() for _ in range(4)]
        pt = [es.enter_context(nc.psum_tensor(tg + "pt%d" % i, [128, 512], F32)) for i in range(6)]
        bpt = [Buf() for _ in range(6)]
        hav = h_all.ap()
        btm, bfm = Buf(), Buf()
        fm_blocks = [(0, 96), (96, 192), (192, 320), (320, 448)] + ([(448, 512)] if has_vres else []) + [(512, 640), (640, 768)]
        tm_blocks = [(0, 512), (512, 1024), (1024, NTM)]
        pk = 0
        fk = 0
        for r in range(4):
            hs = r % 2
            for i in range(4):
                hv = hav[i * 2048 + r * 512:i * 2048 + (r + 1) * 512, :].rearrange("(c p) t -> p c t", p=128)
                P.dma("sync" if i % 2 == 0 else "gpsimd", hT[hs][:, i * 4:(i + 1) * 4, :], hv, reads=[bhall], writes=[bhT[hs]])
            for tl in range(8):
                os_ = (r * 8 + tl) % 2
                tsl = slice(tl * 128, (tl + 1) * 128)
                for bi, (c0, c1) in enumerate(tm_blocks):
                    q = pk % 6
                    pk += 1
                    for c in range(KC):
                        P.op("tensor", lambda e, q=q, c=c, hs=hs, tsl=tsl, c0=c0, c1=c1: e.matmul(pt[q][:, 0:c1 - c0], lhsT=hT[hs][:, c, tsl], rhs=wtm[:, c, c0:c1],
                                                                                                  start=(c == 0), stop=(c == KC - 1)),
                             reads=[bhT[hs], bwtm], writes=[bpt[q]])
                    if bi % 2 == 0:
                        P.op("scalar", lambda e, q=q, os_=os_, c0=c0, c1=c1: e.copy(out=otm[os_][:, c0:c1], in_=pt[q][:, 0:c1 - c0]), reads=[bpt[q]], writes=[botm[os_]])
                    else:
                        P.op("vector", lambda e, q=q, os_=os_, c0=c0, c1=c1: e.tensor_scalar(out=otm[os_][:, c0:c1], in0=pt[q][:, 0:c1 - c0], scalar1=1.0, scalar2=None, op0=ALU.mult),
                             reads=[bpt[q]], writes=[botm[os_]])
                t0 = r * 1024 + tl * 128
                P.dma("sync", tm_all[t0:t0 + 128, :], otm[os_][:], reads=[botm[os_]], writes=[btm])
            for (b0, b1) in fm_blocks:
                mw = b1 - b0
                for hf in range(2):
                    q = pk % 6
                    pk += 1
                    fs = fk % 4
                    fk += 1
                    sl = slice(hf * 512, (hf + 1) * 512)
                    for c in range(KC):
                        P.op("tensor", lambda e, q=q, c=c, hs=hs, sl=sl, b0=b0, b1=b1, mw=mw: e.matmul(pt[q][:mw, :], lhsT=wfm[:, c, b0:b1], rhs=hT[hs][:, c, sl],
                                                                                                       start=(c == 0), stop=(c == KC - 1)),
                             reads=[bhT[hs], bwfm], writes=[bpt[q]])
                    if fk % 2 == 0:
                        P.op("scalar", lambda e, q=q, fs=fs, mw=mw: e.copy(out=ofm[fs][:mw, :], in_=pt[q][:mw, :]), reads=[bpt[q]], writes=[bofm[fs]])
                    else:
                        P.op("vector", lambda e, q=q, fs=fs, mw=mw: e.tensor_scalar(out=ofm[fs][:mw, :], in0=pt[q][:mw, :], scalar1=1.0, scalar2=None, op0=ALU.mult),
                             reads=[bpt[q]], writes=[bofm[fs]])
                    P.dma("gpsimd", fm_all[b0:b1, r * 1024 + hf * 512:r * 1024 + (hf + 1) * 512], ofm[fs][:mw, :], reads=[bofm[fs]], writes=[bfm])
        P.final_wait("sync", [btm, bfm]); P.final_wait("gpsimd", [btm, bfm])
        P.emit()


DFF = 5632
NFB = DFF // 128
GRP = 4
KY = 18


def phase_p3(nc, io, l, x_ap, out_ap, final):
    tg = "f%d_" % l
    y_src, y_all = io["y_src_t"], io["y_all_t"]
    wout, g2, wg, wu, wd = io["wout%d" % l], io["g2_%d" % l], io["wg%d" % l], io["wu%d" % l], io["wd%d" % l]
    P = Prog(nc, tg)
    with ExitStack() as es:
        sb = lambda n, shp, dt: es.enter_context(nc.sbuf_tensor(tg + n, shp, dt))
        pst = lambda n: es.enter_context(nc.psum_tensor(tg + n, [128, 512], F32))
        byall = Buf()
        for kk in range(6):
            P.cc(lambda e, kk=kk: e.collective_compute("AllGather", mybir.AluOpType.bypass, replica_groups=GROUPS,
                                                       ins=[y_src.ap()[kk * 96:(kk + 1) * 96, :]], outs=[y_all.ap()[kk * 384:(kk + 1) * 384, :]]), writes=[byall])
        xs = sb("xs", [128, KC, TOK], F32)
        hT = sb("hT", [128, KY, TOK], BF16)
        bxs, bh = Buf(), Buf()
        NST = 4
        stg = [sb("stg%d" % i, [128, 2048], F32) for i in range(NST)]
        wbf = [sb("wbf%d" % i, [128, 2048], BF16) for i in range(NST)]
        bstg = [Buf() for _ in range(NST)]
        bwbf = [Buf() for _ in range(NST)]
        act = [sb("act%d" % i, [128, GRP, TOK], BF16) for i in range(2)]
        bact = [Buf(), Buf()]
        sg = [sb("sg%d" % i, [128, 512], F32) for i in range(2)]
        bsg = [Buf(), Buf()]
        sel = sb("sel", [128, 4], F32); bsel = Buf()
        pg = [pst("pg%d" % i) for i in range(2)]
        pu = [pst("pu%d" % i) for i in range(2)]
        pd = [pst("pd%d" % i) for i in range(2)]
        bpg, bpu, bpd = [Buf(), Buf()], [Buf(), Buf()], [Buf(), Buf()]
        xv = x_ap.rearrange("(c p) t -> p c t", p=128)
        for i in range(4):
            P.dma("sync", xs[:, i * 4:(i + 1) * 4, :], xv[:, i * 4:(i + 1) * 4, :], writes=[bxs])
        P.dma("sync", sel[:], io["sel"], writes=[bsel])
        st = [0]

        def load_cast(src_ap, view3=None):
            s = st[0] % NST
            st[0] += 1
            dst = stg[s][:] if view3 is None else stg[s][:].rearrange(view3[0], **view3[1])
            P.dma("sync", dst, src_ap, writes=[bstg[s]])
            P.op("gpsimd", lambda e, s=s: e.tensor_copy(out=wbf[s][:], in_=stg[s][:]), reads=[bstg[s]], writes=[bwbf[s]])
            return wbf[s], bwbf[s]

        yst = [stg[i][:].bitcast(BF16) for i in range(2)]
        yav = y_all.ap()
        for c in range(KY):
            s = c % 2
            P.dma("sync" if s == 0 else "gpsimd", yst[s], yav[c * 128:(c + 1) * 128, :], reads=[byall], writes=[bstg[s]])
            P.op("vector", lambda e, c=c, s=s: e.tensor_scalar(out=hT[:, c, :], in0=yst[s][:, 0:TOK], scalar1=sel[:, 0:1], scalar2=None, op0=ALU.mult),
                 reads=[bstg[s], bsel], writes=[bh])
            for gi in range(1, 4):
                P.op("vector", lambda e, c=c, s=s, gi=gi: e.scalar_tensor_tensor(out=hT[:, c, :], in0=yst[s][:, gi * TOK:(gi + 1) * TOK], scalar=sel[:, gi:gi + 1],
                                                                                 in1=hT[:, c, :], op0=ALU.mult, op1=ALU.add),
                     reads=[bstg[s], bsel, bh], writes=[bh])
        st[0] = 2
        wov = wout.rearrange("(c p) n -> p c n", p=128)
        k = 0
        for m in range(KC):
            t, b = load_cast(wov[:, 0:KC, m * 128:(m + 1) * 128], ("p (c n) -> p c n", dict(c=KC)))
            tv = t[:].rearrange("p (c n) -> p c n", c=KC)
            s2 = st[0] % NST
            st[0] += 1
            P.dma("sync", stg[s2][:, 0:256].rearrange("p (c n) -> p c n", c=2), wov[:, KC:KY, m * 128:(m + 1) * 128], writes=[bstg[s2]])
            P.op("gpsimd", lambda e, s2=s2: e.tensor_copy(out=wbf[s2][:, 0:256], in_=stg[s2][:, 0:256]), reads=[bstg[s2]], writes=[bwbf[s2]])
            t2v = wbf[s2][:, 0:256].rearrange("p (c n) -> p c n", c=2)
            b2 = bwbf[s2]
            for hf in range(2):
                q = k % 2
                k += 1
                sl = slice(hf * 512, (hf + 1) * 512)
                for c in range(KY):
                    lt = tv[:, c, :] if c < KC else t2v[:, c - KC, :]
                    P.op("tensor", lambda e, lt=lt, c=c, q=q, sl=sl: e.matmul(pd[q][:], lhsT=lt, rhs=hT[:, c, sl], start=(c == 0), stop=(c == KY - 1)),
                         reads=[b, b2, bh], writes=[bpd[q]])
                P.op("vector", lambda e, m=m, q=q, sl=sl: e.tensor_tensor(out=xs[:, m, sl], in0=xs[:, m, sl], in1=pd[q][:], op=ALU.add),
                     reads=[bpd[q], bxs], writes=[bxs])
        scr = {}
        rms_to_bf16(P, nc, es, xs, bxs, g2, tg + "n2", out_tile=hT, out_buf=bh, scr=scr)
        wgv = wg.rearrange("(c p) n -> p c n", p=128)
        wuv = wu.rearrange("(c p) n -> p c n", p=128)
        for grp in range(NFB // GRP):
            a = act[grp % 2]
            ba = bact[grp % 2]
            wds = []
            for j in range(GRP):
                fb = grp * GRP + j
                tg_, bg_ = load_cast(wgv[:, :, fb * 128:(fb + 1) * 128], ("p (c n) -> p c n", dict(c=KC)))
                tu, bu_ = load_cast(wuv[:, :, fb * 128:(fb + 1) * 128], ("p (c n) -> p c n", dict(c=KC)))
                tgv = tg_[:].rearrange("p (c n) -> p c n", c=KC)
                tuv = tu[:].rearrange("p (c n) -> p c n", c=KC)
                for hf in range(2):
                    sl = slice(hf * 512, (hf + 1) * 512)
                    for c in range(KC):
                        P.op("tensor", lambda e, tgv=tgv, c=c, hf=hf, sl=sl: e.matmul(pg[hf][:], lhsT=tgv[:, c, :], rhs=hT[:, c, sl],
                                                                                   start=(c == 0), stop=(c == KC - 1)),
                             reads=[bg_, bh], writes=[bpg[hf]])
                    for c in range(KC):
                        P.op("tensor", lambda e, tuv=tuv, c=c, hf=hf, sl=sl: e.matmul(pu[hf][:], lhsT=tuv[:, c, :], rhs=hT[:, c, sl],
                                                                                   start=(c == 0), stop=(c == KC - 1)),
                             reads=[bu_, bh], writes=[bpu[hf]])
                    P.op("scalar", lambda e, hf=hf: e.activation(out=sg[hf][:], in_=pg[hf][:], func=AF.Silu),
                         reads=[bpg[hf]], writes=[bsg[hf]])
                    P.op("vector", lambda e, hf=hf, a=a, j=j, sl=sl: e.tensor_tensor(out=a[:, j, sl], in0=sg[hf][:], in1=pu[hf][:], op=ALU.mult),
                         reads=[bsg[hf], bpu[hf]], writes=[ba])
            for j in range(GRP):
                fb = grp * GRP + j
                wds.append(load_cast(wd[fb * 128:(fb + 1) * 128, :]))
            for m in range(KC):
                for hf in range(2):
                    q = k % 2
                    k += 1
                    sl = slice(hf * 512, (hf + 1) * 512)
                    for j in range(GRP):
                        td, bd_ = wds[j]
                        P.op("tensor", lambda e, td=td, j=j, m=m, q=q, sl=sl, a=a: e.matmul(pd[q][:], lhsT=td[:, m * 128:(m + 1) * 128], rhs=a[:, j, sl],
                                                                                         start=(j == 0), stop=(j == GRP - 1)),
                             reads=[bd_, ba], writes=[bpd[q]])
                    P.op("vector", lambda e, m=m, q=q, sl=sl: e.tensor_tensor(out=xs[:, m, sl], in0=xs[:, m, sl], in1=pd[q][:], op=ALU.add),
                         reads=[bpd[q], bxs], writes=[bxs])
        bout = Buf()
        ov = out_ap.rearrange("(c p) t -> p c t", p=128)
        if final:
            rms_to_bf16(P, nc, es, xs, bxs, io["gf"], tg + "n3", out_tile=xs, out_buf=bxs, scr=scr)
        for i in range(4):
            P.dma("sync", ov[:, i * 4:(i + 1) * 4, :], xs[:, i * 4:(i + 1) * 4, :], reads=[bxs], writes=[bout])
        P.final_wait("sync", [bout]); P.final_wait("gpsimd", [bout])
        P.emit()


def build_fused(upto=99, dbg=None, only=None):
    nc = bass.Bass("TRN2", target_bir_lowering=False)
    _POOL[0] = SemPool(nc)
    io = {}
    ext = lambda n, shp, d=F32: io.__setitem__(n, nc.dram_tensor(n, shp, d, kind="ExternalInput").ap())
    ext("xT", [D, TOK]); ext("sel", [128, 4]); ext("gf", [128, KC])
    ext("cst", [128, 4, 128]); ext("mask4", [128, 4, 128])
    ext("pos", [128, NT], I32); ext("invf", [128, 32]); ext("ident", [128, 128]); ext("maskT", [2, 128, 128]); ext("qdec", [2, 64, 128])
    ext("kdec", [128, 2]); ext("g128", [64, 2]); ext("v2c", [64, W3])
    for l in range(2):
        ext("g1_%d" % l, [128, KC]); ext("wtm%d" % l, [D, NTM]); ext("wfm%d" % l, [D, NFM])
        ext("mu_rkv%d" % l, [128, 576]); ext("lmu%d" % l, [128, 5]); ext("w2c%d" % l, [96, W3]); ext("a2c%d" % l, [96, W3])
        ext("g2c%d" % l, [256, W3]); ext("par%d" % l, [128, 8, W3])
        ext("sm%d" % l, [128, 8]); ext("wa%d" % l, [2, 64, 64]); ext("wx%d" % l, [2, 64, 64])
        ext("gng%d" % l, [2, 128, 128])
        if upto > 6 * l + 5:
            ext("wout%d" % l, [KY * 128, D]); ext("g2_%d" % l, [128, KC]); ext("wg%d" % l, [D, DFF]); ext("wu%d" % l, [D, DFF]); ext("wd%d" % l, [DFF, D])
    out = nc.dram_tensor("outT", [D, TOK], F32, kind="ExternalOutput").ap()
    io["h_src_t"] = nc.dram_tensor("h_src", [D, TOK], BF16); io["h_src"] = io["h_src_t"].ap()
    io["h_all_t"] = nc.dram_tensor("h_all", [4 * D, TOK], BF16)
    io["tm_all"] = nc.dram_tensor("tm_all", [S, NTM], F32).ap()
    io["fm_all"] = nc.dram_tensor("fm_all", [NFM, S], F32).ap()
    io["y_src_t"] = nc.dram_tensor("y_src", [NY, S], BF16); io["y_src"] = io["y_src_t"].ap()
    io["y_all_t"] = nc.dram_tensor("y_all", [4 * NY, S], BF16)
    io["vfirst"] = nc.dram_tensor("vfirst", [S, W3], F32).ap()
    x_cur = nc.dram_tensor("x_cur", [D, TOK], F32).ap()
    io["x_cur"] = x_cur
    ph = 0
    for l in range(2):
        steps = [lambda l=l: phase_norm(nc, io, l, io["xT"] if l == 0 else x_cur),
                 lambda l=l: phase_inproj(nc, io, l),
                 lambda l=l: phase_p2a_il(nc, io, l),
                 lambda l=l: phase_p2b(nc, io, l),
                 lambda l=l: phase_p2c(nc, io, l),
                 lambda l=l: phase_p3(nc, io, l, io["xT"] if l == 0 else x_cur, out if l == 1 else x_cur, final=(l == 1))]
        for st_ in steps:
            if ph < upto and (only is None or ph in only):
                st_()
            ph += 1
    if dbg is not None:
        src_ap = io[dbg]
        dout = nc.dram_tensor("dbg", list(src_ap.shape), src_ap.dtype, kind="ExternalOutput").ap()
        P = Prog(nc, "dbg_")
        b = Buf()
        rows = src_ap.shape[0]
        stp = max(1, rows // 8)
        for r0 in range(0, rows, stp):
            P.dma("sync", dout[r0:r0 + stp], src_ap[r0:r0 + stp], writes=[b])
        P.final_wait("sync", [b])
        P.emit()
    _POOL[0].close()
    _POOL[0] = None
    return nc


def core_inputs(d, c):
    b, q = c // 4, c % 4
    f32 = lambda a: np.ascontiguousarray(np.asarray(a, dtype=np.float32))
    lay = lambda g: f32(g.reshape(16, 128).T)
    rep = lambda v: f32(np.broadcast_to(np.asarray(v, np.float32), (128, v.shape[-1])))
    xf = d["x"].reshape(-1, D)
    im = {"xT": f32(xf[c * 1024:(c + 1) * 1024].T), "gf": lay(d["final_norm_g"])}
    sel = np.zeros((128, 4), np.float32); sel[:, q] = 1.0
    im["sel"] = sel
    im.update(p2a_consts())
    heads = C_SLOTS[q]
    im.update(p2c_consts(heads))
    im["pos"] = np.ascontiguousarray(d["positions"][b].reshape(32, 128).T.astype(np.int32))
    cA = slice(192 * q, 192 * q + 192)
    cols_rkv = np.r_[192 * q:192 * q + 192, 768 + 192 * q:768 + 192 * q + 192, 1536 + 192 * q:1536 + 192 * q + 192]
    c0 = 2752 + 1024
    cols_c = np.concatenate([np.r_[c0 + 64 * h:c0 + 64 * h + 64] for h in heads] + [np.r_[c0 + 384 + 64 * h:c0 + 384 + 64 * h + 64] for h in heads]
                            + [np.r_[c0 + 768 + 128 * h:c0 + 768 + 128 * h + 128] for h in heads]
                            + [np.r_[c0 + 1536 + 128 * h:c0 + 1536 + 128 * h + 128] for h in heads])
    im["v2c"] = f32(d["rwkv_v2"][0][:, cA])
    perm = np.full(KY * 128, -1, np.int64)
    for r in range(4):
        loc = np.full(NY, -1, np.int64)
        loc[0:192] = np.r_[192 * r:192 * r + 192]
        loc[192:320] = 768 + np.r_[128 * r:128 * r + 128]
        if r < 3:
            for s_, h in enumerate(C_SLOTS[r]):
                loc[320 + 128 * s_:448 + 128 * s_] = 1280 + np.r_[128 * h:128 * h + 128]
        for kk in range(6):
            perm[kk * 384 + r * 96:kk * 384 + (r + 1) * 96] = loc[96 * kk:96 * (kk + 1)]
    for l in range(2):
        w = d["w_in"][l]
        mu = d["tshift_mu"][l]
        im["g1_%d" % l] = lay(d["norm1_g"][l])
        im["wtm%d" % l] = f32(w[:, np.concatenate([cols_rkv, cols_c])])
        wfm = np.zeros((D, NFM), np.float32)
        wfm[:, 0:448] = w[:, 2304:2752]
        if l > 0:
            wfm[:, 448:512] = d["w_in_vres"][l - 1]
        A0 = 2752
        wfm[:, 512:640] = w[:, A0 + 128 * q:A0 + 128 * q + 128]
        wfm[:, 640:768] = w[:, A0 + 512 + 128 * q:A0 + 512 + 128 * q + 128]
        im["wfm%d" % l] = wfm
        im["mu_rkv%d" % l] = rep(mu[cols_rkv])
        lmu = np.zeros((128, 5), np.float32)
        lmu[:96, 0] = mu[2304:2400]; lmu[:96, 1] = mu[2400:2496]; lmu[:, 2] = mu[2496:2624]; lmu[:, 3] = mu[2624:2752]
        if l > 0:
            lmu[:64, 4] = d["tshift_mu_vres"][l - 1]
        im["lmu%d" % l] = lmu
        im["w2c%d" % l] = f32(d["rwkv_w2"][l][:, cA]); im["a2c%d" % l] = f32(d["rwkv_a2"][l][:, cA]); im["g2c%d" % l] = f32(d["rwkv_g2"][l][:, cA])
        v0 = d["rwkv_v0"][l - 1][cA] if l > 0 else np.zeros(192, np.float32)
        par = np.stack([d["rwkv_w0"][l][cA], d["rwkv_a0"][l][cA], v0, d["rwkv_k_k"][l][cA], d["rwkv_k_a"][l][cA],
                        d["rwkv_r_k"][l].reshape(-1)[cA], d["rwkv_ln_g"][l][cA], d["rwkv_ln_b"][l][cA]])
        im["par%d" % l] = f32(np.broadcast_to(par[None].astype(np.float32), (128, 8, 192)))
        ch = slice(128 * q, 128 * q + 128)
        cw = d["lru_conv_w"][l]
        im["sm%d" % l] = f32(np.stack([cw[0, ch], cw[1, ch], cw[2, ch], cw[3, ch], d["lru_conv_b"][l][ch], d["lru_ba"][l][ch],
                                       d["lru_bx"][l][ch], d["lru_lambda"][l][ch]], axis=1))
        im["wa%d" % l] = f32(d["lru_wa"][l][2 * q:2 * q + 2]); im["wx%d" % l] = f32(d["lru_wx"][l][2 * q:2 * q + 2])
        im["gng%d" % l] = f32(np.stack([np.broadcast_to(d["ret_gn_g"][l][128 * h:128 * h + 128], (128, 128)) for h in heads]))
        wo = np.zeros((KY * 128, D), np.float32)
        ok = perm >= 0
        wo[ok] = d["w_out"][l][perm[ok]]
        im["wout%d" % l] = wo
        im["g2_%d" % l] = lay(d["norm2_g"][l])
        im["wg%d" % l] = f32(d["ffn_w_gate"][l]); im["wu%d" % l] = f32(d["ffn_w_up"][l]); im["wd%d" % l] = f32(d["ffn_w_down"][l])
    return im


def kernel(**inputs):
    d = {k: np.asarray(v) for k, v in inputs.items()}
    Bn, Sn, Dn = d["x"].shape
    nc = build_fused()
    in_maps = [core_inputs(d, c) for c in range(8)]
    res = run_bass_kernel_spmd(nc, in_maps, core_ids=list(range(8)))
    out = np.concatenate([r["outT"].T for r in res.results], axis=0).reshape(Bn, Sn, Dn)
    return np.ascontiguousarray(out.astype(np.float32))
```
